# Optimizing a Trainium2 kernel written in Bass

```python
import math
import jax, jax.numpy as jnp
from jax import lax
import numpy as np

D_MODEL = 1024
BATCH = 8
SEQ = 4096
DEPTH = 2

HEAD_DIM = 64
GROUP_WIDTH = D_MODEL // 4
MIX_WIDTH = 4 * GROUP_WIDTH
N_GROUP_HEADS = GROUP_WIDTH // HEAD_DIM
DIFF_HEADS = N_GROUP_HEADS
DIFF_QK_DIM = HEAD_DIM // 2
MLA_HEADS = N_GROUP_HEADS
MLA_NOPE_DIM = HEAD_DIM
MLA_ROPE_DIM = HEAD_DIM // 2
MLA_V_DIM = HEAD_DIM
MLA_Q_RANK = MLA_HEADS * (MLA_NOPE_DIM + MLA_ROPE_DIM)
MLA_KV_RANK = 4 * MLA_V_DIM
GQA_HEADS = N_GROUP_HEADS
GQA_KV_HEADS = GQA_HEADS // 2
GRID_W = 64
DIL_HEADS = N_GROUP_HEADS
DIL_PAIRS = ((128, 1), (512, 4), (2048, 16))
Q_BLOCK = 128
MLP_HIDDEN = 4 * D_MODEL
ROPE_THETA = 10000.0
RMS_EPS = 1e-6
LN_EPS = 1e-5
NEG_INF = -1e30

kernel_name = "hybrid_parallel_head_group_encoder"


def _column_widths():
    a = DIFF_HEADS * 2 * DIFF_QK_DIM
    return [a, a, a,
            MLA_Q_RANK, MLA_KV_RANK, MLA_ROPE_DIM,
            GQA_HEADS * HEAD_DIM, GQA_KV_HEADS * HEAD_DIM, GQA_KV_HEADS * HEAD_DIM,
            DIL_HEADS * HEAD_DIM, DIL_HEADS * HEAD_DIM, DIL_HEADS * HEAD_DIM]


def _split_points():
    pts, acc = [], 0
    for w in _column_widths()[:-1]:
        acc += w
        pts.append(acc)
    return pts


def _alibi_slopes(n):
    return jnp.exp2(-8.0 * (jnp.arange(n, dtype=jnp.float32) + 1.0) / n)


def _rms(x, g):
    xf = x.astype(jnp.float32)
    y = xf * lax.rsqrt(jnp.mean(jnp.square(xf), -1, keepdims=True) + RMS_EPS)
    return y.astype(x.dtype) * g


def _layer_norm(x, g, b):
    xf = x.astype(jnp.float32)
    mu = jnp.mean(xf, -1, keepdims=True)
    var = jnp.mean(jnp.square(xf - mu), -1, keepdims=True)
    return ((xf - mu) * lax.rsqrt(var + LN_EPS)).astype(x.dtype) * g + b


def _rope(x, pos):
    half = x.shape[-1] // 2
    freqs = ROPE_THETA ** (-jnp.arange(half, dtype=jnp.float32) / half)
    ang = pos[:, None] * freqs[None, :]
    cos = jnp.cos(ang)[None, :, None, :]
    sin = jnp.sin(ang)[None, :, None, :]
    xf = x.astype(jnp.float32)
    x1, x2 = xf[..., :half], xf[..., half:]
    return jnp.concatenate([x1 * cos - x2 * sin, x2 * cos + x1 * sin], -1).astype(x.dtype)


def _axial_rope(x, row, col):
    half = x.shape[-1] // 2
    return jnp.concatenate([_rope(x[..., :half], row), _rope(x[..., half:], col)], -1)


def _sweep_query_blocks(fn, *q_arrays):
    bsz, seq = q_arrays[0].shape[:2]
    nb = seq // Q_BLOCK

    def to_blocks(a):
        return jnp.swapaxes(a.reshape(bsz, nb, Q_BLOCK, *a.shape[2:]), 0, 1)

    qpos = jnp.arange(seq, dtype=jnp.int32).reshape(nb, Q_BLOCK)
    out = lax.map(lambda args: fn(*args), (qpos,) + tuple(to_blocks(a) for a in q_arrays))
    out = jnp.swapaxes(out, 0, 1)
    return out.reshape(bsz, seq, *out.shape[3:])


def _diff_attention(q, k, v, lam, subln_g, lambda_init, slopes):
    bsz, seq = q.shape[:2]
    kpos = jnp.arange(seq, dtype=jnp.int32)
    scale = DIFF_QK_DIM ** -0.5

    def block(qpos, qb):
        s = jnp.einsum('bqhcd,bkhcd->bhcqk', qb, k).astype(jnp.float32) * scale
        dist = jnp.abs(qpos[:, None] - kpos[None, :]).astype(jnp.float32)
        s = s - slopes[:, None, None, None] * dist
        p = jax.nn.softmax(s, axis=-1)
        a = p[:, :, 0] - lam * p[:, :, 1]
        return jnp.einsum('bhqk,bkhd->bqhd', a.astype(v.dtype), v)

    o = _sweep_query_blocks(block, q)
    o = _rms(o, subln_g) * (1.0 - lambda_init)
    return o.reshape(bsz, seq, -1)


def _mla(cq, ckv, k_rope, q_norm_g, w_uq, kv_norm_g, w_ukv, pos):
    bsz, seq = cq.shape[:2]
    q = (_rms(cq, q_norm_g) @ w_uq).reshape(bsz, seq, MLA_HEADS, MLA_NOPE_DIM + MLA_ROPE_DIM)
    q_nope, q_pe = q[..., :MLA_NOPE_DIM], _rope(q[..., MLA_NOPE_DIM:], pos)
    kv = (_rms(ckv, kv_norm_g) @ w_ukv).reshape(bsz, seq, MLA_HEADS, MLA_NOPE_DIM + MLA_V_DIM)
    k_nope, v = kv[..., :MLA_NOPE_DIM], kv[..., MLA_NOPE_DIM:]
    k_pe = _rope(k_rope[:, :, None, :], pos)[:, :, 0, :]
    scale = (MLA_NOPE_DIM + MLA_ROPE_DIM) ** -0.5

    def block(qpos, qn, qp):
        s = (jnp.einsum('bqhd,bkhd->bhqk', qn, k_nope)
             + jnp.einsum('bqhd,bkd->bhqk', qp, k_pe)).astype(jnp.float32) * scale
        p = jax.nn.softmax(s, axis=-1)
        return jnp.einsum('bhqk,bkhd->bqhd', p.astype(v.dtype), v)

    return _sweep_query_blocks(block, q_nope, q_pe).reshape(bsz, seq, -1)


def _gqa_axial(q, k, v, q_norm_g, k_norm_g, row, col):
    bsz, seq = q.shape[:2]
    q = _axial_rope(_rms(q, q_norm_g), row, col)
    k = _axial_rope(_rms(k, k_norm_g), row, col)
    q = q.reshape(bsz, seq, GQA_KV_HEADS, GQA_HEADS // GQA_KV_HEADS, HEAD_DIM)
    scale = HEAD_DIM ** -0.5

    def block(qpos, qb):
        s = jnp.einsum('bqngd,bknd->bngqk', qb, k).astype(jnp.float32) * scale
        p = jax.nn.softmax(s, axis=-1)
        return jnp.einsum('bngqk,bknd->bqngd', p.astype(v.dtype), v)

    return _sweep_query_blocks(block, q).reshape(bsz, seq, -1)


def _dilated_attention(q, k, v, slopes):
    bsz, seq = q.shape[:2]
    pad = max((w // (2 * d)) * d for w, d in DIL_PAIRS)
    kp = jnp.pad(k, ((0, 0), (pad, pad), (0, 0), (0, 0)))
    vp = jnp.pad(v, ((0, 0), (pad, pad), (0, 0), (0, 0)))
    scale = HEAD_DIM ** -0.5

    def block(qpos, qb):
        outs, lses = [], []
        for w, d in DIL_PAIRS:
            r = w // (2 * d)
            offs = jnp.arange(-r, r + 1, dtype=jnp.int32) * d
            kidx = qpos[:, None] + offs[None, :]
            valid = (kidx >= 0) & (kidx < seq)
            kg = jnp.take(kp, kidx + pad, axis=1)
            vg = jnp.take(vp, kidx + pad, axis=1)
            s = jnp.einsum('bqhd,bqkhd->bhqk', qb, kg).astype(jnp.float32) * scale
            s = s - slopes[:, None, None] * jnp.abs(offs).astype(jnp.float32)[None, None, :]
            s = jnp.where(valid, s, NEG_INF)
            lse = jax.nn.logsumexp(s, axis=-1, keepdims=True)
            p = jnp.exp(s - lse)
            outs.append(jnp.einsum('bhqk,bqkhd->bqhd', p.astype(vg.dtype), vg))
            lses.append(lse[..., 0])
        wts = jax.nn.softmax(jnp.stack(lses, -1), axis=-1)
        wts = jnp.transpose(wts, (0, 2, 1, 3)).astype(qb.dtype)
        return jnp.einsum('bqhdn,bqhn->bqhd', jnp.stack(outs, -1), wts)

    return _sweep_query_blocks(block, q).reshape(bsz, seq, -1)


def _token_mixers(h, lambda_init, w_in, w_o, diff_lambda, diff_subln_g, mla_q_norm_g, mla_w_uq,
                  mla_kv_norm_g, mla_w_ukv, gqa_q_norm_g, gqa_k_norm_g, pos, row, col,
                  slopes_a, slopes_d):
    bsz, seq, _ = h.shape
    proj = h @ w_in
    (a_q, a_k, a_v, b_cq, b_ckv, b_kr, c_q, c_k, c_v, d_q, d_k, d_v) = jnp.split(
        proj, _split_points(), axis=-1)
    lf = diff_lambda.astype(jnp.float32)
    lam = jnp.exp(jnp.sum(lf[0] * lf[1])) - jnp.exp(jnp.sum(lf[2] * lf[3])) + lambda_init
    y_a = _diff_attention(a_q.reshape(bsz, seq, DIFF_HEADS, 2, DIFF_QK_DIM),
                          a_k.reshape(bsz, seq, DIFF_HEADS, 2, DIFF_QK_DIM),
                          a_v.reshape(bsz, seq, DIFF_HEADS, 2 * DIFF_QK_DIM),
                          lam, diff_subln_g, lambda_init, slopes_a)
    y_b = _mla(b_cq, b_ckv, b_kr, mla_q_norm_g, mla_w_uq, mla_kv_norm_g, mla_w_ukv, pos)
    y_c = _gqa_axial(c_q.reshape(bsz, seq, GQA_HEADS, HEAD_DIM),
                     c_k.reshape(bsz, seq, GQA_KV_HEADS, HEAD_DIM),
                     c_v.reshape(bsz, seq, GQA_KV_HEADS, HEAD_DIM),
                     gqa_q_norm_g, gqa_k_norm_g, row, col)
    y_d = _dilated_attention(d_q.reshape(bsz, seq, DIL_HEADS, HEAD_DIM),
                             d_k.reshape(bsz, seq, DIL_HEADS, HEAD_DIM),
                             d_v.reshape(bsz, seq, DIL_HEADS, HEAD_DIM), slopes_d)
    return jnp.concatenate([y_a, y_b, y_c, y_d], axis=-1) @ w_o


def setup_inputs(seed: int = 0) -> dict:
    key = jax.random.key(seed)
    ks = jax.random.split(key, 22)
    beta = (8 * DEPTH) ** -0.25
    L = DEPTH
    in_cols = sum(_column_widths())

    def nrm(k, shape, scale):
        return jax.random.normal(k, shape, jnp.float32) * scale

    return dict(
        x=nrm(ks[0], (BATCH, SEQ, D_MODEL), 1.0),
        c=nrm(ks[1], (BATCH, D_MODEL), 1.0),
        w_ada=nrm(ks[2], (L, D_MODEL, 6 * D_MODEL), 0.5 * D_MODEL ** -0.5),
        b_ada=nrm(ks[3], (L, 6 * D_MODEL), 0.02),
        w_in=nrm(ks[4], (L, D_MODEL, in_cols), D_MODEL ** -0.5),
        w_o=nrm(ks[5], (L, MIX_WIDTH, D_MODEL), beta * MIX_WIDTH ** -0.5),
        diff_lambda=nrm(ks[6], (L, 4, DIFF_QK_DIM), 0.1),
        diff_subln_g=1.0 + nrm(ks[7], (L, 2 * DIFF_QK_DIM), 0.02),
        mla_q_norm_g=1.0 + nrm(ks[8], (L, MLA_Q_RANK), 0.02),
        mla_w_uq=nrm(ks[9], (L, MLA_Q_RANK, MLA_HEADS * (MLA_NOPE_DIM + MLA_ROPE_DIM)), MLA_Q_RANK ** -0.5),
        mla_kv_norm_g=1.0 + nrm(ks[10], (L, MLA_KV_RANK), 0.02),
        mla_w_ukv=nrm(ks[11], (L, MLA_KV_RANK, MLA_HEADS * (MLA_NOPE_DIM + MLA_V_DIM)), MLA_KV_RANK ** -0.5),
        gqa_q_norm_g=1.0 + nrm(ks[12], (L, HEAD_DIM), 0.02),
        gqa_k_norm_g=1.0 + nrm(ks[13], (L, HEAD_DIM), 0.02),
        ln_attn_g=1.0 + nrm(ks[14], (L, D_MODEL), 0.02),
        ln_attn_b=nrm(ks[15], (L, D_MODEL), 0.02),
        w_up=nrm(ks[16], (L, D_MODEL, MLP_HIDDEN), D_MODEL ** -0.5),
        w_down=nrm(ks[17], (L, MLP_HIDDEN, D_MODEL), beta * MLP_HIDDEN ** -0.5),
        ln_mlp_g=1.0 + nrm(ks[18], (L, D_MODEL), 0.02),
        ln_mlp_b=nrm(ks[19], (L, D_MODEL), 0.02),
    )


def reference(x, c, w_ada, b_ada, w_in, w_o, diff_lambda, diff_subln_g, mla_q_norm_g, mla_w_uq,
              mla_kv_norm_g, mla_w_ukv, gqa_q_norm_g, gqa_k_norm_g, ln_attn_g, ln_attn_b,
              w_up, w_down, ln_mlp_g, ln_mlp_b):
    alpha = (2 * DEPTH) ** 0.25
    seq = x.shape[1]
    rows = seq // GRID_W
    pos = jnp.arange(seq, dtype=jnp.float32)
    row = jnp.repeat(jnp.arange(rows, dtype=jnp.float32), GRID_W)
    col = jnp.tile(jnp.arange(GRID_W, dtype=jnp.float32), rows)
    slopes = _alibi_slopes(DIFF_HEADS + DIL_HEADS)
    slopes_a, slopes_d = slopes[0::2], slopes[1::2]
    cond = jax.nn.silu(c)
    for l in range(DEPTH):
        lambda_init = 0.8 - 0.6 * math.exp(-0.3 * l)
        mod = cond @ w_ada[l] + b_ada[l]
        sh_a, sc_a, g_a, sh_m, sc_m, g_m = [m[:, None, :] for m in jnp.split(mod, 6, axis=-1)]
        h = x * (1.0 + sc_a) + sh_a
        y = _token_mixers(h, lambda_init, w_in[l], w_o[l], diff_lambda[l], diff_subln_g[l],
                          mla_q_norm_g[l], mla_w_uq[l], mla_kv_norm_g[l], mla_w_ukv[l],
                          gqa_q_norm_g[l], gqa_k_norm_g[l], pos, row, col, slopes_a, slopes_d)
        x = _layer_norm(alpha * x + g_a * y, ln_attn_g[l], ln_attn_b[l])
        h = x * (1.0 + sc_m) + sh_m
        u = jnp.square(jax.nn.relu(h @ w_up[l])) @ w_down[l]
        x = _layer_norm(alpha * x + g_m * u, ln_mlp_g[l], ln_mlp_b[l])
    return x
```

```python
import math
from contextlib import ExitStack
import numpy as np
import concourse.bass as bass
import concourse.mybir as mybir
from concourse.bass_utils import run_bass_kernel_spmd

F32 = mybir.dt.float32
BF16 = mybir.dt.bfloat16
AF = mybir.ActivationFunctionType
ALU = mybir.AluOpType
AX = mybir.AxisListType

SEQ = 4096
D = 1024
NSUB = SEQ // 128
HID = 4096
INC = 2720
ALPHA = 4 ** 0.25
SLOPES_A = [2.0 ** -1, 2.0 ** -3, 2.0 ** -5, 2.0 ** -7]
SLOPES_D = [2.0 ** -2, 2.0 ** -4, 2.0 ** -6, 2.0 ** -8]
DILS = [1, 4, 16]
SC_A = 32 ** -0.5
SC_B = 96 ** -0.5
SC_C = 0.125
SC_D = 0.125
VP = 66
VW = 4 * VP


class Buf:
    __slots__ = ("name", "w", "r", "excl")

    def __init__(self, name="", excl=False):
        self.name = name
        self.w = None
        self.r = []
        self.excl = excl


def PB(name=""):
    return Buf(name, excl=True)


class Tok:
    __slots__ = ("eng", "sem", "val")

    def __init__(self, eng, sem=None, val=None):
        self.eng = eng
        self.sem = sem
        self.val = val


class Sched:
    EPOCH = 20000
    NDMA = 8

    def __init__(self, nc, stack):
        self.nc = nc
        self.stack = stack
        self.engs = {"pe": nc.tensor, "act": nc.scalar, "dve": nc.vector,
                     "pool": nc.gpsimd, "sp": nc.sync}
        self.count = {e: 0 for e in self.engs}
        self.cursem = {}
        self.pending = {e: [] for e in self.engs}
        self.waited = {e: {} for e in self.engs}
        self.nsem = 0
        for e in self.engs:
            self._new_epoch(e)
        self.dma_sems, self.dma_cnt, self.dma_last, self.dma_i = {}, {}, {}, {}
        for q in ("sp", "act", "pool"):
            self.dma_sems[q] = [self._sem(f"dma_{q}_{i}") for i in range(self.NDMA)]
            self.dma_cnt[q] = [0] * self.NDMA
            self.dma_last[q] = [None] * self.NDMA
            self.dma_i[q] = 0
        self.n_ops = {e: 0 for e in self.engs}
        self.n_dma = 0

    def _sem(self, name):
        self.nsem += 1
        return self.stack.enter_context(self.nc.semaphore(name))

    def _new_epoch(self, e):
        self.cursem[e] = self._sem(f"s_{e}_{self.nsem}")
        self.count[e] = 0

    def _wait(self, eng, tok):
        if tok is None:
            return
        if tok.sem is None:
            raise RuntimeError(f"dependency on unsignalled op on {tok.eng}")
        key = id(tok.sem)
        w = self.waited[eng]
        if w.get(key, 0) >= tok.val:
            return
        w[key] = tok.val
        self.engs[eng].wait_ge(tok.sem, tok.val)

    def _deps(self, eng, reads, writes):
        for b in reads:
            t = b.w
            if t is not None and not (t.eng == eng and eng == "pe"):
                self._wait(eng, t)
        for b in writes:
            t = b.w
            if t is not None and t.eng != eng:
                self._wait(eng, t)
            for t in b.r:
                if t.eng != eng:
                    self._wait(eng, t)

    def _record(self, tok, reads, writes):
        for b in reads:
            b.r.append(tok)
            if len(b.r) > 16:
                last = {}
                for t in b.r:
                    last[(t.eng, id(t.sem))] = t
                b.r = list(last.values())
        for b in writes:
            b.w = tok
            b.r = []

    def op(self, eng, fn, reads=(), writes=(), signal=True):
        if any(b.excl for b in reads):
            writes = list(writes) + [b for b in reads if b.excl]
            reads = [b for b in reads if not b.excl]
        self._deps(eng, reads, writes)
        inst = fn()
        self.n_ops[eng] += 1
        tok = Tok(eng)
        self.pending[eng].append(tok)
        if signal:
            if self.count[eng] >= self.EPOCH:
                self._new_epoch(eng)
            self.count[eng] += 1
            sem = self.cursem[eng]
            inst.then_inc(sem, 1)
            for t in self.pending[eng]:
                t.sem = sem
                t.val = self.count[eng]
            self.pending[eng] = []
        self._record(tok, reads, writes)
        return tok

    def prewait(self, eng, reads=(), writes=()):
        writes = list(writes) + [b for b in reads if b.excl]
        reads = [b for b in reads if not b.excl]
        self._deps(eng, reads, writes)

    def dma(self, q, out, in_, reads=(), writes=(), **kw):
        i = self.dma_i[q]
        self.dma_i[q] = (i + 1) % self.NDMA
        prev = self.dma_last[q][i]
        if prev is not None:
            self._wait(q, prev)
        self._deps(q, reads, writes)
        sem = self.dma_sems[q][i]
        self.dma_cnt[q][i] += 16
        inst = self.engs[q].dma_start(out=out, in_=in_, **kw)
        inst.then_inc(sem, 16)
        tok = Tok("dma_" + q + str(i), sem, self.dma_cnt[q][i])
        self.dma_last[q][i] = tok
        self._record(tok, reads, writes)
        self.n_dma += 1
        return tok

    def barrier(self):
        toks = []
        for e in self.engs:
            if self.pending[e]:
                raise RuntimeError(f"barrier with unsignalled ops on {e}")
            if self.count[e] > 0:
                toks.append(Tok(e, self.cursem[e], self.count[e]))
        for q in self.dma_last:
            for t in self.dma_last[q]:
                if t is not None:
                    toks.append(t)
        for e in self.engs:
            for t in toks:
                if t.eng != e:
                    self._wait(e, t)


def _host_consts():
    c = {}
    c["ident"] = np.eye(128, dtype=np.float32)
    tok = (np.arange(NSUB)[None, :] * 128 + np.arange(128)[:, None]).astype(np.float64)
    freqs = 10000.0 ** (-np.arange(16, dtype=np.float64) / 16)

    def cs(pos):
        ang = pos[..., None].astype(np.float32).astype(np.float64) * freqs.astype(np.float32).astype(np.float64)
        ang = (pos[..., None].astype(np.float32) * freqs.astype(np.float32)).astype(np.float32)
        return np.cos(ang.astype(np.float64)), np.sin(ang.astype(np.float64))

    cp, sp_ = cs(tok)
    rb = np.zeros((128, NSUB, 2, 64), np.float64)
    for i, s in enumerate([SC_B, 1.0]):
        rb[:, :, i, 0:16] = cp * s
        rb[:, :, i, 16:32] = cp * s
        rb[:, :, i, 32:48] = -sp_ * s
        rb[:, :, i, 48:64] = sp_ * s
    c["ropeB"] = rb.astype(np.float32)
    cr, sr = cs(np.floor(tok / 64))
    cc, sc_ = cs(np.mod(tok, 64))
    rc = np.zeros((128, NSUB, 128), np.float64)
    rc[:, :, 0:16] = cr
    rc[:, :, 16:32] = cr
    rc[:, :, 32:48] = cc
    rc[:, :, 48:64] = cc
    rc[:, :, 64:80] = -sr
    rc[:, :, 80:96] = sr
    rc[:, :, 96:112] = -sc_
    rc[:, :, 112:128] = sc_
    c["ropeC"] = rc.astype(np.float32)
    ki = np.arange(128)[:, None].astype(np.float64)
    qi = np.arange(512)[None, :].astype(np.float64)
    al = np.zeros((128, 5, 512), np.float64)
    al[:, 0, :] = qi - ki
    for o in range(4):
        al[:, 1 + o, :] = np.abs(qi - ki - 128 * o)
    c["alibi"] = al.astype(np.float32)
    q128 = (np.arange(512) % 128)[None, :].astype(np.float64)
    dt = np.zeros((128, 2, 512), np.float64)
    da = np.abs(ki - 64 - q128)
    db = np.abs(ki + 64 - q128)
    dt[:, 0, :] = np.where(da <= 64, da, 1.0e6)
    dt[:, 1, :] = np.where(db <= 64, db, 1.0e6)
    c["dtab"] = dt.astype(np.float32)
    return c


W_NAMES = ["w_ada", "b_ada", "w_in", "w_o", "diff_lambda", "diff_subln_g", "mla_q_norm_g", "mla_w_uq",
           "mla_kv_norm_g", "mla_w_ukv", "gqa_q_norm_g", "gqa_k_norm_g", "ln_attn_g", "ln_attn_b",
           "w_up", "w_down", "ln_mlp_g", "ln_mlp_b"]
W_SHAPES = {"w_ada": [2, 1024, 6144], "b_ada": [2, 6144], "w_in": [2, 1024, INC], "w_o": [2, 1024, 1024],
            "diff_lambda": [2, 128], "diff_subln_g": [2, 64], "mla_q_norm_g": [2, 384],
            "mla_w_uq": [2, 384, 384], "mla_kv_norm_g": [2, 256], "mla_w_ukv": [2, 256, 512],
            "gqa_q_norm_g": [2, 64], "gqa_k_norm_g": [2, 64], "ln_attn_g": [2, 1024], "ln_attn_b": [2, 1024],
            "w_up": [2, 1024, HID], "w_down": [2, HID, 1024], "ln_mlp_g": [2, 1024], "ln_mlp_b": [2, 1024]}
C_SHAPES = {"ident": [128, 128], "ropeB": [128, NSUB, 2, 64], "ropeC": [128, NSUB, 128],
            "alibi": [128, 5, 512], "dtab": [128, 2, 512]}


def build(nlayers=2, dbg=()):
    nc = bass.Bass("TRN2", target_bir_lowering=False)
    I = {}
    I["x"] = nc.dram_tensor("x", [SEQ, D], F32, kind="ExternalInput").ap()
    I["c"] = nc.dram_tensor("c", [128, 8], F32, kind="ExternalInput").ap()
    for n in W_NAMES:
        I[n] = nc.dram_tensor(n, W_SHAPES[n], F32, kind="ExternalInput").ap()
    for n in C_SHAPES:
        I[n] = nc.dram_tensor("k_" + n, C_SHAPES[n], F32, kind="ExternalInput").ap()
    out = nc.dram_tensor("out", [SEQ, D], F32, kind="ExternalOutput").ap()

    def scratch(name, shape, dt):
        kind = "ExternalOutput" if name in dbg else "Internal"
        return nc.dram_tensor(name, shape, dt, kind=kind).ap()

    QKT = scratch("QKT", [15, 128, SEQ], BF16)
    VG = scratch("VG", [3, SEQ, VW], BF16)
    DTOK = scratch("DTOK", [SEQ, 512 + VW], BF16)
    OD = scratch("OD", [3, SEQ, 260], F32)
    X1 = scratch("X1", [SEQ, D], F32)
    H2T = scratch("H2T", [8, 128, SEQ], BF16)
    XN = scratch("XN", [SEQ, D], F32)
    GB = scratch("GB", [2, 2, 128, D], F32)
    YDBG = scratch("YDBG", [SEQ, D], BF16) if "YDBG" in dbg else None

    with ExitStack() as top:
        S = Sched(nc, top)

        uid = [0]

        def sb(st, name, shape, dt):
            uid[0] += 1
            return st.enter_context(nc.sbuf_tensor(f"s{uid[0]}_{name}", shape, dt))

        def pbank(st, name):
            uid[0] += 1
            return st.enter_context(nc.psum_tensor(f"p{uid[0]}_{name}", [128, 512], F32))

        def V(fn, reads, writes):
            return S.op("dve", fn, reads, writes)

        def A(fn, reads, writes):
            return S.op("act", fn, reads, writes)

        def G(fn, reads, writes):
            return S.op("pool", fn, reads, writes)

        def PE(fn, reads, writes, signal=True):
            return S.op("pe", fn, reads, writes, signal)

        ident_f = sb(top, "ident_f", [128, 128], F32)
        ident_b = sb(top, "ident_b", [128, 128], BF16)
        modT = sb(top, "modT", [128, 2, 48], F32)
        eps6 = sb(top, "eps6", [128, 1], F32)
        eps5 = sb(top, "eps5", [128, 1], F32)
        b_const = Buf("const")
        b_modT = Buf("modT")
        S.dma("sp", ident_f[:], I["ident"], writes=[b_const])
        S.dma("pool", ident_b[:], I["ident"], writes=[b_const])
        V(lambda: nc.vector.memset(eps6[:], 1e-6), [], [b_const])
        V(lambda: nc.vector.memset(eps5[:], 1e-5), [], [b_const])

        with ExitStack() as st:
            condT = sb(st, "condT", [128, 8], F32)
            ones_row = sb(st, "ones_row", [1, 128], F32)
            modrow = sb(st, "modrow", [1, 6144], F32)
            brow = sb(st, "brow", [1, 6144], F32)
            wa = [sb(st, f"wa{i}", [128, 8, 512], F32) for i in range(2)]
            gbt = sb(st, "gbt", [128, 1024], F32)
            ps = [pbank(st, f"sps{i}") for i in range(2)]
            b_cond, b_ones, b_mrow, b_brow, b_gbt = Buf(), Buf(), Buf(), Buf(), Buf()
            b_wa = [Buf(), Buf()]
            b_ps = [PB(), PB()]
            S.dma("sp", condT[:], I["c"], writes=[b_cond])
            A(lambda: nc.scalar.activation(out=condT[:], in_=condT[:], func=AF.Silu), [b_cond], [b_cond])
            V(lambda: nc.vector.memset(ones_row[:], 1.0), [], [b_ones])
            for l in range(nlayers):
                S.dma("sp", brow[:], I["b_ada"][l:l + 1, :], writes=[b_brow])
                wsrc = I["w_ada"][l].rearrange("(kc p) n -> p kc n", p=128)
                for pc in range(12):
                    S.dma("sp", wa[pc % 2][:], wsrc[:, :, pc * 512:(pc + 1) * 512], writes=[b_wa[pc % 2]])
                    for kc in range(8):
                        PE(lambda: nc.tensor.matmul(ps[pc % 2][0:1, :], lhsT=condT[:, kc:kc + 1], rhs=wa[pc % 2][:, kc, :],
                                                    start=(kc == 0), stop=(kc == 7)),
                           [b_cond, b_wa[pc % 2]], [b_ps[pc % 2]], signal=(kc == 7))
                    V(lambda: nc.vector.tensor_tensor(out=modrow[0:1, pc * 512:(pc + 1) * 512], in0=ps[pc % 2][0:1, :],
                                                      in1=brow[0:1, pc * 512:(pc + 1) * 512], op=ALU.add),
                      [b_ps[pc % 2], b_brow], [b_mrow])
                for j in range(48):
                    PE(lambda: nc.tensor.matmul(ps[0][:, j:j + 1], lhsT=modrow[0:1, j * 128:(j + 1) * 128],
                                                rhs=ones_row[0:1, 0:1], start=True, stop=True),
                       [b_mrow, b_ones], [b_ps[0]], signal=(j == 47))
                V(lambda: nc.vector.tensor_copy(modT[:, l, :], ps[0][:, 0:48]), [b_ps[0]], [b_modT])
                V(lambda: nc.vector.tensor_scalar(out=modT[:, l, 8:16], in0=modT[:, l, 8:16], scalar1=1.0, scalar2=None,
                                                  op0=ALU.add), [b_modT], [b_modT])
                V(lambda: nc.vector.tensor_scalar(out=modT[:, l, 32:40], in0=modT[:, l, 32:40], scalar1=1.0, scalar2=None,
                                                  op0=ALU.add), [b_modT], [b_modT])
                for gi, base in enumerate([2048, 5120]):
                    for hf in range(2):
                        PE(lambda: nc.tensor.matmul(ps[1][:, :], lhsT=ones_row[0:1, :],
                                                    rhs=modrow[0:1, base + hf * 512: base + (hf + 1) * 512],
                                                    start=True, stop=True), [b_mrow, b_ones], [b_ps[1]])
                        V(lambda: nc.vector.tensor_copy(gbt[:, hf * 512:(hf + 1) * 512], ps[1][:, :]), [b_ps[1]], [b_gbt])
                    S.dma("sp", GB[l, gi], gbt[:], reads=[b_gbt])
            S.barrier()

        def phase_P(l, xsrc):
            with ExitStack() as st:
                w_in = sb(st, "w_in", [128, 8, INC], BF16)
                w_uq = sb(st, "w_uq", [128, 3, 384], BF16)
                w_ukv = sb(st, "w_ukv", [128, 2, 512], BF16)
                gq_bc = sb(st, "gq_bc", [128, 384], F32)
                gkv_bc = sb(st, "gkv_bc", [128, 256], F32)
                gC_bc = sb(st, "gC_bc", [128, 6, 64], F32)
                ropeB = sb(st, "ropeB", [128, NSUB, 2, 64], F32)
                ropeC = sb(st, "ropeC", [128, NSUB, 128], F32)
                xs = [sb(st, f"xs{i}", [128, D], F32) for i in range(2)]
                hT = [sb(st, f"hT{i}", [128, 8, 512], BF16) for i in range(2)]
                bf32_l = [sb(st, f"bf32{i}", [128, 672], F32) for i in range(2)]
                cqk_l = [sb(st, f"cqk{i}", [128, 384], F32) for i in range(2)]
                junk_l = [sb(st, f"junk{i}", [128, 384], F32) for i in range(2)]
                qkA_l = [sb(st, f"qkA{i}", [128, 512], BF16) for i in range(2)]
                qkD_l = [sb(st, f"qkD{i}", [128, 512], BF16) for i in range(2)]
                VA_l = [sb(st, f"VA{i}", [128, 4, VP], BF16) for i in range(2)]
                VB_l = [sb(st, f"VB{i}", [128, 4, VP], BF16) for i in range(2)]
                VC_l = [sb(st, f"VC{i}", [128, 4, VP], BF16) for i in range(2)]
                VD_l = [sb(st, f"VD{i}", [128, 4, VP], BF16) for i in range(2)]
                cqn_l = [sb(st, f"cqn{i}", [128, 640], BF16) for i in range(2)]
                cT_l = [sb(st, f"cT{i}", [128, 5, 128], BF16) for i in range(2)]
                qB_l = [sb(st, f"qB{i}", [128, 4, 96], BF16) for i in range(2)]
                kB_l = [sb(st, f"kB{i}", [128, 4, 96], BF16) for i in range(2)]
                cC_l = [sb(st, f"cC{i}", [128, 384], BF16) for i in range(2)]
                t1_l = [sb(st, f"t1{i}", [128, 384], F32) for i in range(2)]
                t2_l = [sb(st, f"t2{i}", [128, 384], F32) for i in range(2)]
                nq_l = [sb(st, f"nq{i}", [128, 384], F32) for i in range(2)]
                stt__l = [sb(st, f"stt_{i}", [128, 16], F32) for i in range(2)]
                stage = [sb(st, f"stage{i}", [128, 15, 512], BF16) for i in range(2)]
                pb = [pbank(st, f"pp{i}") for i in range(8)]
                bp = [PB(f"pp{i}") for i in range(8)]
                b_w, b_tab = Buf(), Buf()
                b_xs = [Buf(), Buf()]
                b_hT = [Buf(), Buf()]
                BL = {n: [Buf(n + '0'), Buf(n + '1')] for n in ['bf32', 'cqk', 'junk', 'qkA', 'qkD', 'VA', 'VB', 'VC', 'VD', 'cqn', 'cT', 'qB', 'kB', 'cC', 't1', 't2', 'nq', 'stt_']}
                b_wk = [Buf() for _ in range(8)]
                b_hk = [[Buf() for _ in range(8)] for _ in range(2)]
                b_stage = [Buf(), Buf()]

                wsrc = I["w_in"][l].rearrange("(kc p) n -> p kc n", p=128)
                for kc in range(8):
                    S.dma("pool", w_in[:, kc, :], wsrc[:, kc, :], writes=[b_wk[kc]])
                S.dma("pool", w_uq[:], I["mla_w_uq"][l].rearrange("(kc p) n -> p kc n", p=128), writes=[b_w])
                S.dma("pool", w_ukv[:], I["mla_w_ukv"][l].rearrange("(kc p) n -> p kc n", p=128), writes=[b_w])
                S.dma("sp", gq_bc[:], I["mla_q_norm_g"][l, :].partition_broadcast(128), writes=[b_tab])
                S.dma("sp", gkv_bc[:], I["mla_kv_norm_g"][l, :].partition_broadcast(128), writes=[b_tab])
                for h in range(6):
                    src = I["gqa_q_norm_g"] if h < 4 else I["gqa_k_norm_g"]
                    S.dma("sp", gC_bc[:, h, :], src[l, :].partition_broadcast(128), writes=[b_tab])
                V(lambda: nc.vector.tensor_scalar(out=gC_bc[:, 0:4, :], in0=gC_bc[:, 0:4, :], scalar1=SC_C, scalar2=None,
                                                  op0=ALU.mult), [b_tab], [b_tab])
                S.dma("sp", ropeB[:], I["ropeB"], writes=[b_tab])
                S.dma("sp", ropeC[:], I["ropeC"], writes=[b_tab])
                for i_ in range(2):
                    for n_ in ('VA', 'VB', 'VC', 'VD'):
                        vt = {'VA': VA_l, 'VB': VB_l, 'VC': VC_l, 'VD': VD_l}[n_][i_]
                        V(lambda: nc.vector.memset(vt[:], 1.0), [], [BL[n_][i_]])

                groups = [(0, 512), (512, 512), (1024, 416), (1440, 512), (1952, 512), (2464, 256)]
                def stage1(sg):
                    T, s = sg // 4, sg % 4
                    tsl = slice(s * 128, (s + 1) * 128)
                    h_t, bhk = hT[T % 2], b_hk[T % 2]
                    stg, bstg = stage[T % 2], b_stage[T % 2]
                    bf32 = bf32_l[sg % 2]
                    b_bf32 = BL['bf32'][sg % 2]
                    cqk = cqk_l[sg % 2]
                    b_cqk = BL['cqk'][sg % 2]
                    junk = junk_l[sg % 2]
                    b_junk = BL['junk'][sg % 2]
                    qkA = qkA_l[sg % 2]
                    b_qkA = BL['qkA'][sg % 2]
                    qkD = qkD_l[sg % 2]
                    b_qkD = BL['qkD'][sg % 2]
                    VA = VA_l[sg % 2]
                    b_VA = BL['VA'][sg % 2]
                    VB = VB_l[sg % 2]
                    b_VB = BL['VB'][sg % 2]
                    VC = VC_l[sg % 2]
                    b_VC = BL['VC'][sg % 2]
                    VD = VD_l[sg % 2]
                    b_VD = BL['VD'][sg % 2]
                    cqn = cqn_l[sg % 2]
                    b_cqn = BL['cqn'][sg % 2]
                    cT = cT_l[sg % 2]
                    b_cT = BL['cT'][sg % 2]
                    qB = qB_l[sg % 2]
                    b_qB = BL['qB'][sg % 2]
                    kB = kB_l[sg % 2]
                    b_kB = BL['kB'][sg % 2]
                    cC = cC_l[sg % 2]
                    b_cC = BL['cC'][sg % 2]
                    t1 = t1_l[sg % 2]
                    b_t1 = BL['t1'][sg % 2]
                    t2 = t2_l[sg % 2]
                    b_t2 = BL['t2'][sg % 2]
                    nq = nq_l[sg % 2]
                    b_nq = BL['nq'][sg % 2]
                    stt_ = stt__l[sg % 2]
                    b_st = BL['stt_'][sg % 2]
                    x_t, bx = xs[sg % 2], b_xs[sg % 2]
                    S.dma("sp", x_t[:], xsrc[sg * 128:(sg + 1) * 128, :], writes=[bx])
                    for hf in range(2):
                        for cc in range(4):
                            kc = hf * 4 + cc
                            PE(lambda: nc.tensor.transpose(pb[hf][:, cc * 128:(cc + 1) * 128], x_t[:, kc * 128:(kc + 1) * 128],
                                                           ident_f[:]), [bx, b_const], [bp[hf]], signal=(cc == 3))
                        for cc in range(4):
                            kc = hf * 4 + cc
                            if cc % 2 == 0:
                                A(lambda: nc.scalar.activation(out=h_t[:, kc, tsl], in_=pb[hf][:, cc * 128:(cc + 1) * 128],
                                                               func=AF.Identity, scale=modT[:, l, 8 + kc:9 + kc],
                                                               bias=modT[:, l, kc:kc + 1]), [bp[hf], b_modT], [bhk[kc]])
                            else:
                                V(lambda: nc.vector.tensor_scalar(out=h_t[:, kc, tsl], in0=pb[hf][:, cc * 128:(cc + 1) * 128],
                                                                  scalar1=modT[:, l, 8 + kc:9 + kc],
                                                                  scalar2=modT[:, l, kc:kc + 1], op0=ALU.mult, op1=ALU.add),
                                  [bp[hf], b_modT], [bhk[kc]])
                    for gi, (c0, ncol) in enumerate(groups):
                        bk = 2 + gi % 3
                        for kc in range(8):
                            PE(lambda: nc.tensor.matmul(pb[bk][:, 0:ncol], lhsT=h_t[:, kc, tsl], rhs=w_in[:, kc, c0:c0 + ncol],
                                                        start=(kc == 0), stop=(kc == 7)), [bhk[kc], b_wk[kc]], [bp[bk]], signal=(kc == 7))
                        P_ = pb[bk]
                        if gi == 0:
                            A(lambda: nc.scalar.activation(out=qkA[:, 0:256], in_=P_[:, 0:256], func=AF.Identity, scale=SC_A),
                              [bp[bk]], [b_qkA])
                            V(lambda: nc.vector.tensor_copy(qkA[:, 256:512], P_[:, 256:512]), [bp[bk]], [b_qkA])
                        elif gi == 1:
                            A(lambda: nc.scalar.activation(func=AF.Identity, out=VA[:, :, 0:64], in_=P_[:, 0:256].rearrange("p (h d) -> p h d", d=64)),
                              [bp[bk]], [b_VA])
                            V(lambda: nc.vector.tensor_copy(bf32[:, 0:256], P_[:, 256:512]), [bp[bk]], [b_bf32])
                        elif gi == 2:
                            V(lambda: nc.vector.tensor_copy(bf32[:, 256:672], P_[:, 0:416]), [bp[bk]], [b_bf32])
                        elif gi == 3:
                            V(lambda: nc.vector.tensor_copy(cqk[:, :], P_[:, 0:384]), [bp[bk]], [b_cqk])
                            A(lambda: nc.scalar.activation(func=AF.Identity, out=VC[:, 0:2, 0:64],
                                                     in_=P_[:, 384:512].rearrange("p (h d) -> p h d", d=64)),
                              [bp[bk]], [b_VC])
                        elif gi == 4:
                            A(lambda: nc.scalar.activation(out=qkD[:, 0:256], in_=P_[:, 0:256], func=AF.Identity, scale=SC_D),
                              [bp[bk]], [b_qkD])
                            V(lambda: nc.vector.tensor_copy(qkD[:, 256:512], P_[:, 256:512]), [bp[bk]], [b_qkD])
                        else:
                            A(lambda: nc.scalar.activation(func=AF.Identity, out=VD[:, :, 0:64], in_=P_[:, 0:256].rearrange("p (h d) -> p h d", d=64)),
                              [bp[bk]], [b_VD])

                def stage2(sg):
                    T, s = sg // 4, sg % 4
                    tsl = slice(s * 128, (s + 1) * 128)
                    h_t, bhk = hT[T % 2], b_hk[T % 2]
                    stg, bstg = stage[T % 2], b_stage[T % 2]
                    bf32 = bf32_l[sg % 2]
                    b_bf32 = BL['bf32'][sg % 2]
                    cqk = cqk_l[sg % 2]
                    b_cqk = BL['cqk'][sg % 2]
                    junk = junk_l[sg % 2]
                    b_junk = BL['junk'][sg % 2]
                    qkA = qkA_l[sg % 2]
                    b_qkA = BL['qkA'][sg % 2]
                    qkD = qkD_l[sg % 2]
                    b_qkD = BL['qkD'][sg % 2]
                    VA = VA_l[sg % 2]
                    b_VA = BL['VA'][sg % 2]
                    VB = VB_l[sg % 2]
                    b_VB = BL['VB'][sg % 2]
                    VC = VC_l[sg % 2]
                    b_VC = BL['VC'][sg % 2]
                    VD = VD_l[sg % 2]
                    b_VD = BL['VD'][sg % 2]
                    cqn = cqn_l[sg % 2]
                    b_cqn = BL['cqn'][sg % 2]
                    cT = cT_l[sg % 2]
                    b_cT = BL['cT'][sg % 2]
                    qB = qB_l[sg % 2]
                    b_qB = BL['qB'][sg % 2]
                    kB = kB_l[sg % 2]
                    b_kB = BL['kB'][sg % 2]
                    cC = cC_l[sg % 2]
                    b_cC = BL['cC'][sg % 2]
                    t1 = t1_l[sg % 2]
                    b_t1 = BL['t1'][sg % 2]
                    t2 = t2_l[sg % 2]
                    b_t2 = BL['t2'][sg % 2]
                    nq = nq_l[sg % 2]
                    b_nq = BL['nq'][sg % 2]
                    stt_ = stt__l[sg % 2]
                    b_st = BL['stt_'][sg % 2]
                    V(lambda: nc.vector.scalar_tensor_tensor(out=junk[:, 0:384], in0=bf32[:, 0:384], scalar=1.0,
                                                             in1=bf32[:, 0:384], op0=ALU.mult, op1=ALU.mult,
                                                             accum_out=stt_[:, 0:1]), [b_bf32], [b_junk, b_st])
                    V(lambda: nc.vector.scalar_tensor_tensor(out=junk[:, 0:256], in0=bf32[:, 384:640], scalar=1.0,
                                                             in1=bf32[:, 384:640], op0=ALU.mult, op1=ALU.mult,
                                                             accum_out=stt_[:, 1:2]), [b_bf32], [b_junk, b_st])
                    A(lambda: nc.scalar.activation(out=stt_[:, 2:3], in_=stt_[:, 0:1], func=AF.Ln, scale=1.0 / 384,
                                                   bias=eps6[:, 0:1]), [b_st, b_const], [b_st])
                    A(lambda: nc.scalar.activation(out=stt_[:, 3:4], in_=stt_[:, 1:2], func=AF.Ln, scale=1.0 / 256,
                                                   bias=eps6[:, 0:1]), [b_st, b_const], [b_st])
                    A(lambda: nc.scalar.activation(out=stt_[:, 4:6], in_=stt_[:, 2:4], func=AF.Exp, scale=-0.5), [b_st], [b_st])
                    V(lambda: nc.vector.scalar_tensor_tensor(out=cqn[:, 0:384], in0=bf32[:, 0:384], scalar=stt_[:, 4:5],
                                                             in1=gq_bc[:, :], op0=ALU.mult, op1=ALU.mult),
                      [b_bf32, b_st, b_tab], [b_cqn])
                    V(lambda: nc.vector.scalar_tensor_tensor(out=cqn[:, 384:640], in0=bf32[:, 384:640], scalar=stt_[:, 5:6],
                                                             in1=gkv_bc[:, :], op0=ALU.mult, op1=ALU.mult),
                      [b_bf32, b_st, b_tab], [b_cqn])
                    p5b = pb[5][:, :].bitcast(BF16)
                    for j in range(5):
                        PE(lambda: nc.tensor.transpose(p5b[:, j * 128:(j + 1) * 128], cqn[:, j * 128:(j + 1) * 128], ident_b[:]),
                           [b_cqn, b_const], [bp[5]], signal=(j == 4))
                    V(lambda: nc.vector.tensor_copy(cT[:, :, :], p5b[:, 0:640].rearrange("p (c t) -> p c t", t=128)),
                      [bp[5]], [b_cT])
                    for j in range(3):
                        PE(lambda: nc.tensor.matmul(pb[6][:, 0:384], lhsT=cT[:, j, :], rhs=w_uq[:, j, :], start=(j == 0),
                                                    stop=(j == 2)), [b_cT, b_w], [bp[6]], signal=(j == 2))
                    for j in range(2):
                        PE(lambda: nc.tensor.matmul(pb[7][:, 0:512], lhsT=cT[:, 3 + j, :], rhs=w_ukv[:, j, :], start=(j == 0),
                                                    stop=(j == 1)), [b_cT, b_w], [bp[7]], signal=(j == 1))
                    q3 = pb[6][:, 0:384].rearrange("p (h d) -> p h d", d=96)
                    kv3 = pb[7][:, 0:512].rearrange("p (h d) -> p h d", d=128)
                    A(lambda: nc.scalar.activation(out=qB[:, :, 0:64], in_=q3[:, :, 0:64], func=AF.Identity, scale=SC_B),
                      [bp[6]], [b_qB])
                    Tq = ropeB[:, sg, 0, :]
                    Tk = ropeB[:, sg, 1, :]
                    t1q = t1[:, 0:128].rearrange("p (h d) -> p h d", d=32)
                    t2q = t2[:, 0:128].rearrange("p (h d) -> p h d", d=32)
                    V(lambda: nc.vector.tensor_tensor(out=t1q, in0=q3[:, :, 64:96],
                                                      in1=Tq[:, 0:32].unsqueeze(1).to_broadcast([128, 4, 32]), op=ALU.mult),
                      [bp[6], b_tab], [b_t1])
                    V(lambda: nc.vector.tensor_tensor(out=t2q[:, :, 0:16], in0=q3[:, :, 80:96],
                                                      in1=Tq[:, 32:48].unsqueeze(1).to_broadcast([128, 4, 16]), op=ALU.mult),
                      [bp[6], b_tab], [b_t2])
                    V(lambda: nc.vector.tensor_tensor(out=t2q[:, :, 16:32], in0=q3[:, :, 64:80],
                                                      in1=Tq[:, 48:64].unsqueeze(1).to_broadcast([128, 4, 16]), op=ALU.mult),
                      [bp[6], b_tab], [b_t2])
                    G(lambda: nc.gpsimd.tensor_tensor(out=qB[:, :, 64:96], in0=t1q, in1=t2q, op=ALU.add), [b_t1, b_t2], [b_qB])
                    V(lambda: nc.vector.tensor_copy(kB[:, :, 0:64], kv3[:, :, 0:64]), [bp[7]], [b_kB])
                    A(lambda: nc.scalar.activation(func=AF.Identity, out=VB[:, :, 0:64], in_=kv3[:, :, 64:128]), [bp[7]], [b_VB])
                    kr = bf32[:, 640:672]
                    V(lambda: nc.vector.tensor_tensor(out=t1[:, 128:160], in0=kr, in1=Tk[:, 0:32], op=ALU.mult),
                      [b_bf32, b_tab], [b_t1])
                    V(lambda: nc.vector.tensor_tensor(out=t2[:, 128:144], in0=bf32[:, 656:672], in1=Tk[:, 32:48], op=ALU.mult),
                      [b_bf32, b_tab], [b_t2])
                    V(lambda: nc.vector.tensor_tensor(out=t2[:, 144:160], in0=bf32[:, 640:656], in1=Tk[:, 48:64], op=ALU.mult),
                      [b_bf32, b_tab], [b_t2])
                    G(lambda: nc.gpsimd.tensor_tensor(out=kB[:, :, 64:96],
                                                      in0=t1[:, 128:160].unsqueeze(1).to_broadcast([128, 4, 32]),
                                                      in1=t2[:, 128:160].unsqueeze(1).to_broadcast([128, 4, 32]), op=ALU.add),
                      [b_t1, b_t2], [b_kB])
                    c3 = cqk[:, :].rearrange("p (h d) -> p h d", d=64)
                    V(lambda: nc.vector.tensor_tensor(out=junk[:, :], in0=cqk[:, :], in1=cqk[:, :], op=ALU.mult),
                      [b_cqk], [b_junk])
                    V(lambda: nc.vector.tensor_reduce(out=stt_[:, 6:12], in_=junk[:, :].rearrange("p (h d) -> p h d", d=64),
                                                      axis=AX.X, op=ALU.add), [b_junk], [b_st])
                    A(lambda: nc.scalar.activation(out=stt_[:, 6:12], in_=stt_[:, 6:12], func=AF.Ln, scale=1.0 / 64,
                                                   bias=eps6[:, 0:1]), [b_st, b_const], [b_st])
                    A(lambda: nc.scalar.activation(out=stt_[:, 6:12], in_=stt_[:, 6:12], func=AF.Exp, scale=-0.5), [b_st], [b_st])
                    n3 = nq[:, :].rearrange("p (h d) -> p h d", d=64)
                    V(lambda: nc.vector.tensor_tensor(out=n3, in0=c3, in1=stt_[:, 6:12].unsqueeze(2).to_broadcast([128, 6, 64]),
                                                      op=ALU.mult), [b_cqk, b_st], [b_nq])
                    G(lambda: nc.gpsimd.tensor_tensor(out=n3, in0=n3, in1=gC_bc[:, :, :], op=ALU.mult), [b_nq, b_tab], [b_nq])
                    cosA = ropeC[:, sg, 0:64]
                    sinS = ropeC[:, sg, 64:128].rearrange("p (r f i) -> p r f i", r=2, f=2)
                    V(lambda: nc.vector.tensor_tensor(out=t1[:, :].rearrange("p (h d) -> p h d", d=64), in0=n3,
                                                      in1=cosA.unsqueeze(1).to_broadcast([128, 6, 64]), op=ALU.mult),
                      [b_nq, b_tab], [b_t1])
                    n5 = nq[:, :].rearrange("p (h r f i) -> p h r f i", h=6, r=2, f=2)
                    t5 = t2[:, :].rearrange("p (h r f i) -> p h r f i", h=6, r=2, f=2)
                    for f in range(2):
                        V(lambda: nc.vector.tensor_tensor(out=t5[:, :, :, f, :], in0=n5[:, :, :, 1 - f, :],
                                                          in1=sinS[:, :, f, :].unsqueeze(1).to_broadcast([128, 6, 2, 16]),
                                                          op=ALU.mult), [b_nq, b_tab], [b_t2])
                    G(lambda: nc.gpsimd.tensor_tensor(out=cC[:, :], in0=t1[:, :], in1=t2[:, :], op=ALU.add), [b_t1, b_t2], [b_cC])
                    tb0 = pb[6][:, :].bitcast(BF16)
                    tb1 = pb[7][:, :].bitcast(BF16)
                    for j in range(4):
                        PE(lambda: nc.tensor.transpose(tb0[:, j * 128:(j + 1) * 128], qkA[:, j * 128:(j + 1) * 128], ident_b[:]),
                           [b_qkA, b_const], [bp[6]], signal=False)
                    for h in range(4):
                        PE(lambda: nc.tensor.transpose(tb0[0:96, (4 + h) * 128:(5 + h) * 128], qB[:, h, :], ident_b[:]),
                           [b_qB, b_const], [bp[6]], signal=(h == 3))
                    for h in range(4):
                        PE(lambda: nc.tensor.transpose(tb1[0:96, h * 128:(h + 1) * 128], kB[:, h, :], ident_b[:]),
                           [b_kB, b_const], [bp[7]], signal=False)
                    for j in range(3):
                        PE(lambda: nc.tensor.transpose(tb1[:, (4 + j) * 128:(5 + j) * 128], cC[:, j * 128:(j + 1) * 128],
                                                       ident_b[:]), [b_cC, b_const], [bp[7]], signal=(j == 2))
                    V(lambda: nc.vector.tensor_copy(stg[:, 0:8, tsl], tb0[:, 0:1024].rearrange("p (c t) -> p c t", t=128)),
                      [bp[6]], [bstg])
                    A(lambda: nc.scalar.activation(func=AF.Identity, out=stg[:, 8:15, tsl], in_=tb1[:, 0:896].rearrange("p (c t) -> p c t", t=128)),
                      [bp[7]], [bstg])
                    rows = slice(sg * 128, (sg + 1) * 128)
                    S.dma("sp", VG[0, rows, :], VA[:, :, :].rearrange("p h d -> p (h d)"), reads=[b_VA])
                    S.dma("sp", VG[1, rows, :], VB[:, :, :].rearrange("p h d -> p (h d)"), reads=[b_VB])
                    S.dma("sp", VG[2, rows, :], VC[:, :, :].rearrange("p h d -> p (h d)"), reads=[b_VC])
                    S.dma("sp", DTOK[rows, 0:512], qkD[:, :], reads=[b_qkD])
                    S.dma("sp", DTOK[rows, 512:512 + VW], VD[:, :, :].rearrange("p h d -> p (h d)"), reads=[b_VD])
                    if s == 3:
                        for (ca, cb) in [(0, 4), (4, 8), (8, 12), (12, 15)]:
                            S.dma("sp", QKT[ca:cb, :, T * 512:(T + 1) * 512].rearrange("c r t -> r c t"), stg[:, ca:cb, :],
                                  reads=[bstg])

                nsub_ = NSUB if 'nsub' not in DBG else DBG['nsub']
                stage1(0)
                for sg in range(nsub_):
                    if sg + 1 < nsub_:
                        stage1(sg + 1)
                    stage2(sg)
                S.barrier()

        def phase_att(l, y_res, b_y):
            with ExitStack() as st:
                qT = [sb(st, f"qT{i}", [128, SEQ], BF16) for i in range(4)]
                kT = [sb(st, f"kT{i}", [128, SEQ], BF16) for i in range(4)]
                Vg = sb(st, "Vg", [128, NSUB, VW], BF16)
                alibi = sb(st, "alibi", [128, 5, 512], F32)
                Sb = [sb(st, f"Sb{i}", [128, 512], F32) for i in range(3)]
                E = [sb(st, f"E{i}", [128, 512], BF16) for i in range(4)]
                lam_t = sb(st, "lam_t", [128, 128], F32)
                lam = sb(st, "lam", [128, 8], F32)
                gA_bc = sb(st, "gA_bc", [128, 64], F32)
                o1 = sb(st, "o1", [128, 4, 64], F32)
                o2 = sb(st, "o2", [128, 4, 64], F32)
                osq = sb(st, "osq", [128, 4, 64], F32)
                rc = sb(st, "rc", [128, 16], F32)
                pS = [pbank(st, f"pS{i}") for i in range(4)]
                pOT = [pbank(st, f"pOT{i}") for i in range(2)]
                pO = [pbank(st, f"pO{i}") for i in range(2)]
                otS = [sb(st, f"otS{i}", [65, 512], F32) for i in range(2)]
                b_pOT = [PB(), PB()]
                b_otS = [Buf(), Buf()]
                b_qT = [Buf() for _ in range(4)]
                b_kT = [Buf() for _ in range(4)]
                b_Vg, b_al, b_lam, b_gA = Buf(), Buf(), Buf(), Buf()
                b_Sb = [Buf() for _ in range(3)]
                b_E = [Buf() for _ in range(4)]
                b_pS = [PB() for _ in range(4)]
                b_pO = [PB() for _ in range(4)]
                b_o1, b_o2, b_osq, b_rc = Buf(), Buf(), Buf(), Buf()
                S.dma("sp", alibi[:], I["alibi"], writes=[b_al])
                lam_init = 0.8 - 0.6 * math.exp(-0.3 * l)
                S.dma("sp", lam_t[:], I["diff_lambda"][l, :].partition_broadcast(128), writes=[b_lam])
                V(lambda: nc.vector.scalar_tensor_tensor(out=lam_t[:, 0:32], in0=lam_t[:, 0:32], scalar=1.0, in1=lam_t[:, 32:64],
                                                         op0=ALU.mult, op1=ALU.mult, accum_out=lam[:, 0:1]), [b_lam], [b_lam])
                V(lambda: nc.vector.scalar_tensor_tensor(out=lam_t[:, 64:96], in0=lam_t[:, 64:96], scalar=1.0,
                                                         in1=lam_t[:, 96:128], op0=ALU.mult, op1=ALU.mult,
                                                         accum_out=lam[:, 1:2]), [b_lam], [b_lam])
                A(lambda: nc.scalar.activation(out=lam[:, 2:4], in_=lam[:, 0:2], func=AF.Exp), [b_lam], [b_lam])
                V(lambda: nc.vector.tensor_tensor(out=lam[:, 4:5], in0=lam[:, 2:3], in1=lam[:, 3:4], op=ALU.subtract),
                  [b_lam], [b_lam])
                V(lambda: nc.vector.tensor_scalar(out=lam[:, 5:6], in0=lam[:, 4:5], scalar1=lam_init, scalar2=-1.0,
                                                  op0=ALU.add, op1=ALU.mult), [b_lam], [b_lam])
                S.dma("sp", gA_bc[:], I["diff_subln_g"][l, :].partition_broadcast(128), writes=[b_gA])
                V(lambda: nc.vector.tensor_scalar(out=gA_bc[:], in0=gA_bc[:], scalar1=1.0 - lam_init, scalar2=None,
                                                  op0=ALU.mult), [b_gA], [b_gA])

                state = {"qi": 0, "ei": 0, "si": 0, "sbi": 0}

                def load_map(chunk, r0, nrows, isq):
                    i = state["qi"] % 4
                    t, b = (qT[i], b_qT[i]) if isq else (kT[i], b_kT[i])
                    S.dma("sp", t[0:nrows, :], QKT[chunk, r0:r0 + nrows, :], writes=[b])
                    return t, b

                def load_V(g):
                    for a in range(4):
                        S.dma("sp", Vg[:, a * 8:(a + 1) * 8, :],
                              VG[g, a * 1024:(a + 1) * 1024, :].rearrange("(t p) c -> p t c", p=128), writes=[b_Vg])

                def job(maps, vcol, ycol, alibi_slope=None):
                    nm = len(maps)
                    pend = [None]
                    for qt in range(8):
                        if nm == 2:
                            groups = [[(kt, 0), (kt, 1)] for kt in range(NSUB)]
                        else:
                            groups = [[(2 * j, 0), (2 * j + 1, 0)] for j in range(NSUB // 2)]
                        ng = len(groups)
                        for gi in range(ng + 2):
                            if gi < ng:
                                banks = [2 * (gi % 2), 2 * (gi % 2) + 1]
                                S.prewait("pe", [mp[1] for mp in maps] + [mp[3] for mp in maps], [b_pS[bk] for bk in banks])
                                for idx, (kt, m) in enumerate(groups[gi]):
                                    q_t, bq, k_t, bk_, nr = maps[m]
                                    si = banks[idx]
                                    PE(lambda: nc.tensor.matmul(pS[si][:, :], lhsT=k_t[0:nr, kt * 128:(kt + 1) * 128],
                                                                rhs=q_t[0:nr, qt * 512:(qt + 1) * 512], start=True, stop=True),
                                       [bq, bk_], [b_pS[si]], signal=(idx == 1))
                            j = gi - 1
                            if 0 <= j < ng:
                                for idx, (kt, m) in enumerate(groups[j]):
                                    si = 2 * (j % 2) + idx
                                    ei = 2 * (j % 2) + idx
                                    if alibi_slope is None:
                                        A(lambda: nc.scalar.activation(out=E[ei][:, :], in_=pS[si][:, :], func=AF.Exp),
                                          [b_pS[si]], [b_E[ei]])
                                    else:
                                        k0, q0 = kt * 128, qt * 512
                                        if k0 + 128 <= q0:
                                            tab, sc_, bias = alibi[:, 0, :], -alibi_slope, -alibi_slope * (q0 - k0)
                                        elif k0 >= q0 + 512:
                                            tab, sc_, bias = alibi[:, 0, :], alibi_slope, -alibi_slope * (k0 - q0)
                                        else:
                                            tab, sc_, bias = alibi[:, 1 + (k0 - q0) // 128, :], -alibi_slope, 0.0
                                        sbi = state["sbi"] % 3
                                        state["sbi"] += 1
                                        V(lambda: nc.vector.scalar_tensor_tensor(out=Sb[sbi][:, :], in0=tab, scalar=sc_,
                                                                                 in1=pS[si][:, :], op0=ALU.mult, op1=ALU.add),
                                          [b_al, b_pS[si]], [b_Sb[sbi]])
                                        A(lambda: nc.scalar.activation(out=E[ei][:, :], in_=Sb[sbi][:, :], func=AF.Exp, bias=bias),
                                          [b_Sb[sbi]], [b_E[ei]])
                            j = gi - 2
                            if 0 <= j < ng:
                                eis_ = [2 * (j % 2), 2 * (j % 2) + 1]
                                S.prewait("pe", [b_E[e] for e in eis_] + [b_Vg], [b_pOT[m] for (_, m) in groups[j]])
                                for idx, (kt, m) in enumerate(groups[j]):
                                    ei = eis_[idx]
                                    PE(lambda: nc.tensor.matmul(pOT[m][0:65, :], lhsT=Vg[:, kt, vcol:vcol + 65], rhs=E[ei][:, :],
                                                                start=(kt == 0), stop=(kt == NSUB - 1)),
                                       [b_E[ei], b_Vg], [b_pOT[m]], signal=(idx == 1))
                            if gi == 4 and pend[0] is not None:
                                pend[0]()
                                pend[0] = None
                        for m in range(nm):
                            if alibi_slope is not None:
                                A(lambda: nc.scalar.activation(out=otS[m][:, :], in_=pOT[m][0:65, :], func=AF.Identity),
                                  [b_pOT[m]], [b_otS[m]])
                            else:
                                V(lambda: nc.vector.tensor_copy(otS[m][:, :], pOT[m][0:65, :]), [b_pOT[m]], [b_otS[m]])
                        pend[0] = (lambda qt=qt: finalize(qt, nm, ycol))
                    pend[0]()
                    pend[0] = None

                def finalize(qt, nm, ycol):
                    if True:
                        for m in range(nm):
                            for jj in range(4):
                                PE(lambda: nc.tensor.transpose(pO[m][:, jj * 65:(jj + 1) * 65], otS[m][0:65, jj * 128:(jj + 1) * 128],
                                                               ident_f[0:65, 0:65]), [b_otS[m], b_const], [b_pO[m]], signal=(jj == 3))
                        O1 = pO[0][:, 0:260].rearrange("p (j d) -> p j d", d=65)
                        ydst = y_res[:, qt * 4:(qt + 1) * 4, ycol:ycol + 64]
                        by = b_y[qt * 4:(qt + 1) * 4]
                        V(lambda: nc.vector.reciprocal(out=rc[:, 0:4], in_=O1[:, :, 64]), [b_pO[0]], [b_rc])
                        if nm == 1:
                            V(lambda: nc.vector.tensor_tensor(out=ydst, in0=O1[:, :, 0:64],
                                                              in1=rc[:, 0:4].unsqueeze(2).to_broadcast([128, 4, 64]),
                                                              op=ALU.mult), [b_pO[0], b_rc], by)
                        else:
                            O2 = pO[1][:, 0:260].rearrange("p (j d) -> p j d", d=65)
                            V(lambda: nc.vector.reciprocal(out=rc[:, 4:8], in_=O2[:, :, 64]), [b_pO[1]], [b_rc])
                            V(lambda: nc.vector.tensor_scalar(out=rc[:, 4:8], in0=rc[:, 4:8], scalar1=lam[:, 5:6], scalar2=None,
                                                              op0=ALU.mult), [b_rc, b_lam], [b_rc])
                            V(lambda: nc.vector.tensor_tensor(out=o1[:, :, :], in0=O1[:, :, 0:64],
                                                              in1=rc[:, 0:4].unsqueeze(2).to_broadcast([128, 4, 64]),
                                                              op=ALU.mult), [b_pO[0], b_rc], [b_o1])
                            V(lambda: nc.vector.tensor_tensor(out=o2[:, :, :], in0=O2[:, :, 0:64],
                                                              in1=rc[:, 4:8].unsqueeze(2).to_broadcast([128, 4, 64]),
                                                              op=ALU.mult), [b_pO[1], b_rc], [b_o2])
                            G(lambda: nc.gpsimd.tensor_tensor(out=o1[:, :, :], in0=o1[:, :, :], in1=o2[:, :, :], op=ALU.add),
                              [b_o1, b_o2], [b_o1])
                            G(lambda: nc.gpsimd.tensor_tensor(out=osq[:, :, :], in0=o1[:, :, :], in1=o1[:, :, :], op=ALU.mult),
                              [b_o1], [b_osq])
                            V(lambda: nc.vector.tensor_reduce(out=rc[:, 8:12], in_=osq[:, :, :], axis=AX.X, op=ALU.add),
                              [b_osq], [b_rc])
                            A(lambda: nc.scalar.activation(out=rc[:, 8:12], in_=rc[:, 8:12], func=AF.Ln, scale=1.0 / 64,
                                                           bias=eps6[:, 0:1]), [b_rc, b_const], [b_rc])
                            A(lambda: nc.scalar.activation(out=rc[:, 12:16], in_=rc[:, 8:12], func=AF.Exp, scale=-0.5),
                              [b_rc], [b_rc])
                            V(lambda: nc.vector.tensor_tensor(out=o2[:, :, :], in0=o1[:, :, :],
                                                              in1=rc[:, 12:16].unsqueeze(2).to_broadcast([128, 4, 64]),
                                                              op=ALU.mult), [b_o1, b_rc], [b_o2])
                            V(lambda: nc.vector.tensor_tensor(out=ydst, in0=o2[:, :, :],
                                                              in1=gA_bc[:, :].unsqueeze(1).to_broadcast([128, 4, 64]),
                                                              op=ALU.mult), [b_o2, b_gA], by)

                load_V(0)
                for h in range(4):
                    maps = []
                    for c_ in range(2):
                        m = 2 * h + c_
                        state["qi"] += 1
                        q_t, bq = load_map(m // 4, (m % 4) * 32, 32, True)
                        k_t, bk_ = load_map(2 + m // 4, (m % 4) * 32, 32, False)
                        maps.append((q_t, bq, k_t, bk_, 32))
                    job(maps, h * VP, h * 64, alibi_slope=SLOPES_A[h])
                load_V(1)
                for h in range(4):
                    state["qi"] += 1
                    q_t, bq = load_map(4 + h, 0, 96, True)
                    k_t, bk_ = load_map(8 + h, 0, 96, False)
                    job([(q_t, bq, k_t, bk_, 96)], h * VP, 256 + h * 64)
                load_V(2)
                for h in range(4):
                    state["qi"] += 1
                    q_t, bq = load_map(12 + h // 2, (h % 2) * 64, 64, True)
                    if h % 2 == 0:
                        k_t, bk_ = load_map(14, (h // 2) * 64, 64, False)
                    job([(q_t, bq, k_t, bk_, 64)], (h // 2) * VP, 512 + h * 64)
                S.barrier()

        def phase_D(l, y_res, b_y):
            with ExitStack() as st:
                tokD = sb(st, "tokD", [128, 8, 512], BF16)
                Vsh = sb(st, "Vsh", [128, 9, VW], BF16)
                QTd = sb(st, "QTd", [128, 2, 1024], BF16)
                KTd = sb(st, "KTd", [128, 2, 1024 + 128], BF16)
                dtab = sb(st, "dtab", [128, 2, 512], F32)
                Sb = [sb(st, f"dSb{i}", [128, 512], F32) for i in range(4)]
                E = [sb(st, f"dE{i}", [128, 512], BF16) for i in range(4)]
                ost = [sb(st, f"ost{i}", [128, 4, 260], F32) for i in range(2)]
                acc = sb(st, "acc", [128, 260], F32)
                od = [sb(st, f"od{i}", [128, 260], F32) for i in range(3)]
                rcd = sb(st, "rcd", [128, 4], F32)
                pT = [pbank(st, f"dT{i}") for i in range(2)]
                pS = [pbank(st, f"dS{i}") for i in range(4)]
                pO = [pbank(st, f"dO{i}") for i in range(2)]
                b_tok, b_Vsh, b_QT, b_KT, b_dt = Buf(), Buf(), Buf(), Buf(), Buf()
                b_Sb = [Buf() for _ in range(4)]
                b_E = [Buf() for _ in range(4)]
                b_ost = [Buf(), Buf()]
                b_pT = [PB(), PB()]
                b_pS = [PB() for _ in range(4)]
                b_pO = [PB(), PB()]
                b_acc, b_od, b_rcd = Buf(), [Buf() for _ in range(3)], Buf()
                S.dma("sp", dtab[:], I["dtab"], writes=[b_dt])
                cnt = {"s": 0, "e": 0, "o": 0}
                for bi, d in enumerate(DILS):
                    Ltot = SEQ // d
                    nseg = max(1, Ltot // 1024)
                    for r in range(d):
                        for seg in range(nseg):
                            Lc = min(Ltot, 1024)
                            i0 = seg * 1024
                            nt = Lc // 128
                            V(lambda: nc.vector.memset(KTd[:, :, :], 0.0), [], [b_KT])
                            V(lambda: nc.vector.memset(Vsh[:, :, :], 0.0), [], [b_Vsh])
                            base = r + d * i0
                            src = DTOK[base: base + d * (Lc - 1) + 1: d, :]
                            S.dma("sp", tokD[:, 0:nt, :], src[:, 0:512].rearrange("(t p) c -> p t c", p=128), writes=[b_tok])
                            lo = i0 - 64
                            hi = i0 + Lc + 64
                            lo_c, hi_c = max(lo, 0), min(hi, Ltot)
                            u0 = lo_c - lo
                            n_rows = hi_c - lo_c
                            pos = 0
                            while pos < n_rows:
                                u = u0 + pos
                                tj, pj = u // 128, u % 128
                                take = min(128 - pj, n_rows - pos)
                                if pj == 0 and take == 128:
                                    nfull = (n_rows - pos) // 128
                                    t_first = r + d * (lo_c + pos)
                                    srcv = DTOK[t_first: t_first + d * (128 * nfull - 1) + 1: d, 512:512 + VW]
                                    S.dma("sp", Vsh[:, tj:tj + nfull, :], srcv.rearrange("(t p) c -> p t c", p=128),
                                          writes=[b_Vsh])
                                    pos += 128 * nfull
                                else:
                                    t_first = r + d * (lo_c + pos)
                                    srcv = DTOK[t_first: t_first + d * (take - 1) + 1: d, 512:512 + VW]
                                    S.dma("sp", Vsh[pj:pj + take, tj, :], srcv, writes=[b_Vsh])
                                    pos += take
                            for t in range(nt):
                                pTb = pT[t % 2][:, :].bitcast(BF16)
                                for j in range(4):
                                    PE(lambda: nc.tensor.transpose(pTb[:, j * 128:(j + 1) * 128], tokD[:, t, j * 128:(j + 1) * 128],
                                                                   ident_b[:]), [b_tok, b_const], [b_pT[t % 2]], signal=(j == 3))
                                V(lambda: nc.vector.tensor_copy(QTd[:, :, t * 128:(t + 1) * 128],
                                                                pTb[:, 0:256].rearrange("p (c t) -> p c t", t=128)),
                                  [b_pT[t % 2]], [b_QT])
                                A(lambda: nc.scalar.activation(func=AF.Identity, out=KTd[:, :, 64 + t * 128: 64 + (t + 1) * 128],
                                                         in_=pTb[:, 256:512].rearrange("p (c t) -> p c t", t=128)),
                                  [b_pT[t % 2]], [b_KT])
                            if nseg > 1:
                                for side in range(2):
                                    hs = i0 - 64 if side == 0 else i0 + Lc
                                    if hs < 0 or hs >= Ltot:
                                        continue
                                    t_first = r + d * hs
                                    srck = DTOK[t_first: t_first + d * 63 + 1: d, 256:512]
                                    S.dma("sp", tokD[0:64, 0, 0:256], srck, writes=[b_tok])
                                    pTb = pT[0][:, :].bitcast(BF16)
                                    for j in range(2):
                                        PE(lambda: nc.tensor.transpose(pTb[:, j * 128: j * 128 + 64],
                                                                       tokD[0:64, 0, j * 128:(j + 1) * 128], ident_b[0:64, 0:64]),
                                           [b_tok, b_const], [b_pT[0]], signal=(j == 1))
                                    col = 0 if side == 0 else 64 + Lc
                                    V(lambda: nc.vector.tensor_copy(
                                        KTd[:, :, col:col + 64],
                                        pTb[:, 0:256].rearrange("p (c t) -> p c t", t=128)[:, :, 0:64]), [b_pT[0]], [b_KT])
                            steps = [(g0, h, ab) for g0 in range(0, nt, 4) for h in range(4) for ab in range(2)]
                            n = len(steps)
                            sis, eis = [0] * n, [0] * n
                            LS, LP = 1, 2
                            for i in range(n + LP):
                                if i < n:
                                    g0, h, ab = steps[i]
                                    ng = min(4, nt - g0)
                                    ch, pr = h // 2, (h % 2) * 64
                                    si = cnt["s"] % 4
                                    cnt["s"] += 1
                                    sis[i] = si
                                    for jj in range(ng):
                                        qc = (g0 + jj) * 128
                                        kc0 = qc + ab * 128
                                        PE(lambda: nc.tensor.matmul(pS[si][:, jj * 128:(jj + 1) * 128],
                                                                    lhsT=KTd[pr:pr + 64, ch, kc0:kc0 + 128],
                                                                    rhs=QTd[pr:pr + 64, ch, qc:qc + 128], start=True, stop=True),
                                           [b_KT, b_QT], [b_pS[si]], signal=(jj == ng - 1))
                                j = i - LS
                                if 0 <= j < n:
                                    g0, h, ab = steps[j]
                                    ng = min(4, nt - g0)
                                    w = ng * 128
                                    si = sis[j]
                                    ei = cnt["e"] % 4
                                    cnt["e"] += 1
                                    eis[j] = ei
                                    V(lambda: nc.vector.scalar_tensor_tensor(out=Sb[ei][:, 0:w], in0=dtab[:, ab, 0:w],
                                                                             scalar=-SLOPES_D[h] * d, in1=pS[si][:, 0:w],
                                                                             op0=ALU.mult, op1=ALU.add),
                                      [b_dt, b_pS[si]], [b_Sb[ei]])
                                    A(lambda: nc.scalar.activation(out=E[ei][:, 0:w], in_=Sb[ei][:, 0:w], func=AF.Exp),
                                      [b_Sb[ei]], [b_E[ei]])
                                j = i - LP
                                if 0 <= j < n:
                                    g0, h, ab = steps[j]
                                    ng = min(4, nt - g0)
                                    ei = eis[j]
                                    oi = (g0 // 4) % 2
                                    for jj in range(ng):
                                        PE(lambda: nc.tensor.matmul(pO[h % 2][:, jj * 65:(jj + 1) * 65],
                                                                    lhsT=E[ei][:, jj * 128:(jj + 1) * 128],
                                                                    rhs=Vsh[:, g0 + jj + ab, h * VP:h * VP + 65],
                                                                    start=(ab == 0 and jj == 0), stop=(ab == 1),
                                                                    skip_group_check=True),
                                           [b_E[ei], b_Vsh], [b_pO[h % 2]], signal=(jj == ng - 1))
                                    if ab == 1:
                                        V(lambda: nc.vector.tensor_copy(ost[oi][:, 0:ng, h * 65:(h + 1) * 65],
                                                                        pO[h % 2][:, 0:ng * 65].rearrange("p (j d) -> p j d", d=65)),
                                          [b_pO[h % 2]], [b_ost[oi]])
                                        if h == 3:
                                            for jj in range(ng):
                                                t_first = r + d * (i0 + (g0 + jj) * 128)
                                                S.dma("sp", OD[bi, t_first: t_first + d * 127 + 1: d, :], ost[oi][:, jj, :],
                                                      reads=[b_ost[oi]])
                S.barrier()
                for sg in range(NSUB):
                    rows = slice(sg * 128, (sg + 1) * 128)
                    for bi in range(3):
                        S.dma("sp", od[bi][:], OD[bi, rows, :], writes=[b_od[bi]])
                    V(lambda: nc.vector.tensor_tensor(out=acc[:], in0=od[0][:], in1=od[1][:], op=ALU.add),
                      [b_od[0], b_od[1]], [b_acc])
                    V(lambda: nc.vector.tensor_tensor(out=acc[:], in0=acc[:], in1=od[2][:], op=ALU.add), [b_acc, b_od[2]], [b_acc])
                    a3 = acc[:, :].rearrange("p (h d) -> p h d", d=65)
                    V(lambda: nc.vector.reciprocal(out=rcd[:, 0:4], in_=a3[:, :, 64]), [b_acc], [b_rcd])
                    V(lambda: nc.vector.tensor_tensor(out=y_res[:, sg, 768:1024].rearrange("p (h d) -> p h d", d=64),
                                                      in0=a3[:, :, 0:64], in1=rcd[:, 0:4].unsqueeze(2).to_broadcast([128, 4, 64]),
                                                      op=ALU.mult), [b_acc, b_rcd], [b_y[sg]])
                S.barrier()

        def layer_norm(v, bv, dst, bdst, g_bc, b_bc, btab, stats, mv, bst, eng2):
            v4 = v.rearrange("p (c f) -> p c f", f=256)
            for c_ in range(4):
                V(lambda: nc.vector.bn_stats(out=stats[:, c_, :], in_=v4[:, c_, :]), [bv], [bst])
            V(lambda: nc.vector.bn_aggr(out=mv[:, 0:2], in_=stats[:, :, :].rearrange("p c f -> p (c f)")), [bst], [bst])
            A(lambda: nc.scalar.activation(out=mv[:, 2:3], in_=mv[:, 1:2], func=AF.Ln, bias=eps5[:, 0:1]), [bst, b_const], [bst])
            A(lambda: nc.scalar.activation(out=mv[:, 3:4], in_=mv[:, 2:3], func=AF.Exp, scale=-0.5), [bst], [bst])
            V(lambda: nc.vector.scalar_tensor_tensor(out=mv[:, 4:5], in0=mv[:, 0:1], scalar=-1.0, in1=mv[:, 3:4], op0=ALU.mult,
                                                     op1=ALU.mult), [bst], [bst])
            A(lambda: nc.scalar.activation(out=v, in_=v, func=AF.Identity, scale=mv[:, 3:4], bias=mv[:, 4:5]), [bv, bst], [bv])
            S.op(eng2, lambda: (nc.gpsimd if eng2 == "pool" else nc.vector).tensor_tensor(out=v, in0=v, in1=g_bc, op=ALU.mult),
                 [bv, btab], [bv])
            S.op(eng2, lambda: (nc.gpsimd if eng2 == "pool" else nc.vector).tensor_tensor(out=dst, in0=v, in1=b_bc, op=ALU.add),
                 [bv, btab], [bdst])

        def phase_O1(l, xsrc, y_res, b_y):
            with ExitStack() as st:
                w_o = sb(st, "w_o", [128, 8, D], BF16)
                gA = sb(st, "gA", [128, D], F32)
                lng = sb(st, "lng", [128, D], F32)
                lnb = sb(st, "lnb", [128, D], F32)
                xs = [sb(st, f"oxs{i}", [128, D], F32) for i in range(2)]
                yT = [sb(st, f"yT{i}", [128, 8, 128], BF16) for i in range(2)]
                v = [sb(st, f"ov{i}", [128, D], F32) for i in range(2)]
                x1 = [sb(st, f"ox1{i}", [128, D], F32) for i in range(2)]
                h2 = [sb(st, f"oh2{i}", [128, 8, 512], BF16) for i in range(2)]
                stats = sb(st, "ostats", [128, 4, 6], F32)
                mv = sb(st, "omv", [128, 8], F32)
                pb = [pbank(st, f"po{i}") for i in range(8)]
                bp = [PB() for _ in range(8)]
                b_w, b_tab, b_stt = Buf(), Buf(), Buf()
                b_xs, b_yT, b_v, b_x1, b_h2 = ([Buf(), Buf()] for _ in range(5))
                wsrc = I["w_o"][l].rearrange("(kc p) n -> p kc n", p=128)
                for kc in range(8):
                    S.dma("pool", w_o[:, kc, :], wsrc[:, kc, :], writes=[b_w])
                S.dma("sp", gA[:], GB[l, 0], writes=[b_tab])
                S.dma("sp", lng[:], I["ln_attn_g"][l, :].partition_broadcast(128), writes=[b_tab])
                S.dma("sp", lnb[:], I["ln_attn_b"][l, :].partition_broadcast(128), writes=[b_tab])
                for sg in range(NSUB):
                    T, s = sg // 4, sg % 4
                    i2 = sg % 2
                    tsl = slice(s * 128, (s + 1) * 128)
                    S.dma("sp", xs[i2][:], xsrc[sg * 128:(sg + 1) * 128, :], writes=[b_xs[i2]])
                    tb = pb[0 + i2][:, :].bitcast(BF16)
                    for kc in range(8):
                        PE(lambda: nc.tensor.transpose(tb[:, kc * 128:(kc + 1) * 128], y_res[:, sg, kc * 128:(kc + 1) * 128],
                                                       ident_b[:]), [b_y[sg], b_const], [bp[i2]], signal=(kc == 7))
                    V(lambda: nc.vector.tensor_copy(yT[i2][:, :, :], tb[:, 0:1024].rearrange("p (c t) -> p c t", t=128)),
                      [bp[i2]], [b_yT[i2]])
                    for hf in range(2):
                        bk = 2 + 2 * i2 + hf
                        for kc in range(8):
                            PE(lambda: nc.tensor.matmul(pb[bk][:, :], lhsT=yT[i2][:, kc, :], rhs=w_o[:, kc, hf * 512:(hf + 1) * 512],
                                                        start=(kc == 0), stop=(kc == 7)), [b_yT[i2], b_w], [bp[bk]], signal=(kc == 7))
                        hs = slice(hf * 512, (hf + 1) * 512)
                        V(lambda: nc.vector.tensor_tensor(out=v[i2][:, hs], in0=pb[bk][:, :], in1=gA[:, hs], op=ALU.mult),
                          [bp[bk], b_tab], [b_v[i2]])
                    V(lambda: nc.vector.scalar_tensor_tensor(out=v[i2][:, :], in0=xs[i2][:, :], scalar=ALPHA,
                                                             in1=v[i2][:, :], op0=ALU.mult, op1=ALU.add),
                      [b_xs[i2], b_v[i2]], [b_v[i2]])
                    layer_norm(v[i2][:, :], b_v[i2], x1[i2][:, :], b_x1[i2], lng[:, :], lnb[:, :], b_tab, stats, mv, b_stt, "pool")
                    S.dma("sp", X1[sg * 128:(sg + 1) * 128, :], x1[i2][:], reads=[b_x1[i2]])
                    for hf in range(2):
                        bk = 6 + hf
                        for cc in range(4):
                            kc = hf * 4 + cc
                            PE(lambda: nc.tensor.transpose(pb[bk][:, cc * 128:(cc + 1) * 128], x1[i2][:, kc * 128:(kc + 1) * 128],
                                                           ident_f[:]), [b_x1[i2], b_const], [bp[bk]], signal=(cc == 3))
                        for cc in range(4):
                            kc = hf * 4 + cc
                            if cc % 2 == 0:
                                A(lambda: nc.scalar.activation(out=h2[T % 2][:, kc, tsl], in_=pb[bk][:, cc * 128:(cc + 1) * 128],
                                                               func=AF.Identity, scale=modT[:, l, 32 + kc:33 + kc],
                                                               bias=modT[:, l, 24 + kc:25 + kc]), [bp[bk], b_modT], [b_h2[T % 2]])
                            else:
                                V(lambda: nc.vector.tensor_scalar(out=h2[T % 2][:, kc, tsl], in0=pb[bk][:, cc * 128:(cc + 1) * 128],
                                                                  scalar1=modT[:, l, 32 + kc:33 + kc],
                                                                  scalar2=modT[:, l, 24 + kc:25 + kc], op0=ALU.mult, op1=ALU.add),
                                  [bp[bk], b_modT], [b_h2[T % 2]])
                    if s == 3:
                        S.dma("sp", H2T[:, :, T * 512:(T + 1) * 512].rearrange("c r t -> r c t"), h2[T % 2][:, :, :],
                              reads=[b_h2[T % 2]])
                S.barrier()

        def phase_O2(l, dst):
            with ExitStack() as st:
                w_up = sb(st, "w_up", [128, 8, HID], BF16)
                w_dn = sb(st, "w_dn", [128, 32, D], BF16)
                gM = sb(st, "gM", [128, D], F32)
                lng = sb(st, "lng2", [128, D], F32)
                lnb = sb(st, "lnb2", [128, D], F32)
                h2 = [sb(st, f"mh2{i}", [128, 8, 256], BF16) for i in range(2)]
                uT = sb(st, "uT", [128, 32, 256], BF16)
                rr = [sb(st, f"rr{i}", [128, 256], F32) for i in range(3)]
                x1 = [sb(st, f"mx1{i}", [128, D], F32) for i in range(2)]
                v = [sb(st, f"mv{i}", [128, D], F32) for i in range(2)]
                stats = sb(st, "mstats", [128, 4, 6], F32)
                mv = sb(st, "mmv", [128, 8], F32)
                pb = [pbank(st, f"pm{i}") for i in range(8)]
                bp = [PB() for _ in range(8)]
                b_wu, b_wd, b_tab, b_stt, b_uT = Buf(), Buf(), Buf(), Buf(), Buf()
                b_h2, b_x1, b_v = ([Buf(), Buf()] for _ in range(3))
                b_rr = [Buf() for _ in range(3)]
                usrc = I["w_up"][l].rearrange("(kc p) n -> p kc n", p=128)
                for kc in range(8):
                    S.dma("pool", w_up[:, kc, :], usrc[:, kc, :], writes=[b_wu])
                dsrc = I["w_down"][l].rearrange("(kc p) n -> p kc n", p=128)
                for k4 in range(8):
                    S.dma("pool", w_dn[:, k4 * 4:(k4 + 1) * 4, :], dsrc[:, k4 * 4:(k4 + 1) * 4, :], writes=[b_wd])
                S.dma("sp", gM[:], GB[l, 1], writes=[b_tab])
                S.dma("sp", lng[:], I["ln_mlp_g"][l, :].partition_broadcast(128), writes=[b_tab])
                S.dma("sp", lnb[:], I["ln_mlp_b"][l, :].partition_broadcast(128), writes=[b_tab])
                ri = 0
                for T in range(16):
                    h_t, bh = h2[T % 2], b_h2[T % 2]
                    S.dma("sp", h_t[:, :, :], H2T[:, :, T * 256:(T + 1) * 256].rearrange("c r t -> r c t"), writes=[bh])
                    for hc in range(32):
                        bk = hc % 4
                        for kc in range(8):
                            PE(lambda: nc.tensor.matmul(pb[bk][:, 0:256], lhsT=w_up[:, kc, hc * 128:(hc + 1) * 128], rhs=h_t[:, kc, :],
                                                        start=(kc == 0), stop=(kc == 7)), [b_wu, bh], [bp[bk]], signal=(kc == 7))
                        r_, br = rr[ri % 3], b_rr[ri % 3]
                        ri += 1
                        A(lambda: nc.scalar.activation(out=r_[:, :], in_=pb[bk][:, 0:256], func=AF.Relu), [bp[bk]], [br])
                        if hc % 2 == 0:
                            V(lambda: nc.vector.tensor_tensor(out=uT[:, hc, :], in0=r_[:, :], in1=r_[:, :], op=ALU.mult), [br], [b_uT])
                        else:
                            G(lambda: nc.gpsimd.tensor_tensor(out=uT[:, hc, :], in0=r_[:, :], in1=r_[:, :], op=ALU.mult), [br], [b_uT])
                    for s in range(2):
                        sg = T * 2 + s
                        i2 = sg % 2
                        rows = slice(sg * 128, (sg + 1) * 128)
                        S.dma("sp", x1[i2][:], X1[rows, :], writes=[b_x1[i2]])
                        for hf in range(2):
                            bk = 4 + 2 * i2 + hf
                            for hc in range(32):
                                PE(lambda: nc.tensor.matmul(pb[bk][:, :], lhsT=uT[:, hc, s * 128:(s + 1) * 128],
                                                            rhs=w_dn[:, hc, hf * 512:(hf + 1) * 512], start=(hc == 0), stop=(hc == 31)),
                                   [b_uT, b_wd], [bp[bk]], signal=(hc == 31))
                            hs = slice(hf * 512, (hf + 1) * 512)
                            V(lambda: nc.vector.tensor_tensor(out=v[i2][:, hs], in0=pb[bk][:, :], in1=gM[:, hs], op=ALU.mult),
                              [bp[bk], b_tab], [b_v[i2]])
                        V(lambda: nc.vector.scalar_tensor_tensor(out=v[i2][:, :], in0=x1[i2][:, :], scalar=ALPHA, in1=v[i2][:, :],
                                                                 op0=ALU.mult, op1=ALU.add), [b_x1[i2], b_v[i2]], [b_v[i2]])
                        layer_norm(v[i2][:, :], b_v[i2], v[i2][:, :], b_v[i2], lng[:, :], lnb[:, :], b_tab, stats, mv, b_stt, "pool")
                        S.dma("sp", dst[rows, :], v[i2][:], reads=[b_v[i2]])
                S.barrier()

        for l in range(nlayers):
            if "stop0" in dbg:
                break
            xsrc = I["x"] if l == 0 else XN
            phase_P(l, xsrc)
            if "stopP" in dbg:
                break
            with ExitStack() as lst:
                y_res = sb(lst, f"y_res{l}", [128, NSUB, D], BF16)
                b_y = [Buf(f"y{i}") for i in range(NSUB)]
                phase_att(l, y_res, b_y)
                phase_D(l, y_res, b_y)
                if YDBG is not None and l == 0:
                    for sg in range(NSUB):
                        S.dma("sp", YDBG[sg * 128:(sg + 1) * 128, :], y_res[:, sg, :], reads=[b_y[sg]])
                    S.barrier()
                if "stopA" in dbg:
                    break
                phase_O1(l, xsrc, y_res, b_y)
            phase_O2(l, out if l == nlayers - 1 else XN)
        S.barrier()
        print("ops", S.n_ops, "dmas", S.n_dma, "sems", S.nsem)
    return nc


DBG = {}
_CONSTS = None


def make_in_maps(inputs):
    global _CONSTS
    if _CONSTS is None:
        _CONSTS = _host_consts()
    shared = {}
    for n in W_NAMES:
        a = np.ascontiguousarray(np.asarray(inputs[n], dtype=np.float32))
        shared[n] = a.reshape(W_SHAPES[n])
    for n, a in _CONSTS.items():
        shared["k_" + n] = a
    x = np.asarray(inputs["x"], dtype=np.float32)
    c = np.asarray(inputs["c"], dtype=np.float32)
    maps = []
    for b in range(8):
        m = dict(shared)
        m["x"] = np.ascontiguousarray(x[b])
        m["c"] = np.ascontiguousarray(c[b].reshape(8, 128).T)
        maps.append(m)
    return maps


def kernel(**inputs):
    nc = build()
    in_maps = make_in_maps(inputs)
    res = run_bass_kernel_spmd(nc, in_maps, core_ids=list(range(8)))
    return np.stack([np.asarray(r["out"]) for r in res.results], axis=0).astype(np.float32)
```

```python
import math
from contextlib import ExitStack
import numpy as np
import concourse.bass as bass
import concourse.mybir as mybir
from concourse.bass_utils import run_bass_kernel_spmd

F32 = mybir.dt.float32
BF16 = mybir.dt.bfloat16
AF = mybir.ActivationFunctionType
ALU = mybir.AluOpType
AX = mybir.AxisListType

SEQ = 4096
D = 1024
NSUB = SEQ // 128
HID = 4096
INC = 2720
ALPHA = 4 ** 0.25
SLOPES_A = [2.0 ** -1, 2.0 ** -3, 2.0 ** -5, 2.0 ** -7]
SLOPES_D = [2.0 ** -2, 2.0 ** -4, 2.0 ** -6, 2.0 ** -8]
DILS = [1, 4, 16]
SC_A = 32 ** -0.5
SC_B = 96 ** -0.5
SC_C = 0.125
SC_D = 0.125
VP = 66
VW = 4 * VP


class Buf:
    __slots__ = ("name", "w", "r", "excl")

    def __init__(self, name="", excl=False):
        self.name = name
        self.w = None
        self.r = []
        self.excl = excl


def PB(name=""):
    return Buf(name, excl=True)


class Tok:
    __slots__ = ("eng", "sem", "val")

    def __init__(self, eng, sem=None, val=None):
        self.eng = eng
        self.sem = sem
        self.val = val


class Sched:
    EPOCH = 20000
    NDMA = 8

    def __init__(self, nc, stack):
        self.nc = nc
        self.stack = stack
        self.engs = {"pe": nc.tensor, "act": nc.scalar, "dve": nc.vector,
                     "pool": nc.gpsimd, "sp": nc.sync}
        self.count = {e: 0 for e in self.engs}
        self.cursem = {}
        self.pending = {e: [] for e in self.engs}
        self.waited = {e: {} for e in self.engs}
        self.nsem = 0
        for e in self.engs:
            self._new_epoch(e)
        self.dma_sems, self.dma_cnt, self.dma_last, self.dma_i = {}, {}, {}, {}
        for q in ("sp", "act", "pool"):
            self.dma_sems[q] = [self._sem(f"dma_{q}_{i}") for i in range(self.NDMA)]
            self.dma_cnt[q] = [0] * self.NDMA
            self.dma_last[q] = [None] * self.NDMA
            self.dma_i[q] = 0
        self.n_ops = {e: 0 for e in self.engs}
        self.n_dma = 0

    def _sem(self, name):
        self.nsem += 1
        return self.stack.enter_context(self.nc.semaphore(name))

    def _new_epoch(self, e):
        self.cursem[e] = self._sem(f"s_{e}_{self.nsem}")
        self.count[e] = 0

    def _wait(self, eng, tok):
        if tok is None:
            return
        if tok.sem is None:
            raise RuntimeError(f"dependency on unsignalled op on {tok.eng}")
        key = id(tok.sem)
        w = self.waited[eng]
        if w.get(key, 0) >= tok.val:
            return
        w[key] = tok.val
        self.engs[eng].wait_ge(tok.sem, tok.val)

    def _deps(self, eng, reads, writes):
        for b in reads:
            t = b.w
            if t is not None and not (t.eng == eng and eng == "pe"):
                self._wait(eng, t)
        for b in writes:
            t = b.w
            if t is not None and t.eng != eng:
                self._wait(eng, t)
            for t in b.r:
                if t.eng != eng:
                    self._wait(eng, t)

    def _record(self, tok, reads, writes):
        for b in reads:
            b.r.append(tok)
            if len(b.r) > 16:
                last = {}
                for t in b.r:
                    last[(t.eng, id(t.sem))] = t
                b.r = list(last.values())
        for b in writes:
            b.w = tok
            b.r = []

    def op(self, eng, fn, reads=(), writes=(), signal=True):
        if any(b.excl for b in reads):
            writes = list(writes) + [b for b in reads if b.excl]
            reads = [b for b in reads if not b.excl]
        self._deps(eng, reads, writes)
        inst = fn()
        self.n_ops[eng] += 1
        tok = Tok(eng)
        self.pending[eng].append(tok)
        if signal:
            if self.count[eng] >= self.EPOCH:
                self._new_epoch(eng)
            self.count[eng] += 1
            sem = self.cursem[eng]
            inst.then_inc(sem, 1)
            for t in self.pending[eng]:
                t.sem = sem
                t.val = self.count[eng]
            self.pending[eng] = []
        self._record(tok, reads, writes)
        return tok

    def prewait(self, eng, reads=(), writes=()):
        writes = list(writes) + [b for b in reads if b.excl]
        reads = [b for b in reads if not b.excl]
        self._deps(eng, reads, writes)

    def dma(self, q, out, in_, reads=(), writes=(), **kw):
        i = self.dma_i[q]
        self.dma_i[q] = (i + 1) % self.NDMA
        prev = self.dma_last[q][i]
        if prev is not None:
            self._wait(q, prev)
        self._deps(q, reads, writes)
        sem = self.dma_sems[q][i]
        self.dma_cnt[q][i] += 16
        inst = self.engs[q].dma_start(out=out, in_=in_, **kw)
        inst.then_inc(sem, 16)
        tok = Tok("dma_" + q + str(i), sem, self.dma_cnt[q][i])
        self.dma_last[q][i] = tok
        self._record(tok, reads, writes)
        self.n_dma += 1
        return tok

    def barrier(self):
        toks = []
        for e in self.engs:
            if self.pending[e]:
                raise RuntimeError(f"barrier with unsignalled ops on {e}")
            if self.count[e] > 0:
                toks.append(Tok(e, self.cursem[e], self.count[e]))
        for q in self.dma_last:
            for t in self.dma_last[q]:
                if t is not None:
                    toks.append(t)
        for e in self.engs:
            for t in toks:
                if t.eng != e:
                    self._wait(e, t)


def _host_consts():
    c = {}
    c["ident"] = np.eye(128, dtype=np.float32)
    tok = (np.arange(NSUB)[None, :] * 128 + np.arange(128)[:, None]).astype(np.float64)
    freqs = 10000.0 ** (-np.arange(16, dtype=np.float64) / 16)

    def cs(pos):
        ang = pos[..., None].astype(np.float32).astype(np.float64) * freqs.astype(np.float32).astype(np.float64)
        ang = (pos[..., None].astype(np.float32) * freqs.astype(np.float32)).astype(np.float32)
        return np.cos(ang.astype(np.float64)), np.sin(ang.astype(np.float64))

    cp, sp_ = cs(tok)
    rb = np.zeros((128, NSUB, 2, 64), np.float64)
    for i, s in enumerate([SC_B, 1.0]):
        rb[:, :, i, 0:16] = cp * s
        rb[:, :, i, 16:32] = cp * s
        rb[:, :, i, 32:48] = -sp_ * s
        rb[:, :, i, 48:64] = sp_ * s
    c["ropeB"] = rb.astype(np.float32)
    cr, sr = cs(np.floor(tok / 64))
    cc, sc_ = cs(np.mod(tok, 64))
    rc = np.zeros((128, NSUB, 128), np.float64)
    rc[:, :, 0:16] = cr
    rc[:, :, 16:32] = cr
    rc[:, :, 32:48] = cc
    rc[:, :, 48:64] = cc
    rc[:, :, 64:80] = -sr
    rc[:, :, 80:96] = sr
    rc[:, :, 96:112] = -sc_
    rc[:, :, 112:128] = sc_
    c["ropeC"] = rc.astype(np.float32)
    ki = np.arange(128)[:, None].astype(np.float64)
    qi = np.arange(512)[None, :].astype(np.float64)
    al = np.zeros((128, 5, 512), np.float64)
    al[:, 0, :] = qi - ki
    for o in range(4):
        al[:, 1 + o, :] = np.abs(qi - ki - 128 * o)
    c["alibi"] = al.astype(np.float32)
    q128 = (np.arange(512) % 128)[None, :].astype(np.float64)
    dt = np.zeros((128, 2, 512), np.float64)
    da = np.abs(ki - 64 - q128)
    db = np.abs(ki + 64 - q128)
    dt[:, 0, :] = np.where(da <= 64, da, 1.0e6)
    dt[:, 1, :] = np.where(db <= 64, db, 1.0e6)
    c["dtab"] = dt.astype(np.float32)
    return c


W_NAMES = ["w_ada", "b_ada", "w_in", "w_o", "diff_lambda", "diff_subln_g", "mla_q_norm_g", "mla_w_uq",
           "mla_kv_norm_g", "mla_w_ukv", "gqa_q_norm_g", "gqa_k_norm_g", "ln_attn_g", "ln_attn_b",
           "w_up", "w_down", "ln_mlp_g", "ln_mlp_b"]
W_SHAPES = {"w_ada": [2, 1024, 6144], "b_ada": [2, 6144], "w_in": [2, 1024, INC], "w_o": [2, 1024, 1024],
            "diff_lambda": [2, 128], "diff_subln_g": [2, 64], "mla_q_norm_g": [2, 384],
            "mla_w_uq": [2, 384, 384], "mla_kv_norm_g": [2, 256], "mla_w_ukv": [2, 256, 512],
            "gqa_q_norm_g": [2, 64], "gqa_k_norm_g": [2, 64], "ln_attn_g": [2, 1024], "ln_attn_b": [2, 1024],
            "w_up": [2, 1024, HID], "w_down": [2, HID, 1024], "ln_mlp_g": [2, 1024], "ln_mlp_b": [2, 1024]}
C_SHAPES = {"ident": [128, 128], "ropeB": [128, NSUB, 2, 64], "ropeC": [128, NSUB, 128],
            "alibi": [128, 5, 512], "dtab": [128, 2, 512]}


def build(nlayers=2, dbg=()):
    nc = bass.Bass("TRN2", target_bir_lowering=False)
    I = {}
    I["x"] = nc.dram_tensor("x", [SEQ, D], F32, kind="ExternalInput").ap()
    I["c"] = nc.dram_tensor("c", [128, 8], F32, kind="ExternalInput").ap()
    for n in W_NAMES:
        I[n] = nc.dram_tensor(n, W_SHAPES[n], F32, kind="ExternalInput").ap()
    for n in C_SHAPES:
        I[n] = nc.dram_tensor("k_" + n, C_SHAPES[n], F32, kind="ExternalInput").ap()
    out = nc.dram_tensor("out", [SEQ, D], F32, kind="ExternalOutput").ap()

    def scratch(name, shape, dt):
        kind = "ExternalOutput" if name in dbg else "Internal"
        return nc.dram_tensor(name, shape, dt, kind=kind).ap()

    QKT = scratch("QKT", [15, 128, SEQ], BF16)
    VG = scratch("VG", [3, SEQ, VW], BF16)
    DTOK = scratch("DTOK", [SEQ, 512 + VW], BF16)
    OD = scratch("OD", [3, SEQ, 260], F32)
    X1 = scratch("X1", [SEQ, D], F32)
    H2T = scratch("H2T", [8, 128, SEQ], BF16)
    XN = scratch("XN", [SEQ, D], F32)
    GB = scratch("GB", [2, 2, 128, D], F32)
    YDBG = scratch("YDBG", [SEQ, D], BF16) if "YDBG" in dbg else None

    with ExitStack() as top:
        S = Sched(nc, top)

        uid = [0]

        def sb(st, name, shape, dt):
            uid[0] += 1
            return st.enter_context(nc.sbuf_tensor(f"s{uid[0]}_{name}", shape, dt))

        def pbank(st, name):
            uid[0] += 1
            return st.enter_context(nc.psum_tensor(f"p{uid[0]}_{name}", [128, 512], F32))

        def V(fn, reads, writes):
            return S.op("dve", fn, reads, writes)

        def A(fn, reads, writes):
            return S.op("act", fn, reads, writes)

        def G(fn, reads, writes):
            return S.op("pool", fn, reads, writes)

        def PE(fn, reads, writes, signal=True):
            return S.op("pe", fn, reads, writes, signal)

        ident_f = sb(top, "ident_f", [128, 128], F32)
        ident_b = sb(top, "ident_b", [128, 128], BF16)
        modT = sb(top, "modT", [128, 2, 48], F32)
        eps6 = sb(top, "eps6", [128, 1], F32)
        eps5 = sb(top, "eps5", [128, 1], F32)
        b_const = Buf("const")
        b_modT = Buf("modT")
        S.dma("sp", ident_f[:], I["ident"], writes=[b_const])
        S.dma("pool", ident_b[:], I["ident"], writes=[b_const])
        V(lambda: nc.vector.memset(eps6[:], 1e-6), [], [b_const])
        V(lambda: nc.vector.memset(eps5[:], 1e-5), [], [b_const])

        with ExitStack() as st:
            condT = sb(st, "condT", [128, 8], F32)
            ones_row = sb(st, "ones_row", [1, 128], F32)
            modrow = sb(st, "modrow", [1, 6144], F32)
            brow = sb(st, "brow", [1, 6144], F32)
            wa = [sb(st, f"wa{i}", [128, 8, 512], F32) for i in range(2)]
            gbt = sb(st, "gbt", [128, 1024], F32)
            ps = [pbank(st, f"sps{i}") for i in range(2)]
            b_cond, b_ones, b_mrow, b_brow, b_gbt = Buf(), Buf(), Buf(), Buf(), Buf()
            b_wa = [Buf(), Buf()]
            b_ps = [PB(), PB()]
            S.dma("sp", condT[:], I["c"], writes=[b_cond])
            A(lambda: nc.scalar.activation(out=condT[:], in_=condT[:], func=AF.Silu), [b_cond], [b_cond])
            V(lambda: nc.vector.memset(ones_row[:], 1.0), [], [b_ones])
            for l in range(nlayers):
                S.dma("sp", brow[:], I["b_ada"][l:l + 1, :], writes=[b_brow])
                wsrc = I["w_ada"][l].rearrange("(kc p) n -> p kc n", p=128)
                for pc in range(12):
                    S.dma("sp", wa[pc % 2][:], wsrc[:, :, pc * 512:(pc + 1) * 512], writes=[b_wa[pc % 2]])
                    for kc in range(8):
                        PE(lambda: nc.tensor.matmul(ps[pc % 2][0:1, :], lhsT=condT[:, kc:kc + 1], rhs=wa[pc % 2][:, kc, :],
                                                    start=(kc == 0), stop=(kc == 7)),
                           [b_cond, b_wa[pc % 2]], [b_ps[pc % 2]], signal=(kc == 7))
                    V(lambda: nc.vector.tensor_tensor(out=modrow[0:1, pc * 512:(pc + 1) * 512], in0=ps[pc % 2][0:1, :],
                                                      in1=brow[0:1, pc * 512:(pc + 1) * 512], op=ALU.add),
                      [b_ps[pc % 2], b_brow], [b_mrow])
                for j in range(48):
                    PE(lambda: nc.tensor.matmul(ps[0][:, j:j + 1], lhsT=modrow[0:1, j * 128:(j + 1) * 128],
                                                rhs=ones_row[0:1, 0:1], start=True, stop=True),
                       [b_mrow, b_ones], [b_ps[0]], signal=(j == 47))
                V(lambda: nc.vector.tensor_copy(modT[:, l, :], ps[0][:, 0:48]), [b_ps[0]], [b_modT])
                V(lambda: nc.vector.tensor_scalar(out=modT[:, l, 8:16], in0=modT[:, l, 8:16], scalar1=1.0, scalar2=None,
                                                  op0=ALU.add), [b_modT], [b_modT])
                V(lambda: nc.vector.tensor_scalar(out=modT[:, l, 32:40], in0=modT[:, l, 32:40], scalar1=1.0, scalar2=None,
                                                  op0=ALU.add), [b_modT], [b_modT])
                for gi, base in enumerate([2048, 5120]):
                    for hf in range(2):
                        PE(lambda: nc.tensor.matmul(ps[1][:, :], lhsT=ones_row[0:1, :],
                                                    rhs=modrow[0:1, base + hf * 512: base + (hf + 1) * 512],
                                                    start=True, stop=True), [b_mrow, b_ones], [b_ps[1]])
                        V(lambda: nc.vector.tensor_copy(gbt[:, hf * 512:(hf + 1) * 512], ps[1][:, :]), [b_ps[1]], [b_gbt])
                    S.dma("sp", GB[l, gi], gbt[:], reads=[b_gbt])
            S.barrier()

        def phase_P(l, xsrc):
            with ExitStack() as st:
                w_in = sb(st, "w_in", [128, 8, INC], BF16)
                w_uq = sb(st, "w_uq", [128, 3, 384], BF16)
                w_ukv = sb(st, "w_ukv", [128, 2, 512], BF16)
                gq_bc = sb(st, "gq_bc", [128, 384], F32)
                gkv_bc = sb(st, "gkv_bc", [128, 256], F32)
                gC_bc = sb(st, "gC_bc", [128, 6, 64], F32)
                ropeB = sb(st, "ropeB", [128, NSUB, 2, 64], F32)
                ropeC = sb(st, "ropeC", [128, NSUB, 128], F32)
                xs = [sb(st, f"xs{i}", [128, D], F32) for i in range(2)]
                hT = [sb(st, f"hT{i}", [128, 8, 512], BF16) for i in range(2)]
                bf32_l = [sb(st, f"bf32{i}", [128, 672], F32) for i in range(2)]
                cqk_l = [sb(st, f"cqk{i}", [128, 384], F32) for i in range(2)]
                junk_l = [sb(st, f"junk{i}", [128, 384], F32) for i in range(2)]
                qkA_l = [sb(st, f"qkA{i}", [128, 512], BF16) for i in range(2)]
                qkD_l = [sb(st, f"qkD{i}", [128, 512], BF16) for i in range(2)]
                VA_l = [sb(st, f"VA{i}", [128, 4, VP], BF16) for i in range(2)]
                VB_l = [sb(st, f"VB{i}", [128, 4, VP], BF16) for i in range(2)]
                VC_l = [sb(st, f"VC{i}", [128, 4, VP], BF16) for i in range(2)]
                VD_l = [sb(st, f"VD{i}", [128, 4, VP], BF16) for i in range(2)]
                cqn_l = [sb(st, f"cqn{i}", [128, 640], BF16) for i in range(2)]
                cT_l = [sb(st, f"cT{i}", [128, 5, 128], BF16) for i in range(2)]
                qB_l = [sb(st, f"qB{i}", [128, 4, 96], BF16) for i in range(2)]
                kB_l = [sb(st, f"kB{i}", [128, 4, 96], BF16) for i in range(2)]
                cC_l = [sb(st, f"cC{i}", [128, 384], BF16) for i in range(2)]
                t1_l = [sb(st, f"t1{i}", [128, 384], F32) for i in range(2)]
                t2_l = [sb(st, f"t2{i}", [128, 384], F32) for i in range(2)]
                nq_l = [sb(st, f"nq{i}", [128, 384], F32) for i in range(2)]
                stt__l = [sb(st, f"stt_{i}", [128, 16], F32) for i in range(2)]
                stage = [sb(st, f"stage{i}", [128, 15, 512], BF16) for i in range(2)]
                pb = [pbank(st, f"pp{i}") for i in range(8)]
                bp = [PB(f"pp{i}") for i in range(8)]
                b_w, b_tab = Buf(), Buf()
                b_xs = [Buf(), Buf()]
                b_hT = [Buf(), Buf()]
                BL = {n: [Buf(n + '0'), Buf(n + '1')] for n in ['bf32', 'cqk', 'junk', 'qkA', 'qkD', 'VA', 'VB', 'VC', 'VD', 'cqn', 'cT', 'qB', 'kB', 'cC', 't1', 't2', 'nq', 'stt_']}
                b_wk = [Buf() for _ in range(8)]
                b_hk = [[Buf() for _ in range(8)] for _ in range(2)]
                b_stage = [Buf(), Buf()]

                wsrc = I["w_in"][l].rearrange("(kc p) n -> p kc n", p=128)
                for kc in range(8):
                    S.dma("pool", w_in[:, kc, :], wsrc[:, kc, :], writes=[b_wk[kc]])
                S.dma("pool", w_uq[:], I["mla_w_uq"][l].rearrange("(kc p) n -> p kc n", p=128), writes=[b_w])
                S.dma("pool", w_ukv[:], I["mla_w_ukv"][l].rearrange("(kc p) n -> p kc n", p=128), writes=[b_w])
                S.dma("sp", gq_bc[:], I["mla_q_norm_g"][l, :].partition_broadcast(128), writes=[b_tab])
                S.dma("sp", gkv_bc[:], I["mla_kv_norm_g"][l, :].partition_broadcast(128), writes=[b_tab])
                for h in range(6):
                    src = I["gqa_q_norm_g"] if h < 4 else I["gqa_k_norm_g"]
                    S.dma("sp", gC_bc[:, h, :], src[l, :].partition_broadcast(128), writes=[b_tab])
                V(lambda: nc.vector.tensor_scalar(out=gC_bc[:, 0:4, :], in0=gC_bc[:, 0:4, :], scalar1=SC_C, scalar2=None,
                                                  op0=ALU.mult), [b_tab], [b_tab])
                S.dma("sp", ropeB[:], I["ropeB"], writes=[b_tab])
                S.dma("sp", ropeC[:], I["ropeC"], writes=[b_tab])
                for i_ in range(2):
                    for n_ in ('VA', 'VB', 'VC', 'VD'):
                        vt = {'VA': VA_l, 'VB': VB_l, 'VC': VC_l, 'VD': VD_l}[n_][i_]
                        V(lambda: nc.vector.memset(vt[:], 1.0), [], [BL[n_][i_]])

                groups = [(0, 512), (512, 512), (1024, 416), (1440, 512), (1952, 512), (2464, 256)]
                def stage1(sg):
                    T, s = sg // 4, sg % 4
                    tsl = slice(s * 128, (s + 1) * 128)
                    h_t, bhk = hT[T % 2], b_hk[T % 2]
                    stg, bstg = stage[T % 2], b_stage[T % 2]
                    bf32 = bf32_l[sg % 2]
                    b_bf32 = BL['bf32'][sg % 2]
                    cqk = cqk_l[sg % 2]
                    b_cqk = BL['cqk'][sg % 2]
                    junk = junk_l[sg % 2]
                    b_junk = BL['junk'][sg % 2]
                    qkA = qkA_l[sg % 2]
                    b_qkA = BL['qkA'][sg % 2]
                    qkD = qkD_l[sg % 2]
                    b_qkD = BL['qkD'][sg % 2]
                    VA = VA_l[sg % 2]
                    b_VA = BL['VA'][sg % 2]
                    VB = VB_l[sg % 2]
                    b_VB = BL['VB'][sg % 2]
                    VC = VC_l[sg % 2]
                    b_VC = BL['VC'][sg % 2]
                    VD = VD_l[sg % 2]
                    b_VD = BL['VD'][sg % 2]
                    cqn = cqn_l[sg % 2]
                    b_cqn = BL['cqn'][sg % 2]
                    cT = cT_l[sg % 2]
                    b_cT = BL['cT'][sg % 2]
                    qB = qB_l[sg % 2]
                    b_qB = BL['qB'][sg % 2]
                    kB = kB_l[sg % 2]
                    b_kB = BL['kB'][sg % 2]
                    cC = cC_l[sg % 2]
                    b_cC = BL['cC'][sg % 2]
                    t1 = t1_l[sg % 2]
                    b_t1 = BL['t1'][sg % 2]
                    t2 = t2_l[sg % 2]
                    b_t2 = BL['t2'][sg % 2]
                    nq = nq_l[sg % 2]
                    b_nq = BL['nq'][sg % 2]
                    stt_ = stt__l[sg % 2]
                    b_st = BL['stt_'][sg % 2]
                    x_t, bx = xs[sg % 2], b_xs[sg % 2]
                    S.dma("sp", x_t[:], xsrc[sg * 128:(sg + 1) * 128, :], writes=[bx])
                    for hf in range(2):
                        for cc in range(4):
                            kc = hf * 4 + cc
                            PE(lambda: nc.tensor.transpose(pb[hf][:, cc * 128:(cc + 1) * 128], x_t[:, kc * 128:(kc + 1) * 128],
                                                           ident_f[:]), [bx, b_const], [bp[hf]], signal=(cc == 3))
                        for cc in range(4):
                            kc = hf * 4 + cc
                            if cc % 2 == 0:
                                A(lambda: nc.scalar.activation(out=h_t[:, kc, tsl], in_=pb[hf][:, cc * 128:(cc + 1) * 128],
                                                               func=AF.Identity, scale=modT[:, l, 8 + kc:9 + kc],
                                                               bias=modT[:, l, kc:kc + 1]), [bp[hf], b_modT], [bhk[kc]])
                            else:
                                V(lambda: nc.vector.tensor_scalar(out=h_t[:, kc, tsl], in0=pb[hf][:, cc * 128:(cc + 1) * 128],
                                                                  scalar1=modT[:, l, 8 + kc:9 + kc],
                                                                  scalar2=modT[:, l, kc:kc + 1], op0=ALU.mult, op1=ALU.add),
                                  [bp[hf], b_modT], [bhk[kc]])
                    for gi, (c0, ncol) in enumerate(groups):
                        bk = 2 + gi % 3
                        for kc in range(8):
                            PE(lambda: nc.tensor.matmul(pb[bk][:, 0:ncol], lhsT=h_t[:, kc, tsl], rhs=w_in[:, kc, c0:c0 + ncol],
                                                        start=(kc == 0), stop=(kc == 7)), [bhk[kc], b_wk[kc]], [bp[bk]], signal=(kc == 7))
                        P_ = pb[bk]
                        if gi == 0:
                            A(lambda: nc.scalar.activation(out=qkA[:, 0:256], in_=P_[:, 0:256], func=AF.Identity, scale=SC_A),
                              [bp[bk]], [b_qkA])
                            V(lambda: nc.vector.tensor_copy(qkA[:, 256:512], P_[:, 256:512]), [bp[bk]], [b_qkA])
                        elif gi == 1:
                            A(lambda: nc.scalar.activation(func=AF.Identity, out=VA[:, :, 0:64], in_=P_[:, 0:256].rearrange("p (h d) -> p h d", d=64)),
                              [bp[bk]], [b_VA])
                            V(lambda: nc.vector.tensor_copy(bf32[:, 0:256], P_[:, 256:512]), [bp[bk]], [b_bf32])
                        elif gi == 2:
                            V(lambda: nc.vector.tensor_copy(bf32[:, 256:672], P_[:, 0:416]), [bp[bk]], [b_bf32])
                        elif gi == 3:
                            V(lambda: nc.vector.tensor_copy(cqk[:, :], P_[:, 0:384]), [bp[bk]], [b_cqk])
                            A(lambda: nc.scalar.activation(func=AF.Identity, out=VC[:, 0:2, 0:64],
                                                     in_=P_[:, 384:512].rearrange("p (h d) -> p h d", d=64)),
                              [bp[bk]], [b_VC])
                        elif gi == 4:
                            A(lambda: nc.scalar.activation(out=qkD[:, 0:256], in_=P_[:, 0:256], func=AF.Identity, scale=SC_D),
                              [bp[bk]], [b_qkD])
                            V(lambda: nc.vector.tensor_copy(qkD[:, 256:512], P_[:, 256:512]), [bp[bk]], [b_qkD])
                        else:
                            A(lambda: nc.scalar.activation(func=AF.Identity, out=VD[:, :, 0:64], in_=P_[:, 0:256].rearrange("p (h d) -> p h d", d=64)),
                              [bp[bk]], [b_VD])

                def stage2(sg):
                    T, s = sg // 4, sg % 4
                    tsl = slice(s * 128, (s + 1) * 128)
                    h_t, bhk = hT[T % 2], b_hk[T % 2]
                    stg, bstg = stage[T % 2], b_stage[T % 2]
                    bf32 = bf32_l[sg % 2]
                    b_bf32 = BL['bf32'][sg % 2]
                    cqk = cqk_l[sg % 2]
                    b_cqk = BL['cqk'][sg % 2]
                    junk = junk_l[sg % 2]
                    b_junk = BL['junk'][sg % 2]
                    qkA = qkA_l[sg % 2]
                    b_qkA = BL['qkA'][sg % 2]
                    qkD = qkD_l[sg % 2]
                    b_qkD = BL['qkD'][sg % 2]
                    VA = VA_l[sg % 2]
                    b_VA = BL['VA'][sg % 2]
                    VB = VB_l[sg % 2]
                    b_VB = BL['VB'][sg % 2]
                    VC = VC_l[sg % 2]
                    b_VC = BL['VC'][sg % 2]
                    VD = VD_l[sg % 2]
                    b_VD = BL['VD'][sg % 2]
                    cqn = cqn_l[sg % 2]
                    b_cqn = BL['cqn'][sg % 2]
                    cT = cT_l[sg % 2]
                    b_cT = BL['cT'][sg % 2]
                    qB = qB_l[sg % 2]
                    b_qB = BL['qB'][sg % 2]
                    kB = kB_l[sg % 2]
                    b_kB = BL['kB'][sg % 2]
                    cC = cC_l[sg % 2]
                    b_cC = BL['cC'][sg % 2]
                    t1 = t1_l[sg % 2]
                    b_t1 = BL['t1'][sg % 2]
                    t2 = t2_l[sg % 2]
                    b_t2 = BL['t2'][sg % 2]
                    nq = nq_l[sg % 2]
                    b_nq = BL['nq'][sg % 2]
                    stt_ = stt__l[sg % 2]
                    b_st = BL['stt_'][sg % 2]
                    V(lambda: nc.vector.scalar_tensor_tensor(out=junk[:, 0:384], in0=bf32[:, 0:384], scalar=1.0,
                                                             in1=bf32[:, 0:384], op0=ALU.mult, op1=ALU.mult,
                                                             accum_out=stt_[:, 0:1]), [b_bf32], [b_junk, b_st])
                    V(lambda: nc.vector.scalar_tensor_tensor(out=junk[:, 0:256], in0=bf32[:, 384:640], scalar=1.0,
                                                             in1=bf32[:, 384:640], op0=ALU.mult, op1=ALU.mult,
                                                             accum_out=stt_[:, 1:2]), [b_bf32], [b_junk, b_st])
                    A(lambda: nc.scalar.activation(out=stt_[:, 2:3], in_=stt_[:, 0:1], func=AF.Ln, scale=1.0 / 384,
                                                   bias=eps6[:, 0:1]), [b_st, b_const], [b_st])
                    A(lambda: nc.scalar.activation(out=stt_[:, 3:4], in_=stt_[:, 1:2], func=AF.Ln, scale=1.0 / 256,
                                                   bias=eps6[:, 0:1]), [b_st, b_const], [b_st])
                    A(lambda: nc.scalar.activation(out=stt_[:, 4:6], in_=stt_[:, 2:4], func=AF.Exp, scale=-0.5), [b_st], [b_st])
                    V(lambda: nc.vector.scalar_tensor_tensor(out=cqn[:, 0:384], in0=bf32[:, 0:384], scalar=stt_[:, 4:5],
                                                             in1=gq_bc[:, :], op0=ALU.mult, op1=ALU.mult),
                      [b_bf32, b_st, b_tab], [b_cqn])
                    V(lambda: nc.vector.scalar_tensor_tensor(out=cqn[:, 384:640], in0=bf32[:, 384:640], scalar=stt_[:, 5:6],
                                                             in1=gkv_bc[:, :], op0=ALU.mult, op1=ALU.mult),
                      [b_bf32, b_st, b_tab], [b_cqn])
                    p5b = pb[5][:, :].bitcast(BF16)
                    for j in range(5):
                        PE(lambda: nc.tensor.transpose(p5b[:, j * 128:(j + 1) * 128], cqn[:, j * 128:(j + 1) * 128], ident_b[:]),
                           [b_cqn, b_const], [bp[5]], signal=(j == 4))
                    V(lambda: nc.vector.tensor_copy(cT[:, :, :], p5b[:, 0:640].rearrange("p (c t) -> p c t", t=128)),
                      [bp[5]], [b_cT])
                    for j in range(3):
                        PE(lambda: nc.tensor.matmul(pb[6][:, 0:384], lhsT=cT[:, j, :], rhs=w_uq[:, j, :], start=(j == 0),
                                                    stop=(j == 2)), [b_cT, b_w], [bp[6]], signal=(j == 2))
                    for j in range(2):
                        PE(lambda: nc.tensor.matmul(pb[7][:, 0:512], lhsT=cT[:, 3 + j, :], rhs=w_ukv[:, j, :], start=(j == 0),
                                                    stop=(j == 1)), [b_cT, b_w], [bp[7]], signal=(j == 1))
                    q3 = pb[6][:, 0:384].rearrange("p (h d) -> p h d", d=96)
                    kv3 = pb[7][:, 0:512].rearrange("p (h d) -> p h d", d=128)
                    A(lambda: nc.scalar.activation(out=qB[:, :, 0:64], in_=q3[:, :, 0:64], func=AF.Identity, scale=SC_B),
                      [bp[6]], [b_qB])
                    Tq = ropeB[:, sg, 0, :]
                    Tk = ropeB[:, sg, 1, :]
                    t1q = t1[:, 0:128].rearrange("p (h d) -> p h d", d=32)
                    t2q = t2[:, 0:128].rearrange("p (h d) -> p h d", d=32)
                    V(lambda: nc.vector.tensor_tensor(out=t1q, in0=q3[:, :, 64:96],
                                                      in1=Tq[:, 0:32].unsqueeze(1).to_broadcast([128, 4, 32]), op=ALU.mult),
                      [bp[6], b_tab], [b_t1])
                    V(lambda: nc.vector.tensor_tensor(out=t2q[:, :, 0:16], in0=q3[:, :, 80:96],
                                                      in1=Tq[:, 32:48].unsqueeze(1).to_broadcast([128, 4, 16]), op=ALU.mult),
                      [bp[6], b_tab], [b_t2])
                    V(lambda: nc.vector.tensor_tensor(out=t2q[:, :, 16:32], in0=q3[:, :, 64:80],
                                                      in1=Tq[:, 48:64].unsqueeze(1).to_broadcast([128, 4, 16]), op=ALU.mult),
                      [bp[6], b_tab], [b_t2])
                    G(lambda: nc.gpsimd.tensor_tensor(out=qB[:, :, 64:96], in0=t1q, in1=t2q, op=ALU.add), [b_t1, b_t2], [b_qB])
                    V(lambda: nc.vector.tensor_copy(kB[:, :, 0:64], kv3[:, :, 0:64]), [bp[7]], [b_kB])
                    A(lambda: nc.scalar.activation(func=AF.Identity, out=VB[:, :, 0:64], in_=kv3[:, :, 64:128]), [bp[7]], [b_VB])
                    kr = bf32[:, 640:672]
                    V(lambda: nc.vector.tensor_tensor(out=t1[:, 128:160], in0=kr, in1=Tk[:, 0:32], op=ALU.mult),
                      [b_bf32, b_tab], [b_t1])
                    V(lambda: nc.vector.tensor_tensor(out=t2[:, 128:144], in0=bf32[:, 656:672], in1=Tk[:, 32:48], op=ALU.mult),
                      [b_bf32, b_tab], [b_t2])
                    V(lambda: nc.vector.tensor_tensor(out=t2[:, 144:160], in0=bf32[:, 640:656], in1=Tk[:, 48:64], op=ALU.mult),
                      [b_bf32, b_tab], [b_t2])
                    G(lambda: nc.gpsimd.tensor_tensor(out=kB[:, :, 64:96],
                                                      in0=t1[:, 128:160].unsqueeze(1).to_broadcast([128, 4, 32]),
                                                      in1=t2[:, 128:160].unsqueeze(1).to_broadcast([128, 4, 32]), op=ALU.add),
                      [b_t1, b_t2], [b_kB])
                    c3 = cqk[:, :].rearrange("p (h d) -> p h d", d=64)
                    V(lambda: nc.vector.tensor_tensor(out=junk[:, :], in0=cqk[:, :], in1=cqk[:, :], op=ALU.mult),
                      [b_cqk], [b_junk])
                    V(lambda: nc.vector.tensor_reduce(out=stt_[:, 6:12], in_=junk[:, :].rearrange("p (h d) -> p h d", d=64),
                                                      axis=AX.X, op=ALU.add), [b_junk], [b_st])
                    A(lambda: nc.scalar.activation(out=stt_[:, 6:12], in_=stt_[:, 6:12], func=AF.Ln, scale=1.0 / 64,
                                                   bias=eps6[:, 0:1]), [b_st, b_const], [b_st])
                    A(lambda: nc.scalar.activation(out=stt_[:, 6:12], in_=stt_[:, 6:12], func=AF.Exp, scale=-0.5), [b_st], [b_st])
                    n3 = nq[:, :].rearrange("p (h d) -> p h d", d=64)
                    V(lambda: nc.vector.tensor_tensor(out=n3, in0=c3, in1=stt_[:, 6:12].unsqueeze(2).to_broadcast([128, 6, 64]),
                                                      op=ALU.mult), [b_cqk, b_st], [b_nq])
                    G(lambda: nc.gpsimd.tensor_tensor(out=n3, in0=n3, in1=gC_bc[:, :, :], op=ALU.mult), [b_nq, b_tab], [b_nq])
                    cosA = ropeC[:, sg, 0:64]
                    sinS = ropeC[:, sg, 64:128].rearrange("p (r f i) -> p r f i", r=2, f=2)
                    V(lambda: nc.vector.tensor_tensor(out=t1[:, :].rearrange("p (h d) -> p h d", d=64), in0=n3,
                                                      in1=cosA.unsqueeze(1).to_broadcast([128, 6, 64]), op=ALU.mult),
                      [b_nq, b_tab], [b_t1])
                    n5 = nq[:, :].rearrange("p (h r f i) -> p h r f i", h=6, r=2, f=2)
                    t5 = t2[:, :].rearrange("p (h r f i) -> p h r f i", h=6, r=2, f=2)
                    for f in range(2):
                        V(lambda: nc.vector.tensor_tensor(out=t5[:, :, :, f, :], in0=n5[:, :, :, 1 - f, :],
                                                          in1=sinS[:, :, f, :].unsqueeze(1).to_broadcast([128, 6, 2, 16]),
                                                          op=ALU.mult), [b_nq, b_tab], [b_t2])
                    G(lambda: nc.gpsimd.tensor_tensor(out=cC[:, :], in0=t1[:, :], in1=t2[:, :], op=ALU.add), [b_t1, b_t2], [b_cC])
                    tb0 = pb[6][:, :].bitcast(BF16)
                    tb1 = pb[7][:, :].bitcast(BF16)
                    for j in range(4):
                        PE(lambda: nc.tensor.transpose(tb0[:, j * 128:(j + 1) * 128], qkA[:, j * 128:(j + 1) * 128], ident_b[:]),
                           [b_qkA, b_const], [bp[6]], signal=False)
                    for h in range(4):
                        PE(lambda: nc.tensor.transpose(tb0[0:96, (4 + h) * 128:(5 + h) * 128], qB[:, h, :], ident_b[:]),
                           [b_qB, b_const], [bp[6]], signal=(h == 3))
                    for h in range(4):
                        PE(lambda: nc.tensor.transpose(tb1[0:96, h * 128:(h + 1) * 128], kB[:, h, :], ident_b[:]),
                           [b_kB, b_const], [bp[7]], signal=False)
                    for j in range(3):
                        PE(lambda: nc.tensor.transpose(tb1[:, (4 + j) * 128:(5 + j) * 128], cC[:, j * 128:(j + 1) * 128],
                                                       ident_b[:]), [b_cC, b_const], [bp[7]], signal=(j == 2))
                    V(lambda: nc.vector.tensor_copy(stg[:, 0:8, tsl], tb0[:, 0:1024].rearrange("p (c t) -> p c t", t=128)),
                      [bp[6]], [bstg])
                    A(lambda: nc.scalar.activation(func=AF.Identity, out=stg[:, 8:15, tsl], in_=tb1[:, 0:896].rearrange("p (c t) -> p c t", t=128)),
                      [bp[7]], [bstg])
                    rows = slice(sg * 128, (sg + 1) * 128)
                    S.dma("sp", VG[0, rows, :], VA[:, :, :].rearrange("p h d -> p (h d)"), reads=[b_VA])
                    S.dma("sp", VG[1, rows, :], VB[:, :, :].rearrange("p h d -> p (h d)"), reads=[b_VB])
                    S.dma("sp", VG[2, rows, :], VC[:, :, :].rearrange("p h d -> p (h d)"), reads=[b_VC])
                    S.dma("sp", DTOK[rows, 0:512], qkD[:, :], reads=[b_qkD])
                    S.dma("sp", DTOK[rows, 512:512 + VW], VD[:, :, :].rearrange("p h d -> p (h d)"), reads=[b_VD])
                    if s == 3:
                        for (ca, cb) in [(0, 4), (4, 8), (8, 12), (12, 15)]:
                            S.dma("sp", QKT[ca:cb, :, T * 512:(T + 1) * 512].rearrange("c r t -> r c t"), stg[:, ca:cb, :],
                                  reads=[bstg])

                nsub_ = NSUB if 'nsub' not in DBG else DBG['nsub']
                stage1(0)
                for sg in range(nsub_):
                    if sg + 1 < nsub_:
                        stage1(sg + 1)
                    stage2(sg)
                S.barrier()

        def phase_att(l, y_res, b_y):
            with ExitStack() as st:
                qT = [sb(st, f"qT{i}", [128, SEQ], BF16) for i in range(4)]
                kT = [sb(st, f"kT{i}", [128, SEQ], BF16) for i in range(4)]
                Vg = sb(st, "Vg", [128, NSUB, VW + 64], BF16)
                alibi = sb(st, "alibi", [128, 5, 512], F32)
                Sb = [sb(st, f"Sb{i}", [128, 512], F32) for i in range(3)]
                E = [sb(st, f"E{i}", [128, 512], BF16) for i in range(4)]
                lam_t = sb(st, "lam_t", [128, 128], F32)
                lam = sb(st, "lam", [128, 8], F32)
                gA_bc = sb(st, "gA_bc", [128, 64], F32)
                o1 = sb(st, "o1", [128, 4, 64], F32)
                o2 = sb(st, "o2", [128, 4, 64], F32)
                osq = sb(st, "osq", [128, 4, 64], F32)
                rc = sb(st, "rc", [128, 16], F32)
                pS = [pbank(st, f"pS{i}") for i in range(4)]
                pOT = [pbank(st, f"pOT{i}") for i in range(2)]
                pO = [pbank(st, f"pO{i}") for i in range(2)]
                otS = [sb(st, f"otS{i}", [65, 512], F32) for i in range(2)]
                b_pOT = [PB(), PB()]
                b_otS = [Buf(), Buf()]
                b_qT = [Buf() for _ in range(4)]
                b_kT = [Buf() for _ in range(4)]
                b_Vg, b_al, b_lam, b_gA = Buf(), Buf(), Buf(), Buf()
                b_Sb = [Buf() for _ in range(3)]
                b_E = [Buf() for _ in range(4)]
                b_pS = [PB() for _ in range(4)]
                b_pO = [PB() for _ in range(4)]
                b_o1, b_o2, b_osq, b_rc = Buf(), Buf(), Buf(), Buf()
                V(lambda: nc.vector.memset(Vg[:, :, VW:VW + 64], 0.0), [], [b_Vg])
                S.dma("sp", alibi[:], I["alibi"], writes=[b_al])
                lam_init = 0.8 - 0.6 * math.exp(-0.3 * l)
                S.dma("sp", lam_t[:], I["diff_lambda"][l, :].partition_broadcast(128), writes=[b_lam])
                V(lambda: nc.vector.scalar_tensor_tensor(out=lam_t[:, 0:32], in0=lam_t[:, 0:32], scalar=1.0, in1=lam_t[:, 32:64],
                                                         op0=ALU.mult, op1=ALU.mult, accum_out=lam[:, 0:1]), [b_lam], [b_lam])
                V(lambda: nc.vector.scalar_tensor_tensor(out=lam_t[:, 64:96], in0=lam_t[:, 64:96], scalar=1.0,
                                                         in1=lam_t[:, 96:128], op0=ALU.mult, op1=ALU.mult,
                                                         accum_out=lam[:, 1:2]), [b_lam], [b_lam])
                A(lambda: nc.scalar.activation(out=lam[:, 2:4], in_=lam[:, 0:2], func=AF.Exp), [b_lam], [b_lam])
                V(lambda: nc.vector.tensor_tensor(out=lam[:, 4:5], in0=lam[:, 2:3], in1=lam[:, 3:4], op=ALU.subtract),
                  [b_lam], [b_lam])
                V(lambda: nc.vector.tensor_scalar(out=lam[:, 5:6], in0=lam[:, 4:5], scalar1=lam_init, scalar2=-1.0,
                                                  op0=ALU.add, op1=ALU.mult), [b_lam], [b_lam])
                S.dma("sp", gA_bc[:], I["diff_subln_g"][l, :].partition_broadcast(128), writes=[b_gA])
                V(lambda: nc.vector.tensor_scalar(out=gA_bc[:], in0=gA_bc[:], scalar1=1.0 - lam_init, scalar2=None,
                                                  op0=ALU.mult), [b_gA], [b_gA])

                state = {"qi": 0, "ei": 0, "si": 0, "sbi": 0}

                def load_map(chunk, r0, nrows, isq):
                    i = state["qi"] % 4
                    t, b = (qT[i], b_qT[i]) if isq else (kT[i], b_kT[i])
                    S.dma("sp", t[0:nrows, :], QKT[chunk, r0:r0 + nrows, :], writes=[b])
                    return t, b

                def load_V(g):
                    for a in range(4):
                        S.dma("sp", Vg[:, a * 8:(a + 1) * 8, 0:VW],
                              VG[g, a * 1024:(a + 1) * 1024, :].rearrange("(t p) c -> p t c", p=128), writes=[b_Vg])

                def job(maps, vcol, ycol, alibi_slope=None):
                    nm = len(maps)
                    pend = [None]
                    for qt in range(8):
                        if nm == 2:
                            groups = [[(kt, 0), (kt, 1)] for kt in range(NSUB)]
                        else:
                            groups = [[(2 * j, 0), (2 * j + 1, 0)] for j in range(NSUB // 2)]
                        ng = len(groups)
                        for gi in range(ng + 2):
                            if gi < ng:
                                banks = [2 * (gi % 2), 2 * (gi % 2) + 1]
                                S.prewait("pe", [mp[1] for mp in maps] + [mp[3] for mp in maps], [b_pS[bk] for bk in banks])
                                for idx, (kt, m) in enumerate(groups[gi]):
                                    q_t, bq, k_t, bk_, nr = maps[m][:5]
                                    p0 = maps[m][5] if len(maps[m]) > 5 else 0
                                    si = banks[idx]
                                    tp_ = (p0, 0) if p0 == 96 else None
                                    PE(lambda: nc.tensor.matmul(pS[si][:, :], lhsT=k_t[p0:p0 + nr, kt * 128:(kt + 1) * 128],
                                                                rhs=q_t[p0:p0 + nr, qt * 512:(qt + 1) * 512], start=True, stop=True,
                                                                tile_position=tp_),
                                       [bq, bk_], [b_pS[si]], signal=(idx == 1))
                            j = gi - 1
                            if 0 <= j < ng:
                                for idx, (kt, m) in enumerate(groups[j]):
                                    si = 2 * (j % 2) + idx
                                    ei = 2 * (j % 2) + idx
                                    if alibi_slope is None:
                                        A(lambda: nc.scalar.activation(out=E[ei][:, :], in_=pS[si][:, :], func=AF.Exp),
                                          [b_pS[si]], [b_E[ei]])
                                    else:
                                        k0, q0 = kt * 128, qt * 512
                                        if k0 + 128 <= q0:
                                            tab, sc_, bias = alibi[:, 0, :], -alibi_slope, -alibi_slope * (q0 - k0)
                                        elif k0 >= q0 + 512:
                                            tab, sc_, bias = alibi[:, 0, :], alibi_slope, -alibi_slope * (k0 - q0)
                                        else:
                                            tab, sc_, bias = alibi[:, 1 + (k0 - q0) // 128, :], -alibi_slope, 0.0
                                        sbi = state["sbi"] % 3
                                        state["sbi"] += 1
                                        V(lambda: nc.vector.scalar_tensor_tensor(out=Sb[sbi][:, :], in0=tab, scalar=sc_,
                                                                                 in1=pS[si][:, :], op0=ALU.mult, op1=ALU.add),
                                          [b_al, b_pS[si]], [b_Sb[sbi]])
                                        A(lambda: nc.scalar.activation(out=E[ei][:, :], in_=Sb[sbi][:, :], func=AF.Exp, bias=bias),
                                          [b_Sb[sbi]], [b_E[ei]])
                            j = gi - 2
                            if 0 <= j < ng:
                                eis_ = [2 * (j % 2), 2 * (j % 2) + 1]
                                S.prewait("pe", [b_E[e] for e in eis_] + [b_Vg], [b_pOT[m] for (_, m) in groups[j]])
                                for idx, (kt, m) in enumerate(groups[j]):
                                    ei = eis_[idx]
                                    PE(lambda: nc.tensor.matmul(pOT[m][:, :], lhsT=Vg[:, kt, vcol:vcol + 128], rhs=E[ei][:, :],
                                                                start=(kt == 0), stop=(kt == NSUB - 1)),
                                       [b_E[ei], b_Vg], [b_pOT[m]], signal=(idx == 1))
                            if gi == 4 and pend[0] is not None:
                                pend[0]()
                                pend[0] = None
                        for m in range(nm):
                            if alibi_slope is not None:
                                A(lambda: nc.scalar.activation(out=otS[m][:, :], in_=pOT[m][0:65, :], func=AF.Identity),
                                  [b_pOT[m]], [b_otS[m]])
                            else:
                                V(lambda: nc.vector.tensor_copy(otS[m][:, :], pOT[m][0:65, :]), [b_pOT[m]], [b_otS[m]])
                        pend[0] = (lambda qt=qt: finalize(qt, nm, ycol))
                    pend[0]()
                    pend[0] = None

                def finalize(qt, nm, ycol):
                    if True:
                        for m in range(nm):
                            for jj in range(4):
                                PE(lambda: nc.tensor.transpose(pO[m][:, jj * 65:(jj + 1) * 65], otS[m][0:65, jj * 128:(jj + 1) * 128],
                                                               ident_f[0:65, 0:65]), [b_otS[m], b_const], [b_pO[m]], signal=(jj == 3))
                        O1 = pO[0][:, 0:260].rearrange("p (j d) -> p j d", d=65)
                        ydst = y_res[:, qt * 4:(qt + 1) * 4, ycol:ycol + 64]
                        by = b_y[qt * 4:(qt + 1) * 4]
                        V(lambda: nc.vector.reciprocal(out=rc[:, 0:4], in_=O1[:, :, 64]), [b_pO[0]], [b_rc])
                        if nm == 1:
                            V(lambda: nc.vector.tensor_tensor(out=ydst, in0=O1[:, :, 0:64],
                                                              in1=rc[:, 0:4].unsqueeze(2).to_broadcast([128, 4, 64]),
                                                              op=ALU.mult), [b_pO[0], b_rc], by)
                        else:
                            O2 = pO[1][:, 0:260].rearrange("p (j d) -> p j d", d=65)
                            V(lambda: nc.vector.reciprocal(out=rc[:, 4:8], in_=O2[:, :, 64]), [b_pO[1]], [b_rc])
                            V(lambda: nc.vector.tensor_scalar(out=rc[:, 4:8], in0=rc[:, 4:8], scalar1=lam[:, 5:6], scalar2=None,
                                                              op0=ALU.mult), [b_rc, b_lam], [b_rc])
                            V(lambda: nc.vector.tensor_tensor(out=o1[:, :, :], in0=O1[:, :, 0:64],
                                                              in1=rc[:, 0:4].unsqueeze(2).to_broadcast([128, 4, 64]),
                                                              op=ALU.mult), [b_pO[0], b_rc], [b_o1])
                            V(lambda: nc.vector.tensor_tensor(out=o2[:, :, :], in0=O2[:, :, 0:64],
                                                              in1=rc[:, 4:8].unsqueeze(2).to_broadcast([128, 4, 64]),
                                                              op=ALU.mult), [b_pO[1], b_rc], [b_o2])
                            G(lambda: nc.gpsimd.tensor_tensor(out=o1[:, :, :], in0=o1[:, :, :], in1=o2[:, :, :], op=ALU.add),
                              [b_o1, b_o2], [b_o1])
                            G(lambda: nc.gpsimd.tensor_tensor(out=osq[:, :, :], in0=o1[:, :, :], in1=o1[:, :, :], op=ALU.mult),
                              [b_o1], [b_osq])
                            V(lambda: nc.vector.tensor_reduce(out=rc[:, 8:12], in_=osq[:, :, :], axis=AX.X, op=ALU.add),
                              [b_osq], [b_rc])
                            A(lambda: nc.scalar.activation(out=rc[:, 8:12], in_=rc[:, 8:12], func=AF.Ln, scale=1.0 / 64,
                                                           bias=eps6[:, 0:1]), [b_rc, b_const], [b_rc])
                            A(lambda: nc.scalar.activation(out=rc[:, 12:16], in_=rc[:, 8:12], func=AF.Exp, scale=-0.5),
                              [b_rc], [b_rc])
                            V(lambda: nc.vector.tensor_tensor(out=o2[:, :, :], in0=o1[:, :, :],
                                                              in1=rc[:, 12:16].unsqueeze(2).to_broadcast([128, 4, 64]),
                                                              op=ALU.mult), [b_o1, b_rc], [b_o2])
                            V(lambda: nc.vector.tensor_tensor(out=ydst, in0=o2[:, :, :],
                                                              in1=gA_bc[:, :].unsqueeze(1).to_broadcast([128, 4, 64]),
                                                              op=ALU.mult), [b_o2, b_gA], by)

                load_V(0)
                for h in range(4):
                    state["qi"] += 1
                    i_ = state["qi"] % 4
                    g0 = 2 * (h % 2)
                    S.dma("sp", qT[i_][32 * g0:32 * g0 + 64, :], QKT[h // 2, 32 * g0:32 * g0 + 64, :], writes=[b_qT[i_]])
                    S.dma("sp", kT[i_][32 * g0:32 * g0 + 64, :], QKT[2 + h // 2, 32 * g0:32 * g0 + 64, :], writes=[b_kT[i_]])
                    maps = [(qT[i_], b_qT[i_], kT[i_], b_kT[i_], 32, 32 * (g0 + c_)) for c_ in range(2)]
                    job(maps, h * VP, h * 64, alibi_slope=SLOPES_A[h])
                load_V(1)
                for h in range(4):
                    state["qi"] += 1
                    q_t, bq = load_map(4 + h, 0, 96, True)
                    k_t, bk_ = load_map(8 + h, 0, 96, False)
                    job([(q_t, bq, k_t, bk_, 96)], h * VP, 256 + h * 64)
                load_V(2)
                for h in range(4):
                    state["qi"] += 1
                    q_t, bq = load_map(12 + h // 2, (h % 2) * 64, 64, True)
                    if h % 2 == 0:
                        k_t, bk_ = load_map(14, (h // 2) * 64, 64, False)
                    job([(q_t, bq, k_t, bk_, 64)], (h // 2) * VP, 512 + h * 64)
                S.barrier()

        def phase_D(l, y_res, b_y):
            with ExitStack() as st:
                tokD = sb(st, "tokD", [128, 8, 512], BF16)
                Vsh = sb(st, "Vsh", [128, 9, VW], BF16)
                QTd = sb(st, "QTd", [128, 2, 1024], BF16)
                KTd = sb(st, "KTd", [128, 2, 1024 + 128], BF16)
                dtab = sb(st, "dtab", [128, 2, 512], F32)
                Sb = [sb(st, f"dSb{i}", [128, 512], F32) for i in range(4)]
                E = [sb(st, f"dE{i}", [128, 512], BF16) for i in range(4)]
                ost = [sb(st, f"ost{i}", [128, 4, 260], F32) for i in range(2)]
                acc = sb(st, "acc", [128, 260], F32)
                od = [sb(st, f"od{i}", [128, 260], F32) for i in range(3)]
                rcd = sb(st, "rcd", [128, 4], F32)
                pT = [pbank(st, f"dT{i}") for i in range(2)]
                pS = [pbank(st, f"dS{i}") for i in range(4)]
                pO = [pbank(st, f"dO{i}") for i in range(2)]
                b_tok, b_Vsh, b_QT, b_KT, b_dt = Buf(), Buf(), Buf(), Buf(), Buf()
                b_Sb = [Buf() for _ in range(4)]
                b_E = [Buf() for _ in range(4)]
                b_ost = [Buf(), Buf()]
                b_pT = [PB(), PB()]
                b_pS = [PB() for _ in range(4)]
                b_pO = [PB(), PB()]
                b_acc, b_od, b_rcd = Buf(), [Buf() for _ in range(3)], Buf()
                S.dma("sp", dtab[:], I["dtab"], writes=[b_dt])
                cnt = {"s": 0, "e": 0, "o": 0}
                for bi, d in enumerate(DILS):
                    Ltot = SEQ // d
                    nseg = max(1, Ltot // 1024)
                    for r in range(d):
                        for seg in range(nseg):
                            Lc = min(Ltot, 1024)
                            i0 = seg * 1024
                            nt = Lc // 128
                            V(lambda: nc.vector.memset(KTd[:, :, :], 0.0), [], [b_KT])
                            V(lambda: nc.vector.memset(Vsh[:, :, :], 0.0), [], [b_Vsh])
                            base = r + d * i0
                            src = DTOK[base: base + d * (Lc - 1) + 1: d, :]
                            S.dma("sp", tokD[:, 0:nt, :], src[:, 0:512].rearrange("(t p) c -> p t c", p=128), writes=[b_tok])
                            lo = i0 - 64
                            hi = i0 + Lc + 64
                            lo_c, hi_c = max(lo, 0), min(hi, Ltot)
                            u0 = lo_c - lo
                            n_rows = hi_c - lo_c
                            pos = 0
                            while pos < n_rows:
                                u = u0 + pos
                                tj, pj = u // 128, u % 128
                                take = min(128 - pj, n_rows - pos)
                                if pj == 0 and take == 128:
                                    nfull = (n_rows - pos) // 128
                                    t_first = r + d * (lo_c + pos)
                                    srcv = DTOK[t_first: t_first + d * (128 * nfull - 1) + 1: d, 512:512 + VW]
                                    S.dma("sp", Vsh[:, tj:tj + nfull, :], srcv.rearrange("(t p) c -> p t c", p=128),
                                          writes=[b_Vsh])
                                    pos += 128 * nfull
                                else:
                                    t_first = r + d * (lo_c + pos)
                                    srcv = DTOK[t_first: t_first + d * (take - 1) + 1: d, 512:512 + VW]
                                    S.dma("sp", Vsh[pj:pj + take, tj, :], srcv, writes=[b_Vsh])
                                    pos += take
                            for t in range(nt):
                                pTb = pT[t % 2][:, :].bitcast(BF16)
                                for j in range(4):
                                    PE(lambda: nc.tensor.transpose(pTb[:, j * 128:(j + 1) * 128], tokD[:, t, j * 128:(j + 1) * 128],
                                                                   ident_b[:]), [b_tok, b_const], [b_pT[t % 2]], signal=(j == 3))
                                V(lambda: nc.vector.tensor_copy(QTd[:, :, t * 128:(t + 1) * 128],
                                                                pTb[:, 0:256].rearrange("p (c t) -> p c t", t=128)),
                                  [b_pT[t % 2]], [b_QT])
                                A(lambda: nc.scalar.activation(func=AF.Identity, out=KTd[:, :, 64 + t * 128: 64 + (t + 1) * 128],
                                                         in_=pTb[:, 256:512].rearrange("p (c t) -> p c t", t=128)),
                                  [b_pT[t % 2]], [b_KT])
                            if nseg > 1:
                                for side in range(2):
                                    hs = i0 - 64 if side == 0 else i0 + Lc
                                    if hs < 0 or hs >= Ltot:
                                        continue
                                    t_first = r + d * hs
                                    srck = DTOK[t_first: t_first + d * 63 + 1: d, 256:512]
                                    S.dma("sp", tokD[0:64, 0, 0:256], srck, writes=[b_tok])
                                    pTb = pT[0][:, :].bitcast(BF16)
                                    for j in range(2):
                                        PE(lambda: nc.tensor.transpose(pTb[:, j * 128: j * 128 + 64],
                                                                       tokD[0:64, 0, j * 128:(j + 1) * 128], ident_b[0:64, 0:64]),
                                           [b_tok, b_const], [b_pT[0]], signal=(j == 1))
                                    col = 0 if side == 0 else 64 + Lc
                                    V(lambda: nc.vector.tensor_copy(
                                        KTd[:, :, col:col + 64],
                                        pTb[:, 0:256].rearrange("p (c t) -> p c t", t=128)[:, :, 0:64]), [b_pT[0]], [b_KT])
                            steps = [(g0, h, ab) for g0 in range(0, nt, 4) for h in range(4) for ab in range(2)]
                            n = len(steps)
                            sis, eis = [0] * n, [0] * n
                            LS, LP = 1, 2
                            for i in range(n + LP):
                                if i < n:
                                    g0, h, ab = steps[i]
                                    ng = min(4, nt - g0)
                                    ch, pr = h // 2, (h % 2) * 64
                                    si = cnt["s"] % 4
                                    cnt["s"] += 1
                                    sis[i] = si
                                    for jj in range(ng):
                                        qc = (g0 + jj) * 128
                                        kc0 = qc + ab * 128
                                        PE(lambda: nc.tensor.matmul(pS[si][:, jj * 128:(jj + 1) * 128],
                                                                    lhsT=KTd[pr:pr + 64, ch, kc0:kc0 + 128],
                                                                    rhs=QTd[pr:pr + 64, ch, qc:qc + 128], start=True, stop=True),
                                           [b_KT, b_QT], [b_pS[si]], signal=(jj == ng - 1))
                                j = i - LS
                                if 0 <= j < n:
                                    g0, h, ab = steps[j]
                                    ng = min(4, nt - g0)
                                    w = ng * 128
                                    si = sis[j]
                                    ei = cnt["e"] % 4
                                    cnt["e"] += 1
                                    eis[j] = ei
                                    V(lambda: nc.vector.scalar_tensor_tensor(out=Sb[ei][:, 0:w], in0=dtab[:, ab, 0:w],
                                                                             scalar=-SLOPES_D[h] * d, in1=pS[si][:, 0:w],
                                                                             op0=ALU.mult, op1=ALU.add),
                                      [b_dt, b_pS[si]], [b_Sb[ei]])
                                    A(lambda: nc.scalar.activation(out=E[ei][:, 0:w], in_=Sb[ei][:, 0:w], func=AF.Exp),
                                      [b_Sb[ei]], [b_E[ei]])
                                j = i - LP
                                if 0 <= j < n:
                                    g0, h, ab = steps[j]
                                    ng = min(4, nt - g0)
                                    ei = eis[j]
                                    oi = (g0 // 4) % 2
                                    for jj in range(ng):
                                        PE(lambda: nc.tensor.matmul(pO[h % 2][:, jj * 65:(jj + 1) * 65],
                                                                    lhsT=E[ei][:, jj * 128:(jj + 1) * 128],
                                                                    rhs=Vsh[:, g0 + jj + ab, h * VP:h * VP + 65],
                                                                    start=(ab == 0 and jj == 0), stop=(ab == 1),
                                                                    skip_group_check=True),
                                           [b_E[ei], b_Vsh], [b_pO[h % 2]], signal=(jj == ng - 1))
                                    if ab == 1:
                                        V(lambda: nc.vector.tensor_copy(ost[oi][:, 0:ng, h * 65:(h + 1) * 65],
                                                                        pO[h % 2][:, 0:ng * 65].rearrange("p (j d) -> p j d", d=65)),
                                          [b_pO[h % 2]], [b_ost[oi]])
                                        if h == 3:
                                            for jj in range(ng):
                                                t_first = r + d * (i0 + (g0 + jj) * 128)
                                                S.dma("sp", OD[bi, t_first: t_first + d * 127 + 1: d, :], ost[oi][:, jj, :],
                                                      reads=[b_ost[oi]])
                S.barrier()
                for sg in range(NSUB):
                    rows = slice(sg * 128, (sg + 1) * 128)
                    for bi in range(3):
                        S.dma("sp", od[bi][:], OD[bi, rows, :], writes=[b_od[bi]])
                    V(lambda: nc.vector.tensor_tensor(out=acc[:], in0=od[0][:], in1=od[1][:], op=ALU.add),
                      [b_od[0], b_od[1]], [b_acc])
                    V(lambda: nc.vector.tensor_tensor(out=acc[:], in0=acc[:], in1=od[2][:], op=ALU.add), [b_acc, b_od[2]], [b_acc])
                    a3 = acc[:, :].rearrange("p (h d) -> p h d", d=65)
                    V(lambda: nc.vector.reciprocal(out=rcd[:, 0:4], in_=a3[:, :, 64]), [b_acc], [b_rcd])
                    V(lambda: nc.vector.tensor_tensor(out=y_res[:, sg, 768:1024].rearrange("p (h d) -> p h d", d=64),
                                                      in0=a3[:, :, 0:64], in1=rcd[:, 0:4].unsqueeze(2).to_broadcast([128, 4, 64]),
                                                      op=ALU.mult), [b_acc, b_rcd], [b_y[sg]])
                S.barrier()

        def layer_norm(v, bv, dst, bdst, g_bc, b_bc, btab, stats, mv, bst, eng2):
            v4 = v.rearrange("p (c f) -> p c f", f=256)
            for c_ in range(4):
                V(lambda: nc.vector.bn_stats(out=stats[:, c_, :], in_=v4[:, c_, :]), [bv], [bst])
            V(lambda: nc.vector.bn_aggr(out=mv[:, 0:2], in_=stats[:, :, :].rearrange("p c f -> p (c f)")), [bst], [bst])
            A(lambda: nc.scalar.activation(out=mv[:, 2:3], in_=mv[:, 1:2], func=AF.Ln, bias=eps5[:, 0:1]), [bst, b_const], [bst])
            A(lambda: nc.scalar.activation(out=mv[:, 3:4], in_=mv[:, 2:3], func=AF.Exp, scale=-0.5), [bst], [bst])
            V(lambda: nc.vector.scalar_tensor_tensor(out=mv[:, 4:5], in0=mv[:, 0:1], scalar=-1.0, in1=mv[:, 3:4], op0=ALU.mult,
                                                     op1=ALU.mult), [bst], [bst])
            A(lambda: nc.scalar.activation(out=v, in_=v, func=AF.Identity, scale=mv[:, 3:4], bias=mv[:, 4:5]), [bv, bst], [bv])
            S.op(eng2, lambda: (nc.gpsimd if eng2 == "pool" else nc.vector).tensor_tensor(out=v, in0=v, in1=g_bc, op=ALU.mult),
                 [bv, btab], [bv])
            S.op(eng2, lambda: (nc.gpsimd if eng2 == "pool" else nc.vector).tensor_tensor(out=dst, in0=v, in1=b_bc, op=ALU.add),
                 [bv, btab], [bdst])

        def phase_O1(l, xsrc, y_res, b_y):
            with ExitStack() as st:
                w_o = sb(st, "w_o", [128, 8, D], BF16)
                gA = sb(st, "gA", [128, D], F32)
                lng = sb(st, "lng", [128, D], F32)
                lnb = sb(st, "lnb", [128, D], F32)
                xs = [sb(st, f"oxs{i}", [128, D], F32) for i in range(2)]
                yT = [sb(st, f"yT{i}", [128, 8, 128], BF16) for i in range(2)]
                v = [sb(st, f"ov{i}", [128, D], F32) for i in range(2)]
                x1 = [sb(st, f"ox1{i}", [128, D], F32) for i in range(2)]
                h2 = [sb(st, f"oh2{i}", [128, 8, 512], BF16) for i in range(2)]
                stats = sb(st, "ostats", [128, 4, 6], F32)
                mv = sb(st, "omv", [128, 8], F32)
                pb = [pbank(st, f"po{i}") for i in range(8)]
                bp = [PB() for _ in range(8)]
                b_w, b_tab, b_stt = Buf(), Buf(), Buf()
                b_xs, b_yT, b_v, b_x1, b_h2 = ([Buf(), Buf()] for _ in range(5))
                wsrc = I["w_o"][l].rearrange("(kc p) n -> p kc n", p=128)
                for kc in range(8):
                    S.dma("pool", w_o[:, kc, :], wsrc[:, kc, :], writes=[b_w])
                S.dma("sp", gA[:], GB[l, 0], writes=[b_tab])
                S.dma("sp", lng[:], I["ln_attn_g"][l, :].partition_broadcast(128), writes=[b_tab])
                S.dma("sp", lnb[:], I["ln_attn_b"][l, :].partition_broadcast(128), writes=[b_tab])
                for sg in range(NSUB):
                    T, s = sg // 4, sg % 4
                    i2 = sg % 2
                    tsl = slice(s * 128, (s + 1) * 128)
                    S.dma("sp", xs[i2][:], xsrc[sg * 128:(sg + 1) * 128, :], writes=[b_xs[i2]])
                    tb = pb[0 + i2][:, :].bitcast(BF16)
                    for kc in range(8):
                        PE(lambda: nc.tensor.transpose(tb[:, kc * 128:(kc + 1) * 128], y_res[:, sg, kc * 128:(kc + 1) * 128],
                                                       ident_b[:]), [b_y[sg], b_const], [bp[i2]], signal=(kc == 7))
                    V(lambda: nc.vector.tensor_copy(yT[i2][:, :, :], tb[:, 0:1024].rearrange("p (c t) -> p c t", t=128)),
                      [bp[i2]], [b_yT[i2]])
                    for hf in range(2):
                        bk = 2 + 2 * i2 + hf
                        for kc in range(8):
                            PE(lambda: nc.tensor.matmul(pb[bk][:, :], lhsT=yT[i2][:, kc, :], rhs=w_o[:, kc, hf * 512:(hf + 1) * 512],
                                                        start=(kc == 0), stop=(kc == 7)), [b_yT[i2], b_w], [bp[bk]], signal=(kc == 7))
                        hs = slice(hf * 512, (hf + 1) * 512)
                        V(lambda: nc.vector.tensor_tensor(out=v[i2][:, hs], in0=pb[bk][:, :], in1=gA[:, hs], op=ALU.mult),
                          [bp[bk], b_tab], [b_v[i2]])
                    V(lambda: nc.vector.scalar_tensor_tensor(out=v[i2][:, :], in0=xs[i2][:, :], scalar=ALPHA,
                                                             in1=v[i2][:, :], op0=ALU.mult, op1=ALU.add),
                      [b_xs[i2], b_v[i2]], [b_v[i2]])
                    layer_norm(v[i2][:, :], b_v[i2], x1[i2][:, :], b_x1[i2], lng[:, :], lnb[:, :], b_tab, stats, mv, b_stt, "pool")
                    S.dma("sp", X1[sg * 128:(sg + 1) * 128, :], x1[i2][:], reads=[b_x1[i2]])
                    for hf in range(2):
                        bk = 6 + hf
                        for cc in range(4):
                            kc = hf * 4 + cc
                            PE(lambda: nc.tensor.transpose(pb[bk][:, cc * 128:(cc + 1) * 128], x1[i2][:, kc * 128:(kc + 1) * 128],
                                                           ident_f[:]), [b_x1[i2], b_const], [bp[bk]], signal=(cc == 3))
                        for cc in range(4):
                            kc = hf * 4 + cc
                            if cc % 2 == 0:
                                A(lambda: nc.scalar.activation(out=h2[T % 2][:, kc, tsl], in_=pb[bk][:, cc * 128:(cc + 1) * 128],
                                                               func=AF.Identity, scale=modT[:, l, 32 + kc:33 + kc],
                                                               bias=modT[:, l, 24 + kc:25 + kc]), [bp[bk], b_modT], [b_h2[T % 2]])
                            else:
                                V(lambda: nc.vector.tensor_scalar(out=h2[T % 2][:, kc, tsl], in0=pb[bk][:, cc * 128:(cc + 1) * 128],
                                                                  scalar1=modT[:, l, 32 + kc:33 + kc],
                                                                  scalar2=modT[:, l, 24 + kc:25 + kc], op0=ALU.mult, op1=ALU.add),
                                  [bp[bk], b_modT], [b_h2[T % 2]])
                    if s == 3:
                        S.dma("sp", H2T[:, :, T * 512:(T + 1) * 512].rearrange("c r t -> r c t"), h2[T % 2][:, :, :],
                              reads=[b_h2[T % 2]])
                S.barrier()

        def phase_O2(l, dst):
            with ExitStack() as st:
                w_up = sb(st, "w_up", [128, 8, HID], BF16)
                w_dn = sb(st, "w_dn", [128, 32, D], BF16)
                gM = sb(st, "gM", [128, D], F32)
                lng = sb(st, "lng2", [128, D], F32)
                lnb = sb(st, "lnb2", [128, D], F32)
                h2 = [sb(st, f"mh2{i}", [128, 8, 512], BF16) for i in range(1)]
                uT = sb(st, "uT", [128, 32, 512], BF16)
                rr = [sb(st, f"rr{i}", [128, 512], F32) for i in range(3)]
                x1 = [sb(st, f"mx1{i}", [128, D], F32) for i in range(2)]
                v = [sb(st, f"mv{i}", [128, D], F32) for i in range(2)]
                stats = sb(st, "mstats", [128, 4, 6], F32)
                mv = sb(st, "mmv", [128, 8], F32)
                pb = [pbank(st, f"pm{i}") for i in range(8)]
                bp = [PB() for _ in range(8)]
                b_wu, b_wd, b_tab, b_stt, b_uT = Buf(), Buf(), Buf(), Buf(), Buf()
                b_h2, b_x1, b_v = ([Buf(), Buf()] for _ in range(3))
                b_rr = [Buf() for _ in range(3)]
                usrc = I["w_up"][l].rearrange("(kc p) n -> p kc n", p=128)
                for kc in range(8):
                    S.dma("pool", w_up[:, kc, :], usrc[:, kc, :], writes=[b_wu])
                dsrc = I["w_down"][l].rearrange("(kc p) n -> p kc n", p=128)
                for k4 in range(8):
                    S.dma("pool", w_dn[:, k4 * 4:(k4 + 1) * 4, :], dsrc[:, k4 * 4:(k4 + 1) * 4, :], writes=[b_wd])
                S.dma("sp", gM[:], GB[l, 1], writes=[b_tab])
                S.dma("sp", lng[:], I["ln_mlp_g"][l, :].partition_broadcast(128), writes=[b_tab])
                S.dma("sp", lnb[:], I["ln_mlp_b"][l, :].partition_broadcast(128), writes=[b_tab])
                ri = 0
                for T in range(8):
                    h_t, bh = h2[0], b_h2[0]
                    S.dma("sp", h_t[:, :, :], H2T[:, :, T * 512:(T + 1) * 512].rearrange("c r t -> r c t"), writes=[bh])
                    for hc in range(32):
                        bk = hc % 4
                        for kc in range(8):
                            PE(lambda: nc.tensor.matmul(pb[bk][:, :], lhsT=w_up[:, kc, hc * 128:(hc + 1) * 128], rhs=h_t[:, kc, :],
                                                        start=(kc == 0), stop=(kc == 7)), [b_wu, bh], [bp[bk]], signal=(kc == 7))
                        r_, br = rr[ri % 3], b_rr[ri % 3]
                        ri += 1
                        A(lambda: nc.scalar.activation(out=r_[:, :], in_=pb[bk][:, :], func=AF.Relu), [bp[bk]], [br])
                        if hc % 2 == 0:
                            V(lambda: nc.vector.tensor_tensor(out=uT[:, hc, :], in0=r_[:, :], in1=r_[:, :], op=ALU.mult), [br], [b_uT])
                        else:
                            G(lambda: nc.gpsimd.tensor_tensor(out=uT[:, hc, :], in0=r_[:, :], in1=r_[:, :], op=ALU.mult), [br], [b_uT])
                    for s in range(4):
                        sg = T * 4 + s
                        i2 = sg % 2
                        rows = slice(sg * 128, (sg + 1) * 128)
                        S.dma("sp", x1[i2][:], X1[rows, :], writes=[b_x1[i2]])
                        for hf in range(2):
                            bk = 4 + 2 * i2 + hf
                            for hc in range(32):
                                PE(lambda: nc.tensor.matmul(pb[bk][:, :], lhsT=uT[:, hc, s * 128:(s + 1) * 128],
                                                            rhs=w_dn[:, hc, hf * 512:(hf + 1) * 512], start=(hc == 0), stop=(hc == 31)),
                                   [b_uT, b_wd], [bp[bk]], signal=(hc == 31))
                            hs = slice(hf * 512, (hf + 1) * 512)
                            V(lambda: nc.vector.tensor_tensor(out=v[i2][:, hs], in0=pb[bk][:, :], in1=gM[:, hs], op=ALU.mult),
                              [bp[bk], b_tab], [b_v[i2]])
                        V(lambda: nc.vector.scalar_tensor_tensor(out=v[i2][:, :], in0=x1[i2][:, :], scalar=ALPHA, in1=v[i2][:, :],
                                                                 op0=ALU.mult, op1=ALU.add), [b_x1[i2], b_v[i2]], [b_v[i2]])
                        layer_norm(v[i2][:, :], b_v[i2], v[i2][:, :], b_v[i2], lng[:, :], lnb[:, :], b_tab, stats, mv, b_stt, "pool")
                        S.dma("sp", dst[rows, :], v[i2][:], reads=[b_v[i2]])
                S.barrier()

        for l in range(nlayers):
            if "stop0" in dbg:
                break
            xsrc = I["x"] if l == 0 else XN
            phase_P(l, xsrc)
            if "stopP" in dbg:
                break
            with ExitStack() as lst:
                y_res = sb(lst, f"y_res{l}", [128, NSUB, D], BF16)
                b_y = [Buf(f"y{i}") for i in range(NSUB)]
                phase_att(l, y_res, b_y)
                phase_D(l, y_res, b_y)
                if YDBG is not None and l == 0:
                    for sg in range(NSUB):
                        S.dma("sp", YDBG[sg * 128:(sg + 1) * 128, :], y_res[:, sg, :], reads=[b_y[sg]])
                    S.barrier()
                if "stopA" in dbg:
                    break
                phase_O1(l, xsrc, y_res, b_y)
            phase_O2(l, out if l == nlayers - 1 else XN)
        S.barrier()
        print("ops", S.n_ops, "dmas", S.n_dma, "sems", S.nsem)
    return nc


DBG = {}
_CONSTS = None


def make_in_maps(inputs):
    global _CONSTS
    if _CONSTS is None:
        _CONSTS = _host_consts()
    shared = {}
    for n in W_NAMES:
        a = np.ascontiguousarray(np.asarray(inputs[n], dtype=np.float32))
        shared[n] = a.reshape(W_SHAPES[n])
    for n, a in _CONSTS.items():
        shared["k_" + n] = a
    x = np.asarray(inputs["x"], dtype=np.float32)
    c = np.asarray(inputs["c"], dtype=np.float32)
    maps = []
    for b in range(8):
        m = dict(shared)
        m["x"] = np.ascontiguousarray(x[b])
        m["c"] = np.ascontiguousarray(c[b].reshape(8, 128).T)
        maps.append(m)
    return maps


def kernel(**inputs):
    nc = build()
    in_maps = make_in_maps(inputs)
    res = run_bass_kernel_spmd(nc, in_maps, core_ids=list(range(8)))
    return np.stack([np.asarray(r["out"]) for r in res.results], axis=0).astype(np.float32)
```

```python
import math
from contextlib import ExitStack
import numpy as np
import concourse.bass as bass
import concourse.mybir as mybir
from concourse.bass_utils import run_bass_kernel_spmd

F32 = mybir.dt.float32
BF16 = mybir.dt.bfloat16
AF = mybir.ActivationFunctionType
ALU = mybir.AluOpType
AX = mybir.AxisListType

SEQ = 4096
D = 1024
NSUB = SEQ // 128
HID = 4096
INC = 2720
ALPHA = 4 ** 0.25
SLOPES_A = [2.0 ** -1, 2.0 ** -3, 2.0 ** -5, 2.0 ** -7]
SLOPES_D = [2.0 ** -2, 2.0 ** -4, 2.0 ** -6, 2.0 ** -8]
DILS = [1, 4, 16]
SC_A = 32 ** -0.5
SC_B = 96 ** -0.5
SC_C = 0.125
SC_D = 0.125
VP = 66
VW = 4 * VP


class Buf:
    __slots__ = ("name", "w", "r", "excl")

    def __init__(self, name="", excl=False):
        self.name = name
        self.w = None
        self.r = []
        self.excl = excl


def PB(name=""):
    return Buf(name, excl=True)


class Tok:
    __slots__ = ("eng", "sem", "val")

    def __init__(self, eng, sem=None, val=None):
        self.eng = eng
        self.sem = sem
        self.val = val


class Sched:
    EPOCH = 20000
    NDMA = 8

    def __init__(self, nc, stack):
        self.nc = nc
        self.stack = stack
        self.engs = {"pe": nc.tensor, "act": nc.scalar, "dve": nc.vector,
                     "pool": nc.gpsimd, "sp": nc.sync}
        self.count = {e: 0 for e in self.engs}
        self.cursem = {}
        self.pending = {e: [] for e in self.engs}
        self.waited = {e: {} for e in self.engs}
        self.nsem = 0
        for e in self.engs:
            self._new_epoch(e)
        self.dma_sems, self.dma_cnt, self.dma_last, self.dma_i = {}, {}, {}, {}
        for q in ("sp", "act", "pool"):
            self.dma_sems[q] = [self._sem(f"dma_{q}_{i}") for i in range(self.NDMA)]
            self.dma_cnt[q] = [0] * self.NDMA
            self.dma_last[q] = [None] * self.NDMA
            self.dma_i[q] = 0
        self.n_ops = {e: 0 for e in self.engs}
        self.n_dma = 0

    def _sem(self, name):
        self.nsem += 1
        return self.stack.enter_context(self.nc.semaphore(name))

    def _new_epoch(self, e):
        self.cursem[e] = self._sem(f"s_{e}_{self.nsem}")
        self.count[e] = 0

    def _wait(self, eng, tok):
        if tok is None:
            return
        if tok.sem is None:
            raise RuntimeError(f"dependency on unsignalled op on {tok.eng}")
        key = id(tok.sem)
        w = self.waited[eng]
        if w.get(key, 0) >= tok.val:
            return
        w[key] = tok.val
        self.engs[eng].wait_ge(tok.sem, tok.val)

    def _deps(self, eng, reads, writes):
        for b in reads:
            t = b.w
            if t is not None and not (t.eng == eng and eng == "pe"):
                self._wait(eng, t)
        for b in writes:
            t = b.w
            if t is not None and t.eng != eng:
                self._wait(eng, t)
            for t in b.r:
                if t.eng != eng:
                    self._wait(eng, t)

    def _record(self, tok, reads, writes):
        for b in reads:
            b.r.append(tok)
            if len(b.r) > 16:
                last = {}
                for t in b.r:
                    last[(t.eng, id(t.sem))] = t
                b.r = list(last.values())
        for b in writes:
            b.w = tok
            b.r = []

    def op(self, eng, fn, reads=(), writes=(), signal=True):
        if any(b.excl for b in reads):
            writes = list(writes) + [b for b in reads if b.excl]
            reads = [b for b in reads if not b.excl]
        self._deps(eng, reads, writes)
        inst = fn()
        self.n_ops[eng] += 1
        tok = Tok(eng)
        self.pending[eng].append(tok)
        if signal:
            if self.count[eng] >= self.EPOCH:
                self._new_epoch(eng)
            self.count[eng] += 1
            sem = self.cursem[eng]
            inst.then_inc(sem, 1)
            for t in self.pending[eng]:
                t.sem = sem
                t.val = self.count[eng]
            self.pending[eng] = []
        self._record(tok, reads, writes)
        return tok

    def prewait(self, eng, reads=(), writes=()):
        writes = list(writes) + [b for b in reads if b.excl]
        reads = [b for b in reads if not b.excl]
        self._deps(eng, reads, writes)

    def dma(self, q, out, in_, reads=(), writes=(), **kw):
        i = self.dma_i[q]
        self.dma_i[q] = (i + 1) % self.NDMA
        prev = self.dma_last[q][i]
        if prev is not None:
            self._wait(q, prev)
        self._deps(q, reads, writes)
        sem = self.dma_sems[q][i]
        self.dma_cnt[q][i] += 16
        inst = self.engs[q].dma_start(out=out, in_=in_, **kw)
        inst.then_inc(sem, 16)
        tok = Tok("dma_" + q + str(i), sem, self.dma_cnt[q][i])
        self.dma_last[q][i] = tok
        self._record(tok, reads, writes)
        self.n_dma += 1
        return tok

    def barrier(self):
        toks = []
        for e in self.engs:
            if self.pending[e]:
                raise RuntimeError(f"barrier with unsignalled ops on {e}")
            if self.count[e] > 0:
                toks.append(Tok(e, self.cursem[e], self.count[e]))
        for q in self.dma_last:
            for t in self.dma_last[q]:
                if t is not None:
                    toks.append(t)
        for e in self.engs:
            for t in toks:
                if t.eng != e:
                    self._wait(e, t)


def _host_consts():
    c = {}
    c["ident"] = np.eye(128, dtype=np.float32)
    tok = (np.arange(NSUB)[None, :] * 128 + np.arange(128)[:, None]).astype(np.float64)
    freqs = 10000.0 ** (-np.arange(16, dtype=np.float64) / 16)

    def cs(pos):
        ang = pos[..., None].astype(np.float32).astype(np.float64) * freqs.astype(np.float32).astype(np.float64)
        ang = (pos[..., None].astype(np.float32) * freqs.astype(np.float32)).astype(np.float32)
        return np.cos(ang.astype(np.float64)), np.sin(ang.astype(np.float64))

    cp, sp_ = cs(tok)
    rb = np.zeros((128, NSUB, 2, 64), np.float64)
    for i, s in enumerate([SC_B, 1.0]):
        rb[:, :, i, 0:16] = cp * s
        rb[:, :, i, 16:32] = cp * s
        rb[:, :, i, 32:48] = -sp_ * s
        rb[:, :, i, 48:64] = sp_ * s
    c["ropeB"] = rb.astype(np.float32)
    cr, sr = cs(np.floor(tok / 64))
    cc, sc_ = cs(np.mod(tok, 64))
    rc = np.zeros((128, NSUB, 128), np.float64)
    rc[:, :, 0:16] = cr
    rc[:, :, 16:32] = cr
    rc[:, :, 32:48] = cc
    rc[:, :, 48:64] = cc
    rc[:, :, 64:80] = -sr
    rc[:, :, 80:96] = sr
    rc[:, :, 96:112] = -sc_
    rc[:, :, 112:128] = sc_
    c["ropeC"] = rc.astype(np.float32)
    ki = np.arange(128)[:, None].astype(np.float64)
    qi = np.arange(512)[None, :].astype(np.float64)
    al = np.zeros((128, 5, 512), np.float64)
    al[:, 0, :] = qi - ki
    for o in range(4):
        al[:, 1 + o, :] = np.abs(qi - ki - 128 * o)
    c["alibi"] = al.astype(np.float32)
    q128 = (np.arange(512) % 128)[None, :].astype(np.float64)
    dt = np.zeros((128, 2, 512), np.float64)
    da = np.abs(ki - 64 - q128)
    db = np.abs(ki + 64 - q128)
    dt[:, 0, :] = np.where(da <= 64, da, 1.0e6)
    dt[:, 1, :] = np.where(db <= 64, db, 1.0e6)
    c["dtab"] = dt.astype(np.float32)
    return c


W_NAMES = ["w_ada", "b_ada", "w_in", "w_o", "diff_lambda", "diff_subln_g", "mla_q_norm_g", "mla_w_uq",
           "mla_kv_norm_g", "mla_w_ukv", "gqa_q_norm_g", "gqa_k_norm_g", "ln_attn_g", "ln_attn_b",
           "w_up", "w_down", "ln_mlp_g", "ln_mlp_b"]
W_SHAPES = {"w_ada": [2, 1024, 6144], "b_ada": [2, 6144], "w_in": [2, 1024, INC], "w_o": [2, 1024, 1024],
            "diff_lambda": [2, 128], "diff_subln_g": [2, 64], "mla_q_norm_g": [2, 384],
            "mla_w_uq": [2, 384, 384], "mla_kv_norm_g": [2, 256], "mla_w_ukv": [2, 256, 512],
            "gqa_q_norm_g": [2, 64], "gqa_k_norm_g": [2, 64], "ln_attn_g": [2, 1024], "ln_attn_b": [2, 1024],
            "w_up": [2, 1024, HID], "w_down": [2, HID, 1024], "ln_mlp_g": [2, 1024], "ln_mlp_b": [2, 1024]}
C_SHAPES = {"ident": [128, 128], "ropeB": [128, NSUB, 2, 64], "ropeC": [128, NSUB, 128],
            "alibi": [128, 5, 512], "dtab": [128, 2, 512]}


def build(nlayers=2, dbg=()):
    nc = bass.Bass("TRN2", target_bir_lowering=False)
    I = {}
    I["x"] = nc.dram_tensor("x", [SEQ, D], F32, kind="ExternalInput").ap()
    I["c"] = nc.dram_tensor("c", [128, 8], F32, kind="ExternalInput").ap()
    for n in W_NAMES:
        I[n] = nc.dram_tensor(n, W_SHAPES[n], F32, kind="ExternalInput").ap()
    for n in C_SHAPES:
        I[n] = nc.dram_tensor("k_" + n, C_SHAPES[n], F32, kind="ExternalInput").ap()
    out = nc.dram_tensor("out", [SEQ, D], F32, kind="ExternalOutput").ap()

    def scratch(name, shape, dt):
        kind = "ExternalOutput" if name in dbg else "Internal"
        return nc.dram_tensor(name, shape, dt, kind=kind).ap()

    QKT = scratch("QKT", [15, 128, SEQ], BF16)
    VG = scratch("VG", [3, SEQ, VW], BF16)
    DTOK = scratch("DTOK", [SEQ, 512 + VW], BF16)
    OD = scratch("OD", [3, SEQ, 260], F32)
    X1 = scratch("X1", [SEQ, D], F32)
    H2T = scratch("H2T", [8, 128, SEQ], BF16)
    XN = scratch("XN", [SEQ, D], F32)
    GB = scratch("GB", [2, 2, 128, D], F32)
    YDBG = scratch("YDBG", [SEQ, D], BF16) if "YDBG" in dbg else None

    with ExitStack() as top:
        S = Sched(nc, top)

        uid = [0]

        def sb(st, name, shape, dt):
            uid[0] += 1
            return st.enter_context(nc.sbuf_tensor(f"s{uid[0]}_{name}", shape, dt))

        def pbank(st, name):
            uid[0] += 1
            return st.enter_context(nc.psum_tensor(f"p{uid[0]}_{name}", [128, 512], F32))

        def V(fn, reads, writes):
            return S.op("dve", fn, reads, writes)

        def A(fn, reads, writes):
            return S.op("act", fn, reads, writes)

        def G(fn, reads, writes):
            return S.op("pool", fn, reads, writes)

        def PE(fn, reads, writes, signal=True):
            return S.op("pe", fn, reads, writes, signal)

        ident_f = sb(top, "ident_f", [128, 128], F32)
        ident_b = sb(top, "ident_b", [128, 128], BF16)
        modT = sb(top, "modT", [128, 2, 48], F32)
        eps6 = sb(top, "eps6", [128, 1], F32)
        eps5 = sb(top, "eps5", [128, 1], F32)
        b_const = Buf("const")
        b_modT = Buf("modT")
        S.dma("sp", ident_f[:], I["ident"], writes=[b_const])
        S.dma("pool", ident_b[:], I["ident"], writes=[b_const])
        V(lambda: nc.vector.memset(eps6[:], 1e-6), [], [b_const])
        V(lambda: nc.vector.memset(eps5[:], 1e-5), [], [b_const])

        with ExitStack() as st:
            condT = sb(st, "condT", [128, 8], F32)
            ones_row = sb(st, "ones_row", [1, 128], F32)
            modrow = sb(st, "modrow", [1, 6144], F32)
            brow = sb(st, "brow", [1, 6144], F32)
            wa = [sb(st, f"wa{i}", [128, 8, 512], F32) for i in range(2)]
            gbt = sb(st, "gbt", [128, 1024], F32)
            ps = [pbank(st, f"sps{i}") for i in range(2)]
            b_cond, b_ones, b_mrow, b_brow, b_gbt = Buf(), Buf(), Buf(), Buf(), Buf()
            b_wa = [Buf(), Buf()]
            b_ps = [PB(), PB()]
            S.dma("sp", condT[:], I["c"], writes=[b_cond])
            A(lambda: nc.scalar.activation(out=condT[:], in_=condT[:], func=AF.Silu), [b_cond], [b_cond])
            V(lambda: nc.vector.memset(ones_row[:], 1.0), [], [b_ones])
            for l in range(nlayers):
                S.dma("sp", brow[:], I["b_ada"][l:l + 1, :], writes=[b_brow])
                wsrc = I["w_ada"][l].rearrange("(kc p) n -> p kc n", p=128)
                for pc in range(12):
                    S.dma("sp", wa[pc % 2][:], wsrc[:, :, pc * 512:(pc + 1) * 512], writes=[b_wa[pc % 2]])
                    for kc in range(8):
                        PE(lambda: nc.tensor.matmul(ps[pc % 2][0:1, :], lhsT=condT[:, kc:kc + 1], rhs=wa[pc % 2][:, kc, :],
                                                    start=(kc == 0), stop=(kc == 7)),
                           [b_cond, b_wa[pc % 2]], [b_ps[pc % 2]], signal=(kc == 7))
                    V(lambda: nc.vector.tensor_tensor(out=modrow[0:1, pc * 512:(pc + 1) * 512], in0=ps[pc % 2][0:1, :],
                                                      in1=brow[0:1, pc * 512:(pc + 1) * 512], op=ALU.add),
                      [b_ps[pc % 2], b_brow], [b_mrow])
                for j in range(48):
                    PE(lambda: nc.tensor.matmul(ps[0][:, j:j + 1], lhsT=modrow[0:1, j * 128:(j + 1) * 128],
                                                rhs=ones_row[0:1, 0:1], start=True, stop=True),
                       [b_mrow, b_ones], [b_ps[0]], signal=(j == 47))
                V(lambda: nc.vector.tensor_copy(modT[:, l, :], ps[0][:, 0:48]), [b_ps[0]], [b_modT])
                V(lambda: nc.vector.tensor_scalar(out=modT[:, l, 8:16], in0=modT[:, l, 8:16], scalar1=1.0, scalar2=None,
                                                  op0=ALU.add), [b_modT], [b_modT])
                V(lambda: nc.vector.tensor_scalar(out=modT[:, l, 32:40], in0=modT[:, l, 32:40], scalar1=1.0, scalar2=None,
                                                  op0=ALU.add), [b_modT], [b_modT])
                for gi, base in enumerate([2048, 5120]):
                    for hf in range(2):
                        PE(lambda: nc.tensor.matmul(ps[1][:, :], lhsT=ones_row[0:1, :],
                                                    rhs=modrow[0:1, base + hf * 512: base + (hf + 1) * 512],
                                                    start=True, stop=True), [b_mrow, b_ones], [b_ps[1]])
                        V(lambda: nc.vector.tensor_copy(gbt[:, hf * 512:(hf + 1) * 512], ps[1][:, :]), [b_ps[1]], [b_gbt])
                    S.dma("sp", GB[l, gi], gbt[:], reads=[b_gbt])
            S.barrier()

        def phase_P(l, xsrc):
            with ExitStack() as st:
                w_in = sb(st, "w_in", [128, 8, INC], BF16)
                w_uq = sb(st, "w_uq", [128, 3, 384], BF16)
                w_ukv = sb(st, "w_ukv", [128, 2, 512], BF16)
                gq_bc = sb(st, "gq_bc", [128, 384], F32)
                gkv_bc = sb(st, "gkv_bc", [128, 256], F32)
                gC_bc = sb(st, "gC_bc", [128, 6, 64], F32)
                ropeB = sb(st, "ropeB", [128, NSUB, 2, 64], F32)
                ropeC = sb(st, "ropeC", [128, NSUB, 128], F32)
                xs = [sb(st, f"xs{i}", [128, D], F32) for i in range(2)]
                hT = [sb(st, f"hT{i}", [128, 8, 512], BF16) for i in range(2)]
                bf32_l = [sb(st, f"bf32{i}", [128, 672], F32) for i in range(2)]
                cqk_l = [sb(st, f"cqk{i}", [128, 384], F32) for i in range(2)]
                junk_l = [sb(st, f"junk{i}", [128, 384], F32) for i in range(2)]
                qkA_l = [sb(st, f"qkA{i}", [128, 512], BF16) for i in range(2)]
                qkD_l = [sb(st, f"qkD{i}", [128, 512], BF16) for i in range(2)]
                VA_l = [sb(st, f"VA{i}", [128, 4, VP], BF16) for i in range(2)]
                VB_l = [sb(st, f"VB{i}", [128, 4, VP], BF16) for i in range(2)]
                VC_l = [sb(st, f"VC{i}", [128, 4, VP], BF16) for i in range(2)]
                VD_l = [sb(st, f"VD{i}", [128, 4, VP], BF16) for i in range(2)]
                cqn_l = [sb(st, f"cqn{i}", [128, 640], BF16) for i in range(2)]
                cT_l = [sb(st, f"cT{i}", [128, 5, 128], BF16) for i in range(2)]
                qB_l = [sb(st, f"qB{i}", [128, 4, 96], BF16) for i in range(2)]
                kB_l = [sb(st, f"kB{i}", [128, 4, 96], BF16) for i in range(2)]
                cC_l = [sb(st, f"cC{i}", [128, 384], BF16) for i in range(2)]
                t1_l = [sb(st, f"t1{i}", [128, 384], F32) for i in range(2)]
                t2_l = [sb(st, f"t2{i}", [128, 384], F32) for i in range(2)]
                nq_l = [sb(st, f"nq{i}", [128, 384], F32) for i in range(2)]
                stt__l = [sb(st, f"stt_{i}", [128, 16], F32) for i in range(2)]
                stage = [sb(st, f"stage{i}", [128, 15, 512], BF16) for i in range(2)]
                pb = [pbank(st, f"pp{i}") for i in range(8)]
                bp = [PB(f"pp{i}") for i in range(8)]
                b_w, b_tab = Buf(), Buf()
                b_xs = [Buf(), Buf()]
                b_hT = [Buf(), Buf()]
                BL = {n: [Buf(n + '0'), Buf(n + '1')] for n in ['bf32', 'cqk', 'junk', 'qkA', 'qkD', 'VA', 'VB', 'VC', 'VD', 'cqn', 'cT', 'qB', 'kB', 'cC', 't1', 't2', 'nq', 'stt_']}
                b_wk = [Buf() for _ in range(8)]
                b_hk = [[Buf() for _ in range(8)] for _ in range(2)]
                b_stage = [Buf(), Buf()]

                wsrc = I["w_in"][l].rearrange("(kc p) n -> p kc n", p=128)
                for kc in range(8):
                    S.dma("pool", w_in[:, kc, :], wsrc[:, kc, :], writes=[b_wk[kc]])
                S.dma("pool", w_uq[:], I["mla_w_uq"][l].rearrange("(kc p) n -> p kc n", p=128), writes=[b_w])
                S.dma("pool", w_ukv[:], I["mla_w_ukv"][l].rearrange("(kc p) n -> p kc n", p=128), writes=[b_w])
                S.dma("sp", gq_bc[:], I["mla_q_norm_g"][l, :].partition_broadcast(128), writes=[b_tab])
                S.dma("sp", gkv_bc[:], I["mla_kv_norm_g"][l, :].partition_broadcast(128), writes=[b_tab])
                for h in range(6):
                    src = I["gqa_q_norm_g"] if h < 4 else I["gqa_k_norm_g"]
                    S.dma("sp", gC_bc[:, h, :], src[l, :].partition_broadcast(128), writes=[b_tab])
                V(lambda: nc.vector.tensor_scalar(out=gC_bc[:, 0:4, :], in0=gC_bc[:, 0:4, :], scalar1=SC_C, scalar2=None,
                                                  op0=ALU.mult), [b_tab], [b_tab])
                S.dma("sp", ropeB[:], I["ropeB"], writes=[b_tab])
                S.dma("sp", ropeC[:], I["ropeC"], writes=[b_tab])
                for i_ in range(2):
                    for n_ in ('VA', 'VB', 'VC', 'VD'):
                        vt = {'VA': VA_l, 'VB': VB_l, 'VC': VC_l, 'VD': VD_l}[n_][i_]
                        V(lambda: nc.vector.memset(vt[:], 1.0), [], [BL[n_][i_]])

                groups = [(0, 512), (512, 512), (1024, 416), (1440, 512), (1952, 512), (2464, 256)]
                def stage1(sg):
                    T, s = sg // 4, sg % 4
                    tsl = slice(s * 128, (s + 1) * 128)
                    h_t, bhk = hT[T % 2], b_hk[T % 2]
                    stg, bstg = stage[T % 2], b_stage[T % 2]
                    bf32 = bf32_l[sg % 2]
                    b_bf32 = BL['bf32'][sg % 2]
                    cqk = cqk_l[sg % 2]
                    b_cqk = BL['cqk'][sg % 2]
                    junk = junk_l[sg % 2]
                    b_junk = BL['junk'][sg % 2]
                    qkA = qkA_l[sg % 2]
                    b_qkA = BL['qkA'][sg % 2]
                    qkD = qkD_l[sg % 2]
                    b_qkD = BL['qkD'][sg % 2]
                    VA = VA_l[sg % 2]
                    b_VA = BL['VA'][sg % 2]
                    VB = VB_l[sg % 2]
                    b_VB = BL['VB'][sg % 2]
                    VC = VC_l[sg % 2]
                    b_VC = BL['VC'][sg % 2]
                    VD = VD_l[sg % 2]
                    b_VD = BL['VD'][sg % 2]
                    cqn = cqn_l[sg % 2]
                    b_cqn = BL['cqn'][sg % 2]
                    cT = cT_l[sg % 2]
                    b_cT = BL['cT'][sg % 2]
                    qB = qB_l[sg % 2]
                    b_qB = BL['qB'][sg % 2]
                    kB = kB_l[sg % 2]
                    b_kB = BL['kB'][sg % 2]
                    cC = cC_l[sg % 2]
                    b_cC = BL['cC'][sg % 2]
                    t1 = t1_l[sg % 2]
                    b_t1 = BL['t1'][sg % 2]
                    t2 = t2_l[sg % 2]
                    b_t2 = BL['t2'][sg % 2]
                    nq = nq_l[sg % 2]
                    b_nq = BL['nq'][sg % 2]
                    stt_ = stt__l[sg % 2]
                    b_st = BL['stt_'][sg % 2]
                    x_t, bx = xs[sg % 2], b_xs[sg % 2]
                    S.dma("sp", x_t[:], xsrc[sg * 128:(sg + 1) * 128, :], writes=[bx])
                    for hf in range(2):
                        for cc in range(4):
                            kc = hf * 4 + cc
                            PE(lambda: nc.tensor.transpose(pb[hf][:, cc * 128:(cc + 1) * 128], x_t[:, kc * 128:(kc + 1) * 128],
                                                           ident_f[:]), [bx, b_const], [bp[hf]], signal=(cc == 3))
                        for cc in range(4):
                            kc = hf * 4 + cc
                            if cc % 2 == 0:
                                A(lambda: nc.scalar.activation(out=h_t[:, kc, tsl], in_=pb[hf][:, cc * 128:(cc + 1) * 128],
                                                               func=AF.Identity, scale=modT[:, l, 8 + kc:9 + kc],
                                                               bias=modT[:, l, kc:kc + 1]), [bp[hf], b_modT], [bhk[kc]])
                            else:
                                V(lambda: nc.vector.tensor_scalar(out=h_t[:, kc, tsl], in0=pb[hf][:, cc * 128:(cc + 1) * 128],
                                                                  scalar1=modT[:, l, 8 + kc:9 + kc],
                                                                  scalar2=modT[:, l, kc:kc + 1], op0=ALU.mult, op1=ALU.add),
                                  [bp[hf], b_modT], [bhk[kc]])
                    yield
                    for gi, (c0, ncol) in enumerate(groups):
                        if gi > 0:
                            yield
                        bk = 2 + gi % 3
                        for kc in range(8):
                            PE(lambda: nc.tensor.matmul(pb[bk][:, 0:ncol], lhsT=h_t[:, kc, tsl], rhs=w_in[:, kc, c0:c0 + ncol],
                                                        start=(kc == 0), stop=(kc == 7)), [bhk[kc], b_wk[kc]], [bp[bk]], signal=(kc == 7))
                        P_ = pb[bk]
                        if gi == 0:
                            A(lambda: nc.scalar.activation(out=qkA[:, 0:256], in_=P_[:, 0:256], func=AF.Identity, scale=SC_A),
                              [bp[bk]], [b_qkA])
                            V(lambda: nc.vector.tensor_copy(qkA[:, 256:512], P_[:, 256:512]), [bp[bk]], [b_qkA])
                        elif gi == 1:
                            A(lambda: nc.scalar.activation(func=AF.Identity, out=VA[:, :, 0:64], in_=P_[:, 0:256].rearrange("p (h d) -> p h d", d=64)),
                              [bp[bk]], [b_VA])
                            V(lambda: nc.vector.tensor_copy(bf32[:, 0:256], P_[:, 256:512]), [bp[bk]], [b_bf32])
                        elif gi == 2:
                            V(lambda: nc.vector.tensor_copy(bf32[:, 256:672], P_[:, 0:416]), [bp[bk]], [b_bf32])
                        elif gi == 3:
                            V(lambda: nc.vector.tensor_copy(cqk[:, :], P_[:, 0:384]), [bp[bk]], [b_cqk])
                            A(lambda: nc.scalar.activation(func=AF.Identity, out=VC[:, 0:2, 0:64],
                                                     in_=P_[:, 384:512].rearrange("p (h d) -> p h d", d=64)),
                              [bp[bk]], [b_VC])
                        elif gi == 4:
                            A(lambda: nc.scalar.activation(out=qkD[:, 0:256], in_=P_[:, 0:256], func=AF.Identity, scale=SC_D),
                              [bp[bk]], [b_qkD])
                            V(lambda: nc.vector.tensor_copy(qkD[:, 256:512], P_[:, 256:512]), [bp[bk]], [b_qkD])
                        else:
                            A(lambda: nc.scalar.activation(func=AF.Identity, out=VD[:, :, 0:64], in_=P_[:, 0:256].rearrange("p (h d) -> p h d", d=64)),
                              [bp[bk]], [b_VD])

                def stage2(sg):
                    T, s = sg // 4, sg % 4
                    tsl = slice(s * 128, (s + 1) * 128)
                    h_t, bhk = hT[T % 2], b_hk[T % 2]
                    stg, bstg = stage[T % 2], b_stage[T % 2]
                    bf32 = bf32_l[sg % 2]
                    b_bf32 = BL['bf32'][sg % 2]
                    cqk = cqk_l[sg % 2]
                    b_cqk = BL['cqk'][sg % 2]
                    junk = junk_l[sg % 2]
                    b_junk = BL['junk'][sg % 2]
                    qkA = qkA_l[sg % 2]
                    b_qkA = BL['qkA'][sg % 2]
                    qkD = qkD_l[sg % 2]
                    b_qkD = BL['qkD'][sg % 2]
                    VA = VA_l[sg % 2]
                    b_VA = BL['VA'][sg % 2]
                    VB = VB_l[sg % 2]
                    b_VB = BL['VB'][sg % 2]
                    VC = VC_l[sg % 2]
                    b_VC = BL['VC'][sg % 2]
                    VD = VD_l[sg % 2]
                    b_VD = BL['VD'][sg % 2]
                    cqn = cqn_l[sg % 2]
                    b_cqn = BL['cqn'][sg % 2]
                    cT = cT_l[sg % 2]
                    b_cT = BL['cT'][sg % 2]
                    qB = qB_l[sg % 2]
                    b_qB = BL['qB'][sg % 2]
                    kB = kB_l[sg % 2]
                    b_kB = BL['kB'][sg % 2]
                    cC = cC_l[sg % 2]
                    b_cC = BL['cC'][sg % 2]
                    t1 = t1_l[sg % 2]
                    b_t1 = BL['t1'][sg % 2]
                    t2 = t2_l[sg % 2]
                    b_t2 = BL['t2'][sg % 2]
                    nq = nq_l[sg % 2]
                    b_nq = BL['nq'][sg % 2]
                    stt_ = stt__l[sg % 2]
                    b_st = BL['stt_'][sg % 2]
                    V(lambda: nc.vector.scalar_tensor_tensor(out=junk[:, 0:384], in0=bf32[:, 0:384], scalar=1.0,
                                                             in1=bf32[:, 0:384], op0=ALU.mult, op1=ALU.mult,
                                                             accum_out=stt_[:, 0:1]), [b_bf32], [b_junk, b_st])
                    V(lambda: nc.vector.scalar_tensor_tensor(out=junk[:, 0:256], in0=bf32[:, 384:640], scalar=1.0,
                                                             in1=bf32[:, 384:640], op0=ALU.mult, op1=ALU.mult,
                                                             accum_out=stt_[:, 1:2]), [b_bf32], [b_junk, b_st])
                    A(lambda: nc.scalar.activation(out=stt_[:, 2:3], in_=stt_[:, 0:1], func=AF.Ln, scale=1.0 / 384,
                                                   bias=eps6[:, 0:1]), [b_st, b_const], [b_st])
                    A(lambda: nc.scalar.activation(out=stt_[:, 3:4], in_=stt_[:, 1:2], func=AF.Ln, scale=1.0 / 256,
                                                   bias=eps6[:, 0:1]), [b_st, b_const], [b_st])
                    A(lambda: nc.scalar.activation(out=stt_[:, 4:6], in_=stt_[:, 2:4], func=AF.Exp, scale=-0.5), [b_st], [b_st])
                    V(lambda: nc.vector.scalar_tensor_tensor(out=cqn[:, 0:384], in0=bf32[:, 0:384], scalar=stt_[:, 4:5],
                                                             in1=gq_bc[:, :], op0=ALU.mult, op1=ALU.mult),
                      [b_bf32, b_st, b_tab], [b_cqn])
                    V(lambda: nc.vector.scalar_tensor_tensor(out=cqn[:, 384:640], in0=bf32[:, 384:640], scalar=stt_[:, 5:6],
                                                             in1=gkv_bc[:, :], op0=ALU.mult, op1=ALU.mult),
                      [b_bf32, b_st, b_tab], [b_cqn])
                    yield
                    p5b = pb[5][:, :].bitcast(BF16)
                    for j in range(5):
                        PE(lambda: nc.tensor.transpose(p5b[:, j * 128:(j + 1) * 128], cqn[:, j * 128:(j + 1) * 128], ident_b[:]),
                           [b_cqn, b_const], [bp[5]], signal=(j == 4))
                    V(lambda: nc.vector.tensor_copy(cT[:, :, :], p5b[:, 0:640].rearrange("p (c t) -> p c t", t=128)),
                      [bp[5]], [b_cT])
                    for j in range(3):
                        PE(lambda: nc.tensor.matmul(pb[6][:, 0:384], lhsT=cT[:, j, :], rhs=w_uq[:, j, :], start=(j == 0),
                                                    stop=(j == 2)), [b_cT, b_w], [bp[6]], signal=(j == 2))
                    for j in range(2):
                        PE(lambda: nc.tensor.matmul(pb[7][:, 0:512], lhsT=cT[:, 3 + j, :], rhs=w_ukv[:, j, :], start=(j == 0),
                                                    stop=(j == 1)), [b_cT, b_w], [bp[7]], signal=(j == 1))
                    yield
                    q3 = pb[6][:, 0:384].rearrange("p (h d) -> p h d", d=96)
                    kv3 = pb[7][:, 0:512].rearrange("p (h d) -> p h d", d=128)
                    A(lambda: nc.scalar.activation(out=qB[:, :, 0:64], in_=q3[:, :, 0:64], func=AF.Identity, scale=SC_B),
                      [bp[6]], [b_qB])
                    Tq = ropeB[:, sg, 0, :]
                    Tk = ropeB[:, sg, 1, :]
                    t1q = t1[:, 0:128].rearrange("p (h d) -> p h d", d=32)
                    t2q = t2[:, 0:128].rearrange("p (h d) -> p h d", d=32)
                    V(lambda: nc.vector.tensor_tensor(out=t1q, in0=q3[:, :, 64:96],
                                                      in1=Tq[:, 0:32].unsqueeze(1).to_broadcast([128, 4, 32]), op=ALU.mult),
                      [bp[6], b_tab], [b_t1])
                    V(lambda: nc.vector.tensor_tensor(out=t2q[:, :, 0:16], in0=q3[:, :, 80:96],
                                                      in1=Tq[:, 32:48].unsqueeze(1).to_broadcast([128, 4, 16]), op=ALU.mult),
                      [bp[6], b_tab], [b_t2])
                    V(lambda: nc.vector.tensor_tensor(out=t2q[:, :, 16:32], in0=q3[:, :, 64:80],
                                                      in1=Tq[:, 48:64].unsqueeze(1).to_broadcast([128, 4, 16]), op=ALU.mult),
                      [bp[6], b_tab], [b_t2])
                    G(lambda: nc.gpsimd.tensor_tensor(out=qB[:, :, 64:96], in0=t1q, in1=t2q, op=ALU.add), [b_t1, b_t2], [b_qB])
                    V(lambda: nc.vector.tensor_copy(kB[:, :, 0:64], kv3[:, :, 0:64]), [bp[7]], [b_kB])
                    A(lambda: nc.scalar.activation(func=AF.Identity, out=VB[:, :, 0:64], in_=kv3[:, :, 64:128]), [bp[7]], [b_VB])
                    yield
                    kr = bf32[:, 640:672]
                    V(lambda: nc.vector.tensor_tensor(out=t1[:, 128:160], in0=kr, in1=Tk[:, 0:32], op=ALU.mult),
                      [b_bf32, b_tab], [b_t1])
                    V(lambda: nc.vector.tensor_tensor(out=t2[:, 128:144], in0=bf32[:, 656:672], in1=Tk[:, 32:48], op=ALU.mult),
                      [b_bf32, b_tab], [b_t2])
                    V(lambda: nc.vector.tensor_tensor(out=t2[:, 144:160], in0=bf32[:, 640:656], in1=Tk[:, 48:64], op=ALU.mult),
                      [b_bf32, b_tab], [b_t2])
                    G(lambda: nc.gpsimd.tensor_tensor(out=kB[:, :, 64:96],
                                                      in0=t1[:, 128:160].unsqueeze(1).to_broadcast([128, 4, 32]),
                                                      in1=t2[:, 128:160].unsqueeze(1).to_broadcast([128, 4, 32]), op=ALU.add),
                      [b_t1, b_t2], [b_kB])
                    yield
                    c3 = cqk[:, :].rearrange("p (h d) -> p h d", d=64)
                    V(lambda: nc.vector.tensor_tensor(out=junk[:, :], in0=cqk[:, :], in1=cqk[:, :], op=ALU.mult),
                      [b_cqk], [b_junk])
                    V(lambda: nc.vector.tensor_reduce(out=stt_[:, 6:12], in_=junk[:, :].rearrange("p (h d) -> p h d", d=64),
                                                      axis=AX.X, op=ALU.add), [b_junk], [b_st])
                    A(lambda: nc.scalar.activation(out=stt_[:, 6:12], in_=stt_[:, 6:12], func=AF.Ln, scale=1.0 / 64,
                                                   bias=eps6[:, 0:1]), [b_st, b_const], [b_st])
                    A(lambda: nc.scalar.activation(out=stt_[:, 6:12], in_=stt_[:, 6:12], func=AF.Exp, scale=-0.5), [b_st], [b_st])
                    yield
                    n3 = nq[:, :].rearrange("p (h d) -> p h d", d=64)
                    V(lambda: nc.vector.tensor_tensor(out=n3, in0=c3, in1=stt_[:, 6:12].unsqueeze(2).to_broadcast([128, 6, 64]),
                                                      op=ALU.mult), [b_cqk, b_st], [b_nq])
                    G(lambda: nc.gpsimd.tensor_tensor(out=n3, in0=n3, in1=gC_bc[:, :, :], op=ALU.mult), [b_nq, b_tab], [b_nq])
                    yield
                    cosA = ropeC[:, sg, 0:64]
                    sinS = ropeC[:, sg, 64:128].rearrange("p (r f i) -> p r f i", r=2, f=2)
                    V(lambda: nc.vector.tensor_tensor(out=t1[:, :].rearrange("p (h d) -> p h d", d=64), in0=n3,
                                                      in1=cosA.unsqueeze(1).to_broadcast([128, 6, 64]), op=ALU.mult),
                      [b_nq, b_tab], [b_t1])
                    n5 = nq[:, :].rearrange("p (h r f i) -> p h r f i", h=6, r=2, f=2)
                    t5 = t2[:, :].rearrange("p (h r f i) -> p h r f i", h=6, r=2, f=2)
                    for f in range(2):
                        V(lambda: nc.vector.tensor_tensor(out=t5[:, :, :, f, :], in0=n5[:, :, :, 1 - f, :],
                                                          in1=sinS[:, :, f, :].unsqueeze(1).to_broadcast([128, 6, 2, 16]),
                                                          op=ALU.mult), [b_nq, b_tab], [b_t2])
                    G(lambda: nc.gpsimd.tensor_tensor(out=cC[:, :], in0=t1[:, :], in1=t2[:, :], op=ALU.add), [b_t1, b_t2], [b_cC])
                    yield
                    tb0 = pb[6][:, :].bitcast(BF16)
                    tb1 = pb[7][:, :].bitcast(BF16)
                    for j in range(4):
                        PE(lambda: nc.tensor.transpose(tb0[:, j * 128:(j + 1) * 128], qkA[:, j * 128:(j + 1) * 128], ident_b[:]),
                           [b_qkA, b_const], [bp[6]], signal=False)
                    for h in range(4):
                        PE(lambda: nc.tensor.transpose(tb0[0:96, (4 + h) * 128:(5 + h) * 128], qB[:, h, :], ident_b[:]),
                           [b_qB, b_const], [bp[6]], signal=(h == 3))
                    for h in range(4):
                        PE(lambda: nc.tensor.transpose(tb1[0:96, h * 128:(h + 1) * 128], kB[:, h, :], ident_b[:]),
                           [b_kB, b_const], [bp[7]], signal=False)
                    for j in range(3):
                        PE(lambda: nc.tensor.transpose(tb1[:, (4 + j) * 128:(5 + j) * 128], cC[:, j * 128:(j + 1) * 128],
                                                       ident_b[:]), [b_cC, b_const], [bp[7]], signal=(j == 2))
                    V(lambda: nc.vector.tensor_copy(stg[:, 0:8, tsl], tb0[:, 0:1024].rearrange("p (c t) -> p c t", t=128)),
                      [bp[6]], [bstg])
                    A(lambda: nc.scalar.activation(func=AF.Identity, out=stg[:, 8:15, tsl], in_=tb1[:, 0:896].rearrange("p (c t) -> p c t", t=128)),
                      [bp[7]], [bstg])
                    yield
                    rows = slice(sg * 128, (sg + 1) * 128)
                    S.dma("sp", VG[0, rows, :], VA[:, :, :].rearrange("p h d -> p (h d)"), reads=[b_VA])
                    S.dma("sp", VG[1, rows, :], VB[:, :, :].rearrange("p h d -> p (h d)"), reads=[b_VB])
                    S.dma("sp", VG[2, rows, :], VC[:, :, :].rearrange("p h d -> p (h d)"), reads=[b_VC])
                    S.dma("sp", DTOK[rows, 0:512], qkD[:, :], reads=[b_qkD])
                    S.dma("sp", DTOK[rows, 512:512 + VW], VD[:, :, :].rearrange("p h d -> p (h d)"), reads=[b_VD])
                    if s == 3:
                        for (ca, cb) in [(0, 4), (4, 8), (8, 12), (12, 15)]:
                            S.dma("sp", QKT[ca:cb, :, T * 512:(T + 1) * 512].rearrange("c r t -> r c t"), stg[:, ca:cb, :],
                                  reads=[bstg])

                nsub_ = NSUB if 'nsub' not in DBG else DBG['nsub']
                for _ in stage1(0):
                    pass
                for sg in range(nsub_):
                    gens = [stage2(sg)] + ([stage1(sg + 1)] if sg + 1 < nsub_ else [])
                    while gens:
                        for g_ in list(gens):
                            try:
                                next(g_)
                            except StopIteration:
                                gens.remove(g_)
                S.barrier()

        def phase_att(l, y_res, b_y):
            with ExitStack() as st:
                qT = [sb(st, f"qT{i}", [128, SEQ], BF16) for i in range(4)]
                kT = [sb(st, f"kT{i}", [128, SEQ], BF16) for i in range(4)]
                Vg = sb(st, "Vg", [128, NSUB, VW + 64], BF16)
                alibi = sb(st, "alibi", [128, 5, 512], F32)
                Sb = [sb(st, f"Sb{i}", [128, 512], F32) for i in range(3)]
                E = [sb(st, f"E{i}", [128, 512], BF16) for i in range(4)]
                lam_t = sb(st, "lam_t", [128, 128], F32)
                lam = sb(st, "lam", [128, 8], F32)
                gA_bc = sb(st, "gA_bc", [128, 64], F32)
                o1 = sb(st, "o1", [128, 4, 64], F32)
                o2 = sb(st, "o2", [128, 4, 64], F32)
                osq = sb(st, "osq", [128, 4, 64], F32)
                rc = sb(st, "rc", [128, 16], F32)
                pS = [pbank(st, f"pS{i}") for i in range(4)]
                pOT = [pbank(st, f"pOT{i}") for i in range(2)]
                pO = [pbank(st, f"pO{i}") for i in range(2)]
                otS = [sb(st, f"otS{i}", [65, 512], F32) for i in range(2)]
                b_pOT = [PB(), PB()]
                b_otS = [Buf(), Buf()]
                b_qT = [Buf() for _ in range(4)]
                b_kT = [Buf() for _ in range(4)]
                b_Vg, b_al, b_lam, b_gA = Buf(), Buf(), Buf(), Buf()
                b_Sb = [Buf() for _ in range(3)]
                b_E = [Buf() for _ in range(4)]
                b_pS = [PB() for _ in range(4)]
                b_pO = [PB() for _ in range(4)]
                b_o1, b_o2, b_osq, b_rc = Buf(), Buf(), Buf(), Buf()
                V(lambda: nc.vector.memset(Vg[:, :, VW:VW + 64], 0.0), [], [b_Vg])
                S.dma("sp", alibi[:], I["alibi"], writes=[b_al])
                lam_init = 0.8 - 0.6 * math.exp(-0.3 * l)
                S.dma("sp", lam_t[:], I["diff_lambda"][l, :].partition_broadcast(128), writes=[b_lam])
                V(lambda: nc.vector.scalar_tensor_tensor(out=lam_t[:, 0:32], in0=lam_t[:, 0:32], scalar=1.0, in1=lam_t[:, 32:64],
                                                         op0=ALU.mult, op1=ALU.mult, accum_out=lam[:, 0:1]), [b_lam], [b_lam])
                V(lambda: nc.vector.scalar_tensor_tensor(out=lam_t[:, 64:96], in0=lam_t[:, 64:96], scalar=1.0,
                                                         in1=lam_t[:, 96:128], op0=ALU.mult, op1=ALU.mult,
                                                         accum_out=lam[:, 1:2]), [b_lam], [b_lam])
                A(lambda: nc.scalar.activation(out=lam[:, 2:4], in_=lam[:, 0:2], func=AF.Exp), [b_lam], [b_lam])
                V(lambda: nc.vector.tensor_tensor(out=lam[:, 4:5], in0=lam[:, 2:3], in1=lam[:, 3:4], op=ALU.subtract),
                  [b_lam], [b_lam])
                V(lambda: nc.vector.tensor_scalar(out=lam[:, 5:6], in0=lam[:, 4:5], scalar1=lam_init, scalar2=-1.0,
                                                  op0=ALU.add, op1=ALU.mult), [b_lam], [b_lam])
                S.dma("sp", gA_bc[:], I["diff_subln_g"][l, :].partition_broadcast(128), writes=[b_gA])
                V(lambda: nc.vector.tensor_scalar(out=gA_bc[:], in0=gA_bc[:], scalar1=1.0 - lam_init, scalar2=None,
                                                  op0=ALU.mult), [b_gA], [b_gA])

                state = {"qi": 0, "ei": 0, "si": 0, "sbi": 0}

                def load_map(chunk, r0, nrows, isq):
                    i = state["qi"] % 4
                    t, b = (qT[i], b_qT[i]) if isq else (kT[i], b_kT[i])
                    S.dma("sp", t[0:nrows, :], QKT[chunk, r0:r0 + nrows, :], writes=[b])
                    return t, b

                def load_V(g):
                    for a in range(4):
                        S.dma("sp", Vg[:, a * 8:(a + 1) * 8, 0:VW],
                              VG[g, a * 1024:(a + 1) * 1024, :].rearrange("(t p) c -> p t c", p=128), writes=[b_Vg])

                def job(maps, vcol, ycol, alibi_slope=None):
                    nm = len(maps)
                    pend = [None]
                    for qt in range(8):
                        if nm == 2:
                            groups = [[(kt, 0), (kt, 1)] for kt in range(NSUB)]
                        else:
                            groups = [[(2 * j, 0), (2 * j + 1, 0)] for j in range(NSUB // 2)]
                        ng = len(groups)
                        for gi in range(ng + 2):
                            if gi < ng:
                                banks = [2 * (gi % 2), 2 * (gi % 2) + 1]
                                S.prewait("pe", [mp[1] for mp in maps] + [mp[3] for mp in maps], [b_pS[bk] for bk in banks])
                                for idx, (kt, m) in enumerate(groups[gi]):
                                    q_t, bq, k_t, bk_, nr = maps[m][:5]
                                    p0 = maps[m][5] if len(maps[m]) > 5 else 0
                                    si = banks[idx]
                                    tp_ = (p0, 0) if p0 == 96 else None
                                    PE(lambda: nc.tensor.matmul(pS[si][:, :], lhsT=k_t[p0:p0 + nr, kt * 128:(kt + 1) * 128],
                                                                rhs=q_t[p0:p0 + nr, qt * 512:(qt + 1) * 512], start=True, stop=True,
                                                                tile_position=tp_),
                                       [bq, bk_], [b_pS[si]], signal=(idx == 1))
                            j = gi - 1
                            if 0 <= j < ng:
                                for idx, (kt, m) in enumerate(groups[j]):
                                    si = 2 * (j % 2) + idx
                                    ei = 2 * (j % 2) + idx
                                    if alibi_slope is None:
                                        A(lambda: nc.scalar.activation(out=E[ei][:, :], in_=pS[si][:, :], func=AF.Exp),
                                          [b_pS[si]], [b_E[ei]])
                                    else:
                                        k0, q0 = kt * 128, qt * 512
                                        if k0 + 128 <= q0:
                                            tab, sc_, bias = alibi[:, 0, :], -alibi_slope, -alibi_slope * (q0 - k0)
                                        elif k0 >= q0 + 512:
                                            tab, sc_, bias = alibi[:, 0, :], alibi_slope, -alibi_slope * (k0 - q0)
                                        else:
                                            tab, sc_, bias = alibi[:, 1 + (k0 - q0) // 128, :], -alibi_slope, 0.0
                                        sbi = state["sbi"] % 3
                                        state["sbi"] += 1
                                        V(lambda: nc.vector.scalar_tensor_tensor(out=Sb[sbi][:, :], in0=tab, scalar=sc_,
                                                                                 in1=pS[si][:, :], op0=ALU.mult, op1=ALU.add),
                                          [b_al, b_pS[si]], [b_Sb[sbi]])
                                        A(lambda: nc.scalar.activation(out=E[ei][:, :], in_=Sb[sbi][:, :], func=AF.Exp, bias=bias),
                                          [b_Sb[sbi]], [b_E[ei]])
                            j = gi - 2
                            if 0 <= j < ng:
                                eis_ = [2 * (j % 2), 2 * (j % 2) + 1]
                                S.prewait("pe", [b_E[e] for e in eis_] + [b_Vg], [b_pOT[m] for (_, m) in groups[j]])
                                for idx, (kt, m) in enumerate(groups[j]):
                                    ei = eis_[idx]
                                    PE(lambda: nc.tensor.matmul(pOT[m][:, :], lhsT=Vg[:, kt, vcol:vcol + 128], rhs=E[ei][:, :],
                                                                start=(kt == 0), stop=(kt == NSUB - 1)),
                                       [b_E[ei], b_Vg], [b_pOT[m]], signal=(idx == 1))
                            if gi == 4 and pend[0] is not None:
                                pend[0]()
                                pend[0] = None
                        for m in range(nm):
                            if alibi_slope is not None:
                                A(lambda: nc.scalar.activation(out=otS[m][:, :], in_=pOT[m][0:65, :], func=AF.Identity),
                                  [b_pOT[m]], [b_otS[m]])
                            else:
                                V(lambda: nc.vector.tensor_copy(otS[m][:, :], pOT[m][0:65, :]), [b_pOT[m]], [b_otS[m]])
                        pend[0] = (lambda qt=qt: finalize(qt, nm, ycol))
                    pend[0]()
                    pend[0] = None

                def finalize(qt, nm, ycol):
                    if True:
                        for m in range(nm):
                            for jj in range(4):
                                PE(lambda: nc.tensor.transpose(pO[m][:, jj * 65:(jj + 1) * 65], otS[m][0:65, jj * 128:(jj + 1) * 128],
                                                               ident_f[0:65, 0:65]), [b_otS[m], b_const], [b_pO[m]], signal=(jj == 3))
                        O1 = pO[0][:, 0:260].rearrange("p (j d) -> p j d", d=65)
                        ydst = y_res[:, qt * 4:(qt + 1) * 4, ycol:ycol + 64]
                        by = b_y[qt * 4:(qt + 1) * 4]
                        V(lambda: nc.vector.reciprocal(out=rc[:, 0:4], in_=O1[:, :, 64]), [b_pO[0]], [b_rc])
                        if nm == 1:
                            V(lambda: nc.vector.tensor_tensor(out=ydst, in0=O1[:, :, 0:64],
                                                              in1=rc[:, 0:4].unsqueeze(2).to_broadcast([128, 4, 64]),
                                                              op=ALU.mult), [b_pO[0], b_rc], by)
                        else:
                            O2 = pO[1][:, 0:260].rearrange("p (j d) -> p j d", d=65)
                            V(lambda: nc.vector.reciprocal(out=rc[:, 4:8], in_=O2[:, :, 64]), [b_pO[1]], [b_rc])
                            V(lambda: nc.vector.tensor_scalar(out=rc[:, 4:8], in0=rc[:, 4:8], scalar1=lam[:, 5:6], scalar2=None,
                                                              op0=ALU.mult), [b_rc, b_lam], [b_rc])
                            V(lambda: nc.vector.tensor_tensor(out=o1[:, :, :], in0=O1[:, :, 0:64],
                                                              in1=rc[:, 0:4].unsqueeze(2).to_broadcast([128, 4, 64]),
                                                              op=ALU.mult), [b_pO[0], b_rc], [b_o1])
                            V(lambda: nc.vector.tensor_tensor(out=o2[:, :, :], in0=O2[:, :, 0:64],
                                                              in1=rc[:, 4:8].unsqueeze(2).to_broadcast([128, 4, 64]),
                                                              op=ALU.mult), [b_pO[1], b_rc], [b_o2])
                            G(lambda: nc.gpsimd.tensor_tensor(out=o1[:, :, :], in0=o1[:, :, :], in1=o2[:, :, :], op=ALU.add),
                              [b_o1, b_o2], [b_o1])
                            G(lambda: nc.gpsimd.tensor_tensor(out=osq[:, :, :], in0=o1[:, :, :], in1=o1[:, :, :], op=ALU.mult),
                              [b_o1], [b_osq])
                            V(lambda: nc.vector.tensor_reduce(out=rc[:, 8:12], in_=osq[:, :, :], axis=AX.X, op=ALU.add),
                              [b_osq], [b_rc])
                            A(lambda: nc.scalar.activation(out=rc[:, 8:12], in_=rc[:, 8:12], func=AF.Ln, scale=1.0 / 64,
                                                           bias=eps6[:, 0:1]), [b_rc, b_const], [b_rc])
                            A(lambda: nc.scalar.activation(out=rc[:, 12:16], in_=rc[:, 8:12], func=AF.Exp, scale=-0.5),
                              [b_rc], [b_rc])
                            V(lambda: nc.vector.tensor_tensor(out=o2[:, :, :], in0=o1[:, :, :],
                                                              in1=rc[:, 12:16].unsqueeze(2).to_broadcast([128, 4, 64]),
                                                              op=ALU.mult), [b_o1, b_rc], [b_o2])
                            V(lambda: nc.vector.tensor_tensor(out=ydst, in0=o2[:, :, :],
                                                              in1=gA_bc[:, :].unsqueeze(1).to_broadcast([128, 4, 64]),
                                                              op=ALU.mult), [b_o2, b_gA], by)

                load_V(0)
                for h in range(4):
                    state["qi"] += 1
                    i_ = state["qi"] % 4
                    g0 = 2 * (h % 2)
                    S.dma("sp", qT[i_][32 * g0:32 * g0 + 64, :], QKT[h // 2, 32 * g0:32 * g0 + 64, :], writes=[b_qT[i_]])
                    S.dma("sp", kT[i_][32 * g0:32 * g0 + 64, :], QKT[2 + h // 2, 32 * g0:32 * g0 + 64, :], writes=[b_kT[i_]])
                    maps = [(qT[i_], b_qT[i_], kT[i_], b_kT[i_], 32, 32 * (g0 + c_)) for c_ in range(2)]
                    job(maps, h * VP, h * 64, alibi_slope=SLOPES_A[h])
                load_V(1)
                for h in range(4):
                    state["qi"] += 1
                    q_t, bq = load_map(4 + h, 0, 96, True)
                    k_t, bk_ = load_map(8 + h, 0, 96, False)
                    job([(q_t, bq, k_t, bk_, 96)], h * VP, 256 + h * 64)
                load_V(2)
                for h in range(4):
                    state["qi"] += 1
                    q_t, bq = load_map(12 + h // 2, (h % 2) * 64, 64, True)
                    if h % 2 == 0:
                        k_t, bk_ = load_map(14, (h // 2) * 64, 64, False)
                    job([(q_t, bq, k_t, bk_, 64)], (h // 2) * VP, 512 + h * 64)
                S.barrier()

        def phase_D(l, y_res, b_y):
            with ExitStack() as st:
                tokD = sb(st, "tokD", [128, 8, 512], BF16)
                Vsh = sb(st, "Vsh", [128, 9, VW], BF16)
                QTd = sb(st, "QTd", [128, 2, 1024], BF16)
                KTd = sb(st, "KTd", [128, 2, 1024 + 128], BF16)
                dtab = sb(st, "dtab", [128, 2, 512], F32)
                Sb = [sb(st, f"dSb{i}", [128, 512], F32) for i in range(4)]
                E = [sb(st, f"dE{i}", [128, 512], BF16) for i in range(4)]
                ost = [sb(st, f"ost{i}", [128, 4, 260], F32) for i in range(2)]
                acc = sb(st, "acc", [128, 260], F32)
                od = [sb(st, f"od{i}", [128, 260], F32) for i in range(3)]
                rcd = sb(st, "rcd", [128, 4], F32)
                pT = [pbank(st, f"dT{i}") for i in range(2)]
                pS = [pbank(st, f"dS{i}") for i in range(4)]
                pO = [pbank(st, f"dO{i}") for i in range(2)]
                b_tok, b_Vsh, b_QT, b_KT, b_dt = Buf(), Buf(), Buf(), Buf(), Buf()
                b_Sb = [Buf() for _ in range(4)]
                b_E = [Buf() for _ in range(4)]
                b_ost = [Buf(), Buf()]
                b_pT = [PB(), PB()]
                b_pS = [PB() for _ in range(4)]
                b_pO = [PB(), PB()]
                b_acc, b_od, b_rcd = Buf(), [Buf() for _ in range(3)], Buf()
                S.dma("sp", dtab[:], I["dtab"], writes=[b_dt])
                cnt = {"s": 0, "e": 0, "o": 0}
                for bi, d in enumerate(DILS):
                    Ltot = SEQ // d
                    nseg = max(1, Ltot // 1024)
                    for r in range(d):
                        for seg in range(nseg):
                            Lc = min(Ltot, 1024)
                            i0 = seg * 1024
                            nt = Lc // 128
                            V(lambda: nc.vector.memset(KTd[:, :, :], 0.0), [], [b_KT])
                            V(lambda: nc.vector.memset(Vsh[:, :, :], 0.0), [], [b_Vsh])
                            base = r + d * i0
                            src = DTOK[base: base + d * (Lc - 1) + 1: d, :]
                            S.dma("sp", tokD[:, 0:nt, :], src[:, 0:512].rearrange("(t p) c -> p t c", p=128), writes=[b_tok])
                            lo = i0 - 64
                            hi = i0 + Lc + 64
                            lo_c, hi_c = max(lo, 0), min(hi, Ltot)
                            u0 = lo_c - lo
                            n_rows = hi_c - lo_c
                            pos = 0
                            while pos < n_rows:
                                u = u0 + pos
                                tj, pj = u // 128, u % 128
                                take = min(128 - pj, n_rows - pos)
                                if pj == 0 and take == 128:
                                    nfull = (n_rows - pos) // 128
                                    t_first = r + d * (lo_c + pos)
                                    srcv = DTOK[t_first: t_first + d * (128 * nfull - 1) + 1: d, 512:512 + VW]
                                    S.dma("sp", Vsh[:, tj:tj + nfull, :], srcv.rearrange("(t p) c -> p t c", p=128),
                                          writes=[b_Vsh])
                                    pos += 128 * nfull
                                else:
                                    t_first = r + d * (lo_c + pos)
                                    srcv = DTOK[t_first: t_first + d * (take - 1) + 1: d, 512:512 + VW]
                                    S.dma("sp", Vsh[pj:pj + take, tj, :], srcv, writes=[b_Vsh])
                                    pos += take
                            for t in range(nt):
                                pTb = pT[t % 2][:, :].bitcast(BF16)
                                for j in range(4):
                                    PE(lambda: nc.tensor.transpose(pTb[:, j * 128:(j + 1) * 128], tokD[:, t, j * 128:(j + 1) * 128],
                                                                   ident_b[:]), [b_tok, b_const], [b_pT[t % 2]], signal=(j == 3))
                                V(lambda: nc.vector.tensor_copy(QTd[:, :, t * 128:(t + 1) * 128],
                                                                pTb[:, 0:256].rearrange("p (c t) -> p c t", t=128)),
                                  [b_pT[t % 2]], [b_QT])
                                A(lambda: nc.scalar.activation(func=AF.Identity, out=KTd[:, :, 64 + t * 128: 64 + (t + 1) * 128],
                                                         in_=pTb[:, 256:512].rearrange("p (c t) -> p c t", t=128)),
                                  [b_pT[t % 2]], [b_KT])
                            if nseg > 1:
                                for side in range(2):
                                    hs = i0 - 64 if side == 0 else i0 + Lc
                                    if hs < 0 or hs >= Ltot:
                                        continue
                                    t_first = r + d * hs
                                    srck = DTOK[t_first: t_first + d * 63 + 1: d, 256:512]
                                    S.dma("sp", tokD[0:64, 0, 0:256], srck, writes=[b_tok])
                                    pTb = pT[0][:, :].bitcast(BF16)
                                    for j in range(2):
                                        PE(lambda: nc.tensor.transpose(pTb[:, j * 128: j * 128 + 64],
                                                                       tokD[0:64, 0, j * 128:(j + 1) * 128], ident_b[0:64, 0:64]),
                                           [b_tok, b_const], [b_pT[0]], signal=(j == 1))
                                    col = 0 if side == 0 else 64 + Lc
                                    V(lambda: nc.vector.tensor_copy(
                                        KTd[:, :, col:col + 64],
                                        pTb[:, 0:256].rearrange("p (c t) -> p c t", t=128)[:, :, 0:64]), [b_pT[0]], [b_KT])
                            steps = [(g0, h, ab) for g0 in range(0, nt, 4) for h in range(4) for ab in range(2)]
                            n = len(steps)
                            sis, eis = [0] * n, [0] * n
                            LS, LP = 1, 2
                            for i in range(n + LP):
                                if i < n:
                                    g0, h, ab = steps[i]
                                    ng = min(4, nt - g0)
                                    ch, pr = h // 2, (h % 2) * 64
                                    si = cnt["s"] % 4
                                    cnt["s"] += 1
                                    sis[i] = si
                                    for jj in range(ng):
                                        qc = (g0 + jj) * 128
                                        kc0 = qc + ab * 128
                                        PE(lambda: nc.tensor.matmul(pS[si][:, jj * 128:(jj + 1) * 128],
                                                                    lhsT=KTd[pr:pr + 64, ch, kc0:kc0 + 128],
                                                                    rhs=QTd[pr:pr + 64, ch, qc:qc + 128], start=True, stop=True),
                                           [b_KT, b_QT], [b_pS[si]], signal=(jj == ng - 1))
                                j = i - LS
                                if 0 <= j < n:
                                    g0, h, ab = steps[j]
                                    ng = min(4, nt - g0)
                                    w = ng * 128
                                    si = sis[j]
                                    ei = cnt["e"] % 4
                                    cnt["e"] += 1
                                    eis[j] = ei
                                    V(lambda: nc.vector.scalar_tensor_tensor(out=Sb[ei][:, 0:w], in0=dtab[:, ab, 0:w],
                                                                             scalar=-SLOPES_D[h] * d, in1=pS[si][:, 0:w],
                                                                             op0=ALU.mult, op1=ALU.add),
                                      [b_dt, b_pS[si]], [b_Sb[ei]])
                                    A(lambda: nc.scalar.activation(out=E[ei][:, 0:w], in_=Sb[ei][:, 0:w], func=AF.Exp),
                                      [b_Sb[ei]], [b_E[ei]])
                                j = i - LP
                                if 0 <= j < n:
                                    g0, h, ab = steps[j]
                                    ng = min(4, nt - g0)
                                    ei = eis[j]
                                    oi = (g0 // 4) % 2
                                    for jj in range(ng):
                                        PE(lambda: nc.tensor.matmul(pO[h % 2][:, jj * 65:(jj + 1) * 65],
                                                                    lhsT=E[ei][:, jj * 128:(jj + 1) * 128],
                                                                    rhs=Vsh[:, g0 + jj + ab, h * VP:h * VP + 65],
                                                                    start=(ab == 0 and jj == 0), stop=(ab == 1),
                                                                    skip_group_check=True),
                                           [b_E[ei], b_Vsh], [b_pO[h % 2]], signal=(jj == ng - 1))
                                    if ab == 1:
                                        V(lambda: nc.vector.tensor_copy(ost[oi][:, 0:ng, h * 65:(h + 1) * 65],
                                                                        pO[h % 2][:, 0:ng * 65].rearrange("p (j d) -> p j d", d=65)),
                                          [b_pO[h % 2]], [b_ost[oi]])
                                        if h == 3:
                                            for jj in range(ng):
                                                t_first = r + d * (i0 + (g0 + jj) * 128)
                                                S.dma("sp", OD[bi, t_first: t_first + d * 127 + 1: d, :], ost[oi][:, jj, :],
                                                      reads=[b_ost[oi]])
                S.barrier()
                for sg in range(NSUB):
                    rows = slice(sg * 128, (sg + 1) * 128)
                    for bi in range(3):
                        S.dma("sp", od[bi][:], OD[bi, rows, :], writes=[b_od[bi]])
                    V(lambda: nc.vector.tensor_tensor(out=acc[:], in0=od[0][:], in1=od[1][:], op=ALU.add),
                      [b_od[0], b_od[1]], [b_acc])
                    V(lambda: nc.vector.tensor_tensor(out=acc[:], in0=acc[:], in1=od[2][:], op=ALU.add), [b_acc, b_od[2]], [b_acc])
                    a3 = acc[:, :].rearrange("p (h d) -> p h d", d=65)
                    V(lambda: nc.vector.reciprocal(out=rcd[:, 0:4], in_=a3[:, :, 64]), [b_acc], [b_rcd])
                    V(lambda: nc.vector.tensor_tensor(out=y_res[:, sg, 768:1024].rearrange("p (h d) -> p h d", d=64),
                                                      in0=a3[:, :, 0:64], in1=rcd[:, 0:4].unsqueeze(2).to_broadcast([128, 4, 64]),
                                                      op=ALU.mult), [b_acc, b_rcd], [b_y[sg]])
                S.barrier()

        def layer_norm(v, bv, dst, bdst, g_bc, b_bc, btab, stats, mv, bst, eng2):
            v4 = v.rearrange("p (c f) -> p c f", f=256)
            for c_ in range(4):
                V(lambda: nc.vector.bn_stats(out=stats[:, c_, :], in_=v4[:, c_, :]), [bv], [bst])
            V(lambda: nc.vector.bn_aggr(out=mv[:, 0:2], in_=stats[:, :, :].rearrange("p c f -> p (c f)")), [bst], [bst])
            A(lambda: nc.scalar.activation(out=mv[:, 2:3], in_=mv[:, 1:2], func=AF.Ln, bias=eps5[:, 0:1]), [bst, b_const], [bst])
            A(lambda: nc.scalar.activation(out=mv[:, 3:4], in_=mv[:, 2:3], func=AF.Exp, scale=-0.5), [bst], [bst])
            V(lambda: nc.vector.scalar_tensor_tensor(out=mv[:, 4:5], in0=mv[:, 0:1], scalar=-1.0, in1=mv[:, 3:4], op0=ALU.mult,
                                                     op1=ALU.mult), [bst], [bst])
            A(lambda: nc.scalar.activation(out=v, in_=v, func=AF.Identity, scale=mv[:, 3:4], bias=mv[:, 4:5]), [bv, bst], [bv])
            S.op(eng2, lambda: (nc.gpsimd if eng2 == "pool" else nc.vector).tensor_tensor(out=v, in0=v, in1=g_bc, op=ALU.mult),
                 [bv, btab], [bv])
            S.op(eng2, lambda: (nc.gpsimd if eng2 == "pool" else nc.vector).tensor_tensor(out=dst, in0=v, in1=b_bc, op=ALU.add),
                 [bv, btab], [bdst])

        def phase_O1(l, xsrc, y_res, b_y):
            with ExitStack() as st:
                w_o = sb(st, "w_o", [128, 8, D], BF16)
                gA = sb(st, "gA", [128, D], F32)
                lng = sb(st, "lng", [128, D], F32)
                lnb = sb(st, "lnb", [128, D], F32)
                xs = [sb(st, f"oxs{i}", [128, D], F32) for i in range(2)]
                yT = [sb(st, f"yT{i}", [128, 8, 128], BF16) for i in range(2)]
                v = [sb(st, f"ov{i}", [128, D], F32) for i in range(2)]
                x1 = [sb(st, f"ox1{i}", [128, D], F32) for i in range(2)]
                h2 = [sb(st, f"oh2{i}", [128, 8, 512], BF16) for i in range(2)]
                stats = sb(st, "ostats", [128, 4, 6], F32)
                mv = sb(st, "omv", [128, 8], F32)
                pb = [pbank(st, f"po{i}") for i in range(8)]
                bp = [PB() for _ in range(8)]
                b_w, b_tab, b_stt = Buf(), Buf(), Buf()
                b_xs, b_yT, b_v, b_x1, b_h2 = ([Buf(), Buf()] for _ in range(5))
                wsrc = I["w_o"][l].rearrange("(kc p) n -> p kc n", p=128)
                for kc in range(8):
                    S.dma("pool", w_o[:, kc, :], wsrc[:, kc, :], writes=[b_w])
                S.dma("sp", gA[:], GB[l, 0], writes=[b_tab])
                S.dma("sp", lng[:], I["ln_attn_g"][l, :].partition_broadcast(128), writes=[b_tab])
                S.dma("sp", lnb[:], I["ln_attn_b"][l, :].partition_broadcast(128), writes=[b_tab])
                for sg in range(NSUB):
                    T, s = sg // 4, sg % 4
                    i2 = sg % 2
                    tsl = slice(s * 128, (s + 1) * 128)
                    S.dma("sp", xs[i2][:], xsrc[sg * 128:(sg + 1) * 128, :], writes=[b_xs[i2]])
                    tb = pb[0 + i2][:, :].bitcast(BF16)
                    for kc in range(8):
                        PE(lambda: nc.tensor.transpose(tb[:, kc * 128:(kc + 1) * 128], y_res[:, sg, kc * 128:(kc + 1) * 128],
                                                       ident_b[:]), [b_y[sg], b_const], [bp[i2]], signal=(kc == 7))
                    V(lambda: nc.vector.tensor_copy(yT[i2][:, :, :], tb[:, 0:1024].rearrange("p (c t) -> p c t", t=128)),
                      [bp[i2]], [b_yT[i2]])
                    for hf in range(2):
                        bk = 2 + 2 * i2 + hf
                        for kc in range(8):
                            PE(lambda: nc.tensor.matmul(pb[bk][:, :], lhsT=yT[i2][:, kc, :], rhs=w_o[:, kc, hf * 512:(hf + 1) * 512],
                                                        start=(kc == 0), stop=(kc == 7)), [b_yT[i2], b_w], [bp[bk]], signal=(kc == 7))
                        hs = slice(hf * 512, (hf + 1) * 512)
                        V(lambda: nc.vector.tensor_tensor(out=v[i2][:, hs], in0=pb[bk][:, :], in1=gA[:, hs], op=ALU.mult),
                          [bp[bk], b_tab], [b_v[i2]])
                    V(lambda: nc.vector.scalar_tensor_tensor(out=v[i2][:, :], in0=xs[i2][:, :], scalar=ALPHA,
                                                             in1=v[i2][:, :], op0=ALU.mult, op1=ALU.add),
                      [b_xs[i2], b_v[i2]], [b_v[i2]])
                    layer_norm(v[i2][:, :], b_v[i2], x1[i2][:, :], b_x1[i2], lng[:, :], lnb[:, :], b_tab, stats, mv, b_stt, "pool")
                    S.dma("sp", X1[sg * 128:(sg + 1) * 128, :], x1[i2][:], reads=[b_x1[i2]])
                    for hf in range(2):
                        bk = 6 + hf
                        for cc in range(4):
                            kc = hf * 4 + cc
                            PE(lambda: nc.tensor.transpose(pb[bk][:, cc * 128:(cc + 1) * 128], x1[i2][:, kc * 128:(kc + 1) * 128],
                                                           ident_f[:]), [b_x1[i2], b_const], [bp[bk]], signal=(cc == 3))
                        for cc in range(4):
                            kc = hf * 4 + cc
                            if cc % 2 == 0:
                                A(lambda: nc.scalar.activation(out=h2[T % 2][:, kc, tsl], in_=pb[bk][:, cc * 128:(cc + 1) * 128],
                                                               func=AF.Identity, scale=modT[:, l, 32 + kc:33 + kc],
                                                               bias=modT[:, l, 24 + kc:25 + kc]), [bp[bk], b_modT], [b_h2[T % 2]])
                            else:
                                V(lambda: nc.vector.tensor_scalar(out=h2[T % 2][:, kc, tsl], in0=pb[bk][:, cc * 128:(cc + 1) * 128],
                                                                  scalar1=modT[:, l, 32 + kc:33 + kc],
                                                                  scalar2=modT[:, l, 24 + kc:25 + kc], op0=ALU.mult, op1=ALU.add),
                                  [bp[bk], b_modT], [b_h2[T % 2]])
                    if s == 3:
                        S.dma("sp", H2T[:, :, T * 512:(T + 1) * 512].rearrange("c r t -> r c t"), h2[T % 2][:, :, :],
                              reads=[b_h2[T % 2]])
                S.barrier()

        def phase_O2(l, dst):
            with ExitStack() as st:
                w_up = sb(st, "w_up", [128, 8, HID], BF16)
                w_dn = sb(st, "w_dn", [128, 32, D], BF16)
                gM = sb(st, "gM", [128, D], F32)
                lng = sb(st, "lng2", [128, D], F32)
                lnb = sb(st, "lnb2", [128, D], F32)
                h2 = [sb(st, f"mh2{i}", [128, 8, 512], BF16) for i in range(1)]
                uT = sb(st, "uT", [128, 32, 512], BF16)
                rr = [sb(st, f"rr{i}", [128, 512], F32) for i in range(3)]
                x1 = [sb(st, f"mx1{i}", [128, D], F32) for i in range(2)]
                v = [sb(st, f"mv{i}", [128, D], F32) for i in range(2)]
                stats = sb(st, "mstats", [128, 4, 6], F32)
                mv = sb(st, "mmv", [128, 8], F32)
                pb = [pbank(st, f"pm{i}") for i in range(8)]
                bp = [PB() for _ in range(8)]
                b_wu, b_wd, b_tab, b_stt, b_uT = Buf(), Buf(), Buf(), Buf(), Buf()
                b_h2, b_x1, b_v = ([Buf(), Buf()] for _ in range(3))
                b_rr = [Buf() for _ in range(3)]
                usrc = I["w_up"][l].rearrange("(kc p) n -> p kc n", p=128)
                for kc in range(8):
                    S.dma("pool", w_up[:, kc, :], usrc[:, kc, :], writes=[b_wu])
                dsrc = I["w_down"][l].rearrange("(kc p) n -> p kc n", p=128)
                for k4 in range(8):
                    S.dma("pool", w_dn[:, k4 * 4:(k4 + 1) * 4, :], dsrc[:, k4 * 4:(k4 + 1) * 4, :], writes=[b_wd])
                S.dma("sp", gM[:], GB[l, 1], writes=[b_tab])
                S.dma("sp", lng[:], I["ln_mlp_g"][l, :].partition_broadcast(128), writes=[b_tab])
                S.dma("sp", lnb[:], I["ln_mlp_b"][l, :].partition_broadcast(128), writes=[b_tab])
                ri = 0
                for T in range(8):
                    h_t, bh = h2[0], b_h2[0]
                    S.dma("sp", h_t[:, :, :], H2T[:, :, T * 512:(T + 1) * 512].rearrange("c r t -> r c t"), writes=[bh])
                    for hc in range(32):
                        bk = hc % 4
                        for kc in range(8):
                            PE(lambda: nc.tensor.matmul(pb[bk][:, :], lhsT=w_up[:, kc, hc * 128:(hc + 1) * 128], rhs=h_t[:, kc, :],
                                                        start=(kc == 0), stop=(kc == 7)), [b_wu, bh], [bp[bk]], signal=(kc == 7))
                        r_, br = rr[ri % 3], b_rr[ri % 3]
                        ri += 1
                        A(lambda: nc.scalar.activation(out=r_[:, :], in_=pb[bk][:, :], func=AF.Relu), [bp[bk]], [br])
                        if hc % 2 == 0:
                            V(lambda: nc.vector.tensor_tensor(out=uT[:, hc, :], in0=r_[:, :], in1=r_[:, :], op=ALU.mult), [br], [b_uT])
                        else:
                            G(lambda: nc.gpsimd.tensor_tensor(out=uT[:, hc, :], in0=r_[:, :], in1=r_[:, :], op=ALU.mult), [br], [b_uT])
                    for s in range(4):
                        sg = T * 4 + s
                        i2 = sg % 2
                        rows = slice(sg * 128, (sg + 1) * 128)
                        S.dma("sp", x1[i2][:], X1[rows, :], writes=[b_x1[i2]])
                        for hf in range(2):
                            bk = 4 + 2 * i2 + hf
                            for hc in range(32):
                                PE(lambda: nc.tensor.matmul(pb[bk][:, :], lhsT=uT[:, hc, s * 128:(s + 1) * 128],
                                                            rhs=w_dn[:, hc, hf * 512:(hf + 1) * 512], start=(hc == 0), stop=(hc == 31)),
                                   [b_uT, b_wd], [bp[bk]], signal=(hc == 31))
                            hs = slice(hf * 512, (hf + 1) * 512)
                            V(lambda: nc.vector.tensor_tensor(out=v[i2][:, hs], in0=pb[bk][:, :], in1=gM[:, hs], op=ALU.mult),
                              [bp[bk], b_tab], [b_v[i2]])
                        V(lambda: nc.vector.scalar_tensor_tensor(out=v[i2][:, :], in0=x1[i2][:, :], scalar=ALPHA, in1=v[i2][:, :],
                                                                 op0=ALU.mult, op1=ALU.add), [b_x1[i2], b_v[i2]], [b_v[i2]])
                        layer_norm(v[i2][:, :], b_v[i2], v[i2][:, :], b_v[i2], lng[:, :], lnb[:, :], b_tab, stats, mv, b_stt, "pool")
                        S.dma("sp", dst[rows, :], v[i2][:], reads=[b_v[i2]])
                S.barrier()

        for l in range(nlayers):
            if "stop0" in dbg:
                break
            xsrc = I["x"] if l == 0 else XN
            phase_P(l, xsrc)
            if "stopP" in dbg:
                break
            with ExitStack() as lst:
                y_res = sb(lst, f"y_res{l}", [128, NSUB, D], BF16)
                b_y = [Buf(f"y{i}") for i in range(NSUB)]
                phase_att(l, y_res, b_y)
                phase_D(l, y_res, b_y)
                if YDBG is not None and l == 0:
                    for sg in range(NSUB):
                        S.dma("sp", YDBG[sg * 128:(sg + 1) * 128, :], y_res[:, sg, :], reads=[b_y[sg]])
                    S.barrier()
                if "stopA" in dbg:
                    break
                phase_O1(l, xsrc, y_res, b_y)
            phase_O2(l, out if l == nlayers - 1 else XN)
        S.barrier()
        print("ops", S.n_ops, "dmas", S.n_dma, "sems", S.nsem)
    return nc


DBG = {}
_CONSTS = None


def make_in_maps(inputs):
    global _CONSTS
    if _CONSTS is None:
        _CONSTS = _host_consts()
    shared = {}
    for n in W_NAMES:
        a = np.ascontiguousarray(np.asarray(inputs[n], dtype=np.float32))
        shared[n] = a.reshape(W_SHAPES[n])
    for n, a in _CONSTS.items():
        shared["k_" + n] = a
    x = np.asarray(inputs["x"], dtype=np.float32)
    c = np.asarray(inputs["c"], dtype=np.float32)
    maps = []
    for b in range(8):
        m = dict(shared)
        m["x"] = np.ascontiguousarray(x[b])
        m["c"] = np.ascontiguousarray(c[b].reshape(8, 128).T)
        maps.append(m)
    return maps


def kernel(**inputs):
    nc = build()
    in_maps = make_in_maps(inputs)
    res = run_bass_kernel_spmd(nc, in_maps, core_ids=list(range(8)))
    return np.stack([np.asarray(r["out"]) for r in res.results], axis=0).astype(np.float32)
```

```python
import math
from contextlib import ExitStack
import numpy as np
import concourse.bass as bass
import concourse.mybir as mybir
from concourse.bass_utils import run_bass_kernel_spmd

F32 = mybir.dt.float32
BF16 = mybir.dt.bfloat16
AF = mybir.ActivationFunctionType
ALU = mybir.AluOpType
AX = mybir.AxisListType

SEQ = 4096
D = 1024
NSUB = SEQ // 128
HID = 4096
INC = 2720
ALPHA = 4 ** 0.25
SLOPES_A = [2.0 ** -1, 2.0 ** -3, 2.0 ** -5, 2.0 ** -7]
SLOPES_D = [2.0 ** -2, 2.0 ** -4, 2.0 ** -6, 2.0 ** -8]
DILS = [1, 4, 16]
SC_A = 32 ** -0.5
SC_B = 96 ** -0.5
SC_C = 0.125
SC_D = 0.125
VP = 66
VW = 4 * VP


class Buf:
    __slots__ = ("name", "w", "r", "excl")

    def __init__(self, name="", excl=False):
        self.name = name
        self.w = None
        self.r = []
        self.excl = excl


def PB(name=""):
    return Buf(name, excl=True)


class Tok:
    __slots__ = ("eng", "sem", "val")

    def __init__(self, eng, sem=None, val=None):
        self.eng = eng
        self.sem = sem
        self.val = val


class Sched:
    EPOCH = 20000
    NDMA = 8

    def __init__(self, nc, stack):
        self.nc = nc
        self.stack = stack
        self.engs = {"pe": nc.tensor, "act": nc.scalar, "dve": nc.vector,
                     "pool": nc.gpsimd, "sp": nc.sync}
        self.count = {e: 0 for e in self.engs}
        self.cursem = {}
        self.pending = {e: [] for e in self.engs}
        self.waited = {e: {} for e in self.engs}
        self.nsem = 0
        for e in self.engs:
            self._new_epoch(e)
        self.dma_sems, self.dma_cnt, self.dma_last, self.dma_i = {}, {}, {}, {}
        for q in ("sp", "act", "pool"):
            self.dma_sems[q] = [self._sem(f"dma_{q}_{i}") for i in range(self.NDMA)]
            self.dma_cnt[q] = [0] * self.NDMA
            self.dma_last[q] = [None] * self.NDMA
            self.dma_i[q] = 0
        self.n_ops = {e: 0 for e in self.engs}
        self.n_dma = 0

    def _sem(self, name):
        self.nsem += 1
        return self.stack.enter_context(self.nc.semaphore(name))

    def _new_epoch(self, e):
        self.cursem[e] = self._sem(f"s_{e}_{self.nsem}")
        self.count[e] = 0

    def _wait(self, eng, tok):
        if tok is None:
            return
        if tok.sem is None:
            raise RuntimeError(f"dependency on unsignalled op on {tok.eng}")
        key = id(tok.sem)
        w = self.waited[eng]
        if w.get(key, 0) >= tok.val:
            return
        w[key] = tok.val
        self.engs[eng].wait_ge(tok.sem, tok.val)

    def _deps(self, eng, reads, writes):
        for b in reads:
            t = b.w
            if t is not None and not (t.eng == eng and eng == "pe"):
                self._wait(eng, t)
        for b in writes:
            t = b.w
            if t is not None and t.eng != eng:
                self._wait(eng, t)
            for t in b.r:
                if t.eng != eng:
                    self._wait(eng, t)

    def _record(self, tok, reads, writes):
        for b in reads:
            b.r.append(tok)
            if len(b.r) > 16:
                last = {}
                for t in b.r:
                    last[(t.eng, id(t.sem))] = t
                b.r = list(last.values())
        for b in writes:
            b.w = tok
            b.r = []

    def op(self, eng, fn, reads=(), writes=(), signal=True):
        if any(b.excl for b in reads):
            writes = list(writes) + [b for b in reads if b.excl]
            reads = [b for b in reads if not b.excl]
        self._deps(eng, reads, writes)
        inst = fn()
        self.n_ops[eng] += 1
        tok = Tok(eng)
        self.pending[eng].append(tok)
        if signal:
            if self.count[eng] >= self.EPOCH:
                self._new_epoch(eng)
            self.count[eng] += 1
            sem = self.cursem[eng]
            inst.then_inc(sem, 1)
            for t in self.pending[eng]:
                t.sem = sem
                t.val = self.count[eng]
            self.pending[eng] = []
        self._record(tok, reads, writes)
        return tok

    def prewait(self, eng, reads=(), writes=()):
        writes = list(writes) + [b for b in reads if b.excl]
        reads = [b for b in reads if not b.excl]
        self._deps(eng, reads, writes)

    def dma(self, q, out, in_, reads=(), writes=(), **kw):
        i = self.dma_i[q]
        self.dma_i[q] = (i + 1) % self.NDMA
        prev = self.dma_last[q][i]
        if prev is not None:
            self._wait(q, prev)
        self._deps(q, reads, writes)
        sem = self.dma_sems[q][i]
        self.dma_cnt[q][i] += 16
        inst = self.engs[q].dma_start(out=out, in_=in_, **kw)
        inst.then_inc(sem, 16)
        tok = Tok("dma_" + q + str(i), sem, self.dma_cnt[q][i])
        self.dma_last[q][i] = tok
        self._record(tok, reads, writes)
        self.n_dma += 1
        return tok

    def barrier(self):
        toks = []
        for e in self.engs:
            if self.pending[e]:
                raise RuntimeError(f"barrier with unsignalled ops on {e}")
            if self.count[e] > 0:
                toks.append(Tok(e, self.cursem[e], self.count[e]))
        for q in self.dma_last:
            for t in self.dma_last[q]:
                if t is not None:
                    toks.append(t)
        for e in self.engs:
            for t in toks:
                if t.eng != e:
                    self._wait(e, t)


def _host_consts():
    c = {}
    c["ident"] = np.eye(128, dtype=np.float32)
    tok = (np.arange(NSUB)[None, :] * 128 + np.arange(128)[:, None]).astype(np.float64)
    freqs = 10000.0 ** (-np.arange(16, dtype=np.float64) / 16)

    def cs(pos):
        ang = pos[..., None].astype(np.float32).astype(np.float64) * freqs.astype(np.float32).astype(np.float64)
        ang = (pos[..., None].astype(np.float32) * freqs.astype(np.float32)).astype(np.float32)
        return np.cos(ang.astype(np.float64)), np.sin(ang.astype(np.float64))

    cp, sp_ = cs(tok)
    rb = np.zeros((128, NSUB, 2, 64), np.float64)
    for i, s in enumerate([SC_B, 1.0]):
        rb[:, :, i, 0:16] = cp * s
        rb[:, :, i, 16:32] = cp * s
        rb[:, :, i, 32:48] = -sp_ * s
        rb[:, :, i, 48:64] = sp_ * s
    c["ropeB"] = rb.astype(np.float32)
    cr, sr = cs(np.floor(tok / 64))
    cc, sc_ = cs(np.mod(tok, 64))
    rc = np.zeros((128, NSUB, 128), np.float64)
    rc[:, :, 0:16] = cr
    rc[:, :, 16:32] = cr
    rc[:, :, 32:48] = cc
    rc[:, :, 48:64] = cc
    rc[:, :, 64:80] = -sr
    rc[:, :, 80:96] = sr
    rc[:, :, 96:112] = -sc_
    rc[:, :, 112:128] = sc_
    c["ropeC"] = rc.astype(np.float32)
    ki = np.arange(128)[:, None].astype(np.float64)
    qi = np.arange(512)[None, :].astype(np.float64)
    al = np.zeros((128, 5, 512), np.float64)
    al[:, 0, :] = qi - ki
    for o in range(4):
        al[:, 1 + o, :] = np.abs(qi - ki - 128 * o)
    c["alibi"] = al.astype(np.float32)
    q128 = (np.arange(512) % 128)[None, :].astype(np.float64)
    dt = np.zeros((128, 2, 512), np.float64)
    da = np.abs(ki - 64 - q128)
    db = np.abs(ki + 64 - q128)
    dt[:, 0, :] = np.where(da <= 64, da, 1.0e6)
    dt[:, 1, :] = np.where(db <= 64, db, 1.0e6)
    c["dtab"] = dt.astype(np.float32)
    return c


W_NAMES = ["w_ada", "b_ada", "w_in", "w_o", "diff_lambda", "diff_subln_g", "mla_q_norm_g", "mla_w_uq",
           "mla_kv_norm_g", "mla_w_ukv", "gqa_q_norm_g", "gqa_k_norm_g", "ln_attn_g", "ln_attn_b",
           "w_up", "w_down", "ln_mlp_g", "ln_mlp_b"]
W_SHAPES = {"w_ada": [2, 1024, 6144], "b_ada": [2, 6144], "w_in": [2, 1024, INC], "w_o": [2, 1024, 1024],
            "diff_lambda": [2, 128], "diff_subln_g": [2, 64], "mla_q_norm_g": [2, 384],
            "mla_w_uq": [2, 384, 384], "mla_kv_norm_g": [2, 256], "mla_w_ukv": [2, 256, 512],
            "gqa_q_norm_g": [2, 64], "gqa_k_norm_g": [2, 64], "ln_attn_g": [2, 1024], "ln_attn_b": [2, 1024],
            "w_up": [2, 1024, HID], "w_down": [2, HID, 1024], "ln_mlp_g": [2, 1024], "ln_mlp_b": [2, 1024]}
C_SHAPES = {"ident": [128, 128], "ropeB": [128, NSUB, 2, 64], "ropeC": [128, NSUB, 128],
            "alibi": [128, 5, 512], "dtab": [128, 2, 512]}


def build(nlayers=2, dbg=()):
    nc = bass.Bass("TRN2", target_bir_lowering=False)
    I = {}
    I["x"] = nc.dram_tensor("x", [SEQ, D], F32, kind="ExternalInput").ap()
    I["c"] = nc.dram_tensor("c", [128, 8], F32, kind="ExternalInput").ap()
    for n in W_NAMES:
        I[n] = nc.dram_tensor(n, W_SHAPES[n], F32, kind="ExternalInput").ap()
    for n in C_SHAPES:
        I[n] = nc.dram_tensor("k_" + n, C_SHAPES[n], F32, kind="ExternalInput").ap()
    out = nc.dram_tensor("out", [SEQ, D], F32, kind="ExternalOutput").ap()

    def scratch(name, shape, dt):
        kind = "ExternalOutput" if name in dbg else "Internal"
        return nc.dram_tensor(name, shape, dt, kind=kind).ap()

    QKT = scratch("QKT", [15, 128, SEQ], BF16)
    VG = scratch("VG", [3, SEQ, VW], BF16)
    DTOK = scratch("DTOK", [SEQ, 512 + VW], BF16)
    OD = scratch("OD", [3, SEQ, 260], F32)
    X1 = scratch("X1", [SEQ, D], F32)
    H2T = scratch("H2T", [8, 128, SEQ], BF16)
    XN = scratch("XN", [SEQ, D], F32)
    GB = scratch("GB", [2, 2, 128, D], F32)
    YDBG = scratch("YDBG", [SEQ, D], BF16) if "YDBG" in dbg else None

    with ExitStack() as top:
        S = Sched(nc, top)

        uid = [0]

        def sb(st, name, shape, dt):
            uid[0] += 1
            return st.enter_context(nc.sbuf_tensor(f"s{uid[0]}_{name}", shape, dt))

        def pbank(st, name):
            uid[0] += 1
            return st.enter_context(nc.psum_tensor(f"p{uid[0]}_{name}", [128, 512], F32))

        def V(fn, reads, writes):
            return S.op("dve", fn, reads, writes)

        def A(fn, reads, writes):
            return S.op("act", fn, reads, writes)

        def G(fn, reads, writes):
            return S.op("pool", fn, reads, writes)

        def PE(fn, reads, writes, signal=True):
            return S.op("pe", fn, reads, writes, signal)

        ident_f = sb(top, "ident_f", [128, 128], F32)
        ident_b = sb(top, "ident_b", [128, 128], BF16)
        modT = sb(top, "modT", [128, 2, 48], F32)
        eps6 = sb(top, "eps6", [128, 1], F32)
        eps5 = sb(top, "eps5", [128, 1], F32)
        b_const = Buf("const")
        b_modT = Buf("modT")
        S.dma("sp", ident_f[:], I["ident"], writes=[b_const])
        S.dma("pool", ident_b[:], I["ident"], writes=[b_const])
        V(lambda: nc.vector.memset(eps6[:], 1e-6), [], [b_const])
        V(lambda: nc.vector.memset(eps5[:], 1e-5), [], [b_const])

        with ExitStack() as st:
            condT = sb(st, "condT", [128, 8], F32)
            ones_row = sb(st, "ones_row", [1, 128], F32)
            modrow = sb(st, "modrow", [1, 6144], F32)
            brow = sb(st, "brow", [1, 6144], F32)
            wa = [sb(st, f"wa{i}", [128, 8, 512], F32) for i in range(2)]
            gbt = sb(st, "gbt", [128, 1024], F32)
            ps = [pbank(st, f"sps{i}") for i in range(2)]
            b_cond, b_ones, b_mrow, b_brow, b_gbt = Buf(), Buf(), Buf(), Buf(), Buf()
            b_wa = [Buf(), Buf()]
            b_ps = [PB(), PB()]
            S.dma("sp", condT[:], I["c"], writes=[b_cond])
            A(lambda: nc.scalar.activation(out=condT[:], in_=condT[:], func=AF.Silu), [b_cond], [b_cond])
            V(lambda: nc.vector.memset(ones_row[:], 1.0), [], [b_ones])
            for l in range(nlayers):
                S.dma("sp", brow[:], I["b_ada"][l:l + 1, :], writes=[b_brow])
                wsrc = I["w_ada"][l].rearrange("(kc p) n -> p kc n", p=128)
                for pc in range(12):
                    S.dma("sp", wa[pc % 2][:], wsrc[:, :, pc * 512:(pc + 1) * 512], writes=[b_wa[pc % 2]])
                    for kc in range(8):
                        PE(lambda: nc.tensor.matmul(ps[pc % 2][0:1, :], lhsT=condT[:, kc:kc + 1], rhs=wa[pc % 2][:, kc, :],
                                                    start=(kc == 0), stop=(kc == 7)),
                           [b_cond, b_wa[pc % 2]], [b_ps[pc % 2]], signal=(kc == 7))
                    V(lambda: nc.vector.tensor_tensor(out=modrow[0:1, pc * 512:(pc + 1) * 512], in0=ps[pc % 2][0:1, :],
                                                      in1=brow[0:1, pc * 512:(pc + 1) * 512], op=ALU.add),
                      [b_ps[pc % 2], b_brow], [b_mrow])
                for j in range(48):
                    PE(lambda: nc.tensor.matmul(ps[0][:, j:j + 1], lhsT=modrow[0:1, j * 128:(j + 1) * 128],
                                                rhs=ones_row[0:1, 0:1], start=True, stop=True),
                       [b_mrow, b_ones], [b_ps[0]], signal=(j == 47))
                V(lambda: nc.vector.tensor_copy(modT[:, l, :], ps[0][:, 0:48]), [b_ps[0]], [b_modT])
                V(lambda: nc.vector.tensor_scalar(out=modT[:, l, 8:16], in0=modT[:, l, 8:16], scalar1=1.0, scalar2=None,
                                                  op0=ALU.add), [b_modT], [b_modT])
                V(lambda: nc.vector.tensor_scalar(out=modT[:, l, 32:40], in0=modT[:, l, 32:40], scalar1=1.0, scalar2=None,
                                                  op0=ALU.add), [b_modT], [b_modT])
                for gi, base in enumerate([2048, 5120]):
                    for hf in range(2):
                        PE(lambda: nc.tensor.matmul(ps[1][:, :], lhsT=ones_row[0:1, :],
                                                    rhs=modrow[0:1, base + hf * 512: base + (hf + 1) * 512],
                                                    start=True, stop=True), [b_mrow, b_ones], [b_ps[1]])
                        V(lambda: nc.vector.tensor_copy(gbt[:, hf * 512:(hf + 1) * 512], ps[1][:, :]), [b_ps[1]], [b_gbt])
                    S.dma("sp", GB[l, gi], gbt[:], reads=[b_gbt])
            S.barrier()

        def phase_P(l, xsrc):
            with ExitStack() as st:
                w_in = sb(st, "w_in", [128, 8, INC], BF16)
                w_uq = sb(st, "w_uq", [128, 3, 384], BF16)
                w_ukv = sb(st, "w_ukv", [128, 2, 512], BF16)
                gq_bc = sb(st, "gq_bc", [128, 384], F32)
                gkv_bc = sb(st, "gkv_bc", [128, 256], F32)
                gC_bc = sb(st, "gC_bc", [128, 6, 64], F32)
                ropeB = sb(st, "ropeB", [128, NSUB, 2, 64], F32)
                ropeC = sb(st, "ropeC", [128, NSUB, 128], F32)
                xs = [sb(st, f"xs{i}", [128, D], F32) for i in range(2)]
                hT = [sb(st, f"hT{i}", [128, 8, 512], BF16) for i in range(2)]
                bf32_l = [sb(st, f"bf32{i}", [128, 672], F32) for i in range(2)]
                cqk_l = [sb(st, f"cqk{i}", [128, 384], F32) for i in range(2)]
                junk_l = [sb(st, f"junk{i}", [128, 384], F32) for i in range(2)]
                qkA_l = [sb(st, f"qkA{i}", [128, 512], BF16) for i in range(2)]
                qkD_l = [sb(st, f"qkD{i}", [128, 512], BF16) for i in range(2)]
                VA_l = [sb(st, f"VA{i}", [128, 4, VP], BF16) for i in range(2)]
                VB_l = [sb(st, f"VB{i}", [128, 4, VP], BF16) for i in range(2)]
                VC_l = [sb(st, f"VC{i}", [128, 4, VP], BF16) for i in range(2)]
                VD_l = [sb(st, f"VD{i}", [128, 4, VP], BF16) for i in range(2)]
                cqn_l = [sb(st, f"cqn{i}", [128, 640], BF16) for i in range(2)]
                cT_l = [sb(st, f"cT{i}", [128, 5, 128], BF16) for i in range(2)]
                qB_l = [sb(st, f"qB{i}", [128, 4, 96], BF16) for i in range(2)]
                kB_l = [sb(st, f"kB{i}", [128, 4, 96], BF16) for i in range(2)]
                cC_l = [sb(st, f"cC{i}", [128, 384], BF16) for i in range(2)]
                t1_l = [sb(st, f"t1{i}", [128, 384], F32) for i in range(2)]
                t2_l = [sb(st, f"t2{i}", [128, 384], F32) for i in range(2)]
                nq_l = [sb(st, f"nq{i}", [128, 384], F32) for i in range(2)]
                stt__l = [sb(st, f"stt_{i}", [128, 16], F32) for i in range(2)]
                stage = [sb(st, f"stage{i}", [128, 15, 512], BF16) for i in range(2)]
                pb = [pbank(st, f"pp{i}") for i in range(8)]
                bp = [PB(f"pp{i}") for i in range(8)]
                b_w, b_tab = Buf(), Buf()
                b_xs = [Buf(), Buf()]
                b_hT = [Buf(), Buf()]
                BL = {n: [Buf(n + '0'), Buf(n + '1')] for n in ['bf32', 'cqk', 'junk', 'qkA', 'qkD', 'VA', 'VB', 'VC', 'VD', 'cqn', 'cT', 'qB', 'kB', 'cC', 't1', 't2', 'nq', 'stt_']}
                b_wk = [Buf() for _ in range(8)]
                b_hk = [[Buf() for _ in range(8)] for _ in range(2)]
                b_stage = [Buf(), Buf()]

                wsrc = I["w_in"][l].rearrange("(kc p) n -> p kc n", p=128)
                for kc in range(8):
                    S.dma("pool", w_in[:, kc, :], wsrc[:, kc, :], writes=[b_wk[kc]])
                S.dma("pool", w_uq[:], I["mla_w_uq"][l].rearrange("(kc p) n -> p kc n", p=128), writes=[b_w])
                S.dma("pool", w_ukv[:], I["mla_w_ukv"][l].rearrange("(kc p) n -> p kc n", p=128), writes=[b_w])
                S.dma("sp", gq_bc[:], I["mla_q_norm_g"][l, :].partition_broadcast(128), writes=[b_tab])
                S.dma("sp", gkv_bc[:], I["mla_kv_norm_g"][l, :].partition_broadcast(128), writes=[b_tab])
                for h in range(6):
                    src = I["gqa_q_norm_g"] if h < 4 else I["gqa_k_norm_g"]
                    S.dma("sp", gC_bc[:, h, :], src[l, :].partition_broadcast(128), writes=[b_tab])
                V(lambda: nc.vector.tensor_scalar(out=gC_bc[:, 0:4, :], in0=gC_bc[:, 0:4, :], scalar1=SC_C, scalar2=None,
                                                  op0=ALU.mult), [b_tab], [b_tab])
                S.dma("sp", ropeB[:], I["ropeB"], writes=[b_tab])
                S.dma("sp", ropeC[:], I["ropeC"], writes=[b_tab])
                for i_ in range(2):
                    for n_ in ('VA', 'VB', 'VC', 'VD'):
                        vt = {'VA': VA_l, 'VB': VB_l, 'VC': VC_l, 'VD': VD_l}[n_][i_]
                        V(lambda: nc.vector.memset(vt[:], 1.0), [], [BL[n_][i_]])

                groups = [(0, 512), (512, 512), (1024, 416), (1440, 512), (1952, 512), (2464, 256)]
                def stage1(sg):
                    T, s = sg // 4, sg % 4
                    tsl = slice(s * 128, (s + 1) * 128)
                    h_t, bhk = hT[T % 2], b_hk[T % 2]
                    stg, bstg = stage[T % 2], b_stage[T % 2]
                    bf32 = bf32_l[sg % 2]
                    b_bf32 = BL['bf32'][sg % 2]
                    cqk = cqk_l[sg % 2]
                    b_cqk = BL['cqk'][sg % 2]
                    junk = junk_l[sg % 2]
                    b_junk = BL['junk'][sg % 2]
                    qkA = qkA_l[sg % 2]
                    b_qkA = BL['qkA'][sg % 2]
                    qkD = qkD_l[sg % 2]
                    b_qkD = BL['qkD'][sg % 2]
                    VA = VA_l[sg % 2]
                    b_VA = BL['VA'][sg % 2]
                    VB = VB_l[sg % 2]
                    b_VB = BL['VB'][sg % 2]
                    VC = VC_l[sg % 2]
                    b_VC = BL['VC'][sg % 2]
                    VD = VD_l[sg % 2]
                    b_VD = BL['VD'][sg % 2]
                    cqn = cqn_l[sg % 2]
                    b_cqn = BL['cqn'][sg % 2]
                    cT = cT_l[sg % 2]
                    b_cT = BL['cT'][sg % 2]
                    qB = qB_l[sg % 2]
                    b_qB = BL['qB'][sg % 2]
                    kB = kB_l[sg % 2]
                    b_kB = BL['kB'][sg % 2]
                    cC = cC_l[sg % 2]
                    b_cC = BL['cC'][sg % 2]
                    t1 = t1_l[sg % 2]
                    b_t1 = BL['t1'][sg % 2]
                    t2 = t2_l[sg % 2]
                    b_t2 = BL['t2'][sg % 2]
                    nq = nq_l[sg % 2]
                    b_nq = BL['nq'][sg % 2]
                    stt_ = stt__l[sg % 2]
                    b_st = BL['stt_'][sg % 2]
                    x_t, bx = xs[sg % 2], b_xs[sg % 2]
                    S.dma("sp", x_t[:], xsrc[sg * 128:(sg + 1) * 128, :], writes=[bx])
                    for hf in range(2):
                        for cc in range(4):
                            kc = hf * 4 + cc
                            PE(lambda: nc.tensor.transpose(pb[hf][:, cc * 128:(cc + 1) * 128], x_t[:, kc * 128:(kc + 1) * 128],
                                                           ident_f[:]), [bx, b_const], [bp[hf]], signal=(cc == 3))
                        for cc in range(4):
                            kc = hf * 4 + cc
                            if cc % 2 == 0:
                                A(lambda: nc.scalar.activation(out=h_t[:, kc, tsl], in_=pb[hf][:, cc * 128:(cc + 1) * 128],
                                                               func=AF.Identity, scale=modT[:, l, 8 + kc:9 + kc],
                                                               bias=modT[:, l, kc:kc + 1]), [bp[hf], b_modT], [bhk[kc]])
                            else:
                                V(lambda: nc.vector.tensor_scalar(out=h_t[:, kc, tsl], in0=pb[hf][:, cc * 128:(cc + 1) * 128],
                                                                  scalar1=modT[:, l, 8 + kc:9 + kc],
                                                                  scalar2=modT[:, l, kc:kc + 1], op0=ALU.mult, op1=ALU.add),
                                  [bp[hf], b_modT], [bhk[kc]])
                    yield
                    for gi, (c0, ncol) in enumerate(groups):
                        if gi > 0:
                            yield
                        bk = 2 + gi % 3
                        for kc in range(8):
                            PE(lambda: nc.tensor.matmul(pb[bk][:, 0:ncol], lhsT=h_t[:, kc, tsl], rhs=w_in[:, kc, c0:c0 + ncol],
                                                        start=(kc == 0), stop=(kc == 7)), [bhk[kc], b_wk[kc]], [bp[bk]], signal=(kc == 7))
                        P_ = pb[bk]
                        if gi == 0:
                            A(lambda: nc.scalar.activation(out=qkA[:, 0:256], in_=P_[:, 0:256], func=AF.Identity, scale=SC_A),
                              [bp[bk]], [b_qkA])
                            V(lambda: nc.vector.tensor_copy(qkA[:, 256:512], P_[:, 256:512]), [bp[bk]], [b_qkA])
                        elif gi == 1:
                            A(lambda: nc.scalar.activation(func=AF.Identity, out=VA[:, :, 0:64], in_=P_[:, 0:256].rearrange("p (h d) -> p h d", d=64)),
                              [bp[bk]], [b_VA])
                            V(lambda: nc.vector.tensor_copy(bf32[:, 0:256], P_[:, 256:512]), [bp[bk]], [b_bf32])
                        elif gi == 2:
                            V(lambda: nc.vector.tensor_copy(bf32[:, 256:672], P_[:, 0:416]), [bp[bk]], [b_bf32])
                        elif gi == 3:
                            V(lambda: nc.vector.tensor_copy(cqk[:, :], P_[:, 0:384]), [bp[bk]], [b_cqk])
                            A(lambda: nc.scalar.activation(func=AF.Identity, out=VC[:, 0:2, 0:64],
                                                     in_=P_[:, 384:512].rearrange("p (h d) -> p h d", d=64)),
                              [bp[bk]], [b_VC])
                        elif gi == 4:
                            A(lambda: nc.scalar.activation(out=qkD[:, 0:256], in_=P_[:, 0:256], func=AF.Identity, scale=SC_D),
                              [bp[bk]], [b_qkD])
                            V(lambda: nc.vector.tensor_copy(qkD[:, 256:512], P_[:, 256:512]), [bp[bk]], [b_qkD])
                        else:
                            A(lambda: nc.scalar.activation(func=AF.Identity, out=VD[:, :, 0:64], in_=P_[:, 0:256].rearrange("p (h d) -> p h d", d=64)),
                              [bp[bk]], [b_VD])

                def stage2(sg):
                    T, s = sg // 4, sg % 4
                    tsl = slice(s * 128, (s + 1) * 128)
                    h_t, bhk = hT[T % 2], b_hk[T % 2]
                    stg, bstg = stage[T % 2], b_stage[T % 2]
                    bf32 = bf32_l[sg % 2]
                    b_bf32 = BL['bf32'][sg % 2]
                    cqk = cqk_l[sg % 2]
                    b_cqk = BL['cqk'][sg % 2]
                    junk = junk_l[sg % 2]
                    b_junk = BL['junk'][sg % 2]
                    qkA = qkA_l[sg % 2]
                    b_qkA = BL['qkA'][sg % 2]
                    qkD = qkD_l[sg % 2]
                    b_qkD = BL['qkD'][sg % 2]
                    VA = VA_l[sg % 2]
                    b_VA = BL['VA'][sg % 2]
                    VB = VB_l[sg % 2]
                    b_VB = BL['VB'][sg % 2]
                    VC = VC_l[sg % 2]
                    b_VC = BL['VC'][sg % 2]
                    VD = VD_l[sg % 2]
                    b_VD = BL['VD'][sg % 2]
                    cqn = cqn_l[sg % 2]
                    b_cqn = BL['cqn'][sg % 2]
                    cT = cT_l[sg % 2]
                    b_cT = BL['cT'][sg % 2]
                    qB = qB_l[sg % 2]
                    b_qB = BL['qB'][sg % 2]
                    kB = kB_l[sg % 2]
                    b_kB = BL['kB'][sg % 2]
                    cC = cC_l[sg % 2]
                    b_cC = BL['cC'][sg % 2]
                    t1 = t1_l[sg % 2]
                    b_t1 = BL['t1'][sg % 2]
                    t2 = t2_l[sg % 2]
                    b_t2 = BL['t2'][sg % 2]
                    nq = nq_l[sg % 2]
                    b_nq = BL['nq'][sg % 2]
                    stt_ = stt__l[sg % 2]
                    b_st = BL['stt_'][sg % 2]
                    V(lambda: nc.vector.scalar_tensor_tensor(out=junk[:, 0:384], in0=bf32[:, 0:384], scalar=1.0,
                                                             in1=bf32[:, 0:384], op0=ALU.mult, op1=ALU.mult,
                                                             accum_out=stt_[:, 0:1]), [b_bf32], [b_junk, b_st])
                    V(lambda: nc.vector.scalar_tensor_tensor(out=junk[:, 0:256], in0=bf32[:, 384:640], scalar=1.0,
                                                             in1=bf32[:, 384:640], op0=ALU.mult, op1=ALU.mult,
                                                             accum_out=stt_[:, 1:2]), [b_bf32], [b_junk, b_st])
                    A(lambda: nc.scalar.activation(out=stt_[:, 2:3], in_=stt_[:, 0:1], func=AF.Ln, scale=1.0 / 384,
                                                   bias=eps6[:, 0:1]), [b_st, b_const], [b_st])
                    A(lambda: nc.scalar.activation(out=stt_[:, 3:4], in_=stt_[:, 1:2], func=AF.Ln, scale=1.0 / 256,
                                                   bias=eps6[:, 0:1]), [b_st, b_const], [b_st])
                    A(lambda: nc.scalar.activation(out=stt_[:, 4:6], in_=stt_[:, 2:4], func=AF.Exp, scale=-0.5), [b_st], [b_st])
                    V(lambda: nc.vector.scalar_tensor_tensor(out=cqn[:, 0:384], in0=bf32[:, 0:384], scalar=stt_[:, 4:5],
                                                             in1=gq_bc[:, :], op0=ALU.mult, op1=ALU.mult),
                      [b_bf32, b_st, b_tab], [b_cqn])
                    V(lambda: nc.vector.scalar_tensor_tensor(out=cqn[:, 384:640], in0=bf32[:, 384:640], scalar=stt_[:, 5:6],
                                                             in1=gkv_bc[:, :], op0=ALU.mult, op1=ALU.mult),
                      [b_bf32, b_st, b_tab], [b_cqn])
                    yield
                    p5b = pb[5][:, :].bitcast(BF16)
                    for j in range(5):
                        PE(lambda: nc.tensor.transpose(p5b[:, j * 128:(j + 1) * 128], cqn[:, j * 128:(j + 1) * 128], ident_b[:]),
                           [b_cqn, b_const], [bp[5]], signal=(j == 4))
                    V(lambda: nc.vector.tensor_copy(cT[:, :, :], p5b[:, 0:640].rearrange("p (c t) -> p c t", t=128)),
                      [bp[5]], [b_cT])
                    for j in range(3):
                        PE(lambda: nc.tensor.matmul(pb[6][:, 0:384], lhsT=cT[:, j, :], rhs=w_uq[:, j, :], start=(j == 0),
                                                    stop=(j == 2)), [b_cT, b_w], [bp[6]], signal=(j == 2))
                    for j in range(2):
                        PE(lambda: nc.tensor.matmul(pb[7][:, 0:512], lhsT=cT[:, 3 + j, :], rhs=w_ukv[:, j, :], start=(j == 0),
                                                    stop=(j == 1)), [b_cT, b_w], [bp[7]], signal=(j == 1))
                    yield
                    q3 = pb[6][:, 0:384].rearrange("p (h d) -> p h d", d=96)
                    kv3 = pb[7][:, 0:512].rearrange("p (h d) -> p h d", d=128)
                    A(lambda: nc.scalar.activation(out=qB[:, :, 0:64], in_=q3[:, :, 0:64], func=AF.Identity, scale=SC_B),
                      [bp[6]], [b_qB])
                    Tq = ropeB[:, sg, 0, :]
                    Tk = ropeB[:, sg, 1, :]
                    t1q = t1[:, 0:128].rearrange("p (h d) -> p h d", d=32)
                    t2q = t2[:, 0:128].rearrange("p (h d) -> p h d", d=32)
                    V(lambda: nc.vector.tensor_tensor(out=t1q, in0=q3[:, :, 64:96],
                                                      in1=Tq[:, 0:32].unsqueeze(1).to_broadcast([128, 4, 32]), op=ALU.mult),
                      [bp[6], b_tab], [b_t1])
                    V(lambda: nc.vector.tensor_tensor(out=t2q[:, :, 0:16], in0=q3[:, :, 80:96],
                                                      in1=Tq[:, 32:48].unsqueeze(1).to_broadcast([128, 4, 16]), op=ALU.mult),
                      [bp[6], b_tab], [b_t2])
                    V(lambda: nc.vector.tensor_tensor(out=t2q[:, :, 16:32], in0=q3[:, :, 64:80],
                                                      in1=Tq[:, 48:64].unsqueeze(1).to_broadcast([128, 4, 16]), op=ALU.mult),
                      [bp[6], b_tab], [b_t2])
                    G(lambda: nc.gpsimd.tensor_tensor(out=qB[:, :, 64:96], in0=t1q, in1=t2q, op=ALU.add), [b_t1, b_t2], [b_qB])
                    V(lambda: nc.vector.tensor_copy(kB[:, :, 0:64], kv3[:, :, 0:64]), [bp[7]], [b_kB])
                    A(lambda: nc.scalar.activation(func=AF.Identity, out=VB[:, :, 0:64], in_=kv3[:, :, 64:128]), [bp[7]], [b_VB])
                    yield
                    kr = bf32[:, 640:672]
                    V(lambda: nc.vector.tensor_tensor(out=t1[:, 128:160], in0=kr, in1=Tk[:, 0:32], op=ALU.mult),
                      [b_bf32, b_tab], [b_t1])
                    V(lambda: nc.vector.tensor_tensor(out=t2[:, 128:144], in0=bf32[:, 656:672], in1=Tk[:, 32:48], op=ALU.mult),
                      [b_bf32, b_tab], [b_t2])
                    V(lambda: nc.vector.tensor_tensor(out=t2[:, 144:160], in0=bf32[:, 640:656], in1=Tk[:, 48:64], op=ALU.mult),
                      [b_bf32, b_tab], [b_t2])
                    G(lambda: nc.gpsimd.tensor_tensor(out=kB[:, :, 64:96],
                                                      in0=t1[:, 128:160].unsqueeze(1).to_broadcast([128, 4, 32]),
                                                      in1=t2[:, 128:160].unsqueeze(1).to_broadcast([128, 4, 32]), op=ALU.add),
                      [b_t1, b_t2], [b_kB])
                    yield
                    c3 = cqk[:, :].rearrange("p (h d) -> p h d", d=64)
                    V(lambda: nc.vector.tensor_tensor(out=junk[:, :], in0=cqk[:, :], in1=cqk[:, :], op=ALU.mult),
                      [b_cqk], [b_junk])
                    V(lambda: nc.vector.tensor_reduce(out=stt_[:, 6:12], in_=junk[:, :].rearrange("p (h d) -> p h d", d=64),
                                                      axis=AX.X, op=ALU.add), [b_junk], [b_st])
                    A(lambda: nc.scalar.activation(out=stt_[:, 6:12], in_=stt_[:, 6:12], func=AF.Ln, scale=1.0 / 64,
                                                   bias=eps6[:, 0:1]), [b_st, b_const], [b_st])
                    A(lambda: nc.scalar.activation(out=stt_[:, 6:12], in_=stt_[:, 6:12], func=AF.Exp, scale=-0.5), [b_st], [b_st])
                    yield
                    n3 = nq[:, :].rearrange("p (h d) -> p h d", d=64)
                    V(lambda: nc.vector.tensor_tensor(out=n3, in0=c3, in1=stt_[:, 6:12].unsqueeze(2).to_broadcast([128, 6, 64]),
                                                      op=ALU.mult), [b_cqk, b_st], [b_nq])
                    G(lambda: nc.gpsimd.tensor_tensor(out=n3, in0=n3, in1=gC_bc[:, :, :], op=ALU.mult), [b_nq, b_tab], [b_nq])
                    yield
                    cosA = ropeC[:, sg, 0:64]
                    sinS = ropeC[:, sg, 64:128].rearrange("p (r f i) -> p r f i", r=2, f=2)
                    V(lambda: nc.vector.tensor_tensor(out=t1[:, :].rearrange("p (h d) -> p h d", d=64), in0=n3,
                                                      in1=cosA.unsqueeze(1).to_broadcast([128, 6, 64]), op=ALU.mult),
                      [b_nq, b_tab], [b_t1])
                    n5 = nq[:, :].rearrange("p (h r f i) -> p h r f i", h=6, r=2, f=2)
                    t5 = t2[:, :].rearrange("p (h r f i) -> p h r f i", h=6, r=2, f=2)
                    for f in range(2):
                        V(lambda: nc.vector.tensor_tensor(out=t5[:, :, :, f, :], in0=n5[:, :, :, 1 - f, :],
                                                          in1=sinS[:, :, f, :].unsqueeze(1).to_broadcast([128, 6, 2, 16]),
                                                          op=ALU.mult), [b_nq, b_tab], [b_t2])
                    G(lambda: nc.gpsimd.tensor_tensor(out=cC[:, :], in0=t1[:, :], in1=t2[:, :], op=ALU.add), [b_t1, b_t2], [b_cC])
                    yield
                    tb0 = pb[6][:, :].bitcast(BF16)
                    tb1 = pb[7][:, :].bitcast(BF16)
                    for j in range(4):
                        PE(lambda: nc.tensor.transpose(tb0[:, j * 128:(j + 1) * 128], qkA[:, j * 128:(j + 1) * 128], ident_b[:]),
                           [b_qkA, b_const], [bp[6]], signal=False)
                    for h in range(4):
                        PE(lambda: nc.tensor.transpose(tb0[0:96, (4 + h) * 128:(5 + h) * 128], qB[:, h, :], ident_b[:]),
                           [b_qB, b_const], [bp[6]], signal=(h == 3))
                    for h in range(4):
                        PE(lambda: nc.tensor.transpose(tb1[0:96, h * 128:(h + 1) * 128], kB[:, h, :], ident_b[:]),
                           [b_kB, b_const], [bp[7]], signal=False)
                    for j in range(3):
                        PE(lambda: nc.tensor.transpose(tb1[:, (4 + j) * 128:(5 + j) * 128], cC[:, j * 128:(j + 1) * 128],
                                                       ident_b[:]), [b_cC, b_const], [bp[7]], signal=(j == 2))
                    V(lambda: nc.vector.tensor_copy(stg[:, 0:8, tsl], tb0[:, 0:1024].rearrange("p (c t) -> p c t", t=128)),
                      [bp[6]], [bstg])
                    A(lambda: nc.scalar.activation(func=AF.Identity, out=stg[:, 8:15, tsl], in_=tb1[:, 0:896].rearrange("p (c t) -> p c t", t=128)),
                      [bp[7]], [bstg])
                    yield
                    rows = slice(sg * 128, (sg + 1) * 128)
                    S.dma("sp", VG[0, rows, :], VA[:, :, :].rearrange("p h d -> p (h d)"), reads=[b_VA])
                    S.dma("sp", VG[1, rows, :], VB[:, :, :].rearrange("p h d -> p (h d)"), reads=[b_VB])
                    S.dma("sp", VG[2, rows, :], VC[:, :, :].rearrange("p h d -> p (h d)"), reads=[b_VC])
                    S.dma("sp", DTOK[rows, 0:512], qkD[:, :], reads=[b_qkD])
                    S.dma("sp", DTOK[rows, 512:512 + VW], VD[:, :, :].rearrange("p h d -> p (h d)"), reads=[b_VD])
                    if s == 3:
                        for (ca, cb) in [(0, 4), (4, 8), (8, 12), (12, 15)]:
                            S.dma("sp", QKT[ca:cb, :, T * 512:(T + 1) * 512].rearrange("c r t -> r c t"), stg[:, ca:cb, :],
                                  reads=[bstg])

                nsub_ = NSUB if 'nsub' not in DBG else DBG['nsub']
                for _ in stage1(0):
                    pass
                for sg in range(nsub_):
                    gens = [stage2(sg)] + ([stage1(sg + 1)] if sg + 1 < nsub_ else [])
                    while gens:
                        for g_ in list(gens):
                            try:
                                next(g_)
                            except StopIteration:
                                gens.remove(g_)
                S.barrier()

        def phase_att(l, y_res, b_y):
            with ExitStack() as st:
                qT = [sb(st, f"qT{i}", [128, SEQ], BF16) for i in range(4)]
                kT = [sb(st, f"kT{i}", [128, SEQ], BF16) for i in range(4)]
                Vg = sb(st, "Vg", [128, NSUB, VW + 64], BF16)
                alibi = sb(st, "alibi", [128, 5, 512], F32)
                Sb = [sb(st, f"Sb{i}", [128, 512], F32) for i in range(3)]
                E = [sb(st, f"E{i}", [128, 512], BF16) for i in range(4)]
                lam_t = sb(st, "lam_t", [128, 128], F32)
                lam = sb(st, "lam", [128, 8], F32)
                gA_bc = sb(st, "gA_bc", [128, 64], F32)
                o1 = sb(st, "o1", [128, 4, 64], F32)
                o2 = sb(st, "o2", [128, 4, 64], F32)
                osq = sb(st, "osq", [128, 4, 64], F32)
                rc = sb(st, "rc", [128, 16], F32)
                pS = [pbank(st, f"pS{i}") for i in range(4)]
                pOT = [pbank(st, f"pOT{i}") for i in range(2)]
                pO = [pbank(st, f"pO{i}") for i in range(2)]
                otS = [sb(st, f"otS{i}", [65, 512], F32) for i in range(2)]
                b_pOT = [PB(), PB()]
                b_otS = [Buf(), Buf()]
                b_qT = [Buf() for _ in range(4)]
                b_kT = [Buf() for _ in range(4)]
                b_Vg, b_al, b_lam, b_gA = Buf(), Buf(), Buf(), Buf()
                b_Sb = [Buf() for _ in range(3)]
                b_E = [Buf() for _ in range(4)]
                b_pS = [PB() for _ in range(4)]
                b_pO = [PB() for _ in range(4)]
                b_o1, b_o2, b_osq, b_rc = Buf(), Buf(), Buf(), Buf()
                V(lambda: nc.vector.memset(Vg[:, :, VW:VW + 64], 0.0), [], [b_Vg])
                S.dma("sp", alibi[:], I["alibi"], writes=[b_al])
                lam_init = 0.8 - 0.6 * math.exp(-0.3 * l)
                S.dma("sp", lam_t[:], I["diff_lambda"][l, :].partition_broadcast(128), writes=[b_lam])
                V(lambda: nc.vector.scalar_tensor_tensor(out=lam_t[:, 0:32], in0=lam_t[:, 0:32], scalar=1.0, in1=lam_t[:, 32:64],
                                                         op0=ALU.mult, op1=ALU.mult, accum_out=lam[:, 0:1]), [b_lam], [b_lam])
                V(lambda: nc.vector.scalar_tensor_tensor(out=lam_t[:, 64:96], in0=lam_t[:, 64:96], scalar=1.0,
                                                         in1=lam_t[:, 96:128], op0=ALU.mult, op1=ALU.mult,
                                                         accum_out=lam[:, 1:2]), [b_lam], [b_lam])
                A(lambda: nc.scalar.activation(out=lam[:, 2:4], in_=lam[:, 0:2], func=AF.Exp), [b_lam], [b_lam])
                V(lambda: nc.vector.tensor_tensor(out=lam[:, 4:5], in0=lam[:, 2:3], in1=lam[:, 3:4], op=ALU.subtract),
                  [b_lam], [b_lam])
                V(lambda: nc.vector.tensor_scalar(out=lam[:, 5:6], in0=lam[:, 4:5], scalar1=lam_init, scalar2=-1.0,
                                                  op0=ALU.add, op1=ALU.mult), [b_lam], [b_lam])
                S.dma("sp", gA_bc[:], I["diff_subln_g"][l, :].partition_broadcast(128), writes=[b_gA])
                V(lambda: nc.vector.tensor_scalar(out=gA_bc[:], in0=gA_bc[:], scalar1=1.0 - lam_init, scalar2=None,
                                                  op0=ALU.mult), [b_gA], [b_gA])

                state = {"qi": 0, "ei": 0, "si": 0, "sbi": 0}

                def load_map(chunk, r0, nrows, isq):
                    i = state["qi"] % 4
                    t, b = (qT[i], b_qT[i]) if isq else (kT[i], b_kT[i])
                    S.dma("sp", t[0:nrows, :], QKT[chunk, r0:r0 + nrows, :], writes=[b])
                    return t, b

                def load_V(g):
                    for a in range(4):
                        S.dma("sp", Vg[:, a * 8:(a + 1) * 8, 0:VW],
                              VG[g, a * 1024:(a + 1) * 1024, :].rearrange("(t p) c -> p t c", p=128), writes=[b_Vg])

                def job(maps, vcol, ycol, alibi_slope=None):
                    nm = len(maps)
                    pend = [None]
                    for qt in range(8):
                        if nm == 2:
                            groups = [[(kt, 0), (kt, 1)] for kt in range(NSUB)]
                        else:
                            groups = [[(2 * j, 0), (2 * j + 1, 0)] for j in range(NSUB // 2)]
                        ng = len(groups)
                        for gi in range(ng + 2):
                            if gi < ng:
                                banks = [2 * (gi % 2), 2 * (gi % 2) + 1]
                                S.prewait("pe", [mp[1] for mp in maps] + [mp[3] for mp in maps], [b_pS[bk] for bk in banks])
                                for idx, (kt, m) in enumerate(groups[gi]):
                                    q_t, bq, k_t, bk_, nr = maps[m][:5]
                                    p0 = maps[m][5] if len(maps[m]) > 5 else 0
                                    si = banks[idx]
                                    tp_ = (p0, 0) if p0 == 96 else None
                                    PE(lambda: nc.tensor.matmul(pS[si][:, :], lhsT=k_t[p0:p0 + nr, kt * 128:(kt + 1) * 128],
                                                                rhs=q_t[p0:p0 + nr, qt * 512:(qt + 1) * 512], start=True, stop=True,
                                                                tile_position=tp_),
                                       [bq, bk_], [b_pS[si]], signal=(idx == 1))
                            j = gi - 1
                            if 0 <= j < ng:
                                for idx, (kt, m) in enumerate(groups[j]):
                                    si = 2 * (j % 2) + idx
                                    ei = 2 * (j % 2) + idx
                                    if alibi_slope is None:
                                        A(lambda: nc.scalar.activation(out=E[ei][:, :], in_=pS[si][:, :], func=AF.Exp),
                                          [b_pS[si]], [b_E[ei]])
                                    else:
                                        k0, q0 = kt * 128, qt * 512
                                        if k0 + 128 <= q0:
                                            tab, sc_, bias = alibi[:, 0, :], -alibi_slope, -alibi_slope * (q0 - k0)
                                        elif k0 >= q0 + 512:
                                            tab, sc_, bias = alibi[:, 0, :], alibi_slope, -alibi_slope * (k0 - q0)
                                        else:
                                            tab, sc_, bias = alibi[:, 1 + (k0 - q0) // 128, :], -alibi_slope, 0.0
                                        sbi = state["sbi"] % 3
                                        state["sbi"] += 1
                                        V(lambda: nc.vector.scalar_tensor_tensor(out=Sb[sbi][:, :], in0=tab, scalar=sc_,
                                                                                 in1=pS[si][:, :], op0=ALU.mult, op1=ALU.add),
                                          [b_al, b_pS[si]], [b_Sb[sbi]])
                                        A(lambda: nc.scalar.activation(out=E[ei][:, :], in_=Sb[sbi][:, :], func=AF.Exp, bias=bias),
                                          [b_Sb[sbi]], [b_E[ei]])
                            j = gi - 2
                            if 0 <= j < ng:
                                eis_ = [2 * (j % 2), 2 * (j % 2) + 1]
                                S.prewait("pe", [b_E[e] for e in eis_] + [b_Vg], [b_pOT[m] for (_, m) in groups[j]])
                                for idx, (kt, m) in enumerate(groups[j]):
                                    ei = eis_[idx]
                                    PE(lambda: nc.tensor.matmul(pOT[m][:, :], lhsT=Vg[:, kt, vcol:vcol + 128], rhs=E[ei][:, :],
                                                                start=(kt == 0), stop=(kt == NSUB - 1)),
                                       [b_E[ei], b_Vg], [b_pOT[m]], signal=(idx == 1))
                            if gi == 4 and pend[0] is not None:
                                pend[0]()
                                pend[0] = None
                        for m in range(nm):
                            if alibi_slope is not None:
                                A(lambda: nc.scalar.activation(out=otS[m][:, :], in_=pOT[m][0:65, :], func=AF.Identity),
                                  [b_pOT[m]], [b_otS[m]])
                            else:
                                V(lambda: nc.vector.tensor_copy(otS[m][:, :], pOT[m][0:65, :]), [b_pOT[m]], [b_otS[m]])
                        pend[0] = (lambda qt=qt: finalize(qt, nm, ycol))
                    pend[0]()
                    pend[0] = None

                def finalize(qt, nm, ycol):
                    if True:
                        for m in range(nm):
                            for jj in range(4):
                                PE(lambda: nc.tensor.transpose(pO[m][:, jj * 65:(jj + 1) * 65], otS[m][0:65, jj * 128:(jj + 1) * 128],
                                                               ident_f[0:65, 0:65]), [b_otS[m], b_const], [b_pO[m]], signal=(jj == 3))
                        O1 = pO[0][:, 0:260].rearrange("p (j d) -> p j d", d=65)
                        ydst = y_res[:, qt * 4:(qt + 1) * 4, ycol:ycol + 64]
                        by = b_y[qt * 4:(qt + 1) * 4]
                        V(lambda: nc.vector.reciprocal(out=rc[:, 0:4], in_=O1[:, :, 64]), [b_pO[0]], [b_rc])
                        if nm == 1:
                            V(lambda: nc.vector.tensor_tensor(out=ydst, in0=O1[:, :, 0:64],
                                                              in1=rc[:, 0:4].unsqueeze(2).to_broadcast([128, 4, 64]),
                                                              op=ALU.mult), [b_pO[0], b_rc], by)
                        else:
                            O2 = pO[1][:, 0:260].rearrange("p (j d) -> p j d", d=65)
                            V(lambda: nc.vector.reciprocal(out=rc[:, 4:8], in_=O2[:, :, 64]), [b_pO[1]], [b_rc])
                            V(lambda: nc.vector.tensor_scalar(out=rc[:, 4:8], in0=rc[:, 4:8], scalar1=lam[:, 5:6], scalar2=None,
                                                              op0=ALU.mult), [b_rc, b_lam], [b_rc])
                            V(lambda: nc.vector.tensor_tensor(out=o1[:, :, :], in0=O1[:, :, 0:64],
                                                              in1=rc[:, 0:4].unsqueeze(2).to_broadcast([128, 4, 64]),
                                                              op=ALU.mult), [b_pO[0], b_rc], [b_o1])
                            V(lambda: nc.vector.tensor_tensor(out=o2[:, :, :], in0=O2[:, :, 0:64],
                                                              in1=rc[:, 4:8].unsqueeze(2).to_broadcast([128, 4, 64]),
                                                              op=ALU.mult), [b_pO[1], b_rc], [b_o2])
                            G(lambda: nc.gpsimd.tensor_tensor(out=o1[:, :, :], in0=o1[:, :, :], in1=o2[:, :, :], op=ALU.add),
                              [b_o1, b_o2], [b_o1])
                            G(lambda: nc.gpsimd.tensor_tensor(out=osq[:, :, :], in0=o1[:, :, :], in1=o1[:, :, :], op=ALU.mult),
                              [b_o1], [b_osq])
                            V(lambda: nc.vector.tensor_reduce(out=rc[:, 8:12], in_=osq[:, :, :], axis=AX.X, op=ALU.add),
                              [b_osq], [b_rc])
                            A(lambda: nc.scalar.activation(out=rc[:, 8:12], in_=rc[:, 8:12], func=AF.Ln, scale=1.0 / 64,
                                                           bias=eps6[:, 0:1]), [b_rc, b_const], [b_rc])
                            A(lambda: nc.scalar.activation(out=rc[:, 12:16], in_=rc[:, 8:12], func=AF.Exp, scale=-0.5),
                              [b_rc], [b_rc])
                            V(lambda: nc.vector.tensor_tensor(out=o2[:, :, :], in0=o1[:, :, :],
                                                              in1=rc[:, 12:16].unsqueeze(2).to_broadcast([128, 4, 64]),
                                                              op=ALU.mult), [b_o1, b_rc], [b_o2])
                            V(lambda: nc.vector.tensor_tensor(out=ydst, in0=o2[:, :, :],
                                                              in1=gA_bc[:, :].unsqueeze(1).to_broadcast([128, 4, 64]),
                                                              op=ALU.mult), [b_o2, b_gA], by)

                load_V(0)
                for h in range(4):
                    state["qi"] += 1
                    i_ = state["qi"] % 4
                    g0 = 2 * (h % 2)
                    S.dma("sp", qT[i_][32 * g0:32 * g0 + 64, :], QKT[h // 2, 32 * g0:32 * g0 + 64, :], writes=[b_qT[i_]])
                    S.dma("sp", kT[i_][32 * g0:32 * g0 + 64, :], QKT[2 + h // 2, 32 * g0:32 * g0 + 64, :], writes=[b_kT[i_]])
                    maps = [(qT[i_], b_qT[i_], kT[i_], b_kT[i_], 32, 32 * (g0 + c_)) for c_ in range(2)]
                    job(maps, h * VP, h * 64, alibi_slope=SLOPES_A[h])
                load_V(1)
                for h in range(4):
                    state["qi"] += 1
                    q_t, bq = load_map(4 + h, 0, 96, True)
                    k_t, bk_ = load_map(8 + h, 0, 96, False)
                    job([(q_t, bq, k_t, bk_, 96)], h * VP, 256 + h * 64)
                load_V(2)
                for h in range(4):
                    state["qi"] += 1
                    q_t, bq = load_map(12 + h // 2, (h % 2) * 64, 64, True)
                    if h % 2 == 0:
                        k_t, bk_ = load_map(14, (h // 2) * 64, 64, False)
                    job([(q_t, bq, k_t, bk_, 64)], (h // 2) * VP, 512 + h * 64)
                S.barrier()

        def phase_D(l, y_res, b_y):
            with ExitStack() as st:
                tokD = sb(st, "tokD", [128, 8, 512], BF16)
                Vsh = sb(st, "Vsh", [128, 9, VW], BF16)
                QTd = sb(st, "QTd", [128, 2, 1024], BF16)
                KTd = sb(st, "KTd", [128, 2, 1024 + 128], BF16)
                dtab = sb(st, "dtab", [128, 2, 512], F32)
                Sb = [sb(st, f"dSb{i}", [128, 512], F32) for i in range(4)]
                E = [sb(st, f"dE{i}", [128, 512], BF16) for i in range(4)]
                ost = [sb(st, f"ost{i}", [128, 4, 260], F32) for i in range(2)]
                acc = sb(st, "acc", [128, 260], F32)
                od = [sb(st, f"od{i}", [128, 260], F32) for i in range(3)]
                rcd = sb(st, "rcd", [128, 4], F32)
                pT = [pbank(st, f"dT{i}") for i in range(2)]
                pS = [pbank(st, f"dS{i}") for i in range(4)]
                pO = [pbank(st, f"dO{i}") for i in range(2)]
                b_tok, b_Vsh, b_QT, b_KT, b_dt = Buf(), Buf(), Buf(), Buf(), Buf()
                b_Sb = [Buf() for _ in range(4)]
                b_E = [Buf() for _ in range(4)]
                b_ost = [Buf(), Buf()]
                b_pT = [PB(), PB()]
                b_pS = [PB() for _ in range(4)]
                b_pO = [PB(), PB()]
                b_acc, b_od, b_rcd = Buf(), [Buf() for _ in range(3)], Buf()
                S.dma("sp", dtab[:], I["dtab"], writes=[b_dt])
                cnt = {"s": 0, "e": 0, "o": 0}
                for bi, d in enumerate(DILS):
                    Ltot = SEQ // d
                    nseg = max(1, Ltot // 1024)
                    for r in range(d):
                        for seg in range(nseg):
                            Lc = min(Ltot, 1024)
                            i0 = seg * 1024
                            nt = Lc // 128
                            V(lambda: nc.vector.memset(KTd[:, :, :], 0.0), [], [b_KT])
                            V(lambda: nc.vector.memset(Vsh[:, :, :], 0.0), [], [b_Vsh])
                            base = r + d * i0
                            src = DTOK[base: base + d * (Lc - 1) + 1: d, :]
                            S.dma("sp", tokD[:, 0:nt, :], src[:, 0:512].rearrange("(t p) c -> p t c", p=128), writes=[b_tok])
                            lo = i0 - 64
                            hi = i0 + Lc + 64
                            lo_c, hi_c = max(lo, 0), min(hi, Ltot)
                            u0 = lo_c - lo
                            n_rows = hi_c - lo_c
                            pos = 0
                            while pos < n_rows:
                                u = u0 + pos
                                tj, pj = u // 128, u % 128
                                take = min(128 - pj, n_rows - pos)
                                if pj == 0 and take == 128:
                                    nfull = (n_rows - pos) // 128
                                    t_first = r + d * (lo_c + pos)
                                    srcv = DTOK[t_first: t_first + d * (128 * nfull - 1) + 1: d, 512:512 + VW]
                                    S.dma("sp", Vsh[:, tj:tj + nfull, :], srcv.rearrange("(t p) c -> p t c", p=128),
                                          writes=[b_Vsh])
                                    pos += 128 * nfull
                                else:
                                    t_first = r + d * (lo_c + pos)
                                    srcv = DTOK[t_first: t_first + d * (take - 1) + 1: d, 512:512 + VW]
                                    S.dma("sp", Vsh[pj:pj + take, tj, :], srcv, writes=[b_Vsh])
                                    pos += take
                            for t in range(nt):
                                pTb = pT[t % 2][:, :].bitcast(BF16)
                                for j in range(4):
                                    PE(lambda: nc.tensor.transpose(pTb[:, j * 128:(j + 1) * 128], tokD[:, t, j * 128:(j + 1) * 128],
                                                                   ident_b[:]), [b_tok, b_const], [b_pT[t % 2]], signal=(j == 3))
                                V(lambda: nc.vector.tensor_copy(QTd[:, :, t * 128:(t + 1) * 128],
                                                                pTb[:, 0:256].rearrange("p (c t) -> p c t", t=128)),
                                  [b_pT[t % 2]], [b_QT])
                                A(lambda: nc.scalar.activation(func=AF.Identity, out=KTd[:, :, 64 + t * 128: 64 + (t + 1) * 128],
                                                         in_=pTb[:, 256:512].rearrange("p (c t) -> p c t", t=128)),
                                  [b_pT[t % 2]], [b_KT])
                            if nseg > 1:
                                for side in range(2):
                                    hs = i0 - 64 if side == 0 else i0 + Lc
                                    if hs < 0 or hs >= Ltot:
                                        continue
                                    t_first = r + d * hs
                                    srck = DTOK[t_first: t_first + d * 63 + 1: d, 256:512]
                                    S.dma("sp", tokD[0:64, 0, 0:256], srck, writes=[b_tok])
                                    pTb = pT[0][:, :].bitcast(BF16)
                                    for j in range(2):
                                        PE(lambda: nc.tensor.transpose(pTb[:, j * 128: j * 128 + 64],
                                                                       tokD[0:64, 0, j * 128:(j + 1) * 128], ident_b[0:64, 0:64]),
                                           [b_tok, b_const], [b_pT[0]], signal=(j == 1))
                                    col = 0 if side == 0 else 64 + Lc
                                    V(lambda: nc.vector.tensor_copy(
                                        KTd[:, :, col:col + 64],
                                        pTb[:, 0:256].rearrange("p (c t) -> p c t", t=128)[:, :, 0:64]), [b_pT[0]], [b_KT])
                            steps = [(g0, h, ab) for g0 in range(0, nt, 4) for h in range(4) for ab in range(2)]
                            n = len(steps)
                            sis, eis = [0] * n, [0] * n
                            LS, LP = 1, 2
                            for i in range(n + LP):
                                if i < n:
                                    g0, h, ab = steps[i]
                                    ng = min(4, nt - g0)
                                    ch, pr = h // 2, (h % 2) * 64
                                    si = cnt["s"] % 4
                                    cnt["s"] += 1
                                    sis[i] = si
                                    for jj in range(ng):
                                        qc = (g0 + jj) * 128
                                        kc0 = qc + ab * 128
                                        PE(lambda: nc.tensor.matmul(pS[si][:, jj * 128:(jj + 1) * 128],
                                                                    lhsT=KTd[pr:pr + 64, ch, kc0:kc0 + 128],
                                                                    rhs=QTd[pr:pr + 64, ch, qc:qc + 128], start=True, stop=True),
                                           [b_KT, b_QT], [b_pS[si]], signal=(jj == ng - 1))
                                j = i - LS
                                if 0 <= j < n:
                                    g0, h, ab = steps[j]
                                    ng = min(4, nt - g0)
                                    w = ng * 128
                                    si = sis[j]
                                    ei = cnt["e"] % 4
                                    cnt["e"] += 1
                                    eis[j] = ei
                                    V(lambda: nc.vector.scalar_tensor_tensor(out=Sb[ei][:, 0:w], in0=dtab[:, ab, 0:w],
                                                                             scalar=-SLOPES_D[h] * d, in1=pS[si][:, 0:w],
                                                                             op0=ALU.mult, op1=ALU.add),
                                      [b_dt, b_pS[si]], [b_Sb[ei]])
                                    A(lambda: nc.scalar.activation(out=E[ei][:, 0:w], in_=Sb[ei][:, 0:w], func=AF.Exp),
                                      [b_Sb[ei]], [b_E[ei]])
                                j = i - LP
                                if 0 <= j < n:
                                    g0, h, ab = steps[j]
                                    ng = min(4, nt - g0)
                                    ei = eis[j]
                                    oi = (g0 // 4) % 2
                                    for jj in range(ng):
                                        PE(lambda: nc.tensor.matmul(pO[h % 2][:, jj * 65:(jj + 1) * 65],
                                                                    lhsT=E[ei][:, jj * 128:(jj + 1) * 128],
                                                                    rhs=Vsh[:, g0 + jj + ab, h * VP:h * VP + 65],
                                                                    start=(ab == 0 and jj == 0), stop=(ab == 1),
                                                                    skip_group_check=True),
                                           [b_E[ei], b_Vsh], [b_pO[h % 2]], signal=(jj == ng - 1))
                                    if ab == 1:
                                        V(lambda: nc.vector.tensor_copy(ost[oi][:, 0:ng, h * 65:(h + 1) * 65],
                                                                        pO[h % 2][:, 0:ng * 65].rearrange("p (j d) -> p j d", d=65)),
                                          [b_pO[h % 2]], [b_ost[oi]])
                                        if h == 3:
                                            for jj in range(ng):
                                                t_first = r + d * (i0 + (g0 + jj) * 128)
                                                S.dma("sp", OD[bi, t_first: t_first + d * 127 + 1: d, :], ost[oi][:, jj, :],
                                                      reads=[b_ost[oi]])
                S.barrier()
                for sg in range(NSUB):
                    rows = slice(sg * 128, (sg + 1) * 128)
                    for bi in range(3):
                        S.dma("sp", od[bi][:], OD[bi, rows, :], writes=[b_od[bi]])
                    V(lambda: nc.vector.tensor_tensor(out=acc[:], in0=od[0][:], in1=od[1][:], op=ALU.add),
                      [b_od[0], b_od[1]], [b_acc])
                    V(lambda: nc.vector.tensor_tensor(out=acc[:], in0=acc[:], in1=od[2][:], op=ALU.add), [b_acc, b_od[2]], [b_acc])
                    a3 = acc[:, :].rearrange("p (h d) -> p h d", d=65)
                    V(lambda: nc.vector.reciprocal(out=rcd[:, 0:4], in_=a3[:, :, 64]), [b_acc], [b_rcd])
                    V(lambda: nc.vector.tensor_tensor(out=y_res[:, sg, 768:1024].rearrange("p (h d) -> p h d", d=64),
                                                      in0=a3[:, :, 0:64], in1=rcd[:, 0:4].unsqueeze(2).to_broadcast([128, 4, 64]),
                                                      op=ALU.mult), [b_acc, b_rcd], [b_y[sg]])
                S.barrier()

        def layer_norm(v, bv, dst, bdst, g_bc, b_bc, btab, stats, mv, bst, eng2):
            v4 = v.rearrange("p (c f) -> p c f", f=256)
            for c_ in range(4):
                V(lambda: nc.vector.bn_stats(out=stats[:, c_, :], in_=v4[:, c_, :]), [bv], [bst])
            V(lambda: nc.vector.bn_aggr(out=mv[:, 0:2], in_=stats[:, :, :].rearrange("p c f -> p (c f)")), [bst], [bst])
            A(lambda: nc.scalar.activation(out=mv[:, 2:3], in_=mv[:, 1:2], func=AF.Ln, bias=eps5[:, 0:1]), [bst, b_const], [bst])
            A(lambda: nc.scalar.activation(out=mv[:, 3:4], in_=mv[:, 2:3], func=AF.Exp, scale=-0.5), [bst], [bst])
            V(lambda: nc.vector.scalar_tensor_tensor(out=mv[:, 4:5], in0=mv[:, 0:1], scalar=-1.0, in1=mv[:, 3:4], op0=ALU.mult,
                                                     op1=ALU.mult), [bst], [bst])
            A(lambda: nc.scalar.activation(out=v, in_=v, func=AF.Identity, scale=mv[:, 3:4], bias=mv[:, 4:5]), [bv, bst], [bv])
            S.op(eng2, lambda: (nc.gpsimd if eng2 == "pool" else nc.vector).tensor_tensor(out=v, in0=v, in1=g_bc, op=ALU.mult),
                 [bv, btab], [bv])
            S.op(eng2, lambda: (nc.gpsimd if eng2 == "pool" else nc.vector).tensor_tensor(out=dst, in0=v, in1=b_bc, op=ALU.add),
                 [bv, btab], [bdst])

        def phase_O1(l, xsrc, y_res, b_y):
            with ExitStack() as st:
                w_o = sb(st, "w_o", [128, 8, D], BF16)
                gA = sb(st, "gA", [128, D], F32)
                lng = sb(st, "lng", [128, D], F32)
                lnb = sb(st, "lnb", [128, D], F32)
                xs = [sb(st, f"oxs{i}", [128, D], F32) for i in range(2)]
                yT = [sb(st, f"yT{i}", [128, 8, 128], BF16) for i in range(2)]
                v = [sb(st, f"ov{i}", [128, D], F32) for i in range(2)]
                x1 = [sb(st, f"ox1{i}", [128, D], F32) for i in range(2)]
                h2 = [sb(st, f"oh2{i}", [128, 8, 512], BF16) for i in range(2)]
                stats_l = [sb(st, f"ostats{i}", [128, 4, 6], F32) for i in range(2)]
                mv_l = [sb(st, f"omv{i}", [128, 8], F32) for i in range(2)]
                pb = [pbank(st, f"po{i}") for i in range(8)]
                bp = [PB() for _ in range(8)]
                b_w, b_tab, b_stt_l = Buf(), Buf(), [Buf(), Buf()]
                b_xs, b_yT, b_v, b_x1, b_h2 = ([Buf(), Buf()] for _ in range(5))
                wsrc = I["w_o"][l].rearrange("(kc p) n -> p kc n", p=128)
                for kc in range(8):
                    S.dma("pool", w_o[:, kc, :], wsrc[:, kc, :], writes=[b_w])
                S.dma("sp", gA[:], GB[l, 0], writes=[b_tab])
                S.dma("sp", lng[:], I["ln_attn_g"][l, :].partition_broadcast(128), writes=[b_tab])
                S.dma("sp", lnb[:], I["ln_attn_b"][l, :].partition_broadcast(128), writes=[b_tab])
                def o1_sub(sg):
                    T, s = sg // 4, sg % 4
                    i2 = sg % 2
                    tsl = slice(s * 128, (s + 1) * 128)
                    S.dma("sp", xs[i2][:], xsrc[sg * 128:(sg + 1) * 128, :], writes=[b_xs[i2]])
                    tb = pb[0 + i2][:, :].bitcast(BF16)
                    for kc in range(8):
                        PE(lambda: nc.tensor.transpose(tb[:, kc * 128:(kc + 1) * 128], y_res[:, sg, kc * 128:(kc + 1) * 128],
                                                       ident_b[:]), [b_y[sg], b_const], [bp[i2]], signal=(kc == 7))
                    yield
                    V(lambda: nc.vector.tensor_copy(yT[i2][:, :, :], tb[:, 0:1024].rearrange("p (c t) -> p c t", t=128)),
                      [bp[i2]], [b_yT[i2]])
                    for hf in range(2):
                        yield
                        bk = 2 + 2 * i2 + hf
                        for kc in range(8):
                            PE(lambda: nc.tensor.matmul(pb[bk][:, :], lhsT=yT[i2][:, kc, :], rhs=w_o[:, kc, hf * 512:(hf + 1) * 512],
                                                        start=(kc == 0), stop=(kc == 7)), [b_yT[i2], b_w], [bp[bk]], signal=(kc == 7))
                        hs = slice(hf * 512, (hf + 1) * 512)
                        V(lambda: nc.vector.tensor_tensor(out=v[i2][:, hs], in0=pb[bk][:, :], in1=gA[:, hs], op=ALU.mult),
                          [bp[bk], b_tab], [b_v[i2]])
                    yield
                    V(lambda: nc.vector.scalar_tensor_tensor(out=v[i2][:, :], in0=xs[i2][:, :], scalar=ALPHA,
                                                             in1=v[i2][:, :], op0=ALU.mult, op1=ALU.add),
                      [b_xs[i2], b_v[i2]], [b_v[i2]])
                    yield
                    layer_norm(v[i2][:, :], b_v[i2], x1[i2][:, :], b_x1[i2], lng[:, :], lnb[:, :], b_tab, stats_l[i2], mv_l[i2], b_stt_l[i2], "pool")
                    yield
                    S.dma("sp", X1[sg * 128:(sg + 1) * 128, :], x1[i2][:], reads=[b_x1[i2]])
                    for hf in range(2):
                        yield
                        bk = 6 + hf
                        for cc in range(4):
                            kc = hf * 4 + cc
                            PE(lambda: nc.tensor.transpose(pb[bk][:, cc * 128:(cc + 1) * 128], x1[i2][:, kc * 128:(kc + 1) * 128],
                                                           ident_f[:]), [b_x1[i2], b_const], [bp[bk]], signal=(cc == 3))
                        for cc in range(4):
                            kc = hf * 4 + cc
                            if cc % 2 == 0:
                                A(lambda: nc.scalar.activation(out=h2[T % 2][:, kc, tsl], in_=pb[bk][:, cc * 128:(cc + 1) * 128],
                                                               func=AF.Identity, scale=modT[:, l, 32 + kc:33 + kc],
                                                               bias=modT[:, l, 24 + kc:25 + kc]), [bp[bk], b_modT], [b_h2[T % 2]])
                            else:
                                V(lambda: nc.vector.tensor_scalar(out=h2[T % 2][:, kc, tsl], in0=pb[bk][:, cc * 128:(cc + 1) * 128],
                                                                  scalar1=modT[:, l, 32 + kc:33 + kc],
                                                                  scalar2=modT[:, l, 24 + kc:25 + kc], op0=ALU.mult, op1=ALU.add),
                                  [bp[bk], b_modT], [b_h2[T % 2]])
                    if s == 3:
                        S.dma("sp", H2T[:, :, T * 512:(T + 1) * 512].rearrange("c r t -> r c t"), h2[T % 2][:, :, :],
                              reads=[b_h2[T % 2]])
                for sg0 in range(0, NSUB, 2):
                    gens = [o1_sub(sg0), o1_sub(sg0 + 1)]
                    while gens:
                        for g_ in list(gens):
                            try:
                                next(g_)
                            except StopIteration:
                                gens.remove(g_)
                S.barrier()

        def phase_O2(l, dst):
            with ExitStack() as st:
                w_up = sb(st, "w_up", [128, 8, HID], BF16)
                w_dn = sb(st, "w_dn", [128, 32, D], BF16)
                gM = sb(st, "gM", [128, D], F32)
                lng = sb(st, "lng2", [128, D], F32)
                lnb = sb(st, "lnb2", [128, D], F32)
                h2 = [sb(st, f"mh2{i}", [128, 8, 512], BF16) for i in range(1)]
                uT = sb(st, "uT", [128, 32, 512], BF16)
                rr = [sb(st, f"rr{i}", [128, 512], F32) for i in range(3)]
                x1 = [sb(st, f"mx1{i}", [128, D], F32) for i in range(2)]
                v = [sb(st, f"mv{i}", [128, D], F32) for i in range(2)]
                stats = sb(st, "mstats", [128, 4, 6], F32)
                mv = sb(st, "mmv", [128, 8], F32)
                pb = [pbank(st, f"pm{i}") for i in range(8)]
                bp = [PB() for _ in range(8)]
                b_wu, b_wd, b_tab, b_stt, b_uT = Buf(), Buf(), Buf(), Buf(), Buf()
                b_h2, b_x1, b_v = ([Buf(), Buf()] for _ in range(3))
                b_rr = [Buf() for _ in range(3)]
                usrc = I["w_up"][l].rearrange("(kc p) n -> p kc n", p=128)
                for kc in range(8):
                    S.dma("pool", w_up[:, kc, :], usrc[:, kc, :], writes=[b_wu])
                dsrc = I["w_down"][l].rearrange("(kc p) n -> p kc n", p=128)
                for k4 in range(8):
                    S.dma("pool", w_dn[:, k4 * 4:(k4 + 1) * 4, :], dsrc[:, k4 * 4:(k4 + 1) * 4, :], writes=[b_wd])
                S.dma("sp", gM[:], GB[l, 1], writes=[b_tab])
                S.dma("sp", lng[:], I["ln_mlp_g"][l, :].partition_broadcast(128), writes=[b_tab])
                S.dma("sp", lnb[:], I["ln_mlp_b"][l, :].partition_broadcast(128), writes=[b_tab])
                ri = 0
                for T in range(8):
                    h_t, bh = h2[0], b_h2[0]
                    S.dma("sp", h_t[:, :, :], H2T[:, :, T * 512:(T + 1) * 512].rearrange("c r t -> r c t"), writes=[bh])
                    for hc in range(32):
                        bk = hc % 4
                        for kc in range(8):
                            PE(lambda: nc.tensor.matmul(pb[bk][:, :], lhsT=w_up[:, kc, hc * 128:(hc + 1) * 128], rhs=h_t[:, kc, :],
                                                        start=(kc == 0), stop=(kc == 7)), [b_wu, bh], [bp[bk]], signal=(kc == 7))
                        r_, br = rr[ri % 3], b_rr[ri % 3]
                        ri += 1
                        A(lambda: nc.scalar.activation(out=r_[:, :], in_=pb[bk][:, :], func=AF.Relu), [bp[bk]], [br])
                        if hc % 2 == 0:
                            V(lambda: nc.vector.tensor_tensor(out=uT[:, hc, :], in0=r_[:, :], in1=r_[:, :], op=ALU.mult), [br], [b_uT])
                        else:
                            G(lambda: nc.gpsimd.tensor_tensor(out=uT[:, hc, :], in0=r_[:, :], in1=r_[:, :], op=ALU.mult), [br], [b_uT])
                    for s in range(4):
                        sg = T * 4 + s
                        i2 = sg % 2
                        rows = slice(sg * 128, (sg + 1) * 128)
                        S.dma("sp", x1[i2][:], X1[rows, :], writes=[b_x1[i2]])
                        for hf in range(2):
                            bk = 4 + 2 * i2 + hf
                            for hc in range(32):
                                PE(lambda: nc.tensor.matmul(pb[bk][:, :], lhsT=uT[:, hc, s * 128:(s + 1) * 128],
                                                            rhs=w_dn[:, hc, hf * 512:(hf + 1) * 512], start=(hc == 0), stop=(hc == 31)),
                                   [b_uT, b_wd], [bp[bk]], signal=(hc == 31))
                            hs = slice(hf * 512, (hf + 1) * 512)
                            V(lambda: nc.vector.tensor_tensor(out=v[i2][:, hs], in0=pb[bk][:, :], in1=gM[:, hs], op=ALU.mult),
                              [bp[bk], b_tab], [b_v[i2]])
                        V(lambda: nc.vector.scalar_tensor_tensor(out=v[i2][:, :], in0=x1[i2][:, :], scalar=ALPHA, in1=v[i2][:, :],
                                                                 op0=ALU.mult, op1=ALU.add), [b_x1[i2], b_v[i2]], [b_v[i2]])
                        layer_norm(v[i2][:, :], b_v[i2], v[i2][:, :], b_v[i2], lng[:, :], lnb[:, :], b_tab, stats, mv, b_stt, "pool")
                        S.dma("sp", dst[rows, :], v[i2][:], reads=[b_v[i2]])
                S.barrier()

        for l in range(nlayers):
            if "stop0" in dbg:
                break
            xsrc = I["x"] if l == 0 else XN
            phase_P(l, xsrc)
            if "stopP" in dbg:
                break
            with ExitStack() as lst:
                y_res = sb(lst, f"y_res{l}", [128, NSUB, D], BF16)
                b_y = [Buf(f"y{i}") for i in range(NSUB)]
                phase_att(l, y_res, b_y)
                phase_D(l, y_res, b_y)
                if YDBG is not None and l == 0:
                    for sg in range(NSUB):
                        S.dma("sp", YDBG[sg * 128:(sg + 1) * 128, :], y_res[:, sg, :], reads=[b_y[sg]])
                    S.barrier()
                if "stopA" in dbg:
                    break
                phase_O1(l, xsrc, y_res, b_y)
            phase_O2(l, out if l == nlayers - 1 else XN)
        S.barrier()
        print("ops", S.n_ops, "dmas", S.n_dma, "sems", S.nsem)
    return nc


DBG = {}
_CONSTS = None


def make_in_maps(inputs):
    global _CONSTS
    if _CONSTS is None:
        _CONSTS = _host_consts()
    shared = {}
    for n in W_NAMES:
        a = np.ascontiguousarray(np.asarray(inputs[n], dtype=np.float32))
        shared[n] = a.reshape(W_SHAPES[n])
    for n, a in _CONSTS.items():
        shared["k_" + n] = a
    x = np.asarray(inputs["x"], dtype=np.float32)
    c = np.asarray(inputs["c"], dtype=np.float32)
    maps = []
    for b in range(8):
        m = dict(shared)
        m["x"] = np.ascontiguousarray(x[b])
        m["c"] = np.ascontiguousarray(c[b].reshape(8, 128).T)
        maps.append(m)
    return maps


def kernel(**inputs):
    nc = build()
    in_maps = make_in_maps(inputs)
    res = run_bass_kernel_spmd(nc, in_maps, core_ids=list(range(8)))
    return np.stack([np.asarray(r["out"]) for r in res.results], axis=0).astype(np.float32)
```

```python
import math
from contextlib import ExitStack
import numpy as np
import concourse.bass as bass
import concourse.mybir as mybir
from concourse.bass_utils import run_bass_kernel_spmd

F32 = mybir.dt.float32
BF16 = mybir.dt.bfloat16
AF = mybir.ActivationFunctionType
ALU = mybir.AluOpType
AX = mybir.AxisListType

SEQ = 4096
D = 1024
NSUB = SEQ // 128
HID = 4096
INC = 2720
ALPHA = 4 ** 0.25
SLOPES_A = [2.0 ** -1, 2.0 ** -3, 2.0 ** -5, 2.0 ** -7]
SLOPES_D = [2.0 ** -2, 2.0 ** -4, 2.0 ** -6, 2.0 ** -8]
DILS = [1, 4, 16]
SC_A = 32 ** -0.5
SC_B = 96 ** -0.5
SC_C = 0.125
SC_D = 0.125
VP = 66
VW = 4 * VP


class Buf:
    __slots__ = ("name", "w", "r", "excl")

    def __init__(self, name="", excl=False):
        self.name = name
        self.w = None
        self.r = []
        self.excl = excl


def PB(name=""):
    return Buf(name, excl=True)


class Tok:
    __slots__ = ("eng", "sem", "val")

    def __init__(self, eng, sem=None, val=None):
        self.eng = eng
        self.sem = sem
        self.val = val


class Sched:
    EPOCH = 20000
    NDMA = 8

    def __init__(self, nc, stack):
        self.nc = nc
        self.stack = stack
        self.engs = {"pe": nc.tensor, "act": nc.scalar, "dve": nc.vector,
                     "pool": nc.gpsimd, "sp": nc.sync}
        self.count = {e: 0 for e in self.engs}
        self.cursem = {}
        self.pending = {e: [] for e in self.engs}
        self.waited = {e: {} for e in self.engs}
        self.nsem = 0
        for e in self.engs:
            self._new_epoch(e)
        self.dma_sems, self.dma_cnt, self.dma_last, self.dma_i = {}, {}, {}, {}
        for q in ("sp", "act", "pool"):
            self.dma_sems[q] = [self._sem(f"dma_{q}_{i}") for i in range(self.NDMA)]
            self.dma_cnt[q] = [0] * self.NDMA
            self.dma_last[q] = [None] * self.NDMA
            self.dma_i[q] = 0
        self.n_ops = {e: 0 for e in self.engs}
        self.n_dma = 0

    def _sem(self, name):
        self.nsem += 1
        return self.stack.enter_context(self.nc.semaphore(name))

    def _new_epoch(self, e):
        self.cursem[e] = self._sem(f"s_{e}_{self.nsem}")
        self.count[e] = 0

    def _wait(self, eng, tok):
        if tok is None:
            return
        if tok.sem is None:
            raise RuntimeError(f"dependency on unsignalled op on {tok.eng}")
        key = id(tok.sem)
        w = self.waited[eng]
        if w.get(key, 0) >= tok.val:
            return
        w[key] = tok.val
        self.engs[eng].wait_ge(tok.sem, tok.val)

    def _deps(self, eng, reads, writes):
        for b in reads:
            t = b.w
            if t is not None and not (t.eng == eng and eng == "pe"):
                self._wait(eng, t)
        for b in writes:
            t = b.w
            if t is not None and t.eng != eng:
                self._wait(eng, t)
            for t in b.r:
                if t.eng != eng:
                    self._wait(eng, t)

    def _record(self, tok, reads, writes):
        for b in reads:
            b.r.append(tok)
            if len(b.r) > 16:
                last = {}
                for t in b.r:
                    last[(t.eng, id(t.sem))] = t
                b.r = list(last.values())
        for b in writes:
            b.w = tok
            b.r = []

    def op(self, eng, fn, reads=(), writes=(), signal=True):
        if any(b.excl for b in reads):
            writes = list(writes) + [b for b in reads if b.excl]
            reads = [b for b in reads if not b.excl]
        self._deps(eng, reads, writes)
        inst = fn()
        self.n_ops[eng] += 1
        tok = Tok(eng)
        self.pending[eng].append(tok)
        if signal:
            if self.count[eng] >= self.EPOCH:
                self._new_epoch(eng)
            self.count[eng] += 1
            sem = self.cursem[eng]
            inst.then_inc(sem, 1)
            for t in self.pending[eng]:
                t.sem = sem
                t.val = self.count[eng]
            self.pending[eng] = []
        self._record(tok, reads, writes)
        return tok

    def prewait(self, eng, reads=(), writes=()):
        writes = list(writes) + [b for b in reads if b.excl]
        reads = [b for b in reads if not b.excl]
        self._deps(eng, reads, writes)

    def dma(self, q, out, in_, reads=(), writes=(), **kw):
        i = self.dma_i[q]
        self.dma_i[q] = (i + 1) % self.NDMA
        prev = self.dma_last[q][i]
        if prev is not None:
            self._wait(q, prev)
        self._deps(q, reads, writes)
        sem = self.dma_sems[q][i]
        self.dma_cnt[q][i] += 16
        inst = self.engs[q].dma_start(out=out, in_=in_, **kw)
        inst.then_inc(sem, 16)
        tok = Tok("dma_" + q + str(i), sem, self.dma_cnt[q][i])
        self.dma_last[q][i] = tok
        self._record(tok, reads, writes)
        self.n_dma += 1
        return tok

    def barrier(self):
        toks = []
        for e in self.engs:
            if self.pending[e]:
                raise RuntimeError(f"barrier with unsignalled ops on {e}")
            if self.count[e] > 0:
                toks.append(Tok(e, self.cursem[e], self.count[e]))
        for q in self.dma_last:
            for t in self.dma_last[q]:
                if t is not None:
                    toks.append(t)
        for e in self.engs:
            for t in toks:
                if t.eng != e:
                    self._wait(e, t)


def _host_consts():
    c = {}
    c["ident"] = np.eye(128, dtype=np.float32)
    tok = (np.arange(NSUB)[None, :] * 128 + np.arange(128)[:, None]).astype(np.float64)
    freqs = 10000.0 ** (-np.arange(16, dtype=np.float64) / 16)

    def cs(pos):
        ang = pos[..., None].astype(np.float32).astype(np.float64) * freqs.astype(np.float32).astype(np.float64)
        ang = (pos[..., None].astype(np.float32) * freqs.astype(np.float32)).astype(np.float32)
        return np.cos(ang.astype(np.float64)), np.sin(ang.astype(np.float64))

    cp, sp_ = cs(tok)
    rb = np.zeros((128, NSUB, 2, 64), np.float64)
    for i, s in enumerate([SC_B, 1.0]):
        rb[:, :, i, 0:16] = cp * s
        rb[:, :, i, 16:32] = cp * s
        rb[:, :, i, 32:48] = -sp_ * s
        rb[:, :, i, 48:64] = sp_ * s
    c["ropeB"] = rb.astype(np.float32)
    cr, sr = cs(np.floor(tok / 64))
    cc, sc_ = cs(np.mod(tok, 64))
    rc = np.zeros((128, NSUB, 128), np.float64)
    rc[:, :, 0:16] = cr
    rc[:, :, 16:32] = cr
    rc[:, :, 32:48] = cc
    rc[:, :, 48:64] = cc
    rc[:, :, 64:80] = -sr
    rc[:, :, 80:96] = sr
    rc[:, :, 96:112] = -sc_
    rc[:, :, 112:128] = sc_
    c["ropeC"] = rc.astype(np.float32)
    ki = np.arange(128)[:, None].astype(np.float64)
    qi = np.arange(512)[None, :].astype(np.float64)
    al = np.zeros((128, 5, 512), np.float64)
    al[:, 0, :] = qi - ki
    for o in range(4):
        al[:, 1 + o, :] = np.abs(qi - ki - 128 * o)
    c["alibi"] = al.astype(np.float32)
    q128 = (np.arange(512) % 128)[None, :].astype(np.float64)
    dt = np.zeros((128, 2, 512), np.float64)
    da = np.abs(ki - 64 - q128)
    db = np.abs(ki + 64 - q128)
    dt[:, 0, :] = np.where(da <= 64, da, 1.0e6)
    dt[:, 1, :] = np.where(db <= 64, db, 1.0e6)
    c["dtab"] = dt.astype(np.float32)
    return c


W_NAMES = ["w_ada", "b_ada", "w_in", "w_o", "diff_lambda", "diff_subln_g", "mla_q_norm_g", "mla_w_uq",
           "mla_kv_norm_g", "mla_w_ukv", "gqa_q_norm_g", "gqa_k_norm_g", "ln_attn_g", "ln_attn_b",
           "w_up", "w_down", "ln_mlp_g", "ln_mlp_b"]
W_SHAPES = {"w_ada": [2, 1024, 6144], "b_ada": [2, 6144], "w_in": [2, 1024, INC], "w_o": [2, 1024, 1024],
            "diff_lambda": [2, 128], "diff_subln_g": [2, 64], "mla_q_norm_g": [2, 384],
            "mla_w_uq": [2, 384, 384], "mla_kv_norm_g": [2, 256], "mla_w_ukv": [2, 256, 512],
            "gqa_q_norm_g": [2, 64], "gqa_k_norm_g": [2, 64], "ln_attn_g": [2, 1024], "ln_attn_b": [2, 1024],
            "w_up": [2, 1024, HID], "w_down": [2, HID, 1024], "ln_mlp_g": [2, 1024], "ln_mlp_b": [2, 1024]}
C_SHAPES = {"ident": [128, 128], "ropeB": [128, NSUB, 2, 64], "ropeC": [128, NSUB, 128],
            "alibi": [128, 5, 512], "dtab": [128, 2, 512]}


def build(nlayers=2, dbg=()):
    nc = bass.Bass("TRN2", target_bir_lowering=False)
    I = {}
    I["x"] = nc.dram_tensor("x", [SEQ, D], F32, kind="ExternalInput").ap()
    I["c"] = nc.dram_tensor("c", [128, 8], F32, kind="ExternalInput").ap()
    for n in W_NAMES:
        I[n] = nc.dram_tensor(n, W_SHAPES[n], F32, kind="ExternalInput").ap()
    for n in C_SHAPES:
        I[n] = nc.dram_tensor("k_" + n, C_SHAPES[n], F32, kind="ExternalInput").ap()
    out = nc.dram_tensor("out", [SEQ, D], F32, kind="ExternalOutput").ap()

    def scratch(name, shape, dt):
        kind = "ExternalOutput" if name in dbg else "Internal"
        return nc.dram_tensor(name, shape, dt, kind=kind).ap()

    QKT = scratch("QKT", [15, 128, SEQ], BF16)
    VG = scratch("VG", [3, SEQ, VW], BF16)
    DTOK = scratch("DTOK", [SEQ, 512 + VW], BF16)
    OD = scratch("OD", [3, SEQ, 260], F32)
    X1 = scratch("X1", [SEQ, D], F32)
    H2T = scratch("H2T", [8, 128, SEQ], BF16)
    XN = scratch("XN", [SEQ, D], F32)
    GB = scratch("GB", [2, 2, 128, D], F32)
    YDBG = scratch("YDBG", [SEQ, D], BF16) if "YDBG" in dbg else None

    with ExitStack() as top:
        S = Sched(nc, top)

        uid = [0]

        def sb(st, name, shape, dt):
            uid[0] += 1
            return st.enter_context(nc.sbuf_tensor(f"s{uid[0]}_{name}", shape, dt))

        def pbank(st, name):
            uid[0] += 1
            return st.enter_context(nc.psum_tensor(f"p{uid[0]}_{name}", [128, 512], F32))

        def V(fn, reads, writes):
            return S.op("dve", fn, reads, writes)

        def A(fn, reads, writes):
            return S.op("act", fn, reads, writes)

        def G(fn, reads, writes):
            return S.op("pool", fn, reads, writes)

        def PE(fn, reads, writes, signal=True):
            return S.op("pe", fn, reads, writes, signal)

        ident_f = sb(top, "ident_f", [128, 128], F32)
        ident_b = sb(top, "ident_b", [128, 128], BF16)
        modT = sb(top, "modT", [128, 2, 48], F32)
        eps6 = sb(top, "eps6", [128, 1], F32)
        eps5 = sb(top, "eps5", [128, 1], F32)
        b_const = Buf("const")
        b_modT = Buf("modT")
        S.dma("sp", ident_f[:], I["ident"], writes=[b_const])
        S.dma("pool", ident_b[:], I["ident"], writes=[b_const])
        V(lambda: nc.vector.memset(eps6[:], 1e-6), [], [b_const])
        V(lambda: nc.vector.memset(eps5[:], 1e-5), [], [b_const])

        with ExitStack() as st:
            condT = sb(st, "condT", [128, 8], F32)
            ones_row = sb(st, "ones_row", [1, 128], F32)
            modrow = sb(st, "modrow", [1, 6144], F32)
            brow = sb(st, "brow", [1, 6144], F32)
            wa = [sb(st, f"wa{i}", [128, 8, 512], F32) for i in range(2)]
            gbt = sb(st, "gbt", [128, 1024], F32)
            ps = [pbank(st, f"sps{i}") for i in range(2)]
            b_cond, b_ones, b_mrow, b_brow, b_gbt = Buf(), Buf(), Buf(), Buf(), Buf()
            b_wa = [Buf(), Buf()]
            b_ps = [PB(), PB()]
            S.dma("sp", condT[:], I["c"], writes=[b_cond])
            A(lambda: nc.scalar.activation(out=condT[:], in_=condT[:], func=AF.Silu), [b_cond], [b_cond])
            V(lambda: nc.vector.memset(ones_row[:], 1.0), [], [b_ones])
            for l in range(nlayers):
                S.dma("sp", brow[:], I["b_ada"][l:l + 1, :], writes=[b_brow])
                wsrc = I["w_ada"][l].rearrange("(kc p) n -> p kc n", p=128)
                for pc in range(12):
                    S.dma("sp", wa[pc % 2][:], wsrc[:, :, pc * 512:(pc + 1) * 512], writes=[b_wa[pc % 2]])
                    for kc in range(8):
                        PE(lambda: nc.tensor.matmul(ps[pc % 2][0:1, :], lhsT=condT[:, kc:kc + 1], rhs=wa[pc % 2][:, kc, :],
                                                    start=(kc == 0), stop=(kc == 7)),
                           [b_cond, b_wa[pc % 2]], [b_ps[pc % 2]], signal=(kc == 7))
                    V(lambda: nc.vector.tensor_tensor(out=modrow[0:1, pc * 512:(pc + 1) * 512], in0=ps[pc % 2][0:1, :],
                                                      in1=brow[0:1, pc * 512:(pc + 1) * 512], op=ALU.add),
                      [b_ps[pc % 2], b_brow], [b_mrow])
                for j in range(48):
                    PE(lambda: nc.tensor.matmul(ps[0][:, j:j + 1], lhsT=modrow[0:1, j * 128:(j + 1) * 128],
                                                rhs=ones_row[0:1, 0:1], start=True, stop=True),
                       [b_mrow, b_ones], [b_ps[0]], signal=(j == 47))
                V(lambda: nc.vector.tensor_copy(modT[:, l, :], ps[0][:, 0:48]), [b_ps[0]], [b_modT])
                V(lambda: nc.vector.tensor_scalar(out=modT[:, l, 8:16], in0=modT[:, l, 8:16], scalar1=1.0, scalar2=None,
                                                  op0=ALU.add), [b_modT], [b_modT])
                V(lambda: nc.vector.tensor_scalar(out=modT[:, l, 32:40], in0=modT[:, l, 32:40], scalar1=1.0, scalar2=None,
                                                  op0=ALU.add), [b_modT], [b_modT])
                for gi, base in enumerate([2048, 5120]):
                    for hf in range(2):
                        PE(lambda: nc.tensor.matmul(ps[1][:, :], lhsT=ones_row[0:1, :],
                                                    rhs=modrow[0:1, base + hf * 512: base + (hf + 1) * 512],
                                                    start=True, stop=True), [b_mrow, b_ones], [b_ps[1]])
                        V(lambda: nc.vector.tensor_copy(gbt[:, hf * 512:(hf + 1) * 512], ps[1][:, :]), [b_ps[1]], [b_gbt])
                    S.dma("sp", GB[l, gi], gbt[:], reads=[b_gbt])
            S.barrier()

        def phase_P(l, xsrc):
            with ExitStack() as st:
                w_in = sb(st, "w_in", [128, 8, INC], BF16)
                w_uq = sb(st, "w_uq", [128, 3, 384], BF16)
                w_ukv = sb(st, "w_ukv", [128, 2, 512], BF16)
                gq_bc = sb(st, "gq_bc", [128, 384], F32)
                gkv_bc = sb(st, "gkv_bc", [128, 256], F32)
                gC_bc = sb(st, "gC_bc", [128, 6, 64], F32)
                ropeB = sb(st, "ropeB", [128, NSUB, 2, 64], F32)
                ropeC = sb(st, "ropeC", [128, NSUB, 128], F32)
                xs = [sb(st, f"xs{i}", [128, D], F32) for i in range(2)]
                hT = [sb(st, f"hT{i}", [128, 8, 512], BF16) for i in range(2)]
                bf32_l = [sb(st, f"bf32{i}", [128, 672], F32) for i in range(2)]
                cqk_l = [sb(st, f"cqk{i}", [128, 384], F32) for i in range(2)]
                junk_l = [sb(st, f"junk{i}", [128, 384], F32) for i in range(2)]
                qkA_l = [sb(st, f"qkA{i}", [128, 512], BF16) for i in range(2)]
                qkD_l = [sb(st, f"qkD{i}", [128, 512], BF16) for i in range(2)]
                VA_l = [sb(st, f"VA{i}", [128, 4, VP], BF16) for i in range(2)]
                VB_l = [sb(st, f"VB{i}", [128, 4, VP], BF16) for i in range(2)]
                VC_l = [sb(st, f"VC{i}", [128, 4, VP], BF16) for i in range(2)]
                VD_l = [sb(st, f"VD{i}", [128, 4, VP], BF16) for i in range(2)]
                cqn_l = [sb(st, f"cqn{i}", [128, 640], BF16) for i in range(2)]
                cT_l = [sb(st, f"cT{i}", [128, 5, 128], BF16) for i in range(2)]
                qB_l = [sb(st, f"qB{i}", [128, 4, 96], BF16) for i in range(2)]
                kB_l = [sb(st, f"kB{i}", [128, 4, 96], BF16) for i in range(2)]
                cC_l = [sb(st, f"cC{i}", [128, 384], BF16) for i in range(2)]
                t1_l = [sb(st, f"t1{i}", [128, 384], F32) for i in range(2)]
                t2_l = [sb(st, f"t2{i}", [128, 384], F32) for i in range(2)]
                nq_l = [sb(st, f"nq{i}", [128, 384], F32) for i in range(2)]
                stt__l = [sb(st, f"stt_{i}", [128, 16], F32) for i in range(2)]
                stage = [sb(st, f"stage{i}", [128, 15, 512], BF16) for i in range(2)]
                pb = [pbank(st, f"pp{i}") for i in range(8)]
                bp = [PB(f"pp{i}") for i in range(8)]
                b_w, b_tab = Buf(), Buf()
                b_xs = [Buf(), Buf()]
                b_hT = [Buf(), Buf()]
                BL = {n: [Buf(n + '0'), Buf(n + '1')] for n in ['bf32', 'cqk', 'junk', 'qkA', 'qkD', 'VA', 'VB', 'VC', 'VD', 'cqn', 'cT', 'qB', 'kB', 'cC', 't1', 't2', 'nq', 'stt_']}
                b_wk = [Buf() for _ in range(8)]
                b_hk = [[Buf() for _ in range(8)] for _ in range(2)]
                b_stage = [Buf(), Buf()]

                wsrc = I["w_in"][l].rearrange("(kc p) n -> p kc n", p=128)
                for kc in range(8):
                    S.dma("pool", w_in[:, kc, :], wsrc[:, kc, :], writes=[b_wk[kc]])
                S.dma("pool", w_uq[:], I["mla_w_uq"][l].rearrange("(kc p) n -> p kc n", p=128), writes=[b_w])
                S.dma("pool", w_ukv[:], I["mla_w_ukv"][l].rearrange("(kc p) n -> p kc n", p=128), writes=[b_w])
                S.dma("sp", gq_bc[:], I["mla_q_norm_g"][l, :].partition_broadcast(128), writes=[b_tab])
                S.dma("sp", gkv_bc[:], I["mla_kv_norm_g"][l, :].partition_broadcast(128), writes=[b_tab])
                for h in range(6):
                    src = I["gqa_q_norm_g"] if h < 4 else I["gqa_k_norm_g"]
                    S.dma("sp", gC_bc[:, h, :], src[l, :].partition_broadcast(128), writes=[b_tab])
                V(lambda: nc.vector.tensor_scalar(out=gC_bc[:, 0:4, :], in0=gC_bc[:, 0:4, :], scalar1=SC_C, scalar2=None,
                                                  op0=ALU.mult), [b_tab], [b_tab])
                S.dma("sp", ropeB[:], I["ropeB"], writes=[b_tab])
                S.dma("sp", ropeC[:], I["ropeC"], writes=[b_tab])
                for i_ in range(2):
                    for n_ in ('VA', 'VB', 'VC', 'VD'):
                        vt = {'VA': VA_l, 'VB': VB_l, 'VC': VC_l, 'VD': VD_l}[n_][i_]
                        V(lambda: nc.vector.memset(vt[:], 1.0), [], [BL[n_][i_]])

                groups = [(0, 512), (512, 512), (1024, 416), (1440, 512), (1952, 512), (2464, 256)]
                def stage1(sg):
                    T, s = sg // 4, sg % 4
                    tsl = slice(s * 128, (s + 1) * 128)
                    h_t, bhk = hT[T % 2], b_hk[T % 2]
                    stg, bstg = stage[T % 2], b_stage[T % 2]
                    bf32 = bf32_l[sg % 2]
                    b_bf32 = BL['bf32'][sg % 2]
                    cqk = cqk_l[sg % 2]
                    b_cqk = BL['cqk'][sg % 2]
                    junk = junk_l[sg % 2]
                    b_junk = BL['junk'][sg % 2]
                    qkA = qkA_l[sg % 2]
                    b_qkA = BL['qkA'][sg % 2]
                    qkD = qkD_l[sg % 2]
                    b_qkD = BL['qkD'][sg % 2]
                    VA = VA_l[sg % 2]
                    b_VA = BL['VA'][sg % 2]
                    VB = VB_l[sg % 2]
                    b_VB = BL['VB'][sg % 2]
                    VC = VC_l[sg % 2]
                    b_VC = BL['VC'][sg % 2]
                    VD = VD_l[sg % 2]
                    b_VD = BL['VD'][sg % 2]
                    cqn = cqn_l[sg % 2]
                    b_cqn = BL['cqn'][sg % 2]
                    cT = cT_l[sg % 2]
                    b_cT = BL['cT'][sg % 2]
                    qB = qB_l[sg % 2]
                    b_qB = BL['qB'][sg % 2]
                    kB = kB_l[sg % 2]
                    b_kB = BL['kB'][sg % 2]
                    cC = cC_l[sg % 2]
                    b_cC = BL['cC'][sg % 2]
                    t1 = t1_l[sg % 2]
                    b_t1 = BL['t1'][sg % 2]
                    t2 = t2_l[sg % 2]
                    b_t2 = BL['t2'][sg % 2]
                    nq = nq_l[sg % 2]
                    b_nq = BL['nq'][sg % 2]
                    stt_ = stt__l[sg % 2]
                    b_st = BL['stt_'][sg % 2]
                    x_t, bx = xs[sg % 2], b_xs[sg % 2]
                    S.dma("sp", x_t[:], xsrc[sg * 128:(sg + 1) * 128, :], writes=[bx])
                    for hf in range(2):
                        for cc in range(4):
                            kc = hf * 4 + cc
                            PE(lambda: nc.tensor.transpose(pb[hf][:, cc * 128:(cc + 1) * 128], x_t[:, kc * 128:(kc + 1) * 128],
                                                           ident_f[:]), [bx, b_const], [bp[hf]], signal=(cc == 3))
                        for cc in range(4):
                            kc = hf * 4 + cc
                            if cc % 2 == 0:
                                A(lambda: nc.scalar.activation(out=h_t[:, kc, tsl], in_=pb[hf][:, cc * 128:(cc + 1) * 128],
                                                               func=AF.Identity, scale=modT[:, l, 8 + kc:9 + kc],
                                                               bias=modT[:, l, kc:kc + 1]), [bp[hf], b_modT], [bhk[kc]])
                            else:
                                V(lambda: nc.vector.tensor_scalar(out=h_t[:, kc, tsl], in0=pb[hf][:, cc * 128:(cc + 1) * 128],
                                                                  scalar1=modT[:, l, 8 + kc:9 + kc],
                                                                  scalar2=modT[:, l, kc:kc + 1], op0=ALU.mult, op1=ALU.add),
                                  [bp[hf], b_modT], [bhk[kc]])
                    yield
                    for gi, (c0, ncol) in enumerate(groups):
                        if gi > 0:
                            yield
                        bk = 2 + gi % 3
                        for kc in range(8):
                            PE(lambda: nc.tensor.matmul(pb[bk][:, 0:ncol], lhsT=h_t[:, kc, tsl], rhs=w_in[:, kc, c0:c0 + ncol],
                                                        start=(kc == 0), stop=(kc == 7)), [bhk[kc], b_wk[kc]], [bp[bk]], signal=(kc == 7))
                        P_ = pb[bk]
                        if gi == 0:
                            A(lambda: nc.scalar.activation(out=qkA[:, 0:256], in_=P_[:, 0:256], func=AF.Identity, scale=SC_A),
                              [bp[bk]], [b_qkA])
                            V(lambda: nc.vector.tensor_copy(qkA[:, 256:512], P_[:, 256:512]), [bp[bk]], [b_qkA])
                        elif gi == 1:
                            A(lambda: nc.scalar.activation(func=AF.Identity, out=VA[:, :, 0:64], in_=P_[:, 0:256].rearrange("p (h d) -> p h d", d=64)),
                              [bp[bk]], [b_VA])
                            V(lambda: nc.vector.tensor_copy(bf32[:, 0:256], P_[:, 256:512]), [bp[bk]], [b_bf32])
                        elif gi == 2:
                            V(lambda: nc.vector.tensor_copy(bf32[:, 256:672], P_[:, 0:416]), [bp[bk]], [b_bf32])
                        elif gi == 3:
                            V(lambda: nc.vector.tensor_copy(cqk[:, :], P_[:, 0:384]), [bp[bk]], [b_cqk])
                            A(lambda: nc.scalar.activation(func=AF.Identity, out=VC[:, 0:2, 0:64],
                                                     in_=P_[:, 384:512].rearrange("p (h d) -> p h d", d=64)),
                              [bp[bk]], [b_VC])
                        elif gi == 4:
                            A(lambda: nc.scalar.activation(out=qkD[:, 0:256], in_=P_[:, 0:256], func=AF.Identity, scale=SC_D),
                              [bp[bk]], [b_qkD])
                            V(lambda: nc.vector.tensor_copy(qkD[:, 256:512], P_[:, 256:512]), [bp[bk]], [b_qkD])
                        else:
                            A(lambda: nc.scalar.activation(func=AF.Identity, out=VD[:, :, 0:64], in_=P_[:, 0:256].rearrange("p (h d) -> p h d", d=64)),
                              [bp[bk]], [b_VD])

                def stage2(sg):
                    T, s = sg // 4, sg % 4
                    tsl = slice(s * 128, (s + 1) * 128)
                    h_t, bhk = hT[T % 2], b_hk[T % 2]
                    stg, bstg = stage[T % 2], b_stage[T % 2]
                    bf32 = bf32_l[sg % 2]
                    b_bf32 = BL['bf32'][sg % 2]
                    cqk = cqk_l[sg % 2]
                    b_cqk = BL['cqk'][sg % 2]
                    junk = junk_l[sg % 2]
                    b_junk = BL['junk'][sg % 2]
                    qkA = qkA_l[sg % 2]
                    b_qkA = BL['qkA'][sg % 2]
                    qkD = qkD_l[sg % 2]
                    b_qkD = BL['qkD'][sg % 2]
                    VA = VA_l[sg % 2]
                    b_VA = BL['VA'][sg % 2]
                    VB = VB_l[sg % 2]
                    b_VB = BL['VB'][sg % 2]
                    VC = VC_l[sg % 2]
                    b_VC = BL['VC'][sg % 2]
                    VD = VD_l[sg % 2]
                    b_VD = BL['VD'][sg % 2]
                    cqn = cqn_l[sg % 2]
                    b_cqn = BL['cqn'][sg % 2]
                    cT = cT_l[sg % 2]
                    b_cT = BL['cT'][sg % 2]
                    qB = qB_l[sg % 2]
                    b_qB = BL['qB'][sg % 2]
                    kB = kB_l[sg % 2]
                    b_kB = BL['kB'][sg % 2]
                    cC = cC_l[sg % 2]
                    b_cC = BL['cC'][sg % 2]
                    t1 = t1_l[sg % 2]
                    b_t1 = BL['t1'][sg % 2]
                    t2 = t2_l[sg % 2]
                    b_t2 = BL['t2'][sg % 2]
                    nq = nq_l[sg % 2]
                    b_nq = BL['nq'][sg % 2]
                    stt_ = stt__l[sg % 2]
                    b_st = BL['stt_'][sg % 2]
                    V(lambda: nc.vector.scalar_tensor_tensor(out=junk[:, 0:384], in0=bf32[:, 0:384], scalar=1.0,
                                                             in1=bf32[:, 0:384], op0=ALU.mult, op1=ALU.mult,
                                                             accum_out=stt_[:, 0:1]), [b_bf32], [b_junk, b_st])
                    V(lambda: nc.vector.scalar_tensor_tensor(out=junk[:, 0:256], in0=bf32[:, 384:640], scalar=1.0,
                                                             in1=bf32[:, 384:640], op0=ALU.mult, op1=ALU.mult,
                                                             accum_out=stt_[:, 1:2]), [b_bf32], [b_junk, b_st])
                    A(lambda: nc.scalar.activation(out=stt_[:, 2:3], in_=stt_[:, 0:1], func=AF.Ln, scale=1.0 / 384,
                                                   bias=eps6[:, 0:1]), [b_st, b_const], [b_st])
                    A(lambda: nc.scalar.activation(out=stt_[:, 3:4], in_=stt_[:, 1:2], func=AF.Ln, scale=1.0 / 256,
                                                   bias=eps6[:, 0:1]), [b_st, b_const], [b_st])
                    A(lambda: nc.scalar.activation(out=stt_[:, 4:6], in_=stt_[:, 2:4], func=AF.Exp, scale=-0.5), [b_st], [b_st])
                    V(lambda: nc.vector.scalar_tensor_tensor(out=cqn[:, 0:384], in0=bf32[:, 0:384], scalar=stt_[:, 4:5],
                                                             in1=gq_bc[:, :], op0=ALU.mult, op1=ALU.mult),
                      [b_bf32, b_st, b_tab], [b_cqn])
                    V(lambda: nc.vector.scalar_tensor_tensor(out=cqn[:, 384:640], in0=bf32[:, 384:640], scalar=stt_[:, 5:6],
                                                             in1=gkv_bc[:, :], op0=ALU.mult, op1=ALU.mult),
                      [b_bf32, b_st, b_tab], [b_cqn])
                    yield
                    p5b = pb[5][:, :].bitcast(BF16)
                    for j in range(5):
                        PE(lambda: nc.tensor.transpose(p5b[:, j * 128:(j + 1) * 128], cqn[:, j * 128:(j + 1) * 128], ident_b[:]),
                           [b_cqn, b_const], [bp[5]], signal=(j == 4))
                    V(lambda: nc.vector.tensor_copy(cT[:, :, :], p5b[:, 0:640].rearrange("p (c t) -> p c t", t=128)),
                      [bp[5]], [b_cT])
                    for j in range(3):
                        PE(lambda: nc.tensor.matmul(pb[6][:, 0:384], lhsT=cT[:, j, :], rhs=w_uq[:, j, :], start=(j == 0),
                                                    stop=(j == 2)), [b_cT, b_w], [bp[6]], signal=(j == 2))
                    for j in range(2):
                        PE(lambda: nc.tensor.matmul(pb[7][:, 0:512], lhsT=cT[:, 3 + j, :], rhs=w_ukv[:, j, :], start=(j == 0),
                                                    stop=(j == 1)), [b_cT, b_w], [bp[7]], signal=(j == 1))
                    yield
                    q3 = pb[6][:, 0:384].rearrange("p (h d) -> p h d", d=96)
                    kv3 = pb[7][:, 0:512].rearrange("p (h d) -> p h d", d=128)
                    A(lambda: nc.scalar.activation(out=qB[:, :, 0:64], in_=q3[:, :, 0:64], func=AF.Identity, scale=SC_B),
                      [bp[6]], [b_qB])
                    Tq = ropeB[:, sg, 0, :]
                    Tk = ropeB[:, sg, 1, :]
                    t1q = t1[:, 0:128].rearrange("p (h d) -> p h d", d=32)
                    t2q = t2[:, 0:128].rearrange("p (h d) -> p h d", d=32)
                    V(lambda: nc.vector.tensor_tensor(out=t1q, in0=q3[:, :, 64:96],
                                                      in1=Tq[:, 0:32].unsqueeze(1).to_broadcast([128, 4, 32]), op=ALU.mult),
                      [bp[6], b_tab], [b_t1])
                    V(lambda: nc.vector.tensor_tensor(out=t2q[:, :, 0:16], in0=q3[:, :, 80:96],
                                                      in1=Tq[:, 32:48].unsqueeze(1).to_broadcast([128, 4, 16]), op=ALU.mult),
                      [bp[6], b_tab], [b_t2])
                    V(lambda: nc.vector.tensor_tensor(out=t2q[:, :, 16:32], in0=q3[:, :, 64:80],
                                                      in1=Tq[:, 48:64].unsqueeze(1).to_broadcast([128, 4, 16]), op=ALU.mult),
                      [bp[6], b_tab], [b_t2])
                    G(lambda: nc.gpsimd.tensor_tensor(out=qB[:, :, 64:96], in0=t1q, in1=t2q, op=ALU.add), [b_t1, b_t2], [b_qB])
                    V(lambda: nc.vector.tensor_copy(kB[:, :, 0:64], kv3[:, :, 0:64]), [bp[7]], [b_kB])
                    A(lambda: nc.scalar.activation(func=AF.Identity, out=VB[:, :, 0:64], in_=kv3[:, :, 64:128]), [bp[7]], [b_VB])
                    yield
                    kr = bf32[:, 640:672]
                    V(lambda: nc.vector.tensor_tensor(out=t1[:, 128:160], in0=kr, in1=Tk[:, 0:32], op=ALU.mult),
                      [b_bf32, b_tab], [b_t1])
                    V(lambda: nc.vector.tensor_tensor(out=t2[:, 128:144], in0=bf32[:, 656:672], in1=Tk[:, 32:48], op=ALU.mult),
                      [b_bf32, b_tab], [b_t2])
                    V(lambda: nc.vector.tensor_tensor(out=t2[:, 144:160], in0=bf32[:, 640:656], in1=Tk[:, 48:64], op=ALU.mult),
                      [b_bf32, b_tab], [b_t2])
                    G(lambda: nc.gpsimd.tensor_tensor(out=kB[:, :, 64:96],
                                                      in0=t1[:, 128:160].unsqueeze(1).to_broadcast([128, 4, 32]),
                                                      in1=t2[:, 128:160].unsqueeze(1).to_broadcast([128, 4, 32]), op=ALU.add),
                      [b_t1, b_t2], [b_kB])
                    yield
                    c3 = cqk[:, :].rearrange("p (h d) -> p h d", d=64)
                    V(lambda: nc.vector.tensor_tensor(out=junk[:, :], in0=cqk[:, :], in1=cqk[:, :], op=ALU.mult),
                      [b_cqk], [b_junk])
                    V(lambda: nc.vector.tensor_reduce(out=stt_[:, 6:12], in_=junk[:, :].rearrange("p (h d) -> p h d", d=64),
                                                      axis=AX.X, op=ALU.add), [b_junk], [b_st])
                    A(lambda: nc.scalar.activation(out=stt_[:, 6:12], in_=stt_[:, 6:12], func=AF.Ln, scale=1.0 / 64,
                                                   bias=eps6[:, 0:1]), [b_st, b_const], [b_st])
                    A(lambda: nc.scalar.activation(out=stt_[:, 6:12], in_=stt_[:, 6:12], func=AF.Exp, scale=-0.5), [b_st], [b_st])
                    yield
                    n3 = nq[:, :].rearrange("p (h d) -> p h d", d=64)
                    V(lambda: nc.vector.tensor_tensor(out=n3, in0=c3, in1=stt_[:, 6:12].unsqueeze(2).to_broadcast([128, 6, 64]),
                                                      op=ALU.mult), [b_cqk, b_st], [b_nq])
                    G(lambda: nc.gpsimd.tensor_tensor(out=n3, in0=n3, in1=gC_bc[:, :, :], op=ALU.mult), [b_nq, b_tab], [b_nq])
                    yield
                    cosA = ropeC[:, sg, 0:64]
                    sinS = ropeC[:, sg, 64:128].rearrange("p (r f i) -> p r f i", r=2, f=2)
                    V(lambda: nc.vector.tensor_tensor(out=t1[:, :].rearrange("p (h d) -> p h d", d=64), in0=n3,
                                                      in1=cosA.unsqueeze(1).to_broadcast([128, 6, 64]), op=ALU.mult),
                      [b_nq, b_tab], [b_t1])
                    n5 = nq[:, :].rearrange("p (h r f i) -> p h r f i", h=6, r=2, f=2)
                    t5 = t2[:, :].rearrange("p (h r f i) -> p h r f i", h=6, r=2, f=2)
                    for f in range(2):
                        V(lambda: nc.vector.tensor_tensor(out=t5[:, :, :, f, :], in0=n5[:, :, :, 1 - f, :],
                                                          in1=sinS[:, :, f, :].unsqueeze(1).to_broadcast([128, 6, 2, 16]),
                                                          op=ALU.mult), [b_nq, b_tab], [b_t2])
                    G(lambda: nc.gpsimd.tensor_tensor(out=cC[:, :], in0=t1[:, :], in1=t2[:, :], op=ALU.add), [b_t1, b_t2], [b_cC])
                    yield
                    tb0 = pb[6][:, :].bitcast(BF16)
                    tb1 = pb[7][:, :].bitcast(BF16)
                    for j in range(4):
                        PE(lambda: nc.tensor.transpose(tb0[:, j * 128:(j + 1) * 128], qkA[:, j * 128:(j + 1) * 128], ident_b[:]),
                           [b_qkA, b_const], [bp[6]], signal=False)
                    for h in range(4):
                        PE(lambda: nc.tensor.transpose(tb0[0:96, (4 + h) * 128:(5 + h) * 128], qB[:, h, :], ident_b[:]),
                           [b_qB, b_const], [bp[6]], signal=(h == 3))
                    for h in range(4):
                        PE(lambda: nc.tensor.transpose(tb1[0:96, h * 128:(h + 1) * 128], kB[:, h, :], ident_b[:]),
                           [b_kB, b_const], [bp[7]], signal=False)
                    for j in range(3):
                        PE(lambda: nc.tensor.transpose(tb1[:, (4 + j) * 128:(5 + j) * 128], cC[:, j * 128:(j + 1) * 128],
                                                       ident_b[:]), [b_cC, b_const], [bp[7]], signal=(j == 2))
                    V(lambda: nc.vector.tensor_copy(stg[:, 0:8, tsl], tb0[:, 0:1024].rearrange("p (c t) -> p c t", t=128)),
                      [bp[6]], [bstg])
                    A(lambda: nc.scalar.activation(func=AF.Identity, out=stg[:, 8:15, tsl], in_=tb1[:, 0:896].rearrange("p (c t) -> p c t", t=128)),
                      [bp[7]], [bstg])
                    yield
                    rows = slice(sg * 128, (sg + 1) * 128)
                    S.dma("sp", VG[0, rows, :], VA[:, :, :].rearrange("p h d -> p (h d)"), reads=[b_VA])
                    S.dma("sp", VG[1, rows, :], VB[:, :, :].rearrange("p h d -> p (h d)"), reads=[b_VB])
                    S.dma("sp", VG[2, rows, :], VC[:, :, :].rearrange("p h d -> p (h d)"), reads=[b_VC])
                    S.dma("sp", DTOK[rows, 0:512], qkD[:, :], reads=[b_qkD])
                    S.dma("sp", DTOK[rows, 512:512 + VW], VD[:, :, :].rearrange("p h d -> p (h d)"), reads=[b_VD])
                    if s == 3:
                        for (ca, cb) in [(0, 4), (4, 8), (8, 12), (12, 15)]:
                            S.dma("sp", QKT[ca:cb, :, T * 512:(T + 1) * 512].rearrange("c r t -> r c t"), stg[:, ca:cb, :],
                                  reads=[bstg])

                nsub_ = NSUB if 'nsub' not in DBG else DBG['nsub']
                for _ in stage1(0):
                    pass
                for sg in range(nsub_):
                    gens = [stage2(sg)] + ([stage1(sg + 1)] if sg + 1 < nsub_ else [])
                    while gens:
                        for g_ in list(gens):
                            try:
                                next(g_)
                            except StopIteration:
                                gens.remove(g_)
                S.barrier()

        def phase_att(l, y_res, b_y):
            with ExitStack() as st:
                qT = [sb(st, f"qT{i}", [128, SEQ], BF16) for i in range(4)]
                kT = [sb(st, f"kT{i}", [128, SEQ], BF16) for i in range(4)]
                Vg = sb(st, "Vg", [128, NSUB, VW + 64], BF16)
                alibi = sb(st, "alibi", [128, 5, 512], F32)
                Sb = [sb(st, f"Sb{i}", [128, 512], F32) for i in range(3)]
                E = [sb(st, f"E{i}", [128, 512], BF16) for i in range(4)]
                lam_t = sb(st, "lam_t", [128, 128], F32)
                lam = sb(st, "lam", [128, 8], F32)
                gA_bc = sb(st, "gA_bc", [128, 64], F32)
                o1 = sb(st, "o1", [128, 4, 64], F32)
                o2 = sb(st, "o2", [128, 4, 64], F32)
                osq = sb(st, "osq", [128, 4, 64], F32)
                rc = sb(st, "rc", [128, 16], F32)
                pS = [pbank(st, f"pS{i}") for i in range(4)]
                pOT = [pbank(st, f"pOT{i}") for i in range(2)]
                pO = [pbank(st, f"pO{i}") for i in range(2)]
                otS = [sb(st, f"otS{i}", [65, 512], F32) for i in range(2)]
                b_pOT = [PB(), PB()]
                b_otS = [Buf(), Buf()]
                b_qT = [Buf() for _ in range(4)]
                b_kT = [Buf() for _ in range(4)]
                b_Vg, b_al, b_lam, b_gA = Buf(), Buf(), Buf(), Buf()
                b_Sb = [Buf() for _ in range(3)]
                b_E = [Buf() for _ in range(4)]
                b_pS = [PB() for _ in range(4)]
                b_pO = [PB() for _ in range(4)]
                b_o1, b_o2, b_osq, b_rc = Buf(), Buf(), Buf(), Buf()
                V(lambda: nc.vector.memset(Vg[:, :, VW:VW + 64], 0.0), [], [b_Vg])
                S.dma("sp", alibi[:], I["alibi"], writes=[b_al])
                lam_init = 0.8 - 0.6 * math.exp(-0.3 * l)
                S.dma("sp", lam_t[:], I["diff_lambda"][l, :].partition_broadcast(128), writes=[b_lam])
                V(lambda: nc.vector.scalar_tensor_tensor(out=lam_t[:, 0:32], in0=lam_t[:, 0:32], scalar=1.0, in1=lam_t[:, 32:64],
                                                         op0=ALU.mult, op1=ALU.mult, accum_out=lam[:, 0:1]), [b_lam], [b_lam])
                V(lambda: nc.vector.scalar_tensor_tensor(out=lam_t[:, 64:96], in0=lam_t[:, 64:96], scalar=1.0,
                                                         in1=lam_t[:, 96:128], op0=ALU.mult, op1=ALU.mult,
                                                         accum_out=lam[:, 1:2]), [b_lam], [b_lam])
                A(lambda: nc.scalar.activation(out=lam[:, 2:4], in_=lam[:, 0:2], func=AF.Exp), [b_lam], [b_lam])
                V(lambda: nc.vector.tensor_tensor(out=lam[:, 4:5], in0=lam[:, 2:3], in1=lam[:, 3:4], op=ALU.subtract),
                  [b_lam], [b_lam])
                V(lambda: nc.vector.tensor_scalar(out=lam[:, 5:6], in0=lam[:, 4:5], scalar1=lam_init, scalar2=-1.0,
                                                  op0=ALU.add, op1=ALU.mult), [b_lam], [b_lam])
                S.dma("sp", gA_bc[:], I["diff_subln_g"][l, :].partition_broadcast(128), writes=[b_gA])
                V(lambda: nc.vector.tensor_scalar(out=gA_bc[:], in0=gA_bc[:], scalar1=1.0 - lam_init, scalar2=None,
                                                  op0=ALU.mult), [b_gA], [b_gA])

                state = {"qi": 0, "ei": 0, "si": 0, "sbi": 0}

                def load_map(chunk, r0, nrows, isq):
                    i = state["qi"] % 4
                    t, b = (qT[i], b_qT[i]) if isq else (kT[i], b_kT[i])
                    S.dma("sp", t[0:nrows, :], QKT[chunk, r0:r0 + nrows, :], writes=[b])
                    return t, b

                def load_V(g):
                    for a in range(4):
                        S.dma("sp", Vg[:, a * 8:(a + 1) * 8, 0:VW],
                              VG[g, a * 1024:(a + 1) * 1024, :].rearrange("(t p) c -> p t c", p=128), writes=[b_Vg])

                def job(maps, vcol, ycol, alibi_slope=None):
                    nm = len(maps)
                    pend = [None]
                    for qt in range(8):
                        if nm == 2:
                            groups = [[(kt, 0), (kt, 1)] for kt in range(NSUB)]
                        else:
                            groups = [[(2 * j, 0), (2 * j + 1, 0)] for j in range(NSUB // 2)]
                        ng = len(groups)
                        for gi in range(ng + 2):
                            if gi < ng:
                                banks = [2 * (gi % 2), 2 * (gi % 2) + 1]
                                S.prewait("pe", [mp[1] for mp in maps] + [mp[3] for mp in maps], [b_pS[bk] for bk in banks])
                                for idx, (kt, m) in enumerate(groups[gi]):
                                    q_t, bq, k_t, bk_, nr = maps[m][:5]
                                    p0 = maps[m][5] if len(maps[m]) > 5 else 0
                                    si = banks[idx]
                                    tp_ = (p0, 0) if p0 == 96 else None
                                    PE(lambda: nc.tensor.matmul(pS[si][:, :], lhsT=k_t[p0:p0 + nr, kt * 128:(kt + 1) * 128],
                                                                rhs=q_t[p0:p0 + nr, qt * 512:(qt + 1) * 512], start=True, stop=True,
                                                                tile_position=tp_),
                                       [bq, bk_], [b_pS[si]], signal=(idx == 1))
                            j = gi - 1
                            if 0 <= j < ng:
                                for idx, (kt, m) in enumerate(groups[j]):
                                    si = 2 * (j % 2) + idx
                                    ei = 2 * (j % 2) + idx
                                    if alibi_slope is None:
                                        A(lambda: nc.scalar.activation(out=E[ei][:, :], in_=pS[si][:, :], func=AF.Exp),
                                          [b_pS[si]], [b_E[ei]])
                                    else:
                                        k0, q0 = kt * 128, qt * 512
                                        if k0 + 128 <= q0:
                                            tab, sc_, bias = alibi[:, 0, :], -alibi_slope, -alibi_slope * (q0 - k0)
                                        elif k0 >= q0 + 512:
                                            tab, sc_, bias = alibi[:, 0, :], alibi_slope, -alibi_slope * (k0 - q0)
                                        else:
                                            tab, sc_, bias = alibi[:, 1 + (k0 - q0) // 128, :], -alibi_slope, 0.0
                                        sbi = state["sbi"] % 3
                                        state["sbi"] += 1
                                        V(lambda: nc.vector.scalar_tensor_tensor(out=Sb[sbi][:, :], in0=tab, scalar=sc_,
                                                                                 in1=pS[si][:, :], op0=ALU.mult, op1=ALU.add),
                                          [b_al, b_pS[si]], [b_Sb[sbi]])
                                        A(lambda: nc.scalar.activation(out=E[ei][:, :], in_=Sb[sbi][:, :], func=AF.Exp, bias=bias),
                                          [b_Sb[sbi]], [b_E[ei]])
                            j = gi - 2
                            if 0 <= j < ng:
                                eis_ = [2 * (j % 2), 2 * (j % 2) + 1]
                                S.prewait("pe", [b_E[e] for e in eis_] + [b_Vg], [b_pOT[m] for (_, m) in groups[j]])
                                for idx, (kt, m) in enumerate(groups[j]):
                                    ei = eis_[idx]
                                    PE(lambda: nc.tensor.matmul(pOT[m][:, :], lhsT=Vg[:, kt, vcol:vcol + 128], rhs=E[ei][:, :],
                                                                start=(kt == 0), stop=(kt == NSUB - 1)),
                                       [b_E[ei], b_Vg], [b_pOT[m]], signal=(idx == 1))
                            if gi == 4 and pend[0] is not None:
                                pend[0]()
                                pend[0] = None
                        for m in range(nm):
                            if alibi_slope is not None:
                                A(lambda: nc.scalar.activation(out=otS[m][:, :], in_=pOT[m][0:65, :], func=AF.Identity),
                                  [b_pOT[m]], [b_otS[m]])
                            else:
                                V(lambda: nc.vector.tensor_copy(otS[m][:, :], pOT[m][0:65, :]), [b_pOT[m]], [b_otS[m]])
                        pend[0] = (lambda qt=qt: finalize(qt, nm, ycol))
                    pend[0]()
                    pend[0] = None

                def finalize(qt, nm, ycol):
                    if True:
                        for m in range(nm):
                            for jj in range(4):
                                PE(lambda: nc.tensor.transpose(pO[m][:, jj * 65:(jj + 1) * 65], otS[m][0:65, jj * 128:(jj + 1) * 128],
                                                               ident_f[0:65, 0:65]), [b_otS[m], b_const], [b_pO[m]], signal=(jj == 3))
                        O1 = pO[0][:, 0:260].rearrange("p (j d) -> p j d", d=65)
                        ydst = y_res[:, qt * 4:(qt + 1) * 4, ycol:ycol + 64]
                        by = b_y[qt * 4:(qt + 1) * 4]
                        V(lambda: nc.vector.reciprocal(out=rc[:, 0:4], in_=O1[:, :, 64]), [b_pO[0]], [b_rc])
                        if nm == 1:
                            V(lambda: nc.vector.tensor_tensor(out=ydst, in0=O1[:, :, 0:64],
                                                              in1=rc[:, 0:4].unsqueeze(2).to_broadcast([128, 4, 64]),
                                                              op=ALU.mult), [b_pO[0], b_rc], by)
                        else:
                            O2 = pO[1][:, 0:260].rearrange("p (j d) -> p j d", d=65)
                            V(lambda: nc.vector.reciprocal(out=rc[:, 4:8], in_=O2[:, :, 64]), [b_pO[1]], [b_rc])
                            V(lambda: nc.vector.tensor_scalar(out=rc[:, 4:8], in0=rc[:, 4:8], scalar1=lam[:, 5:6], scalar2=None,
                                                              op0=ALU.mult), [b_rc, b_lam], [b_rc])
                            V(lambda: nc.vector.tensor_tensor(out=o1[:, :, :], in0=O1[:, :, 0:64],
                                                              in1=rc[:, 0:4].unsqueeze(2).to_broadcast([128, 4, 64]),
                                                              op=ALU.mult), [b_pO[0], b_rc], [b_o1])
                            V(lambda: nc.vector.tensor_tensor(out=o2[:, :, :], in0=O2[:, :, 0:64],
                                                              in1=rc[:, 4:8].unsqueeze(2).to_broadcast([128, 4, 64]),
                                                              op=ALU.mult), [b_pO[1], b_rc], [b_o2])
                            G(lambda: nc.gpsimd.tensor_tensor(out=o1[:, :, :], in0=o1[:, :, :], in1=o2[:, :, :], op=ALU.add),
                              [b_o1, b_o2], [b_o1])
                            G(lambda: nc.gpsimd.tensor_tensor(out=osq[:, :, :], in0=o1[:, :, :], in1=o1[:, :, :], op=ALU.mult),
                              [b_o1], [b_osq])
                            V(lambda: nc.vector.tensor_reduce(out=rc[:, 8:12], in_=osq[:, :, :], axis=AX.X, op=ALU.add),
                              [b_osq], [b_rc])
                            A(lambda: nc.scalar.activation(out=rc[:, 8:12], in_=rc[:, 8:12], func=AF.Ln, scale=1.0 / 64,
                                                           bias=eps6[:, 0:1]), [b_rc, b_const], [b_rc])
                            A(lambda: nc.scalar.activation(out=rc[:, 12:16], in_=rc[:, 8:12], func=AF.Exp, scale=-0.5),
                              [b_rc], [b_rc])
                            V(lambda: nc.vector.tensor_tensor(out=o2[:, :, :], in0=o1[:, :, :],
                                                              in1=rc[:, 12:16].unsqueeze(2).to_broadcast([128, 4, 64]),
                                                              op=ALU.mult), [b_o1, b_rc], [b_o2])
                            V(lambda: nc.vector.tensor_tensor(out=ydst, in0=o2[:, :, :],
                                                              in1=gA_bc[:, :].unsqueeze(1).to_broadcast([128, 4, 64]),
                                                              op=ALU.mult), [b_o2, b_gA], by)

                load_V(0)
                for h in range(4):
                    state["qi"] += 1
                    i_ = state["qi"] % 4
                    g0 = 2 * (h % 2)
                    S.dma("sp", qT[i_][32 * g0:32 * g0 + 64, :], QKT[h // 2, 32 * g0:32 * g0 + 64, :], writes=[b_qT[i_]])
                    S.dma("sp", kT[i_][32 * g0:32 * g0 + 64, :], QKT[2 + h // 2, 32 * g0:32 * g0 + 64, :], writes=[b_kT[i_]])
                    maps = [(qT[i_], b_qT[i_], kT[i_], b_kT[i_], 32, 32 * (g0 + c_)) for c_ in range(2)]
                    job(maps, h * VP, h * 64, alibi_slope=SLOPES_A[h])
                load_V(1)
                for h in range(4):
                    state["qi"] += 1
                    q_t, bq = load_map(4 + h, 0, 96, True)
                    k_t, bk_ = load_map(8 + h, 0, 96, False)
                    job([(q_t, bq, k_t, bk_, 96)], h * VP, 256 + h * 64)
                load_V(2)
                for h in range(4):
                    state["qi"] += 1
                    q_t, bq = load_map(12 + h // 2, (h % 2) * 64, 64, True)
                    if h % 2 == 0:
                        k_t, bk_ = load_map(14, (h // 2) * 64, 64, False)
                    job([(q_t, bq, k_t, bk_, 64)], (h // 2) * VP, 512 + h * 64)
                S.barrier()

        def phase_D(l, y_res, b_y):
            with ExitStack() as st:
                tokD = sb(st, "tokD", [128, 8, 512], BF16)
                Vsh = sb(st, "Vsh", [128, 9, VW], BF16)
                QTd = sb(st, "QTd", [128, 2, 1024], BF16)
                KTd = sb(st, "KTd", [128, 2, 1024 + 128], BF16)
                dtab = sb(st, "dtab", [128, 2, 512], F32)
                Sb = [sb(st, f"dSb{i}", [128, 512], F32) for i in range(4)]
                E = [sb(st, f"dE{i}", [128, 512], BF16) for i in range(4)]
                ost = [sb(st, f"ost{i}", [128, 4, 260], F32) for i in range(2)]
                acc_l = [sb(st, f"acc{i}", [128, 260], F32) for i in range(2)]
                od_l = [[sb(st, f"od{j}_{i}", [128, 260], F32) for i in range(3)] for j in range(2)]
                rcd_l = [sb(st, f"rcd{i}", [128, 4], F32) for i in range(2)]
                pT = [pbank(st, f"dT{i}") for i in range(2)]
                pS = [pbank(st, f"dS{i}") for i in range(4)]
                pO = [pbank(st, f"dO{i}") for i in range(2)]
                b_tok, b_Vsh, b_QT, b_KT, b_dt = Buf(), Buf(), Buf(), Buf(), Buf()
                b_Sb = [Buf() for _ in range(4)]
                b_E = [Buf() for _ in range(4)]
                b_ost = [Buf(), Buf()]
                b_pT = [PB(), PB()]
                b_pS = [PB() for _ in range(4)]
                b_pO = [PB(), PB()]
                b_acc_l, b_od_l, b_rcd_l = [Buf(), Buf()], [[Buf() for _ in range(3)] for _ in range(2)], [Buf(), Buf()]
                S.dma("sp", dtab[:], I["dtab"], writes=[b_dt])
                cnt = {"s": 0, "e": 0, "o": 0}
                for bi, d in enumerate(DILS):
                    Ltot = SEQ // d
                    nseg = max(1, Ltot // 1024)
                    for r in range(d):
                        for seg in range(nseg):
                            Lc = min(Ltot, 1024)
                            i0 = seg * 1024
                            nt = Lc // 128
                            Lc_ = min(Ltot, 1024)
                            G(lambda: nc.gpsimd.memset(KTd[:, :, 0:64], 0.0), [], [b_KT])
                            G(lambda: nc.gpsimd.memset(KTd[:, :, 64 + Lc_:128 + Lc_], 0.0), [], [b_KT])
                            if seg * 1024 - 64 < 0:
                                G(lambda: nc.gpsimd.memset(Vsh[0:64, 0, :], 0.0), [], [b_Vsh])
                            if seg * 1024 + Lc_ + 64 > Ltot:
                                G(lambda: nc.gpsimd.memset(Vsh[64:128, Lc_ // 128, :], 0.0), [], [b_Vsh])
                            base = r + d * i0
                            src = DTOK[base: base + d * (Lc - 1) + 1: d, :]
                            S.dma("sp", tokD[:, 0:nt, :], src[:, 0:512].rearrange("(t p) c -> p t c", p=128), writes=[b_tok])
                            lo = i0 - 64
                            hi = i0 + Lc + 64
                            lo_c, hi_c = max(lo, 0), min(hi, Ltot)
                            u0 = lo_c - lo
                            n_rows = hi_c - lo_c
                            pos = 0
                            while pos < n_rows:
                                u = u0 + pos
                                tj, pj = u // 128, u % 128
                                take = min(128 - pj, n_rows - pos)
                                if pj == 0 and take == 128:
                                    nfull = (n_rows - pos) // 128
                                    t_first = r + d * (lo_c + pos)
                                    srcv = DTOK[t_first: t_first + d * (128 * nfull - 1) + 1: d, 512:512 + VW]
                                    S.dma("sp", Vsh[:, tj:tj + nfull, :], srcv.rearrange("(t p) c -> p t c", p=128),
                                          writes=[b_Vsh])
                                    pos += 128 * nfull
                                else:
                                    t_first = r + d * (lo_c + pos)
                                    srcv = DTOK[t_first: t_first + d * (take - 1) + 1: d, 512:512 + VW]
                                    S.dma("sp", Vsh[pj:pj + take, tj, :], srcv, writes=[b_Vsh])
                                    pos += take
                            for t in range(nt):
                                pTb = pT[t % 2][:, :].bitcast(BF16)
                                for j in range(4):
                                    PE(lambda: nc.tensor.transpose(pTb[:, j * 128:(j + 1) * 128], tokD[:, t, j * 128:(j + 1) * 128],
                                                                   ident_b[:]), [b_tok, b_const], [b_pT[t % 2]], signal=(j == 3))
                                V(lambda: nc.vector.tensor_copy(QTd[:, :, t * 128:(t + 1) * 128],
                                                                pTb[:, 0:256].rearrange("p (c t) -> p c t", t=128)),
                                  [b_pT[t % 2]], [b_QT])
                                A(lambda: nc.scalar.activation(func=AF.Identity, out=KTd[:, :, 64 + t * 128: 64 + (t + 1) * 128],
                                                         in_=pTb[:, 256:512].rearrange("p (c t) -> p c t", t=128)),
                                  [b_pT[t % 2]], [b_KT])
                            if nseg > 1:
                                for side in range(2):
                                    hs = i0 - 64 if side == 0 else i0 + Lc
                                    if hs < 0 or hs >= Ltot:
                                        continue
                                    t_first = r + d * hs
                                    srck = DTOK[t_first: t_first + d * 63 + 1: d, 256:512]
                                    S.dma("sp", tokD[0:64, 0, 0:256], srck, writes=[b_tok])
                                    pTb = pT[0][:, :].bitcast(BF16)
                                    for j in range(2):
                                        PE(lambda: nc.tensor.transpose(pTb[:, j * 128: j * 128 + 64],
                                                                       tokD[0:64, 0, j * 128:(j + 1) * 128], ident_b[0:64, 0:64]),
                                           [b_tok, b_const], [b_pT[0]], signal=(j == 1))
                                    col = 0 if side == 0 else 64 + Lc
                                    V(lambda: nc.vector.tensor_copy(
                                        KTd[:, :, col:col + 64],
                                        pTb[:, 0:256].rearrange("p (c t) -> p c t", t=128)[:, :, 0:64]), [b_pT[0]], [b_KT])
                            steps = [(g0, h, ab) for g0 in range(0, nt, 4) for h in range(4) for ab in range(2)]
                            n = len(steps)
                            sis, eis = [0] * n, [0] * n
                            LS, LP = 1, 2
                            for i in range(n + LP):
                                if i < n:
                                    g0, h, ab = steps[i]
                                    ng = min(4, nt - g0)
                                    ch, pr = h // 2, (h % 2) * 64
                                    si = cnt["s"] % 4
                                    cnt["s"] += 1
                                    sis[i] = si
                                    for jj in range(ng):
                                        qc = (g0 + jj) * 128
                                        kc0 = qc + ab * 128
                                        PE(lambda: nc.tensor.matmul(pS[si][:, jj * 128:(jj + 1) * 128],
                                                                    lhsT=KTd[pr:pr + 64, ch, kc0:kc0 + 128],
                                                                    rhs=QTd[pr:pr + 64, ch, qc:qc + 128], start=True, stop=True),
                                           [b_KT, b_QT], [b_pS[si]], signal=(jj == ng - 1))
                                j = i - LS
                                if 0 <= j < n:
                                    g0, h, ab = steps[j]
                                    ng = min(4, nt - g0)
                                    w = ng * 128
                                    si = sis[j]
                                    ei = cnt["e"] % 4
                                    cnt["e"] += 1
                                    eis[j] = ei
                                    V(lambda: nc.vector.scalar_tensor_tensor(out=Sb[ei][:, 0:w], in0=dtab[:, ab, 0:w],
                                                                             scalar=-SLOPES_D[h] * d, in1=pS[si][:, 0:w],
                                                                             op0=ALU.mult, op1=ALU.add),
                                      [b_dt, b_pS[si]], [b_Sb[ei]])
                                    A(lambda: nc.scalar.activation(out=E[ei][:, 0:w], in_=Sb[ei][:, 0:w], func=AF.Exp),
                                      [b_Sb[ei]], [b_E[ei]])
                                j = i - LP
                                if 0 <= j < n:
                                    g0, h, ab = steps[j]
                                    ng = min(4, nt - g0)
                                    ei = eis[j]
                                    oi = (g0 // 4) % 2
                                    for jj in range(ng):
                                        PE(lambda: nc.tensor.matmul(pO[h % 2][:, jj * 65:(jj + 1) * 65],
                                                                    lhsT=E[ei][:, jj * 128:(jj + 1) * 128],
                                                                    rhs=Vsh[:, g0 + jj + ab, h * VP:h * VP + 65],
                                                                    start=(ab == 0 and jj == 0), stop=(ab == 1),
                                                                    skip_group_check=True),
                                           [b_E[ei], b_Vsh], [b_pO[h % 2]], signal=(jj == ng - 1))
                                    if ab == 1:
                                        V(lambda: nc.vector.tensor_copy(ost[oi][:, 0:ng, h * 65:(h + 1) * 65],
                                                                        pO[h % 2][:, 0:ng * 65].rearrange("p (j d) -> p j d", d=65)),
                                          [b_pO[h % 2]], [b_ost[oi]])
                                        if h == 3:
                                            for jj in range(ng):
                                                t_first = r + d * (i0 + (g0 + jj) * 128)
                                                S.dma("sp", OD[bi, t_first: t_first + d * 127 + 1: d, :], ost[oi][:, jj, :],
                                                      reads=[b_ost[oi]])
                S.barrier()
                for sg in range(NSUB):
                    rows = slice(sg * 128, (sg + 1) * 128)
                    acc, od, rcd = acc_l[sg % 2], od_l[sg % 2], rcd_l[sg % 2]
                    b_acc, b_od, b_rcd = b_acc_l[sg % 2], b_od_l[sg % 2], b_rcd_l[sg % 2]
                    for bi in range(3):
                        S.dma("sp", od[bi][:], OD[bi, rows, :], writes=[b_od[bi]])
                    V(lambda: nc.vector.tensor_tensor(out=acc[:], in0=od[0][:], in1=od[1][:], op=ALU.add),
                      [b_od[0], b_od[1]], [b_acc])
                    V(lambda: nc.vector.tensor_tensor(out=acc[:], in0=acc[:], in1=od[2][:], op=ALU.add), [b_acc, b_od[2]], [b_acc])
                    a3 = acc[:, :].rearrange("p (h d) -> p h d", d=65)
                    V(lambda: nc.vector.reciprocal(out=rcd[:, 0:4], in_=a3[:, :, 64]), [b_acc], [b_rcd])
                    V(lambda: nc.vector.tensor_tensor(out=y_res[:, sg, 768:1024].rearrange("p (h d) -> p h d", d=64),
                                                      in0=a3[:, :, 0:64], in1=rcd[:, 0:4].unsqueeze(2).to_broadcast([128, 4, 64]),
                                                      op=ALU.mult), [b_acc, b_rcd], [b_y[sg]])
                S.barrier()

        def layer_norm(v, bv, dst, bdst, g_bc, b_bc, btab, stats, mv, bst, eng2):
            v4 = v.rearrange("p (c f) -> p c f", f=256)
            for c_ in range(4):
                V(lambda: nc.vector.bn_stats(out=stats[:, c_, :], in_=v4[:, c_, :]), [bv], [bst])
            V(lambda: nc.vector.bn_aggr(out=mv[:, 0:2], in_=stats[:, :, :].rearrange("p c f -> p (c f)")), [bst], [bst])
            A(lambda: nc.scalar.activation(out=mv[:, 2:3], in_=mv[:, 1:2], func=AF.Ln, bias=eps5[:, 0:1]), [bst, b_const], [bst])
            A(lambda: nc.scalar.activation(out=mv[:, 3:4], in_=mv[:, 2:3], func=AF.Exp, scale=-0.5), [bst], [bst])
            V(lambda: nc.vector.scalar_tensor_tensor(out=mv[:, 4:5], in0=mv[:, 0:1], scalar=-1.0, in1=mv[:, 3:4], op0=ALU.mult,
                                                     op1=ALU.mult), [bst], [bst])
            A(lambda: nc.scalar.activation(out=v, in_=v, func=AF.Identity, scale=mv[:, 3:4], bias=mv[:, 4:5]), [bv, bst], [bv])
            S.op(eng2, lambda: (nc.gpsimd if eng2 == "pool" else nc.vector).tensor_tensor(out=v, in0=v, in1=g_bc, op=ALU.mult),
                 [bv, btab], [bv])
            S.op(eng2, lambda: (nc.gpsimd if eng2 == "pool" else nc.vector).tensor_tensor(out=dst, in0=v, in1=b_bc, op=ALU.add),
                 [bv, btab], [bdst])

        def phase_O1(l, xsrc, y_res, b_y):
            with ExitStack() as st:
                w_o = sb(st, "w_o", [128, 8, D], BF16)
                gA = sb(st, "gA", [128, D], F32)
                lng = sb(st, "lng", [128, D], F32)
                lnb = sb(st, "lnb", [128, D], F32)
                xs = [sb(st, f"oxs{i}", [128, D], F32) for i in range(2)]
                yT = [sb(st, f"yT{i}", [128, 8, 128], BF16) for i in range(2)]
                v = [sb(st, f"ov{i}", [128, D], F32) for i in range(2)]
                x1 = [sb(st, f"ox1{i}", [128, D], F32) for i in range(2)]
                h2 = [sb(st, f"oh2{i}", [128, 8, 512], BF16) for i in range(2)]
                stats_l = [sb(st, f"ostats{i}", [128, 4, 6], F32) for i in range(2)]
                mv_l = [sb(st, f"omv{i}", [128, 8], F32) for i in range(2)]
                pb = [pbank(st, f"po{i}") for i in range(8)]
                bp = [PB() for _ in range(8)]
                b_w, b_tab, b_stt_l = Buf(), Buf(), [Buf(), Buf()]
                b_xs, b_yT, b_v, b_x1, b_h2 = ([Buf(), Buf()] for _ in range(5))
                wsrc = I["w_o"][l].rearrange("(kc p) n -> p kc n", p=128)
                for kc in range(8):
                    S.dma("pool", w_o[:, kc, :], wsrc[:, kc, :], writes=[b_w])
                S.dma("sp", gA[:], GB[l, 0], writes=[b_tab])
                S.dma("sp", lng[:], I["ln_attn_g"][l, :].partition_broadcast(128), writes=[b_tab])
                S.dma("sp", lnb[:], I["ln_attn_b"][l, :].partition_broadcast(128), writes=[b_tab])
                def o1_sub(sg):
                    T, s = sg // 4, sg % 4
                    i2 = sg % 2
                    tsl = slice(s * 128, (s + 1) * 128)
                    S.dma("sp", xs[i2][:], xsrc[sg * 128:(sg + 1) * 128, :], writes=[b_xs[i2]])
                    tb = pb[0 + i2][:, :].bitcast(BF16)
                    for kc in range(8):
                        PE(lambda: nc.tensor.transpose(tb[:, kc * 128:(kc + 1) * 128], y_res[:, sg, kc * 128:(kc + 1) * 128],
                                                       ident_b[:]), [b_y[sg], b_const], [bp[i2]], signal=(kc == 7))
                    yield
                    V(lambda: nc.vector.tensor_copy(yT[i2][:, :, :], tb[:, 0:1024].rearrange("p (c t) -> p c t", t=128)),
                      [bp[i2]], [b_yT[i2]])
                    for hf in range(2):
                        yield
                        bk = 2 + 2 * i2 + hf
                        for kc in range(8):
                            PE(lambda: nc.tensor.matmul(pb[bk][:, :], lhsT=yT[i2][:, kc, :], rhs=w_o[:, kc, hf * 512:(hf + 1) * 512],
                                                        start=(kc == 0), stop=(kc == 7)), [b_yT[i2], b_w], [bp[bk]], signal=(kc == 7))
                        hs = slice(hf * 512, (hf + 1) * 512)
                        V(lambda: nc.vector.tensor_tensor(out=v[i2][:, hs], in0=pb[bk][:, :], in1=gA[:, hs], op=ALU.mult),
                          [bp[bk], b_tab], [b_v[i2]])
                    yield
                    V(lambda: nc.vector.scalar_tensor_tensor(out=v[i2][:, :], in0=xs[i2][:, :], scalar=ALPHA,
                                                             in1=v[i2][:, :], op0=ALU.mult, op1=ALU.add),
                      [b_xs[i2], b_v[i2]], [b_v[i2]])
                    yield
                    layer_norm(v[i2][:, :], b_v[i2], x1[i2][:, :], b_x1[i2], lng[:, :], lnb[:, :], b_tab, stats_l[i2], mv_l[i2], b_stt_l[i2], "pool")
                    yield
                    S.dma("sp", X1[sg * 128:(sg + 1) * 128, :], x1[i2][:], reads=[b_x1[i2]])
                    for hf in range(2):
                        yield
                        bk = 6 + hf
                        for cc in range(4):
                            kc = hf * 4 + cc
                            PE(lambda: nc.tensor.transpose(pb[bk][:, cc * 128:(cc + 1) * 128], x1[i2][:, kc * 128:(kc + 1) * 128],
                                                           ident_f[:]), [b_x1[i2], b_const], [bp[bk]], signal=(cc == 3))
                        for cc in range(4):
                            kc = hf * 4 + cc
                            if cc % 2 == 0:
                                A(lambda: nc.scalar.activation(out=h2[T % 2][:, kc, tsl], in_=pb[bk][:, cc * 128:(cc + 1) * 128],
                                                               func=AF.Identity, scale=modT[:, l, 32 + kc:33 + kc],
                                                               bias=modT[:, l, 24 + kc:25 + kc]), [bp[bk], b_modT], [b_h2[T % 2]])
                            else:
                                V(lambda: nc.vector.tensor_scalar(out=h2[T % 2][:, kc, tsl], in0=pb[bk][:, cc * 128:(cc + 1) * 128],
                                                                  scalar1=modT[:, l, 32 + kc:33 + kc],
                                                                  scalar2=modT[:, l, 24 + kc:25 + kc], op0=ALU.mult, op1=ALU.add),
                                  [bp[bk], b_modT], [b_h2[T % 2]])
                    if s == 3:
                        S.dma("sp", H2T[:, :, T * 512:(T + 1) * 512].rearrange("c r t -> r c t"), h2[T % 2][:, :, :],
                              reads=[b_h2[T % 2]])
                for sg0 in range(0, NSUB, 2):
                    gens = [o1_sub(sg0), o1_sub(sg0 + 1)]
                    while gens:
                        for g_ in list(gens):
                            try:
                                next(g_)
                            except StopIteration:
                                gens.remove(g_)
                S.barrier()

        def phase_O2(l, dst):
            with ExitStack() as st:
                w_up = sb(st, "w_up", [128, 8, HID], BF16)
                w_dn = sb(st, "w_dn", [128, 32, D], BF16)
                gM = sb(st, "gM", [128, D], F32)
                lng = sb(st, "lng2", [128, D], F32)
                lnb = sb(st, "lnb2", [128, D], F32)
                h2 = [sb(st, f"mh2{i}", [128, 8, 512], BF16) for i in range(1)]
                uT = sb(st, "uT", [128, 32, 512], BF16)
                rr = [sb(st, f"rr{i}", [128, 512], F32) for i in range(3)]
                x1 = [sb(st, f"mx1{i}", [128, D], F32) for i in range(2)]
                v = [sb(st, f"mv{i}", [128, D], F32) for i in range(2)]
                stats = sb(st, "mstats", [128, 4, 6], F32)
                mv = sb(st, "mmv", [128, 8], F32)
                pb = [pbank(st, f"pm{i}") for i in range(8)]
                bp = [PB() for _ in range(8)]
                b_wu, b_wd, b_tab, b_stt, b_uT = Buf(), Buf(), Buf(), Buf(), Buf()
                b_h2, b_x1, b_v = ([Buf(), Buf()] for _ in range(3))
                b_rr = [Buf() for _ in range(3)]
                usrc = I["w_up"][l].rearrange("(kc p) n -> p kc n", p=128)
                for kc in range(8):
                    S.dma("pool", w_up[:, kc, :], usrc[:, kc, :], writes=[b_wu])
                dsrc = I["w_down"][l].rearrange("(kc p) n -> p kc n", p=128)
                for k4 in range(8):
                    S.dma("pool", w_dn[:, k4 * 4:(k4 + 1) * 4, :], dsrc[:, k4 * 4:(k4 + 1) * 4, :], writes=[b_wd])
                S.dma("sp", gM[:], GB[l, 1], writes=[b_tab])
                S.dma("sp", lng[:], I["ln_mlp_g"][l, :].partition_broadcast(128), writes=[b_tab])
                S.dma("sp", lnb[:], I["ln_mlp_b"][l, :].partition_broadcast(128), writes=[b_tab])
                ri = 0
                for T in range(8):
                    h_t, bh = h2[0], b_h2[0]
                    S.dma("sp", h_t[:, :, :], H2T[:, :, T * 512:(T + 1) * 512].rearrange("c r t -> r c t"), writes=[bh])
                    for hc in range(32):
                        bk = hc % 4
                        for kc in range(8):
                            PE(lambda: nc.tensor.matmul(pb[bk][:, :], lhsT=w_up[:, kc, hc * 128:(hc + 1) * 128], rhs=h_t[:, kc, :],
                                                        start=(kc == 0), stop=(kc == 7)), [b_wu, bh], [bp[bk]], signal=(kc == 7))
                        r_, br = rr[ri % 3], b_rr[ri % 3]
                        ri += 1
                        A(lambda: nc.scalar.activation(out=r_[:, :], in_=pb[bk][:, :], func=AF.Relu), [bp[bk]], [br])
                        if hc % 2 == 0:
                            V(lambda: nc.vector.tensor_tensor(out=uT[:, hc, :], in0=r_[:, :], in1=r_[:, :], op=ALU.mult), [br], [b_uT])
                        else:
                            G(lambda: nc.gpsimd.tensor_tensor(out=uT[:, hc, :], in0=r_[:, :], in1=r_[:, :], op=ALU.mult), [br], [b_uT])
                    for s in range(4):
                        sg = T * 4 + s
                        i2 = sg % 2
                        rows = slice(sg * 128, (sg + 1) * 128)
                        S.dma("sp", x1[i2][:], X1[rows, :], writes=[b_x1[i2]])
                        for hf in range(2):
                            bk = 4 + 2 * i2 + hf
                            for hc in range(32):
                                PE(lambda: nc.tensor.matmul(pb[bk][:, :], lhsT=uT[:, hc, s * 128:(s + 1) * 128],
                                                            rhs=w_dn[:, hc, hf * 512:(hf + 1) * 512], start=(hc == 0), stop=(hc == 31)),
                                   [b_uT, b_wd], [bp[bk]], signal=(hc == 31))
                            hs = slice(hf * 512, (hf + 1) * 512)
                            V(lambda: nc.vector.tensor_tensor(out=v[i2][:, hs], in0=pb[bk][:, :], in1=gM[:, hs], op=ALU.mult),
                              [bp[bk], b_tab], [b_v[i2]])
                        V(lambda: nc.vector.scalar_tensor_tensor(out=v[i2][:, :], in0=x1[i2][:, :], scalar=ALPHA, in1=v[i2][:, :],
                                                                 op0=ALU.mult, op1=ALU.add), [b_x1[i2], b_v[i2]], [b_v[i2]])
                        layer_norm(v[i2][:, :], b_v[i2], v[i2][:, :], b_v[i2], lng[:, :], lnb[:, :], b_tab, stats, mv, b_stt, "pool")
                        S.dma("sp", dst[rows, :], v[i2][:], reads=[b_v[i2]])
                S.barrier()

        for l in range(nlayers):
            if "stop0" in dbg:
                break
            xsrc = I["x"] if l == 0 else XN
            phase_P(l, xsrc)
            if "stopP" in dbg:
                break
            with ExitStack() as lst:
                y_res = sb(lst, f"y_res{l}", [128, NSUB, D], BF16)
                b_y = [Buf(f"y{i}") for i in range(NSUB)]
                phase_att(l, y_res, b_y)
                phase_D(l, y_res, b_y)
                if YDBG is not None and l == 0:
                    for sg in range(NSUB):
                        S.dma("sp", YDBG[sg * 128:(sg + 1) * 128, :], y_res[:, sg, :], reads=[b_y[sg]])
                    S.barrier()
                if "stopA" in dbg:
                    break
                phase_O1(l, xsrc, y_res, b_y)
            phase_O2(l, out if l == nlayers - 1 else XN)
        S.barrier()
        print("ops", S.n_ops, "dmas", S.n_dma, "sems", S.nsem)
    return nc


DBG = {}
_CONSTS = None


def make_in_maps(inputs):
    global _CONSTS
    if _CONSTS is None:
        _CONSTS = _host_consts()
    shared = {}
    for n in W_NAMES:
        a = np.ascontiguousarray(np.asarray(inputs[n], dtype=np.float32))
        shared[n] = a.reshape(W_SHAPES[n])
    for n, a in _CONSTS.items():
        shared["k_" + n] = a
    x = np.asarray(inputs["x"], dtype=np.float32)
    c = np.asarray(inputs["c"], dtype=np.float32)
    maps = []
    for b in range(8):
        m = dict(shared)
        m["x"] = np.ascontiguousarray(x[b])
        m["c"] = np.ascontiguousarray(c[b].reshape(8, 128).T)
        maps.append(m)
    return maps


def kernel(**inputs):
    nc = build()
    in_maps = make_in_maps(inputs)
    res = run_bass_kernel_spmd(nc, in_maps, core_ids=list(range(8)))
    return np.stack([np.asarray(r["out"]) for r in res.results], axis=0).astype(np.float32)
```

```python
import math
from contextlib import ExitStack
import numpy as np
import concourse.bass as bass
import concourse.mybir as mybir
from concourse.bass_utils import run_bass_kernel_spmd

F32 = mybir.dt.float32
BF16 = mybir.dt.bfloat16
AF = mybir.ActivationFunctionType
ALU = mybir.AluOpType
AX = mybir.AxisListType

SEQ = 4096
D = 1024
NSUB = SEQ // 128
HID = 4096
INC = 2720
ALPHA = 4 ** 0.25
SLOPES_A = [2.0 ** -1, 2.0 ** -3, 2.0 ** -5, 2.0 ** -7]
SLOPES_D = [2.0 ** -2, 2.0 ** -4, 2.0 ** -6, 2.0 ** -8]
DILS = [1, 4, 16]
SC_A = 32 ** -0.5
SC_B = 96 ** -0.5
SC_C = 0.125
SC_D = 0.125
VP = 66
VW = 4 * VP


class Buf:
    __slots__ = ("name", "w", "r", "excl")

    def __init__(self, name="", excl=False):
        self.name = name
        self.w = None
        self.r = []
        self.excl = excl


def PB(name=""):
    return Buf(name, excl=True)


class Tok:
    __slots__ = ("eng", "sem", "val")

    def __init__(self, eng, sem=None, val=None):
        self.eng = eng
        self.sem = sem
        self.val = val


class Sched:
    EPOCH = 20000
    NDMA = 8

    def __init__(self, nc, stack):
        self.nc = nc
        self.stack = stack
        self.engs = {"pe": nc.tensor, "act": nc.scalar, "dve": nc.vector,
                     "pool": nc.gpsimd, "sp": nc.sync}
        self.count = {e: 0 for e in self.engs}
        self.cursem = {}
        self.pending = {e: [] for e in self.engs}
        self.waited = {e: {} for e in self.engs}
        self.nsem = 0
        for e in self.engs:
            self._new_epoch(e)
        self.dma_sems, self.dma_cnt, self.dma_last, self.dma_i = {}, {}, {}, {}
        for q in ("sp", "act", "pool"):
            self.dma_sems[q] = [self._sem(f"dma_{q}_{i}") for i in range(self.NDMA)]
            self.dma_cnt[q] = [0] * self.NDMA
            self.dma_last[q] = [None] * self.NDMA
            self.dma_i[q] = 0
        self.n_ops = {e: 0 for e in self.engs}
        self.n_dma = 0

    def _sem(self, name):
        self.nsem += 1
        return self.stack.enter_context(self.nc.semaphore(name))

    def _new_epoch(self, e):
        self.cursem[e] = self._sem(f"s_{e}_{self.nsem}")
        self.count[e] = 0

    def _wait(self, eng, tok):
        if tok is None:
            return
        if tok.sem is None:
            raise RuntimeError(f"dependency on unsignalled op on {tok.eng}")
        key = id(tok.sem)
        w = self.waited[eng]
        if w.get(key, 0) >= tok.val:
            return
        w[key] = tok.val
        self.engs[eng].wait_ge(tok.sem, tok.val)

    def _deps(self, eng, reads, writes):
        for b in reads:
            t = b.w
            if t is not None and not (t.eng == eng and eng == "pe"):
                self._wait(eng, t)
        for b in writes:
            t = b.w
            if t is not None and t.eng != eng:
                self._wait(eng, t)
            for t in b.r:
                if t.eng != eng:
                    self._wait(eng, t)

    def _record(self, tok, reads, writes):
        for b in reads:
            b.r.append(tok)
            if len(b.r) > 16:
                last = {}
                for t in b.r:
                    last[(t.eng, id(t.sem))] = t
                b.r = list(last.values())
        for b in writes:
            b.w = tok
            b.r = []

    def op(self, eng, fn, reads=(), writes=(), signal=True):
        if any(b.excl for b in reads):
            writes = list(writes) + [b for b in reads if b.excl]
            reads = [b for b in reads if not b.excl]
        self._deps(eng, reads, writes)
        inst = fn()
        self.n_ops[eng] += 1
        tok = Tok(eng)
        self.pending[eng].append(tok)
        if signal:
            if self.count[eng] >= self.EPOCH:
                self._new_epoch(eng)
            self.count[eng] += 1
            sem = self.cursem[eng]
            inst.then_inc(sem, 1)
            for t in self.pending[eng]:
                t.sem = sem
                t.val = self.count[eng]
            self.pending[eng] = []
        self._record(tok, reads, writes)
        return tok

    def prewait(self, eng, reads=(), writes=()):
        writes = list(writes) + [b for b in reads if b.excl]
        reads = [b for b in reads if not b.excl]
        self._deps(eng, reads, writes)

    def dma(self, q, out, in_, reads=(), writes=(), **kw):
        i = self.dma_i[q]
        self.dma_i[q] = (i + 1) % self.NDMA
        prev = self.dma_last[q][i]
        if prev is not None:
            self._wait(q, prev)
        self._deps(q, reads, writes)
        sem = self.dma_sems[q][i]
        self.dma_cnt[q][i] += 16
        inst = self.engs[q].dma_start(out=out, in_=in_, **kw)
        inst.then_inc(sem, 16)
        tok = Tok("dma_" + q + str(i), sem, self.dma_cnt[q][i])
        self.dma_last[q][i] = tok
        self._record(tok, reads, writes)
        self.n_dma += 1
        return tok

    def barrier(self):
        toks = []
        for e in self.engs:
            if self.pending[e]:
                raise RuntimeError(f"barrier with unsignalled ops on {e}")
            if self.count[e] > 0:
                toks.append(Tok(e, self.cursem[e], self.count[e]))
        for q in self.dma_last:
            for t in self.dma_last[q]:
                if t is not None:
                    toks.append(t)
        for e in self.engs:
            for t in toks:
                if t.eng != e:
                    self._wait(e, t)


def _host_consts():
    c = {}
    c["ident"] = np.eye(128, dtype=np.float32)
    tok = (np.arange(NSUB)[None, :] * 128 + np.arange(128)[:, None]).astype(np.float64)
    freqs = 10000.0 ** (-np.arange(16, dtype=np.float64) / 16)

    def cs(pos):
        ang = pos[..., None].astype(np.float32).astype(np.float64) * freqs.astype(np.float32).astype(np.float64)
        ang = (pos[..., None].astype(np.float32) * freqs.astype(np.float32)).astype(np.float32)
        return np.cos(ang.astype(np.float64)), np.sin(ang.astype(np.float64))

    cp, sp_ = cs(tok)
    rb = np.zeros((128, NSUB, 2, 64), np.float64)
    for i, s in enumerate([SC_B, 1.0]):
        rb[:, :, i, 0:16] = cp * s
        rb[:, :, i, 16:32] = cp * s
        rb[:, :, i, 32:48] = -sp_ * s
        rb[:, :, i, 48:64] = sp_ * s
    c["ropeB"] = rb.astype(np.float32)
    cr, sr = cs(np.floor(tok / 64))
    cc, sc_ = cs(np.mod(tok, 64))
    rc = np.zeros((128, NSUB, 128), np.float64)
    rc[:, :, 0:16] = cr
    rc[:, :, 16:32] = cr
    rc[:, :, 32:48] = cc
    rc[:, :, 48:64] = cc
    rc[:, :, 64:80] = -sr
    rc[:, :, 80:96] = sr
    rc[:, :, 96:112] = -sc_
    rc[:, :, 112:128] = sc_
    c["ropeC"] = rc.astype(np.float32)
    ki = np.arange(128)[:, None].astype(np.float64)
    qi = np.arange(512)[None, :].astype(np.float64)
    al = np.zeros((128, 5, 512), np.float64)
    al[:, 0, :] = qi - ki
    for o in range(4):
        al[:, 1 + o, :] = np.abs(qi - ki - 128 * o)
    c["alibi"] = al.astype(np.float32)
    q128 = (np.arange(512) % 128)[None, :].astype(np.float64)
    dt = np.zeros((128, 2, 512), np.float64)
    da = np.abs(ki - 64 - q128)
    db = np.abs(ki + 64 - q128)
    dt[:, 0, :] = np.where(da <= 64, da, 1.0e6)
    dt[:, 1, :] = np.where(db <= 64, db, 1.0e6)
    c["dtab"] = dt.astype(np.float32)
    return c


W_NAMES = ["w_ada", "b_ada", "w_in", "w_o", "diff_lambda", "diff_subln_g", "mla_q_norm_g", "mla_w_uq",
           "mla_kv_norm_g", "mla_w_ukv", "gqa_q_norm_g", "gqa_k_norm_g", "ln_attn_g", "ln_attn_b",
           "w_up", "w_down", "ln_mlp_g", "ln_mlp_b"]
W_SHAPES = {"w_ada": [2, 1024, 6144], "b_ada": [2, 6144], "w_in": [2, 1024, INC], "w_o": [2, 1024, 1024],
            "diff_lambda": [2, 128], "diff_subln_g": [2, 64], "mla_q_norm_g": [2, 384],
            "mla_w_uq": [2, 384, 384], "mla_kv_norm_g": [2, 256], "mla_w_ukv": [2, 256, 512],
            "gqa_q_norm_g": [2, 64], "gqa_k_norm_g": [2, 64], "ln_attn_g": [2, 1024], "ln_attn_b": [2, 1024],
            "w_up": [2, 1024, HID], "w_down": [2, HID, 1024], "ln_mlp_g": [2, 1024], "ln_mlp_b": [2, 1024]}
C_SHAPES = {"ident": [128, 128], "ropeB": [128, NSUB, 2, 64], "ropeC": [128, NSUB, 128],
            "alibi": [128, 5, 512], "dtab": [128, 2, 512]}


def build(nlayers=2, dbg=()):
    nc = bass.Bass("TRN2", target_bir_lowering=False)
    I = {}
    I["x"] = nc.dram_tensor("x", [SEQ, D], F32, kind="ExternalInput").ap()
    I["c"] = nc.dram_tensor("c", [128, 8], F32, kind="ExternalInput").ap()
    for n in W_NAMES:
        I[n] = nc.dram_tensor(n, W_SHAPES[n], F32, kind="ExternalInput").ap()
    for n in C_SHAPES:
        I[n] = nc.dram_tensor("k_" + n, C_SHAPES[n], F32, kind="ExternalInput").ap()
    out = nc.dram_tensor("out", [SEQ, D], F32, kind="ExternalOutput").ap()

    def scratch(name, shape, dt):
        kind = "ExternalOutput" if name in dbg else "Internal"
        return nc.dram_tensor(name, shape, dt, kind=kind).ap()

    QKT = scratch("QKT", [15, 128, SEQ], BF16)
    VG = scratch("VG", [3, SEQ, VW], BF16)
    DTOK = scratch("DTOK", [SEQ, 512 + VW], BF16)
    OD = scratch("OD", [3, SEQ, 260], F32)
    X1 = scratch("X1", [SEQ, D], F32)
    H2T = scratch("H2T", [8, 128, SEQ], BF16)
    XN = scratch("XN", [SEQ, D], F32)
    GB = scratch("GB", [2, 2, 128, D], F32)
    YDBG = scratch("YDBG", [SEQ, D], BF16) if "YDBG" in dbg else None

    with ExitStack() as top:
        S = Sched(nc, top)

        uid = [0]

        def sb(st, name, shape, dt):
            uid[0] += 1
            return st.enter_context(nc.sbuf_tensor(f"s{uid[0]}_{name}", shape, dt))

        def pbank(st, name):
            uid[0] += 1
            return st.enter_context(nc.psum_tensor(f"p{uid[0]}_{name}", [128, 512], F32))

        def V(fn, reads, writes):
            return S.op("dve", fn, reads, writes)

        def A(fn, reads, writes):
            return S.op("act", fn, reads, writes)

        def G(fn, reads, writes):
            return S.op("pool", fn, reads, writes)

        def PE(fn, reads, writes, signal=True):
            return S.op("pe", fn, reads, writes, signal)

        ident_f = sb(top, "ident_f", [128, 128], F32)
        ident_b = sb(top, "ident_b", [128, 128], BF16)
        modT = sb(top, "modT", [128, 2, 48], F32)
        eps6 = sb(top, "eps6", [128, 1], F32)
        eps5 = sb(top, "eps5", [128, 1], F32)
        b_const = Buf("const")
        b_modT = Buf("modT")
        S.dma("sp", ident_f[:], I["ident"], writes=[b_const])
        S.dma("pool", ident_b[:], I["ident"], writes=[b_const])
        V(lambda: nc.vector.memset(eps6[:], 1e-6), [], [b_const])
        V(lambda: nc.vector.memset(eps5[:], 1e-5), [], [b_const])

        with ExitStack() as st:
            condT = sb(st, "condT", [128, 8], F32)
            ones_row = sb(st, "ones_row", [1, 128], F32)
            modrow = sb(st, "modrow", [1, 6144], F32)
            brow = sb(st, "brow", [1, 6144], F32)
            wa = [sb(st, f"wa{i}", [128, 8, 512], F32) for i in range(2)]
            gbt = sb(st, "gbt", [128, 1024], F32)
            ps = [pbank(st, f"sps{i}") for i in range(2)]
            b_cond, b_ones, b_mrow, b_brow, b_gbt = Buf(), Buf(), Buf(), Buf(), Buf()
            b_wa = [Buf(), Buf()]
            b_ps = [PB(), PB()]
            S.dma("sp", condT[:], I["c"], writes=[b_cond])
            A(lambda: nc.scalar.activation(out=condT[:], in_=condT[:], func=AF.Silu), [b_cond], [b_cond])
            V(lambda: nc.vector.memset(ones_row[:], 1.0), [], [b_ones])
            for l in range(nlayers):
                S.dma("sp", brow[:], I["b_ada"][l:l + 1, :], writes=[b_brow])
                wsrc = I["w_ada"][l].rearrange("(kc p) n -> p kc n", p=128)
                for pc in range(12):
                    S.dma("sp", wa[pc % 2][:], wsrc[:, :, pc * 512:(pc + 1) * 512], writes=[b_wa[pc % 2]])
                    for kc in range(8):
                        PE(lambda: nc.tensor.matmul(ps[pc % 2][0:1, :], lhsT=condT[:, kc:kc + 1], rhs=wa[pc % 2][:, kc, :],
                                                    start=(kc == 0), stop=(kc == 7)),
                           [b_cond, b_wa[pc % 2]], [b_ps[pc % 2]], signal=(kc == 7))
                    V(lambda: nc.vector.tensor_tensor(out=modrow[0:1, pc * 512:(pc + 1) * 512], in0=ps[pc % 2][0:1, :],
                                                      in1=brow[0:1, pc * 512:(pc + 1) * 512], op=ALU.add),
                      [b_ps[pc % 2], b_brow], [b_mrow])
                for j in range(48):
                    PE(lambda: nc.tensor.matmul(ps[0][:, j:j + 1], lhsT=modrow[0:1, j * 128:(j + 1) * 128],
                                                rhs=ones_row[0:1, 0:1], start=True, stop=True),
                       [b_mrow, b_ones], [b_ps[0]], signal=(j == 47))
                V(lambda: nc.vector.tensor_copy(modT[:, l, :], ps[0][:, 0:48]), [b_ps[0]], [b_modT])
                V(lambda: nc.vector.tensor_scalar(out=modT[:, l, 8:16], in0=modT[:, l, 8:16], scalar1=1.0, scalar2=None,
                                                  op0=ALU.add), [b_modT], [b_modT])
                V(lambda: nc.vector.tensor_scalar(out=modT[:, l, 32:40], in0=modT[:, l, 32:40], scalar1=1.0, scalar2=None,
                                                  op0=ALU.add), [b_modT], [b_modT])
                for gi, base in enumerate([2048, 5120]):
                    for hf in range(2):
                        PE(lambda: nc.tensor.matmul(ps[1][:, :], lhsT=ones_row[0:1, :],
                                                    rhs=modrow[0:1, base + hf * 512: base + (hf + 1) * 512],
                                                    start=True, stop=True), [b_mrow, b_ones], [b_ps[1]])
                        V(lambda: nc.vector.tensor_copy(gbt[:, hf * 512:(hf + 1) * 512], ps[1][:, :]), [b_ps[1]], [b_gbt])
                    S.dma("sp", GB[l, gi], gbt[:], reads=[b_gbt])
            S.barrier()

        def phase_P(l, xsrc):
            with ExitStack() as st:
                w_in = sb(st, "w_in", [128, 8, INC], BF16)
                w_uq = sb(st, "w_uq", [128, 3, 384], BF16)
                w_ukv = sb(st, "w_ukv", [128, 2, 512], BF16)
                gq_bc = sb(st, "gq_bc", [128, 384], F32)
                gkv_bc = sb(st, "gkv_bc", [128, 256], F32)
                gC_bc = sb(st, "gC_bc", [128, 6, 64], F32)
                ropeB = sb(st, "ropeB", [128, NSUB, 2, 64], F32)
                ropeC = sb(st, "ropeC", [128, NSUB, 128], F32)
                xs = [sb(st, f"xs{i}", [128, D], F32) for i in range(2)]
                hT = [sb(st, f"hT{i}", [128, 8, 512], BF16) for i in range(2)]
                bf32_l = [sb(st, f"bf32{i}", [128, 672], F32) for i in range(2)]
                cqk_l = [sb(st, f"cqk{i}", [128, 384], F32) for i in range(2)]
                junk_l = [sb(st, f"junk{i}", [128, 384], F32) for i in range(2)]
                qkA_l = [sb(st, f"qkA{i}", [128, 512], BF16) for i in range(2)]
                qkD_l = [sb(st, f"qkD{i}", [128, 512], BF16) for i in range(2)]
                VA_l = [sb(st, f"VA{i}", [128, 4, VP], BF16) for i in range(2)]
                VB_l = [sb(st, f"VB{i}", [128, 4, VP], BF16) for i in range(2)]
                VC_l = [sb(st, f"VC{i}", [128, 4, VP], BF16) for i in range(2)]
                VD_l = [sb(st, f"VD{i}", [128, 4, VP], BF16) for i in range(2)]
                cqn_l = [sb(st, f"cqn{i}", [128, 640], BF16) for i in range(2)]
                cT_l = [sb(st, f"cT{i}", [128, 5, 128], BF16) for i in range(2)]
                qB_l = [sb(st, f"qB{i}", [128, 4, 96], BF16) for i in range(2)]
                kB_l = [sb(st, f"kB{i}", [128, 4, 96], BF16) for i in range(2)]
                cC_l = [sb(st, f"cC{i}", [128, 384], BF16) for i in range(2)]
                t1_l = [sb(st, f"t1{i}", [128, 384], F32) for i in range(2)]
                t2_l = [sb(st, f"t2{i}", [128, 384], F32) for i in range(2)]
                nq_l = [sb(st, f"nq{i}", [128, 384], F32) for i in range(2)]
                stt__l = [sb(st, f"stt_{i}", [128, 16], F32) for i in range(2)]
                stage = [sb(st, f"stage{i}", [128, 15, 512], BF16) for i in range(2)]
                pb = [pbank(st, f"pp{i}") for i in range(8)]
                bp = [PB(f"pp{i}") for i in range(8)]
                b_w, b_tab = Buf(), Buf()
                b_xs = [Buf(), Buf()]
                b_hT = [Buf(), Buf()]
                BL = {n: [Buf(n + '0'), Buf(n + '1')] for n in ['bf32', 'cqk', 'junk', 'qkA', 'qkD', 'VA', 'VB', 'VC', 'VD', 'cqn', 'cT', 'qB', 'kB', 'cC', 't1', 't2', 'nq', 'stt_']}
                b_wk = [Buf() for _ in range(8)]
                b_hk = [[Buf() for _ in range(8)] for _ in range(2)]
                b_stage = [Buf(), Buf()]

                wsrc = I["w_in"][l].rearrange("(kc p) n -> p kc n", p=128)
                for kc in range(8):
                    S.dma("pool", w_in[:, kc, :], wsrc[:, kc, :], writes=[b_wk[kc]])
                S.dma("pool", w_uq[:], I["mla_w_uq"][l].rearrange("(kc p) n -> p kc n", p=128), writes=[b_w])
                S.dma("pool", w_ukv[:], I["mla_w_ukv"][l].rearrange("(kc p) n -> p kc n", p=128), writes=[b_w])
                S.dma("sp", gq_bc[:], I["mla_q_norm_g"][l, :].partition_broadcast(128), writes=[b_tab])
                S.dma("sp", gkv_bc[:], I["mla_kv_norm_g"][l, :].partition_broadcast(128), writes=[b_tab])
                for h in range(6):
                    src = I["gqa_q_norm_g"] if h < 4 else I["gqa_k_norm_g"]
                    S.dma("sp", gC_bc[:, h, :], src[l, :].partition_broadcast(128), writes=[b_tab])
                V(lambda: nc.vector.tensor_scalar(out=gC_bc[:, 0:4, :], in0=gC_bc[:, 0:4, :], scalar1=SC_C, scalar2=None,
                                                  op0=ALU.mult), [b_tab], [b_tab])
                S.dma("sp", ropeB[:], I["ropeB"], writes=[b_tab])
                S.dma("sp", ropeC[:], I["ropeC"], writes=[b_tab])
                for i_ in range(2):
                    for n_ in ('VA', 'VB', 'VC', 'VD'):
                        vt = {'VA': VA_l, 'VB': VB_l, 'VC': VC_l, 'VD': VD_l}[n_][i_]
                        V(lambda: nc.vector.memset(vt[:], 1.0), [], [BL[n_][i_]])

                groups = [(0, 512), (512, 512), (1024, 416), (1440, 512), (1952, 512), (2464, 256)]
                def stage1(sg):
                    T, s = sg // 4, sg % 4
                    tsl = slice(s * 128, (s + 1) * 128)
                    h_t, bhk = hT[T % 2], b_hk[T % 2]
                    stg, bstg = stage[T % 2], b_stage[T % 2]
                    bf32 = bf32_l[sg % 2]
                    b_bf32 = BL['bf32'][sg % 2]
                    cqk = cqk_l[sg % 2]
                    b_cqk = BL['cqk'][sg % 2]
                    junk = junk_l[sg % 2]
                    b_junk = BL['junk'][sg % 2]
                    qkA = qkA_l[sg % 2]
                    b_qkA = BL['qkA'][sg % 2]
                    qkD = qkD_l[sg % 2]
                    b_qkD = BL['qkD'][sg % 2]
                    VA = VA_l[sg % 2]
                    b_VA = BL['VA'][sg % 2]
                    VB = VB_l[sg % 2]
                    b_VB = BL['VB'][sg % 2]
                    VC = VC_l[sg % 2]
                    b_VC = BL['VC'][sg % 2]
                    VD = VD_l[sg % 2]
                    b_VD = BL['VD'][sg % 2]
                    cqn = cqn_l[sg % 2]
                    b_cqn = BL['cqn'][sg % 2]
                    cT = cT_l[sg % 2]
                    b_cT = BL['cT'][sg % 2]
                    qB = qB_l[sg % 2]
                    b_qB = BL['qB'][sg % 2]
                    kB = kB_l[sg % 2]
                    b_kB = BL['kB'][sg % 2]
                    cC = cC_l[sg % 2]
                    b_cC = BL['cC'][sg % 2]
                    t1 = t1_l[sg % 2]
                    b_t1 = BL['t1'][sg % 2]
                    t2 = t2_l[sg % 2]
                    b_t2 = BL['t2'][sg % 2]
                    nq = nq_l[sg % 2]
                    b_nq = BL['nq'][sg % 2]
                    stt_ = stt__l[sg % 2]
                    b_st = BL['stt_'][sg % 2]
                    x_t, bx = xs[sg % 2], b_xs[sg % 2]
                    S.dma("sp", x_t[:], xsrc[sg * 128:(sg + 1) * 128, :], writes=[bx])
                    for hf in range(2):
                        for cc in range(4):
                            kc = hf * 4 + cc
                            PE(lambda: nc.tensor.transpose(pb[hf][:, cc * 128:(cc + 1) * 128], x_t[:, kc * 128:(kc + 1) * 128],
                                                           ident_f[:]), [bx, b_const], [bp[hf]], signal=(cc == 3))
                        for cc in range(4):
                            kc = hf * 4 + cc
                            if cc % 2 == 0:
                                A(lambda: nc.scalar.activation(out=h_t[:, kc, tsl], in_=pb[hf][:, cc * 128:(cc + 1) * 128],
                                                               func=AF.Identity, scale=modT[:, l, 8 + kc:9 + kc],
                                                               bias=modT[:, l, kc:kc + 1]), [bp[hf], b_modT], [bhk[kc]])
                            else:
                                V(lambda: nc.vector.tensor_scalar(out=h_t[:, kc, tsl], in0=pb[hf][:, cc * 128:(cc + 1) * 128],
                                                                  scalar1=modT[:, l, 8 + kc:9 + kc],
                                                                  scalar2=modT[:, l, kc:kc + 1], op0=ALU.mult, op1=ALU.add),
                                  [bp[hf], b_modT], [bhk[kc]])
                    yield
                    for gi, (c0, ncol) in enumerate(groups):
                        if gi > 0:
                            yield
                        bk = 2 + gi % 3
                        for kc in range(8):
                            PE(lambda: nc.tensor.matmul(pb[bk][:, 0:ncol], lhsT=h_t[:, kc, tsl], rhs=w_in[:, kc, c0:c0 + ncol],
                                                        start=(kc == 0), stop=(kc == 7)), [bhk[kc], b_wk[kc]], [bp[bk]], signal=(kc == 7))
                        P_ = pb[bk]
                        if gi == 0:
                            A(lambda: nc.scalar.activation(out=qkA[:, 0:256], in_=P_[:, 0:256], func=AF.Identity, scale=SC_A),
                              [bp[bk]], [b_qkA])
                            V(lambda: nc.vector.tensor_copy(qkA[:, 256:512], P_[:, 256:512]), [bp[bk]], [b_qkA])
                        elif gi == 1:
                            A(lambda: nc.scalar.activation(func=AF.Identity, out=VA[:, :, 0:64], in_=P_[:, 0:256].rearrange("p (h d) -> p h d", d=64)),
                              [bp[bk]], [b_VA])
                            V(lambda: nc.vector.tensor_copy(bf32[:, 0:256], P_[:, 256:512]), [bp[bk]], [b_bf32])
                        elif gi == 2:
                            V(lambda: nc.vector.tensor_copy(bf32[:, 256:672], P_[:, 0:416]), [bp[bk]], [b_bf32])
                        elif gi == 3:
                            V(lambda: nc.vector.tensor_copy(cqk[:, :], P_[:, 0:384]), [bp[bk]], [b_cqk])
                            A(lambda: nc.scalar.activation(func=AF.Identity, out=VC[:, 0:2, 0:64],
                                                     in_=P_[:, 384:512].rearrange("p (h d) -> p h d", d=64)),
                              [bp[bk]], [b_VC])
                        elif gi == 4:
                            A(lambda: nc.scalar.activation(out=qkD[:, 0:256], in_=P_[:, 0:256], func=AF.Identity, scale=SC_D),
                              [bp[bk]], [b_qkD])
                            V(lambda: nc.vector.tensor_copy(qkD[:, 256:512], P_[:, 256:512]), [bp[bk]], [b_qkD])
                        else:
                            A(lambda: nc.scalar.activation(func=AF.Identity, out=VD[:, :, 0:64], in_=P_[:, 0:256].rearrange("p (h d) -> p h d", d=64)),
                              [bp[bk]], [b_VD])

                def stage2(sg):
                    T, s = sg // 4, sg % 4
                    tsl = slice(s * 128, (s + 1) * 128)
                    h_t, bhk = hT[T % 2], b_hk[T % 2]
                    stg, bstg = stage[T % 2], b_stage[T % 2]
                    bf32 = bf32_l[sg % 2]
                    b_bf32 = BL['bf32'][sg % 2]
                    cqk = cqk_l[sg % 2]
                    b_cqk = BL['cqk'][sg % 2]
                    junk = junk_l[sg % 2]
                    b_junk = BL['junk'][sg % 2]
                    qkA = qkA_l[sg % 2]
                    b_qkA = BL['qkA'][sg % 2]
                    qkD = qkD_l[sg % 2]
                    b_qkD = BL['qkD'][sg % 2]
                    VA = VA_l[sg % 2]
                    b_VA = BL['VA'][sg % 2]
                    VB = VB_l[sg % 2]
                    b_VB = BL['VB'][sg % 2]
                    VC = VC_l[sg % 2]
                    b_VC = BL['VC'][sg % 2]
                    VD = VD_l[sg % 2]
                    b_VD = BL['VD'][sg % 2]
                    cqn = cqn_l[sg % 2]
                    b_cqn = BL['cqn'][sg % 2]
                    cT = cT_l[sg % 2]
                    b_cT = BL['cT'][sg % 2]
                    qB = qB_l[sg % 2]
                    b_qB = BL['qB'][sg % 2]
                    kB = kB_l[sg % 2]
                    b_kB = BL['kB'][sg % 2]
                    cC = cC_l[sg % 2]
                    b_cC = BL['cC'][sg % 2]
                    t1 = t1_l[sg % 2]
                    b_t1 = BL['t1'][sg % 2]
                    t2 = t2_l[sg % 2]
                    b_t2 = BL['t2'][sg % 2]
                    nq = nq_l[sg % 2]
                    b_nq = BL['nq'][sg % 2]
                    stt_ = stt__l[sg % 2]
                    b_st = BL['stt_'][sg % 2]
                    V(lambda: nc.vector.scalar_tensor_tensor(out=junk[:, 0:384], in0=bf32[:, 0:384], scalar=1.0,
                                                             in1=bf32[:, 0:384], op0=ALU.mult, op1=ALU.mult,
                                                             accum_out=stt_[:, 0:1]), [b_bf32], [b_junk, b_st])
                    V(lambda: nc.vector.scalar_tensor_tensor(out=junk[:, 0:256], in0=bf32[:, 384:640], scalar=1.0,
                                                             in1=bf32[:, 384:640], op0=ALU.mult, op1=ALU.mult,
                                                             accum_out=stt_[:, 1:2]), [b_bf32], [b_junk, b_st])
                    A(lambda: nc.scalar.activation(out=stt_[:, 2:3], in_=stt_[:, 0:1], func=AF.Ln, scale=1.0 / 384,
                                                   bias=eps6[:, 0:1]), [b_st, b_const], [b_st])
                    A(lambda: nc.scalar.activation(out=stt_[:, 3:4], in_=stt_[:, 1:2], func=AF.Ln, scale=1.0 / 256,
                                                   bias=eps6[:, 0:1]), [b_st, b_const], [b_st])
                    A(lambda: nc.scalar.activation(out=stt_[:, 4:6], in_=stt_[:, 2:4], func=AF.Exp, scale=-0.5), [b_st], [b_st])
                    V(lambda: nc.vector.scalar_tensor_tensor(out=cqn[:, 0:384], in0=bf32[:, 0:384], scalar=stt_[:, 4:5],
                                                             in1=gq_bc[:, :], op0=ALU.mult, op1=ALU.mult),
                      [b_bf32, b_st, b_tab], [b_cqn])
                    V(lambda: nc.vector.scalar_tensor_tensor(out=cqn[:, 384:640], in0=bf32[:, 384:640], scalar=stt_[:, 5:6],
                                                             in1=gkv_bc[:, :], op0=ALU.mult, op1=ALU.mult),
                      [b_bf32, b_st, b_tab], [b_cqn])
                    yield
                    p5b = pb[5][:, :].bitcast(BF16)
                    for j in range(5):
                        PE(lambda: nc.tensor.transpose(p5b[:, j * 128:(j + 1) * 128], cqn[:, j * 128:(j + 1) * 128], ident_b[:]),
                           [b_cqn, b_const], [bp[5]], signal=(j == 4))
                    V(lambda: nc.vector.tensor_copy(cT[:, :, :], p5b[:, 0:640].rearrange("p (c t) -> p c t", t=128)),
                      [bp[5]], [b_cT])
                    for j in range(3):
                        PE(lambda: nc.tensor.matmul(pb[6][:, 0:384], lhsT=cT[:, j, :], rhs=w_uq[:, j, :], start=(j == 0),
                                                    stop=(j == 2)), [b_cT, b_w], [bp[6]], signal=(j == 2))
                    for j in range(2):
                        PE(lambda: nc.tensor.matmul(pb[7][:, 0:512], lhsT=cT[:, 3 + j, :], rhs=w_ukv[:, j, :], start=(j == 0),
                                                    stop=(j == 1)), [b_cT, b_w], [bp[7]], signal=(j == 1))
                    yield
                    q3 = pb[6][:, 0:384].rearrange("p (h d) -> p h d", d=96)
                    kv3 = pb[7][:, 0:512].rearrange("p (h d) -> p h d", d=128)
                    A(lambda: nc.scalar.activation(out=qB[:, :, 0:64], in_=q3[:, :, 0:64], func=AF.Identity, scale=SC_B),
                      [bp[6]], [b_qB])
                    Tq = ropeB[:, sg, 0, :]
                    Tk = ropeB[:, sg, 1, :]
                    t1q = t1[:, 0:128].rearrange("p (h d) -> p h d", d=32)
                    t2q = t2[:, 0:128].rearrange("p (h d) -> p h d", d=32)
                    V(lambda: nc.vector.tensor_tensor(out=t1q, in0=q3[:, :, 64:96],
                                                      in1=Tq[:, 0:32].unsqueeze(1).to_broadcast([128, 4, 32]), op=ALU.mult),
                      [bp[6], b_tab], [b_t1])
                    V(lambda: nc.vector.tensor_tensor(out=t2q[:, :, 0:16], in0=q3[:, :, 80:96],
                                                      in1=Tq[:, 32:48].unsqueeze(1).to_broadcast([128, 4, 16]), op=ALU.mult),
                      [bp[6], b_tab], [b_t2])
                    V(lambda: nc.vector.tensor_tensor(out=t2q[:, :, 16:32], in0=q3[:, :, 64:80],
                                                      in1=Tq[:, 48:64].unsqueeze(1).to_broadcast([128, 4, 16]), op=ALU.mult),
                      [bp[6], b_tab], [b_t2])
                    G(lambda: nc.gpsimd.tensor_tensor(out=qB[:, :, 64:96], in0=t1q, in1=t2q, op=ALU.add), [b_t1, b_t2], [b_qB])
                    V(lambda: nc.vector.tensor_copy(kB[:, :, 0:64], kv3[:, :, 0:64]), [bp[7]], [b_kB])
                    A(lambda: nc.scalar.activation(func=AF.Identity, out=VB[:, :, 0:64], in_=kv3[:, :, 64:128]), [bp[7]], [b_VB])
                    yield
                    kr = bf32[:, 640:672]
                    V(lambda: nc.vector.tensor_tensor(out=t1[:, 128:160], in0=kr, in1=Tk[:, 0:32], op=ALU.mult),
                      [b_bf32, b_tab], [b_t1])
                    V(lambda: nc.vector.tensor_tensor(out=t2[:, 128:144], in0=bf32[:, 656:672], in1=Tk[:, 32:48], op=ALU.mult),
                      [b_bf32, b_tab], [b_t2])
                    V(lambda: nc.vector.tensor_tensor(out=t2[:, 144:160], in0=bf32[:, 640:656], in1=Tk[:, 48:64], op=ALU.mult),
                      [b_bf32, b_tab], [b_t2])
                    G(lambda: nc.gpsimd.tensor_tensor(out=kB[:, :, 64:96],
                                                      in0=t1[:, 128:160].unsqueeze(1).to_broadcast([128, 4, 32]),
                                                      in1=t2[:, 128:160].unsqueeze(1).to_broadcast([128, 4, 32]), op=ALU.add),
                      [b_t1, b_t2], [b_kB])
                    yield
                    c3 = cqk[:, :].rearrange("p (h d) -> p h d", d=64)
                    V(lambda: nc.vector.tensor_tensor(out=junk[:, :], in0=cqk[:, :], in1=cqk[:, :], op=ALU.mult),
                      [b_cqk], [b_junk])
                    V(lambda: nc.vector.tensor_reduce(out=stt_[:, 6:12], in_=junk[:, :].rearrange("p (h d) -> p h d", d=64),
                                                      axis=AX.X, op=ALU.add), [b_junk], [b_st])
                    A(lambda: nc.scalar.activation(out=stt_[:, 6:12], in_=stt_[:, 6:12], func=AF.Ln, scale=1.0 / 64,
                                                   bias=eps6[:, 0:1]), [b_st, b_const], [b_st])
                    A(lambda: nc.scalar.activation(out=stt_[:, 6:12], in_=stt_[:, 6:12], func=AF.Exp, scale=-0.5), [b_st], [b_st])
                    yield
                    n3 = nq[:, :].rearrange("p (h d) -> p h d", d=64)
                    V(lambda: nc.vector.tensor_tensor(out=n3, in0=c3, in1=stt_[:, 6:12].unsqueeze(2).to_broadcast([128, 6, 64]),
                                                      op=ALU.mult), [b_cqk, b_st], [b_nq])
                    G(lambda: nc.gpsimd.tensor_tensor(out=n3, in0=n3, in1=gC_bc[:, :, :], op=ALU.mult), [b_nq, b_tab], [b_nq])
                    yield
                    cosA = ropeC[:, sg, 0:64]
                    sinS = ropeC[:, sg, 64:128].rearrange("p (r f i) -> p r f i", r=2, f=2)
                    V(lambda: nc.vector.tensor_tensor(out=t1[:, :].rearrange("p (h d) -> p h d", d=64), in0=n3,
                                                      in1=cosA.unsqueeze(1).to_broadcast([128, 6, 64]), op=ALU.mult),
                      [b_nq, b_tab], [b_t1])
                    n5 = nq[:, :].rearrange("p (h r f i) -> p h r f i", h=6, r=2, f=2)
                    t5 = t2[:, :].rearrange("p (h r f i) -> p h r f i", h=6, r=2, f=2)
                    for f in range(2):
                        V(lambda: nc.vector.tensor_tensor(out=t5[:, :, :, f, :], in0=n5[:, :, :, 1 - f, :],
                                                          in1=sinS[:, :, f, :].unsqueeze(1).to_broadcast([128, 6, 2, 16]),
                                                          op=ALU.mult), [b_nq, b_tab], [b_t2])
                    G(lambda: nc.gpsimd.tensor_tensor(out=cC[:, :], in0=t1[:, :], in1=t2[:, :], op=ALU.add), [b_t1, b_t2], [b_cC])
                    yield
                    tb0 = pb[6][:, :].bitcast(BF16)
                    tb1 = pb[7][:, :].bitcast(BF16)
                    for j in range(4):
                        PE(lambda: nc.tensor.transpose(tb0[:, j * 128:(j + 1) * 128], qkA[:, j * 128:(j + 1) * 128], ident_b[:]),
                           [b_qkA, b_const], [bp[6]], signal=False)
                    for h in range(4):
                        PE(lambda: nc.tensor.transpose(tb0[0:96, (4 + h) * 128:(5 + h) * 128], qB[:, h, :], ident_b[:]),
                           [b_qB, b_const], [bp[6]], signal=(h == 3))
                    for h in range(4):
                        PE(lambda: nc.tensor.transpose(tb1[0:96, h * 128:(h + 1) * 128], kB[:, h, :], ident_b[:]),
                           [b_kB, b_const], [bp[7]], signal=False)
                    for j in range(3):
                        PE(lambda: nc.tensor.transpose(tb1[:, (4 + j) * 128:(5 + j) * 128], cC[:, j * 128:(j + 1) * 128],
                                                       ident_b[:]), [b_cC, b_const], [bp[7]], signal=(j == 2))
                    V(lambda: nc.vector.tensor_copy(stg[:, 0:8, tsl], tb0[:, 0:1024].rearrange("p (c t) -> p c t", t=128)),
                      [bp[6]], [bstg])
                    A(lambda: nc.scalar.activation(func=AF.Identity, out=stg[:, 8:15, tsl], in_=tb1[:, 0:896].rearrange("p (c t) -> p c t", t=128)),
                      [bp[7]], [bstg])
                    yield
                    rows = slice(sg * 128, (sg + 1) * 128)
                    S.dma("sp", VG[0, rows, :], VA[:, :, :].rearrange("p h d -> p (h d)"), reads=[b_VA])
                    S.dma("sp", VG[1, rows, :], VB[:, :, :].rearrange("p h d -> p (h d)"), reads=[b_VB])
                    S.dma("sp", VG[2, rows, :], VC[:, :, :].rearrange("p h d -> p (h d)"), reads=[b_VC])
                    S.dma("sp", DTOK[rows, 0:512], qkD[:, :], reads=[b_qkD])
                    S.dma("sp", DTOK[rows, 512:512 + VW], VD[:, :, :].rearrange("p h d -> p (h d)"), reads=[b_VD])
                    if s == 3:
                        for (ca, cb) in [(0, 4), (4, 8), (8, 12), (12, 15)]:
                            S.dma("sp", QKT[ca:cb, :, T * 512:(T + 1) * 512].rearrange("c r t -> r c t"), stg[:, ca:cb, :],
                                  reads=[bstg])

                nsub_ = NSUB if 'nsub' not in DBG else DBG['nsub']
                for _ in stage1(0):
                    pass
                for sg in range(nsub_):
                    gens = [stage2(sg)] + ([stage1(sg + 1)] if sg + 1 < nsub_ else [])
                    while gens:
                        for g_ in list(gens):
                            try:
                                next(g_)
                            except StopIteration:
                                gens.remove(g_)
                S.barrier()

        def phase_att(l, y_res, b_y):
            with ExitStack() as st:
                qT = [sb(st, f"qT{i}", [128, SEQ], BF16) for i in range(4)]
                kT = [sb(st, f"kT{i}", [128, SEQ], BF16) for i in range(4)]
                Vg = sb(st, "Vg", [128, NSUB, VW + 64], BF16)
                alibi = sb(st, "alibi", [128, 5, 512], F32)
                Sb = [sb(st, f"Sb{i}", [128, 512], F32) for i in range(3)]
                E = [sb(st, f"E{i}", [128, 512], BF16) for i in range(4)]
                lam_t = sb(st, "lam_t", [128, 128], F32)
                lam = sb(st, "lam", [128, 8], F32)
                gA_bc = sb(st, "gA_bc", [128, 64], F32)
                o1 = sb(st, "o1", [128, 4, 64], F32)
                o2 = sb(st, "o2", [128, 4, 64], F32)
                osq = sb(st, "osq", [128, 4, 64], F32)
                rc = sb(st, "rc", [128, 16], F32)
                pS = [pbank(st, f"pS{i}") for i in range(4)]
                pOT = [pbank(st, f"pOT{i}") for i in range(2)]
                pO = [pbank(st, f"pO{i}") for i in range(2)]
                otS = [sb(st, f"otS{i}", [65, 512], F32) for i in range(2)]
                b_pOT = [PB(), PB()]
                b_otS = [Buf(), Buf()]
                b_qT = [Buf() for _ in range(4)]
                b_kT = [Buf() for _ in range(4)]
                b_Vg, b_al, b_lam, b_gA = Buf(), Buf(), Buf(), Buf()
                b_Sb = [Buf() for _ in range(3)]
                b_E = [Buf() for _ in range(4)]
                b_pS = [PB() for _ in range(4)]
                b_pO = [PB() for _ in range(4)]
                b_o1, b_o2, b_osq, b_rc = Buf(), Buf(), Buf(), Buf()
                V(lambda: nc.vector.memset(Vg[:, :, VW:VW + 64], 0.0), [], [b_Vg])
                S.dma("sp", alibi[:], I["alibi"], writes=[b_al])
                lam_init = 0.8 - 0.6 * math.exp(-0.3 * l)
                S.dma("sp", lam_t[:], I["diff_lambda"][l, :].partition_broadcast(128), writes=[b_lam])
                V(lambda: nc.vector.scalar_tensor_tensor(out=lam_t[:, 0:32], in0=lam_t[:, 0:32], scalar=1.0, in1=lam_t[:, 32:64],
                                                         op0=ALU.mult, op1=ALU.mult, accum_out=lam[:, 0:1]), [b_lam], [b_lam])
                V(lambda: nc.vector.scalar_tensor_tensor(out=lam_t[:, 64:96], in0=lam_t[:, 64:96], scalar=1.0,
                                                         in1=lam_t[:, 96:128], op0=ALU.mult, op1=ALU.mult,
                                                         accum_out=lam[:, 1:2]), [b_lam], [b_lam])
                A(lambda: nc.scalar.activation(out=lam[:, 2:4], in_=lam[:, 0:2], func=AF.Exp), [b_lam], [b_lam])
                V(lambda: nc.vector.tensor_tensor(out=lam[:, 4:5], in0=lam[:, 2:3], in1=lam[:, 3:4], op=ALU.subtract),
                  [b_lam], [b_lam])
                V(lambda: nc.vector.tensor_scalar(out=lam[:, 5:6], in0=lam[:, 4:5], scalar1=lam_init, scalar2=-1.0,
                                                  op0=ALU.add, op1=ALU.mult), [b_lam], [b_lam])
                S.dma("sp", gA_bc[:], I["diff_subln_g"][l, :].partition_broadcast(128), writes=[b_gA])
                V(lambda: nc.vector.tensor_scalar(out=gA_bc[:], in0=gA_bc[:], scalar1=1.0 - lam_init, scalar2=None,
                                                  op0=ALU.mult), [b_gA], [b_gA])

                state = {"qi": 0, "ei": 0, "si": 0, "sbi": 0}

                def load_map(chunk, r0, nrows, isq):
                    i = state["qi"] % 4
                    t, b = (qT[i], b_qT[i]) if isq else (kT[i], b_kT[i])
                    S.dma("sp", t[0:nrows, :], QKT[chunk, r0:r0 + nrows, :], writes=[b])
                    return t, b

                def load_V(g):
                    for a in range(4):
                        S.dma("sp", Vg[:, a * 8:(a + 1) * 8, 0:VW],
                              VG[g, a * 1024:(a + 1) * 1024, :].rearrange("(t p) c -> p t c", p=128), writes=[b_Vg])

                def job(maps, vcol, ycol, alibi_slope=None, ycols=None):
                    nm = len(maps)
                    pend = [None]
                    for qt in range(8):
                        if nm == 2:
                            groups = [[(kt, 0), (kt, 1)] for kt in range(NSUB)]
                        else:
                            groups = [[(2 * j, 0), (2 * j + 1, 0)] for j in range(NSUB // 2)]
                        ng = len(groups)
                        for gi in range(ng + 2):
                            if gi < ng:
                                banks = [2 * (gi % 2), 2 * (gi % 2) + 1]
                                S.prewait("pe", [mp[1] for mp in maps] + [mp[3] for mp in maps], [b_pS[bk] for bk in banks])
                                for idx, (kt, m) in enumerate(groups[gi]):
                                    q_t, bq, k_t, bk_, nr = maps[m][:5]
                                    p0 = maps[m][5] if len(maps[m]) > 5 else 0
                                    si = banks[idx]
                                    tp_ = (p0, 0) if p0 == 96 else None
                                    PE(lambda: nc.tensor.matmul(pS[si][:, :], lhsT=k_t[p0:p0 + nr, kt * 128:(kt + 1) * 128],
                                                                rhs=q_t[p0:p0 + nr, qt * 512:(qt + 1) * 512], start=True, stop=True,
                                                                tile_position=tp_),
                                       [bq, bk_], [b_pS[si]], signal=(idx == 1))
                            j = gi - 1
                            if 0 <= j < ng:
                                for idx, (kt, m) in enumerate(groups[j]):
                                    si = 2 * (j % 2) + idx
                                    ei = 2 * (j % 2) + idx
                                    if alibi_slope is None:
                                        A(lambda: nc.scalar.activation(out=E[ei][:, :], in_=pS[si][:, :], func=AF.Exp),
                                          [b_pS[si]], [b_E[ei]])
                                    else:
                                        k0, q0 = kt * 128, qt * 512
                                        if k0 + 128 <= q0:
                                            tab, sc_, bias = alibi[:, 0, :], -alibi_slope, -alibi_slope * (q0 - k0)
                                        elif k0 >= q0 + 512:
                                            tab, sc_, bias = alibi[:, 0, :], alibi_slope, -alibi_slope * (k0 - q0)
                                        else:
                                            tab, sc_, bias = alibi[:, 1 + (k0 - q0) // 128, :], -alibi_slope, 0.0
                                        sbi = state["sbi"] % 3
                                        state["sbi"] += 1
                                        V(lambda: nc.vector.scalar_tensor_tensor(out=Sb[sbi][:, :], in0=tab, scalar=sc_,
                                                                                 in1=pS[si][:, :], op0=ALU.mult, op1=ALU.add),
                                          [b_al, b_pS[si]], [b_Sb[sbi]])
                                        A(lambda: nc.scalar.activation(out=E[ei][:, :], in_=Sb[sbi][:, :], func=AF.Exp, bias=bias),
                                          [b_Sb[sbi]], [b_E[ei]])
                            j = gi - 2
                            if 0 <= j < ng:
                                eis_ = [2 * (j % 2), 2 * (j % 2) + 1]
                                S.prewait("pe", [b_E[e] for e in eis_] + [b_Vg], [b_pOT[m] for (_, m) in groups[j]])
                                for idx, (kt, m) in enumerate(groups[j]):
                                    ei = eis_[idx]
                                    PE(lambda: nc.tensor.matmul(pOT[m][:, :], lhsT=Vg[:, kt, vcol:vcol + 128], rhs=E[ei][:, :],
                                                                start=(kt == 0), stop=(kt == NSUB - 1)),
                                       [b_E[ei], b_Vg], [b_pOT[m]], signal=(idx == 1))
                            if gi == 4 and pend[0] is not None:
                                pend[0]()
                                pend[0] = None
                        for m in range(nm):
                            if alibi_slope is not None:
                                A(lambda: nc.scalar.activation(out=otS[m][:, :], in_=pOT[m][0:65, :], func=AF.Identity),
                                  [b_pOT[m]], [b_otS[m]])
                            else:
                                V(lambda: nc.vector.tensor_copy(otS[m][:, :], pOT[m][0:65, :]), [b_pOT[m]], [b_otS[m]])
                        pend[0] = (lambda qt=qt: finalize(qt, nm, ycol, ycols))
                    pend[0]()
                    pend[0] = None

                def finalize(qt, nm, ycol, ycols=None):
                    if True:
                        for m in range(nm):
                            for jj in range(4):
                                PE(lambda: nc.tensor.transpose(pO[m][:, jj * 65:(jj + 1) * 65], otS[m][0:65, jj * 128:(jj + 1) * 128],
                                                               ident_f[0:65, 0:65]), [b_otS[m], b_const], [b_pO[m]], signal=(jj == 3))
                        if ycols is not None:
                            for m in range(nm):
                                Om = pO[m][:, 0:260].rearrange("p (j d) -> p j d", d=65)
                                V(lambda: nc.vector.reciprocal(out=rc[:, 4 * m:4 * m + 4], in_=Om[:, :, 64]), [b_pO[m]], [b_rc])
                                V(lambda: nc.vector.tensor_tensor(
                                    out=y_res[:, qt * 4:(qt + 1) * 4, ycols[m]:ycols[m] + 64], in0=Om[:, :, 0:64],
                                    in1=rc[:, 4 * m:4 * m + 4].unsqueeze(2).to_broadcast([128, 4, 64]), op=ALU.mult),
                                  [b_pO[m], b_rc], b_y[qt * 4:(qt + 1) * 4])
                            return
                        O1 = pO[0][:, 0:260].rearrange("p (j d) -> p j d", d=65)
                        ydst = y_res[:, qt * 4:(qt + 1) * 4, ycol:ycol + 64]
                        by = b_y[qt * 4:(qt + 1) * 4]
                        V(lambda: nc.vector.reciprocal(out=rc[:, 0:4], in_=O1[:, :, 64]), [b_pO[0]], [b_rc])
                        if nm == 1:
                            V(lambda: nc.vector.tensor_tensor(out=ydst, in0=O1[:, :, 0:64],
                                                              in1=rc[:, 0:4].unsqueeze(2).to_broadcast([128, 4, 64]),
                                                              op=ALU.mult), [b_pO[0], b_rc], by)
                        else:
                            O2 = pO[1][:, 0:260].rearrange("p (j d) -> p j d", d=65)
                            V(lambda: nc.vector.reciprocal(out=rc[:, 4:8], in_=O2[:, :, 64]), [b_pO[1]], [b_rc])
                            V(lambda: nc.vector.tensor_scalar(out=rc[:, 4:8], in0=rc[:, 4:8], scalar1=lam[:, 5:6], scalar2=None,
                                                              op0=ALU.mult), [b_rc, b_lam], [b_rc])
                            V(lambda: nc.vector.tensor_tensor(out=o1[:, :, :], in0=O1[:, :, 0:64],
                                                              in1=rc[:, 0:4].unsqueeze(2).to_broadcast([128, 4, 64]),
                                                              op=ALU.mult), [b_pO[0], b_rc], [b_o1])
                            V(lambda: nc.vector.tensor_tensor(out=o2[:, :, :], in0=O2[:, :, 0:64],
                                                              in1=rc[:, 4:8].unsqueeze(2).to_broadcast([128, 4, 64]),
                                                              op=ALU.mult), [b_pO[1], b_rc], [b_o2])
                            G(lambda: nc.gpsimd.tensor_tensor(out=o1[:, :, :], in0=o1[:, :, :], in1=o2[:, :, :], op=ALU.add),
                              [b_o1, b_o2], [b_o1])
                            G(lambda: nc.gpsimd.tensor_tensor(out=osq[:, :, :], in0=o1[:, :, :], in1=o1[:, :, :], op=ALU.mult),
                              [b_o1], [b_osq])
                            V(lambda: nc.vector.tensor_reduce(out=rc[:, 8:12], in_=osq[:, :, :], axis=AX.X, op=ALU.add),
                              [b_osq], [b_rc])
                            A(lambda: nc.scalar.activation(out=rc[:, 8:12], in_=rc[:, 8:12], func=AF.Ln, scale=1.0 / 64,
                                                           bias=eps6[:, 0:1]), [b_rc, b_const], [b_rc])
                            A(lambda: nc.scalar.activation(out=rc[:, 12:16], in_=rc[:, 8:12], func=AF.Exp, scale=-0.5),
                              [b_rc], [b_rc])
                            V(lambda: nc.vector.tensor_tensor(out=o2[:, :, :], in0=o1[:, :, :],
                                                              in1=rc[:, 12:16].unsqueeze(2).to_broadcast([128, 4, 64]),
                                                              op=ALU.mult), [b_o1, b_rc], [b_o2])
                            V(lambda: nc.vector.tensor_tensor(out=ydst, in0=o2[:, :, :],
                                                              in1=gA_bc[:, :].unsqueeze(1).to_broadcast([128, 4, 64]),
                                                              op=ALU.mult), [b_o2, b_gA], by)

                load_V(0)
                for h in range(4):
                    state["qi"] += 1
                    i_ = state["qi"] % 4
                    g0 = 2 * (h % 2)
                    S.dma("sp", qT[i_][32 * g0:32 * g0 + 64, :], QKT[h // 2, 32 * g0:32 * g0 + 64, :], writes=[b_qT[i_]])
                    S.dma("sp", kT[i_][32 * g0:32 * g0 + 64, :], QKT[2 + h // 2, 32 * g0:32 * g0 + 64, :], writes=[b_kT[i_]])
                    maps = [(qT[i_], b_qT[i_], kT[i_], b_kT[i_], 32, 32 * (g0 + c_)) for c_ in range(2)]
                    job(maps, h * VP, h * 64, alibi_slope=SLOPES_A[h])
                load_V(1)
                for h in range(4):
                    state["qi"] += 1
                    q_t, bq = load_map(4 + h, 0, 96, True)
                    k_t, bk_ = load_map(8 + h, 0, 96, False)
                    job([(q_t, bq, k_t, bk_, 96)], h * VP, 256 + h * 64)
                load_V(2)
                for j in range(2):
                    state["qi"] += 1
                    i_ = state["qi"] % 4
                    S.dma("sp", qT[i_][:, :], QKT[12 + j, :, :], writes=[b_qT[i_]])
                    S.dma("sp", kT[i_][0:64, :], QKT[14, 64 * j:64 * j + 64, :], writes=[b_kT[i_]])
                    S.dma("sp", kT[i_][64:128, :], QKT[14, 64 * j:64 * j + 64, :], writes=[b_kT[i_]])
                    maps = [(qT[i_], b_qT[i_], kT[i_], b_kT[i_], 64, 64 * c_) for c_ in range(2)]
                    job(maps, j * VP, None, ycols=[512 + (2 * j + c_) * 64 for c_ in range(2)])
                S.barrier()

        def phase_D(l, y_res, b_y):
            with ExitStack() as st:
                tokD = sb(st, "tokD", [128, 8, 512], BF16)
                Vsh = sb(st, "Vsh", [128, 9, VW], BF16)
                QTd = sb(st, "QTd", [128, 2, 1024], BF16)
                KTd = sb(st, "KTd", [128, 2, 1024 + 128], BF16)
                dtab = sb(st, "dtab", [128, 2, 512], F32)
                Sb = [sb(st, f"dSb{i}", [128, 512], F32) for i in range(4)]
                E = [sb(st, f"dE{i}", [128, 512], BF16) for i in range(4)]
                ost = [sb(st, f"ost{i}", [128, 4, 260], F32) for i in range(2)]
                acc_l = [sb(st, f"acc{i}", [128, 260], F32) for i in range(2)]
                od_l = [[sb(st, f"od{j}_{i}", [128, 260], F32) for i in range(3)] for j in range(2)]
                rcd_l = [sb(st, f"rcd{i}", [128, 4], F32) for i in range(2)]
                pT = [pbank(st, f"dT{i}") for i in range(2)]
                pS = [pbank(st, f"dS{i}") for i in range(4)]
                pO = [pbank(st, f"dO{i}") for i in range(2)]
                b_tok, b_Vsh, b_QT, b_KT, b_dt = Buf(), Buf(), Buf(), Buf(), Buf()
                b_Sb = [Buf() for _ in range(4)]
                b_E = [Buf() for _ in range(4)]
                b_ost = [Buf(), Buf()]
                b_pT = [PB(), PB()]
                b_pS = [PB() for _ in range(4)]
                b_pO = [PB(), PB()]
                b_acc_l, b_od_l, b_rcd_l = [Buf(), Buf()], [[Buf() for _ in range(3)] for _ in range(2)], [Buf(), Buf()]
                S.dma("sp", dtab[:], I["dtab"], writes=[b_dt])
                cnt = {"s": 0, "e": 0, "o": 0}
                for bi, d in enumerate(DILS):
                    Ltot = SEQ // d
                    nseg = max(1, Ltot // 1024)
                    for r in range(d):
                        for seg in range(nseg):
                            Lc = min(Ltot, 1024)
                            i0 = seg * 1024
                            nt = Lc // 128
                            Lc_ = min(Ltot, 1024)
                            G(lambda: nc.gpsimd.memset(KTd[:, :, 0:64], 0.0), [], [b_KT])
                            G(lambda: nc.gpsimd.memset(KTd[:, :, 64 + Lc_:128 + Lc_], 0.0), [], [b_KT])
                            if seg * 1024 - 64 < 0:
                                G(lambda: nc.gpsimd.memset(Vsh[0:64, 0, :], 0.0), [], [b_Vsh])
                            if seg * 1024 + Lc_ + 64 > Ltot:
                                G(lambda: nc.gpsimd.memset(Vsh[64:128, Lc_ // 128, :], 0.0), [], [b_Vsh])
                            base = r + d * i0
                            src = DTOK[base: base + d * (Lc - 1) + 1: d, :]
                            S.dma("sp", tokD[:, 0:nt, :], src[:, 0:512].rearrange("(t p) c -> p t c", p=128), writes=[b_tok])
                            lo = i0 - 64
                            hi = i0 + Lc + 64
                            lo_c, hi_c = max(lo, 0), min(hi, Ltot)
                            u0 = lo_c - lo
                            n_rows = hi_c - lo_c
                            pos = 0
                            while pos < n_rows:
                                u = u0 + pos
                                tj, pj = u // 128, u % 128
                                take = min(128 - pj, n_rows - pos)
                                if pj == 0 and take == 128:
                                    nfull = (n_rows - pos) // 128
                                    t_first = r + d * (lo_c + pos)
                                    srcv = DTOK[t_first: t_first + d * (128 * nfull - 1) + 1: d, 512:512 + VW]
                                    S.dma("sp", Vsh[:, tj:tj + nfull, :], srcv.rearrange("(t p) c -> p t c", p=128),
                                          writes=[b_Vsh])
                                    pos += 128 * nfull
                                else:
                                    t_first = r + d * (lo_c + pos)
                                    srcv = DTOK[t_first: t_first + d * (take - 1) + 1: d, 512:512 + VW]
                                    S.dma("sp", Vsh[pj:pj + take, tj, :], srcv, writes=[b_Vsh])
                                    pos += take
                            for t in range(nt):
                                pTb = pT[t % 2][:, :].bitcast(BF16)
                                for j in range(4):
                                    PE(lambda: nc.tensor.transpose(pTb[:, j * 128:(j + 1) * 128], tokD[:, t, j * 128:(j + 1) * 128],
                                                                   ident_b[:]), [b_tok, b_const], [b_pT[t % 2]], signal=(j == 3))
                                V(lambda: nc.vector.tensor_copy(QTd[:, :, t * 128:(t + 1) * 128],
                                                                pTb[:, 0:256].rearrange("p (c t) -> p c t", t=128)),
                                  [b_pT[t % 2]], [b_QT])
                                A(lambda: nc.scalar.activation(func=AF.Identity, out=KTd[:, :, 64 + t * 128: 64 + (t + 1) * 128],
                                                         in_=pTb[:, 256:512].rearrange("p (c t) -> p c t", t=128)),
                                  [b_pT[t % 2]], [b_KT])
                            if nseg > 1:
                                for side in range(2):
                                    hs = i0 - 64 if side == 0 else i0 + Lc
                                    if hs < 0 or hs >= Ltot:
                                        continue
                                    t_first = r + d * hs
                                    srck = DTOK[t_first: t_first + d * 63 + 1: d, 256:512]
                                    S.dma("sp", tokD[0:64, 0, 0:256], srck, writes=[b_tok])
                                    pTb = pT[0][:, :].bitcast(BF16)
                                    for j in range(2):
                                        PE(lambda: nc.tensor.transpose(pTb[:, j * 128: j * 128 + 64],
                                                                       tokD[0:64, 0, j * 128:(j + 1) * 128], ident_b[0:64, 0:64]),
                                           [b_tok, b_const], [b_pT[0]], signal=(j == 1))
                                    col = 0 if side == 0 else 64 + Lc
                                    V(lambda: nc.vector.tensor_copy(
                                        KTd[:, :, col:col + 64],
                                        pTb[:, 0:256].rearrange("p (c t) -> p c t", t=128)[:, :, 0:64]), [b_pT[0]], [b_KT])
                            steps = [(g0, h, ab) for g0 in range(0, nt, 4) for h in range(4) for ab in range(2)]
                            n = len(steps)
                            sis, eis = [0] * n, [0] * n
                            LS, LP = 1, 2
                            for i in range(n + LP):
                                if i < n:
                                    g0, h, ab = steps[i]
                                    ng = min(4, nt - g0)
                                    ch, pr = h // 2, (h % 2) * 64
                                    si = cnt["s"] % 4
                                    cnt["s"] += 1
                                    sis[i] = si
                                    for jj in range(ng):
                                        qc = (g0 + jj) * 128
                                        kc0 = qc + ab * 128
                                        PE(lambda: nc.tensor.matmul(pS[si][:, jj * 128:(jj + 1) * 128],
                                                                    lhsT=KTd[pr:pr + 64, ch, kc0:kc0 + 128],
                                                                    rhs=QTd[pr:pr + 64, ch, qc:qc + 128], start=True, stop=True),
                                           [b_KT, b_QT], [b_pS[si]], signal=(jj == ng - 1))
                                j = i - LS
                                if 0 <= j < n:
                                    g0, h, ab = steps[j]
                                    ng = min(4, nt - g0)
                                    w = ng * 128
                                    si = sis[j]
                                    ei = cnt["e"] % 4
                                    cnt["e"] += 1
                                    eis[j] = ei
                                    V(lambda: nc.vector.scalar_tensor_tensor(out=Sb[ei][:, 0:w], in0=dtab[:, ab, 0:w],
                                                                             scalar=-SLOPES_D[h] * d, in1=pS[si][:, 0:w],
                                                                             op0=ALU.mult, op1=ALU.add),
                                      [b_dt, b_pS[si]], [b_Sb[ei]])
                                    A(lambda: nc.scalar.activation(out=E[ei][:, 0:w], in_=Sb[ei][:, 0:w], func=AF.Exp),
                                      [b_Sb[ei]], [b_E[ei]])
                                j = i - LP
                                if 0 <= j < n:
                                    g0, h, ab = steps[j]
                                    ng = min(4, nt - g0)
                                    ei = eis[j]
                                    oi = (g0 // 4) % 2
                                    for jj in range(ng):
                                        PE(lambda: nc.tensor.matmul(pO[h % 2][:, jj * 65:(jj + 1) * 65],
                                                                    lhsT=E[ei][:, jj * 128:(jj + 1) * 128],
                                                                    rhs=Vsh[:, g0 + jj + ab, h * VP:h * VP + 65],
                                                                    start=(ab == 0 and jj == 0), stop=(ab == 1),
                                                                    skip_group_check=True),
                                           [b_E[ei], b_Vsh], [b_pO[h % 2]], signal=(jj == ng - 1))
                                    if ab == 1:
                                        V(lambda: nc.vector.tensor_copy(ost[oi][:, 0:ng, h * 65:(h + 1) * 65],
                                                                        pO[h % 2][:, 0:ng * 65].rearrange("p (j d) -> p j d", d=65)),
                                          [b_pO[h % 2]], [b_ost[oi]])
                                        if h == 3:
                                            for jj in range(ng):
                                                t_first = r + d * (i0 + (g0 + jj) * 128)
                                                S.dma("sp", OD[bi, t_first: t_first + d * 127 + 1: d, :], ost[oi][:, jj, :],
                                                      reads=[b_ost[oi]])
                S.barrier()
                for sg in range(NSUB):
                    rows = slice(sg * 128, (sg + 1) * 128)
                    acc, od, rcd = acc_l[sg % 2], od_l[sg % 2], rcd_l[sg % 2]
                    b_acc, b_od, b_rcd = b_acc_l[sg % 2], b_od_l[sg % 2], b_rcd_l[sg % 2]
                    for bi in range(3):
                        S.dma("sp", od[bi][:], OD[bi, rows, :], writes=[b_od[bi]])
                    V(lambda: nc.vector.tensor_tensor(out=acc[:], in0=od[0][:], in1=od[1][:], op=ALU.add),
                      [b_od[0], b_od[1]], [b_acc])
                    V(lambda: nc.vector.tensor_tensor(out=acc[:], in0=acc[:], in1=od[2][:], op=ALU.add), [b_acc, b_od[2]], [b_acc])
                    a3 = acc[:, :].rearrange("p (h d) -> p h d", d=65)
                    V(lambda: nc.vector.reciprocal(out=rcd[:, 0:4], in_=a3[:, :, 64]), [b_acc], [b_rcd])
                    V(lambda: nc.vector.tensor_tensor(out=y_res[:, sg, 768:1024].rearrange("p (h d) -> p h d", d=64),
                                                      in0=a3[:, :, 0:64], in1=rcd[:, 0:4].unsqueeze(2).to_broadcast([128, 4, 64]),
                                                      op=ALU.mult), [b_acc, b_rcd], [b_y[sg]])
                S.barrier()

        def layer_norm(v, bv, dst, bdst, g_bc, b_bc, btab, stats, mv, bst, eng2):
            v4 = v.rearrange("p (c f) -> p c f", f=256)
            for c_ in range(4):
                V(lambda: nc.vector.bn_stats(out=stats[:, c_, :], in_=v4[:, c_, :]), [bv], [bst])
            V(lambda: nc.vector.bn_aggr(out=mv[:, 0:2], in_=stats[:, :, :].rearrange("p c f -> p (c f)")), [bst], [bst])
            A(lambda: nc.scalar.activation(out=mv[:, 2:3], in_=mv[:, 1:2], func=AF.Ln, bias=eps5[:, 0:1]), [bst, b_const], [bst])
            A(lambda: nc.scalar.activation(out=mv[:, 3:4], in_=mv[:, 2:3], func=AF.Exp, scale=-0.5), [bst], [bst])
            V(lambda: nc.vector.scalar_tensor_tensor(out=mv[:, 4:5], in0=mv[:, 0:1], scalar=-1.0, in1=mv[:, 3:4], op0=ALU.mult,
                                                     op1=ALU.mult), [bst], [bst])
            A(lambda: nc.scalar.activation(out=v, in_=v, func=AF.Identity, scale=mv[:, 3:4], bias=mv[:, 4:5]), [bv, bst], [bv])
            S.op(eng2, lambda: (nc.gpsimd if eng2 == "pool" else nc.vector).tensor_tensor(out=v, in0=v, in1=g_bc, op=ALU.mult),
                 [bv, btab], [bv])
            S.op(eng2, lambda: (nc.gpsimd if eng2 == "pool" else nc.vector).tensor_tensor(out=dst, in0=v, in1=b_bc, op=ALU.add),
                 [bv, btab], [bdst])

        def phase_O1(l, xsrc, y_res, b_y):
            with ExitStack() as st:
                w_o = sb(st, "w_o", [128, 8, D], BF16)
                gA = sb(st, "gA", [128, D], F32)
                lng = sb(st, "lng", [128, D], F32)
                lnb = sb(st, "lnb", [128, D], F32)
                xs = [sb(st, f"oxs{i}", [128, D], F32) for i in range(2)]
                yT = [sb(st, f"yT{i}", [128, 8, 128], BF16) for i in range(2)]
                v = [sb(st, f"ov{i}", [128, D], F32) for i in range(2)]
                x1 = [sb(st, f"ox1{i}", [128, D], F32) for i in range(2)]
                h2 = [sb(st, f"oh2{i}", [128, 8, 512], BF16) for i in range(2)]
                stats_l = [sb(st, f"ostats{i}", [128, 4, 6], F32) for i in range(2)]
                mv_l = [sb(st, f"omv{i}", [128, 8], F32) for i in range(2)]
                pb = [pbank(st, f"po{i}") for i in range(8)]
                bp = [PB() for _ in range(8)]
                b_w, b_tab, b_stt_l = Buf(), Buf(), [Buf(), Buf()]
                b_xs, b_yT, b_v, b_x1, b_h2 = ([Buf(), Buf()] for _ in range(5))
                wsrc = I["w_o"][l].rearrange("(kc p) n -> p kc n", p=128)
                for kc in range(8):
                    S.dma("pool", w_o[:, kc, :], wsrc[:, kc, :], writes=[b_w])
                S.dma("sp", gA[:], GB[l, 0], writes=[b_tab])
                S.dma("sp", lng[:], I["ln_attn_g"][l, :].partition_broadcast(128), writes=[b_tab])
                S.dma("sp", lnb[:], I["ln_attn_b"][l, :].partition_broadcast(128), writes=[b_tab])
                def o1_sub(sg):
                    T, s = sg // 4, sg % 4
                    i2 = sg % 2
                    tsl = slice(s * 128, (s + 1) * 128)
                    S.dma("sp", xs[i2][:], xsrc[sg * 128:(sg + 1) * 128, :], writes=[b_xs[i2]])
                    tb = pb[0 + i2][:, :].bitcast(BF16)
                    for kc in range(8):
                        PE(lambda: nc.tensor.transpose(tb[:, kc * 128:(kc + 1) * 128], y_res[:, sg, kc * 128:(kc + 1) * 128],
                                                       ident_b[:]), [b_y[sg], b_const], [bp[i2]], signal=(kc == 7))
                    yield
                    V(lambda: nc.vector.tensor_copy(yT[i2][:, :, :], tb[:, 0:1024].rearrange("p (c t) -> p c t", t=128)),
                      [bp[i2]], [b_yT[i2]])
                    for hf in range(2):
                        yield
                        bk = 2 + 2 * i2 + hf
                        for kc in range(8):
                            PE(lambda: nc.tensor.matmul(pb[bk][:, :], lhsT=yT[i2][:, kc, :], rhs=w_o[:, kc, hf * 512:(hf + 1) * 512],
                                                        start=(kc == 0), stop=(kc == 7)), [b_yT[i2], b_w], [bp[bk]], signal=(kc == 7))
                        hs = slice(hf * 512, (hf + 1) * 512)
                        V(lambda: nc.vector.tensor_tensor(out=v[i2][:, hs], in0=pb[bk][:, :], in1=gA[:, hs], op=ALU.mult),
                          [bp[bk], b_tab], [b_v[i2]])
                    yield
                    V(lambda: nc.vector.scalar_tensor_tensor(out=v[i2][:, :], in0=xs[i2][:, :], scalar=ALPHA,
                                                             in1=v[i2][:, :], op0=ALU.mult, op1=ALU.add),
                      [b_xs[i2], b_v[i2]], [b_v[i2]])
                    yield
                    layer_norm(v[i2][:, :], b_v[i2], x1[i2][:, :], b_x1[i2], lng[:, :], lnb[:, :], b_tab, stats_l[i2], mv_l[i2], b_stt_l[i2], "pool")
                    yield
                    S.dma("sp", X1[sg * 128:(sg + 1) * 128, :], x1[i2][:], reads=[b_x1[i2]])
                    for hf in range(2):
                        yield
                        bk = 6 + hf
                        for cc in range(4):
                            kc = hf * 4 + cc
                            PE(lambda: nc.tensor.transpose(pb[bk][:, cc * 128:(cc + 1) * 128], x1[i2][:, kc * 128:(kc + 1) * 128],
                                                           ident_f[:]), [b_x1[i2], b_const], [bp[bk]], signal=(cc == 3))
                        for cc in range(4):
                            kc = hf * 4 + cc
                            if cc % 2 == 0:
                                A(lambda: nc.scalar.activation(out=h2[T % 2][:, kc, tsl], in_=pb[bk][:, cc * 128:(cc + 1) * 128],
                                                               func=AF.Identity, scale=modT[:, l, 32 + kc:33 + kc],
                                                               bias=modT[:, l, 24 + kc:25 + kc]), [bp[bk], b_modT], [b_h2[T % 2]])
                            else:
                                V(lambda: nc.vector.tensor_scalar(out=h2[T % 2][:, kc, tsl], in0=pb[bk][:, cc * 128:(cc + 1) * 128],
                                                                  scalar1=modT[:, l, 32 + kc:33 + kc],
                                                                  scalar2=modT[:, l, 24 + kc:25 + kc], op0=ALU.mult, op1=ALU.add),
                                  [bp[bk], b_modT], [b_h2[T % 2]])
                    if s == 3:
                        S.dma("sp", H2T[:, :, T * 512:(T + 1) * 512].rearrange("c r t -> r c t"), h2[T % 2][:, :, :],
                              reads=[b_h2[T % 2]])
                for sg0 in range(0, NSUB, 2):
                    gens = [o1_sub(sg0), o1_sub(sg0 + 1)]
                    while gens:
                        for g_ in list(gens):
                            try:
                                next(g_)
                            except StopIteration:
                                gens.remove(g_)
                S.barrier()

        def phase_O2(l, dst):
            with ExitStack() as st:
                w_up = sb(st, "w_up", [128, 8, HID], BF16)
                w_dn = sb(st, "w_dn", [128, 32, D], BF16)
                gM = sb(st, "gM", [128, D], F32)
                lng = sb(st, "lng2", [128, D], F32)
                lnb = sb(st, "lnb2", [128, D], F32)
                h2 = [sb(st, f"mh2{i}", [128, 8, 512], BF16) for i in range(1)]
                uT = sb(st, "uT", [128, 32, 512], BF16)
                rr = [sb(st, f"rr{i}", [128, 512], F32) for i in range(3)]
                x1 = [sb(st, f"mx1{i}", [128, D], F32) for i in range(2)]
                v = [sb(st, f"mv{i}", [128, D], F32) for i in range(2)]
                stats = sb(st, "mstats", [128, 4, 6], F32)
                mv = sb(st, "mmv", [128, 8], F32)
                pb = [pbank(st, f"pm{i}") for i in range(8)]
                bp = [PB() for _ in range(8)]
                b_wu, b_wd, b_tab, b_stt, b_uT = Buf(), Buf(), Buf(), Buf(), Buf()
                b_h2, b_x1, b_v = ([Buf(), Buf()] for _ in range(3))
                b_rr = [Buf() for _ in range(3)]
                usrc = I["w_up"][l].rearrange("(kc p) n -> p kc n", p=128)
                for kc in range(8):
                    S.dma("pool", w_up[:, kc, :], usrc[:, kc, :], writes=[b_wu])
                dsrc = I["w_down"][l].rearrange("(kc p) n -> p kc n", p=128)
                for k4 in range(8):
                    S.dma("pool", w_dn[:, k4 * 4:(k4 + 1) * 4, :], dsrc[:, k4 * 4:(k4 + 1) * 4, :], writes=[b_wd])
                S.dma("sp", gM[:], GB[l, 1], writes=[b_tab])
                S.dma("sp", lng[:], I["ln_mlp_g"][l, :].partition_broadcast(128), writes=[b_tab])
                S.dma("sp", lnb[:], I["ln_mlp_b"][l, :].partition_broadcast(128), writes=[b_tab])
                ri = 0
                for T in range(8):
                    h_t, bh = h2[0], b_h2[0]
                    S.dma("sp", h_t[:, :, :], H2T[:, :, T * 512:(T + 1) * 512].rearrange("c r t -> r c t"), writes=[bh])
                    for hc in range(32):
                        bk = hc % 4
                        for kc in range(8):
                            PE(lambda: nc.tensor.matmul(pb[bk][:, :], lhsT=w_up[:, kc, hc * 128:(hc + 1) * 128], rhs=h_t[:, kc, :],
                                                        start=(kc == 0), stop=(kc == 7)), [b_wu, bh], [bp[bk]], signal=(kc == 7))
                        r_, br = rr[ri % 3], b_rr[ri % 3]
                        ri += 1
                        A(lambda: nc.scalar.activation(out=r_[:, :], in_=pb[bk][:, :], func=AF.Relu), [bp[bk]], [br])
                        if hc % 2 == 0:
                            V(lambda: nc.vector.tensor_tensor(out=uT[:, hc, :], in0=r_[:, :], in1=r_[:, :], op=ALU.mult), [br], [b_uT])
                        else:
                            G(lambda: nc.gpsimd.tensor_tensor(out=uT[:, hc, :], in0=r_[:, :], in1=r_[:, :], op=ALU.mult), [br], [b_uT])
                    for s in range(4):
                        sg = T * 4 + s
                        i2 = sg % 2
                        rows = slice(sg * 128, (sg + 1) * 128)
                        S.dma("sp", x1[i2][:], X1[rows, :], writes=[b_x1[i2]])
                        for hf in range(2):
                            bk = 4 + 2 * i2 + hf
                            for hc in range(32):
                                PE(lambda: nc.tensor.matmul(pb[bk][:, :], lhsT=uT[:, hc, s * 128:(s + 1) * 128],
                                                            rhs=w_dn[:, hc, hf * 512:(hf + 1) * 512], start=(hc == 0), stop=(hc == 31)),
                                   [b_uT, b_wd], [bp[bk]], signal=(hc == 31))
                            hs = slice(hf * 512, (hf + 1) * 512)
                            V(lambda: nc.vector.tensor_tensor(out=v[i2][:, hs], in0=pb[bk][:, :], in1=gM[:, hs], op=ALU.mult),
                              [bp[bk], b_tab], [b_v[i2]])
                        V(lambda: nc.vector.scalar_tensor_tensor(out=v[i2][:, :], in0=x1[i2][:, :], scalar=ALPHA, in1=v[i2][:, :],
                                                                 op0=ALU.mult, op1=ALU.add), [b_x1[i2], b_v[i2]], [b_v[i2]])
                        layer_norm(v[i2][:, :], b_v[i2], v[i2][:, :], b_v[i2], lng[:, :], lnb[:, :], b_tab, stats, mv, b_stt, "pool")
                        S.dma("sp", dst[rows, :], v[i2][:], reads=[b_v[i2]])
                S.barrier()

        for l in range(nlayers):
            if "stop0" in dbg:
                break
            xsrc = I["x"] if l == 0 else XN
            phase_P(l, xsrc)
            if "stopP" in dbg:
                break
            with ExitStack() as lst:
                y_res = sb(lst, f"y_res{l}", [128, NSUB, D], BF16)
                b_y = [Buf(f"y{i}") for i in range(NSUB)]
                phase_att(l, y_res, b_y)
                phase_D(l, y_res, b_y)
                if YDBG is not None and l == 0:
                    for sg in range(NSUB):
                        S.dma("sp", YDBG[sg * 128:(sg + 1) * 128, :], y_res[:, sg, :], reads=[b_y[sg]])
                    S.barrier()
                if "stopA" in dbg:
                    break
                phase_O1(l, xsrc, y_res, b_y)
            phase_O2(l, out if l == nlayers - 1 else XN)
        S.barrier()
        print("ops", S.n_ops, "dmas", S.n_dma, "sems", S.nsem)
    return nc


DBG = {}
_CONSTS = None


def make_in_maps(inputs):
    global _CONSTS
    if _CONSTS is None:
        _CONSTS = _host_consts()
    shared = {}
    for n in W_NAMES:
        a = np.ascontiguousarray(np.asarray(inputs[n], dtype=np.float32))
        shared[n] = a.reshape(W_SHAPES[n])
    for n, a in _CONSTS.items():
        shared["k_" + n] = a
    x = np.asarray(inputs["x"], dtype=np.float32)
    c = np.asarray(inputs["c"], dtype=np.float32)
    maps = []
    for b in range(8):
        m = dict(shared)
        m["x"] = np.ascontiguousarray(x[b])
        m["c"] = np.ascontiguousarray(c[b].reshape(8, 128).T)
        maps.append(m)
    return maps


def kernel(**inputs):
    nc = build()
    in_maps = make_in_maps(inputs)
    res = run_bass_kernel_spmd(nc, in_maps, core_ids=list(range(8)))
    return np.stack([np.asarray(r["out"]) for r in res.results], axis=0).astype(np.float32)
```

```python
import math
from contextlib import ExitStack
import numpy as np
import concourse.bass as bass
import concourse.mybir as mybir
from concourse.bass_utils import run_bass_kernel_spmd

F32 = mybir.dt.float32
BF16 = mybir.dt.bfloat16
AF = mybir.ActivationFunctionType
ALU = mybir.AluOpType
AX = mybir.AxisListType

SEQ = 4096
D = 1024
NSUB = SEQ // 128
HID = 4096
INC = 2720
ALPHA = 4 ** 0.25
SLOPES_A = [2.0 ** -1, 2.0 ** -3, 2.0 ** -5, 2.0 ** -7]
SLOPES_D = [2.0 ** -2, 2.0 ** -4, 2.0 ** -6, 2.0 ** -8]
DILS = [1, 4, 16]
SC_A = 32 ** -0.5
SC_B = 96 ** -0.5
SC_C = 0.125
SC_D = 0.125
VP = 66
VW = 4 * VP


class Buf:
    __slots__ = ("name", "w", "r", "excl")

    def __init__(self, name="", excl=False):
        self.name = name
        self.w = None
        self.r = []
        self.excl = excl


def PB(name=""):
    return Buf(name, excl=True)


class Tok:
    __slots__ = ("eng", "sem", "val")

    def __init__(self, eng, sem=None, val=None):
        self.eng = eng
        self.sem = sem
        self.val = val


class Sched:
    EPOCH = 20000
    NDMA = 8

    def __init__(self, nc, stack):
        self.nc = nc
        self.stack = stack
        self.engs = {"pe": nc.tensor, "act": nc.scalar, "dve": nc.vector,
                     "pool": nc.gpsimd, "sp": nc.sync}
        self.count = {e: 0 for e in self.engs}
        self.cursem = {}
        self.pending = {e: [] for e in self.engs}
        self.waited = {e: {} for e in self.engs}
        self.nsem = 0
        for e in self.engs:
            self._new_epoch(e)
        self.dma_sems, self.dma_cnt, self.dma_last, self.dma_i = {}, {}, {}, {}
        for q in ("sp", "act", "pool"):
            self.dma_sems[q] = [self._sem(f"dma_{q}_{i}") for i in range(self.NDMA)]
            self.dma_cnt[q] = [0] * self.NDMA
            self.dma_last[q] = [None] * self.NDMA
            self.dma_i[q] = 0
        self.n_ops = {e: 0 for e in self.engs}
        self.n_dma = 0

    def _sem(self, name):
        self.nsem += 1
        return self.stack.enter_context(self.nc.semaphore(name))

    def _new_epoch(self, e):
        self.cursem[e] = self._sem(f"s_{e}_{self.nsem}")
        self.count[e] = 0

    def _wait(self, eng, tok):
        if tok is None:
            return
        if tok.sem is None:
            raise RuntimeError(f"dependency on unsignalled op on {tok.eng}")
        key = id(tok.sem)
        w = self.waited[eng]
        if w.get(key, 0) >= tok.val:
            return
        w[key] = tok.val
        self.engs[eng].wait_ge(tok.sem, tok.val)

    def _deps(self, eng, reads, writes):
        for b in reads:
            t = b.w
            if t is not None and not (t.eng == eng and eng == "pe"):
                self._wait(eng, t)
        for b in writes:
            t = b.w
            if t is not None and t.eng != eng:
                self._wait(eng, t)
            for t in b.r:
                if t.eng != eng:
                    self._wait(eng, t)

    def _record(self, tok, reads, writes):
        for b in reads:
            b.r.append(tok)
            if len(b.r) > 16:
                last = {}
                for t in b.r:
                    last[(t.eng, id(t.sem))] = t
                b.r = list(last.values())
        for b in writes:
            b.w = tok
            b.r = []

    def op(self, eng, fn, reads=(), writes=(), signal=True):
        if any(b.excl for b in reads):
            writes = list(writes) + [b for b in reads if b.excl]
            reads = [b for b in reads if not b.excl]
        self._deps(eng, reads, writes)
        inst = fn()
        self.n_ops[eng] += 1
        tok = Tok(eng)
        self.pending[eng].append(tok)
        if signal:
            if self.count[eng] >= self.EPOCH:
                self._new_epoch(eng)
            self.count[eng] += 1
            sem = self.cursem[eng]
            inst.then_inc(sem, 1)
            for t in self.pending[eng]:
                t.sem = sem
                t.val = self.count[eng]
            self.pending[eng] = []
        self._record(tok, reads, writes)
        return tok

    def prewait(self, eng, reads=(), writes=()):
        writes = list(writes) + [b for b in reads if b.excl]
        reads = [b for b in reads if not b.excl]
        self._deps(eng, reads, writes)

    def dma(self, q, out, in_, reads=(), writes=(), **kw):
        i = self.dma_i[q]
        self.dma_i[q] = (i + 1) % self.NDMA
        prev = self.dma_last[q][i]
        if prev is not None:
            self._wait(q, prev)
        self._deps(q, reads, writes)
        sem = self.dma_sems[q][i]
        self.dma_cnt[q][i] += 16
        inst = self.engs[q].dma_start(out=out, in_=in_, **kw)
        inst.then_inc(sem, 16)
        tok = Tok("dma_" + q + str(i), sem, self.dma_cnt[q][i])
        self.dma_last[q][i] = tok
        self._record(tok, reads, writes)
        self.n_dma += 1
        return tok

    def barrier(self):
        toks = []
        for e in self.engs:
            if self.pending[e]:
                raise RuntimeError(f"barrier with unsignalled ops on {e}")
            if self.count[e] > 0:
                toks.append(Tok(e, self.cursem[e], self.count[e]))
        for q in self.dma_last:
            for t in self.dma_last[q]:
                if t is not None:
                    toks.append(t)
        for e in self.engs:
            for t in toks:
                if t.eng != e:
                    self._wait(e, t)


def _host_consts():
    c = {}
    c["ident"] = np.eye(128, dtype=np.float32)
    tok = (np.arange(NSUB)[None, :] * 128 + np.arange(128)[:, None]).astype(np.float64)
    freqs = 10000.0 ** (-np.arange(16, dtype=np.float64) / 16)

    def cs(pos):
        ang = pos[..., None].astype(np.float32).astype(np.float64) * freqs.astype(np.float32).astype(np.float64)
        ang = (pos[..., None].astype(np.float32) * freqs.astype(np.float32)).astype(np.float32)
        return np.cos(ang.astype(np.float64)), np.sin(ang.astype(np.float64))

    cp, sp_ = cs(tok)
    rb = np.zeros((128, NSUB, 2, 64), np.float64)
    for i, s in enumerate([SC_B, 1.0]):
        rb[:, :, i, 0:16] = cp * s
        rb[:, :, i, 16:32] = cp * s
        rb[:, :, i, 32:48] = -sp_ * s
        rb[:, :, i, 48:64] = sp_ * s
    c["ropeB"] = rb.astype(np.float32)
    cr, sr = cs(np.floor(tok / 64))
    cc, sc_ = cs(np.mod(tok, 64))
    rc = np.zeros((128, NSUB, 128), np.float64)
    rc[:, :, 0:16] = cr
    rc[:, :, 16:32] = cr
    rc[:, :, 32:48] = cc
    rc[:, :, 48:64] = cc
    rc[:, :, 64:80] = -sr
    rc[:, :, 80:96] = sr
    rc[:, :, 96:112] = -sc_
    rc[:, :, 112:128] = sc_
    c["ropeC"] = rc.astype(np.float32)
    ki = np.arange(128)[:, None].astype(np.float64)
    qi = np.arange(512)[None, :].astype(np.float64)
    al = np.zeros((128, 5, 512), np.float64)
    al[:, 0, :] = qi - ki
    for o in range(4):
        al[:, 1 + o, :] = np.abs(qi - ki - 128 * o)
    c["alibi"] = al.astype(np.float32)
    q128 = (np.arange(512) % 128)[None, :].astype(np.float64)
    dt = np.zeros((128, 2, 512), np.float64)
    da = np.abs(ki - 64 - q128)
    db = np.abs(ki + 64 - q128)
    dt[:, 0, :] = np.where(da <= 64, da, 1.0e6)
    dt[:, 1, :] = np.where(db <= 64, db, 1.0e6)
    c["dtab"] = dt.astype(np.float32)
    return c


W_NAMES = ["w_ada", "b_ada", "w_in", "w_o", "diff_lambda", "diff_subln_g", "mla_q_norm_g", "mla_w_uq",
           "mla_kv_norm_g", "mla_w_ukv", "gqa_q_norm_g", "gqa_k_norm_g", "ln_attn_g", "ln_attn_b",
           "w_up", "w_down", "ln_mlp_g", "ln_mlp_b"]
W_SHAPES = {"w_ada": [2, 1024, 6144], "b_ada": [2, 6144], "w_in": [2, 1024, INC], "w_o": [2, 1024, 1024],
            "diff_lambda": [2, 128], "diff_subln_g": [2, 64], "mla_q_norm_g": [2, 384],
            "mla_w_uq": [2, 384, 384], "mla_kv_norm_g": [2, 256], "mla_w_ukv": [2, 256, 512],
            "gqa_q_norm_g": [2, 64], "gqa_k_norm_g": [2, 64], "ln_attn_g": [2, 1024], "ln_attn_b": [2, 1024],
            "w_up": [2, 1024, HID], "w_down": [2, HID, 1024], "ln_mlp_g": [2, 1024], "ln_mlp_b": [2, 1024]}
C_SHAPES = {"ident": [128, 128], "ropeB": [128, NSUB, 2, 64], "ropeC": [128, NSUB, 128],
            "alibi": [128, 5, 512], "dtab": [128, 2, 512]}


def build(nlayers=2, dbg=()):
    nc = bass.Bass("TRN2", target_bir_lowering=False)
    I = {}
    I["x"] = nc.dram_tensor("x", [SEQ, D], F32, kind="ExternalInput").ap()
    I["c"] = nc.dram_tensor("c", [128, 8], F32, kind="ExternalInput").ap()
    for n in W_NAMES:
        I[n] = nc.dram_tensor(n, W_SHAPES[n], F32, kind="ExternalInput").ap()
    for n in C_SHAPES:
        I[n] = nc.dram_tensor("k_" + n, C_SHAPES[n], F32, kind="ExternalInput").ap()
    out = nc.dram_tensor("out", [SEQ, D], F32, kind="ExternalOutput").ap()

    def scratch(name, shape, dt):
        kind = "ExternalOutput" if name in dbg else "Internal"
        return nc.dram_tensor(name, shape, dt, kind=kind).ap()

    QKT = scratch("QKT", [15, 128, SEQ], BF16)
    VG = scratch("VG", [3, SEQ, VW], BF16)
    DTOK = scratch("DTOK", [SEQ, 512 + VW], BF16)
    OD = scratch("OD", [3, SEQ, 260], F32)
    X1 = scratch("X1", [SEQ, D], F32)
    H2T = scratch("H2T", [8, 128, SEQ], BF16)
    XN = scratch("XN", [SEQ, D], F32)
    GB = scratch("GB", [2, 2, 128, D], F32)
    YDBG = scratch("YDBG", [SEQ, D], BF16) if "YDBG" in dbg else None

    with ExitStack() as top:
        S = Sched(nc, top)

        uid = [0]

        def sb(st, name, shape, dt):
            uid[0] += 1
            return st.enter_context(nc.sbuf_tensor(f"s{uid[0]}_{name}", shape, dt))

        def pbank(st, name):
            uid[0] += 1
            return st.enter_context(nc.psum_tensor(f"p{uid[0]}_{name}", [128, 512], F32))

        def V(fn, reads, writes):
            return S.op("dve", fn, reads, writes)

        def A(fn, reads, writes):
            return S.op("act", fn, reads, writes)

        def G(fn, reads, writes):
            return S.op("pool", fn, reads, writes)

        def PE(fn, reads, writes, signal=True):
            return S.op("pe", fn, reads, writes, signal)

        ident_f = sb(top, "ident_f", [128, 128], F32)
        ident_b = sb(top, "ident_b", [128, 128], BF16)
        modT = sb(top, "modT", [128, 2, 48], F32)
        eps6 = sb(top, "eps6", [128, 1], F32)
        eps5 = sb(top, "eps5", [128, 1], F32)
        b_const = Buf("const")
        b_modT = Buf("modT")
        S.dma("sp", ident_f[:], I["ident"], writes=[b_const])
        S.dma("pool", ident_b[:], I["ident"], writes=[b_const])
        V(lambda: nc.vector.memset(eps6[:], 1e-6), [], [b_const])
        V(lambda: nc.vector.memset(eps5[:], 1e-5), [], [b_const])

        with ExitStack() as st:
            condT = sb(st, "condT", [128, 8], F32)
            ones_row = sb(st, "ones_row", [1, 128], F32)
            modrow = sb(st, "modrow", [1, 6144], F32)
            brow = sb(st, "brow", [1, 6144], F32)
            wa = [sb(st, f"wa{i}", [128, 8, 512], F32) for i in range(2)]
            gbt = sb(st, "gbt", [128, 1024], F32)
            ps = [pbank(st, f"sps{i}") for i in range(2)]
            b_cond, b_ones, b_mrow, b_brow, b_gbt = Buf(), Buf(), Buf(), Buf(), Buf()
            b_wa = [Buf(), Buf()]
            b_ps = [PB(), PB()]
            S.dma("sp", condT[:], I["c"], writes=[b_cond])
            A(lambda: nc.scalar.activation(out=condT[:], in_=condT[:], func=AF.Silu), [b_cond], [b_cond])
            V(lambda: nc.vector.memset(ones_row[:], 1.0), [], [b_ones])
            for l in range(nlayers):
                S.dma("sp", brow[:], I["b_ada"][l:l + 1, :], writes=[b_brow])
                wsrc = I["w_ada"][l].rearrange("(kc p) n -> p kc n", p=128)
                for pc in range(12):
                    S.dma("sp", wa[pc % 2][:], wsrc[:, :, pc * 512:(pc + 1) * 512], writes=[b_wa[pc % 2]])
                    for kc in range(8):
                        PE(lambda: nc.tensor.matmul(ps[pc % 2][0:1, :], lhsT=condT[:, kc:kc + 1], rhs=wa[pc % 2][:, kc, :],
                                                    start=(kc == 0), stop=(kc == 7)),
                           [b_cond, b_wa[pc % 2]], [b_ps[pc % 2]], signal=(kc == 7))
                    V(lambda: nc.vector.tensor_tensor(out=modrow[0:1, pc * 512:(pc + 1) * 512], in0=ps[pc % 2][0:1, :],
                                                      in1=brow[0:1, pc * 512:(pc + 1) * 512], op=ALU.add),
                      [b_ps[pc % 2], b_brow], [b_mrow])
                for j in range(48):
                    PE(lambda: nc.tensor.matmul(ps[0][:, j:j + 1], lhsT=modrow[0:1, j * 128:(j + 1) * 128],
                                                rhs=ones_row[0:1, 0:1], start=True, stop=True),
                       [b_mrow, b_ones], [b_ps[0]], signal=(j == 47))
                V(lambda: nc.vector.tensor_copy(modT[:, l, :], ps[0][:, 0:48]), [b_ps[0]], [b_modT])
                V(lambda: nc.vector.tensor_scalar(out=modT[:, l, 8:16], in0=modT[:, l, 8:16], scalar1=1.0, scalar2=None,
                                                  op0=ALU.add), [b_modT], [b_modT])
                V(lambda: nc.vector.tensor_scalar(out=modT[:, l, 32:40], in0=modT[:, l, 32:40], scalar1=1.0, scalar2=None,
                                                  op0=ALU.add), [b_modT], [b_modT])
                for gi, base in enumerate([2048, 5120]):
                    for hf in range(2):
                        PE(lambda: nc.tensor.matmul(ps[1][:, :], lhsT=ones_row[0:1, :],
                                                    rhs=modrow[0:1, base + hf * 512: base + (hf + 1) * 512],
                                                    start=True, stop=True), [b_mrow, b_ones], [b_ps[1]])
                        V(lambda: nc.vector.tensor_copy(gbt[:, hf * 512:(hf + 1) * 512], ps[1][:, :]), [b_ps[1]], [b_gbt])
                    S.dma("sp", GB[l, gi], gbt[:], reads=[b_gbt])
            S.barrier()

        def phase_P(l, xsrc):
            with ExitStack() as st:
                w_in = sb(st, "w_in", [128, 8, INC], BF16)
                w_uq = sb(st, "w_uq", [128, 3, 384], BF16)
                w_ukv = sb(st, "w_ukv", [128, 2, 512], BF16)
                gq_bc = sb(st, "gq_bc", [128, 384], F32)
                gkv_bc = sb(st, "gkv_bc", [128, 256], F32)
                gC_bc = sb(st, "gC_bc", [128, 6, 64], F32)
                ropeB = sb(st, "ropeB", [128, NSUB, 2, 64], F32)
                ropeC = sb(st, "ropeC", [128, NSUB, 128], F32)
                xs = [sb(st, f"xs{i}", [128, D], F32) for i in range(2)]
                hT = [sb(st, f"hT{i}", [128, 8, 512], BF16) for i in range(2)]
                bf32_l = [sb(st, f"bf32{i}", [128, 672], F32) for i in range(2)]
                cqk_l = [sb(st, f"cqk{i}", [128, 384], F32) for i in range(2)]
                junk_l = [sb(st, f"junk{i}", [128, 384], F32) for i in range(2)]
                qkA_l = [sb(st, f"qkA{i}", [128, 512], BF16) for i in range(2)]
                qkD_l = [sb(st, f"qkD{i}", [128, 512], BF16) for i in range(2)]
                VA_l = [sb(st, f"VA{i}", [128, 4, VP], BF16) for i in range(2)]
                VB_l = [sb(st, f"VB{i}", [128, 4, VP], BF16) for i in range(2)]
                VC_l = [sb(st, f"VC{i}", [128, 4, VP], BF16) for i in range(2)]
                VD_l = [sb(st, f"VD{i}", [128, 4, VP], BF16) for i in range(2)]
                cqn_l = [sb(st, f"cqn{i}", [128, 640], BF16) for i in range(2)]
                cT_l = [sb(st, f"cT{i}", [128, 5, 128], BF16) for i in range(2)]
                qB_l = [sb(st, f"qB{i}", [128, 4, 96], BF16) for i in range(2)]
                kB_l = [sb(st, f"kB{i}", [128, 4, 96], BF16) for i in range(2)]
                cC_l = [sb(st, f"cC{i}", [128, 384], BF16) for i in range(2)]
                t1_l = [sb(st, f"t1{i}", [128, 384], F32) for i in range(2)]
                t2_l = [sb(st, f"t2{i}", [128, 384], F32) for i in range(2)]
                nq_l = [sb(st, f"nq{i}", [128, 384], F32) for i in range(2)]
                stt__l = [sb(st, f"stt_{i}", [128, 16], F32) for i in range(2)]
                stage = [sb(st, f"stage{i}", [128, 15, 512], BF16) for i in range(2)]
                pb = [pbank(st, f"pp{i}") for i in range(8)]
                bp = [PB(f"pp{i}") for i in range(8)]
                b_w, b_tab = Buf(), Buf()
                b_xs = [Buf(), Buf()]
                b_hT = [Buf(), Buf()]
                BL = {n: [Buf(n + '0'), Buf(n + '1')] for n in ['bf32', 'cqk', 'junk', 'qkA', 'qkD', 'VA', 'VB', 'VC', 'VD', 'cqn', 'cT', 'qB', 'kB', 'cC', 't1', 't2', 'nq', 'stt_']}
                b_wk = [Buf() for _ in range(8)]
                b_hk = [[Buf() for _ in range(8)] for _ in range(2)]
                b_stage = [Buf(), Buf()]

                wsrc = I["w_in"][l].rearrange("(kc p) n -> p kc n", p=128)
                for gi_, (c0_, n_) in enumerate([(0, 512), (512, 512), (1024, 416), (1440, 512), (1952, 512), (2464, 256)]):
                    S.dma("pool", w_in[:, :, c0_:c0_ + n_], wsrc[:, :, c0_:c0_ + n_], writes=[b_wk[gi_]])
                S.dma("pool", w_uq[:], I["mla_w_uq"][l].rearrange("(kc p) n -> p kc n", p=128), writes=[b_w])
                S.dma("pool", w_ukv[:], I["mla_w_ukv"][l].rearrange("(kc p) n -> p kc n", p=128), writes=[b_w])
                S.dma("sp", gq_bc[:], I["mla_q_norm_g"][l, :].partition_broadcast(128), writes=[b_tab])
                S.dma("sp", gkv_bc[:], I["mla_kv_norm_g"][l, :].partition_broadcast(128), writes=[b_tab])
                for h in range(6):
                    src = I["gqa_q_norm_g"] if h < 4 else I["gqa_k_norm_g"]
                    S.dma("sp", gC_bc[:, h, :], src[l, :].partition_broadcast(128), writes=[b_tab])
                V(lambda: nc.vector.tensor_scalar(out=gC_bc[:, 0:4, :], in0=gC_bc[:, 0:4, :], scalar1=SC_C, scalar2=None,
                                                  op0=ALU.mult), [b_tab], [b_tab])
                S.dma("sp", ropeB[:], I["ropeB"], writes=[b_tab])
                S.dma("sp", ropeC[:], I["ropeC"], writes=[b_tab])
                for i_ in range(2):
                    for n_ in ('VA', 'VB', 'VC', 'VD'):
                        vt = {'VA': VA_l, 'VB': VB_l, 'VC': VC_l, 'VD': VD_l}[n_][i_]
                        V(lambda: nc.vector.memset(vt[:], 1.0), [], [BL[n_][i_]])

                groups = [(0, 512), (512, 512), (1024, 416), (1440, 512), (1952, 512), (2464, 256)]
                def stage1(sg):
                    T, s = sg // 4, sg % 4
                    tsl = slice(s * 128, (s + 1) * 128)
                    h_t, bhk = hT[T % 2], b_hk[T % 2]
                    stg, bstg = stage[T % 2], b_stage[T % 2]
                    bf32 = bf32_l[sg % 2]
                    b_bf32 = BL['bf32'][sg % 2]
                    cqk = cqk_l[sg % 2]
                    b_cqk = BL['cqk'][sg % 2]
                    junk = junk_l[sg % 2]
                    b_junk = BL['junk'][sg % 2]
                    qkA = qkA_l[sg % 2]
                    b_qkA = BL['qkA'][sg % 2]
                    qkD = qkD_l[sg % 2]
                    b_qkD = BL['qkD'][sg % 2]
                    VA = VA_l[sg % 2]
                    b_VA = BL['VA'][sg % 2]
                    VB = VB_l[sg % 2]
                    b_VB = BL['VB'][sg % 2]
                    VC = VC_l[sg % 2]
                    b_VC = BL['VC'][sg % 2]
                    VD = VD_l[sg % 2]
                    b_VD = BL['VD'][sg % 2]
                    cqn = cqn_l[sg % 2]
                    b_cqn = BL['cqn'][sg % 2]
                    cT = cT_l[sg % 2]
                    b_cT = BL['cT'][sg % 2]
                    qB = qB_l[sg % 2]
                    b_qB = BL['qB'][sg % 2]
                    kB = kB_l[sg % 2]
                    b_kB = BL['kB'][sg % 2]
                    cC = cC_l[sg % 2]
                    b_cC = BL['cC'][sg % 2]
                    t1 = t1_l[sg % 2]
                    b_t1 = BL['t1'][sg % 2]
                    t2 = t2_l[sg % 2]
                    b_t2 = BL['t2'][sg % 2]
                    nq = nq_l[sg % 2]
                    b_nq = BL['nq'][sg % 2]
                    stt_ = stt__l[sg % 2]
                    b_st = BL['stt_'][sg % 2]
                    x_t, bx = xs[sg % 2], b_xs[sg % 2]
                    S.dma("sp", x_t[:], xsrc[sg * 128:(sg + 1) * 128, :], writes=[bx])
                    for hf in range(2):
                        for cc in range(4):
                            kc = hf * 4 + cc
                            PE(lambda: nc.tensor.transpose(pb[hf][:, cc * 128:(cc + 1) * 128], x_t[:, kc * 128:(kc + 1) * 128],
                                                           ident_f[:]), [bx, b_const], [bp[hf]], signal=(cc == 3))
                        for cc in range(4):
                            kc = hf * 4 + cc
                            if cc % 2 == 0:
                                A(lambda: nc.scalar.activation(out=h_t[:, kc, tsl], in_=pb[hf][:, cc * 128:(cc + 1) * 128],
                                                               func=AF.Identity, scale=modT[:, l, 8 + kc:9 + kc],
                                                               bias=modT[:, l, kc:kc + 1]), [bp[hf], b_modT], [bhk[kc]])
                            else:
                                V(lambda: nc.vector.tensor_scalar(out=h_t[:, kc, tsl], in0=pb[hf][:, cc * 128:(cc + 1) * 128],
                                                                  scalar1=modT[:, l, 8 + kc:9 + kc],
                                                                  scalar2=modT[:, l, kc:kc + 1], op0=ALU.mult, op1=ALU.add),
                                  [bp[hf], b_modT], [bhk[kc]])
                    yield
                    for gi, (c0, ncol) in enumerate(groups):
                        if gi > 0:
                            yield
                        bk = 2 + gi % 3
                        for kc in range(8):
                            PE(lambda: nc.tensor.matmul(pb[bk][:, 0:ncol], lhsT=h_t[:, kc, tsl], rhs=w_in[:, kc, c0:c0 + ncol],
                                                        start=(kc == 0), stop=(kc == 7)), [bhk[kc], b_wk[gi]], [bp[bk]], signal=(kc == 7))
                        P_ = pb[bk]
                        if gi == 0:
                            A(lambda: nc.scalar.activation(out=qkA[:, 0:256], in_=P_[:, 0:256], func=AF.Identity, scale=SC_A),
                              [bp[bk]], [b_qkA])
                            V(lambda: nc.vector.tensor_copy(qkA[:, 256:512], P_[:, 256:512]), [bp[bk]], [b_qkA])
                        elif gi == 1:
                            A(lambda: nc.scalar.activation(func=AF.Identity, out=VA[:, :, 0:64], in_=P_[:, 0:256].rearrange("p (h d) -> p h d", d=64)),
                              [bp[bk]], [b_VA])
                            V(lambda: nc.vector.tensor_copy(bf32[:, 0:256], P_[:, 256:512]), [bp[bk]], [b_bf32])
                        elif gi == 2:
                            V(lambda: nc.vector.tensor_copy(bf32[:, 256:672], P_[:, 0:416]), [bp[bk]], [b_bf32])
                        elif gi == 3:
                            V(lambda: nc.vector.tensor_copy(cqk[:, :], P_[:, 0:384]), [bp[bk]], [b_cqk])
                            A(lambda: nc.scalar.activation(func=AF.Identity, out=VC[:, 0:2, 0:64],
                                                     in_=P_[:, 384:512].rearrange("p (h d) -> p h d", d=64)),
                              [bp[bk]], [b_VC])
                        elif gi == 4:
                            A(lambda: nc.scalar.activation(out=qkD[:, 0:256], in_=P_[:, 0:256], func=AF.Identity, scale=SC_D),
                              [bp[bk]], [b_qkD])
                            V(lambda: nc.vector.tensor_copy(qkD[:, 256:512], P_[:, 256:512]), [bp[bk]], [b_qkD])
                        else:
                            A(lambda: nc.scalar.activation(func=AF.Identity, out=VD[:, :, 0:64], in_=P_[:, 0:256].rearrange("p (h d) -> p h d", d=64)),
                              [bp[bk]], [b_VD])

                def stage2(sg):
                    T, s = sg // 4, sg % 4
                    tsl = slice(s * 128, (s + 1) * 128)
                    h_t, bhk = hT[T % 2], b_hk[T % 2]
                    stg, bstg = stage[T % 2], b_stage[T % 2]
                    bf32 = bf32_l[sg % 2]
                    b_bf32 = BL['bf32'][sg % 2]
                    cqk = cqk_l[sg % 2]
                    b_cqk = BL['cqk'][sg % 2]
                    junk = junk_l[sg % 2]
                    b_junk = BL['junk'][sg % 2]
                    qkA = qkA_l[sg % 2]
                    b_qkA = BL['qkA'][sg % 2]
                    qkD = qkD_l[sg % 2]
                    b_qkD = BL['qkD'][sg % 2]
                    VA = VA_l[sg % 2]
                    b_VA = BL['VA'][sg % 2]
                    VB = VB_l[sg % 2]
                    b_VB = BL['VB'][sg % 2]
                    VC = VC_l[sg % 2]
                    b_VC = BL['VC'][sg % 2]
                    VD = VD_l[sg % 2]
                    b_VD = BL['VD'][sg % 2]
                    cqn = cqn_l[sg % 2]
                    b_cqn = BL['cqn'][sg % 2]
                    cT = cT_l[sg % 2]
                    b_cT = BL['cT'][sg % 2]
                    qB = qB_l[sg % 2]
                    b_qB = BL['qB'][sg % 2]
                    kB = kB_l[sg % 2]
                    b_kB = BL['kB'][sg % 2]
                    cC = cC_l[sg % 2]
                    b_cC = BL['cC'][sg % 2]
                    t1 = t1_l[sg % 2]
                    b_t1 = BL['t1'][sg % 2]
                    t2 = t2_l[sg % 2]
                    b_t2 = BL['t2'][sg % 2]
                    nq = nq_l[sg % 2]
                    b_nq = BL['nq'][sg % 2]
                    stt_ = stt__l[sg % 2]
                    b_st = BL['stt_'][sg % 2]
                    V(lambda: nc.vector.scalar_tensor_tensor(out=junk[:, 0:384], in0=bf32[:, 0:384], scalar=1.0,
                                                             in1=bf32[:, 0:384], op0=ALU.mult, op1=ALU.mult,
                                                             accum_out=stt_[:, 0:1]), [b_bf32], [b_junk, b_st])
                    V(lambda: nc.vector.scalar_tensor_tensor(out=junk[:, 0:256], in0=bf32[:, 384:640], scalar=1.0,
                                                             in1=bf32[:, 384:640], op0=ALU.mult, op1=ALU.mult,
                                                             accum_out=stt_[:, 1:2]), [b_bf32], [b_junk, b_st])
                    A(lambda: nc.scalar.activation(out=stt_[:, 2:3], in_=stt_[:, 0:1], func=AF.Ln, scale=1.0 / 384,
                                                   bias=eps6[:, 0:1]), [b_st, b_const], [b_st])
                    A(lambda: nc.scalar.activation(out=stt_[:, 3:4], in_=stt_[:, 1:2], func=AF.Ln, scale=1.0 / 256,
                                                   bias=eps6[:, 0:1]), [b_st, b_const], [b_st])
                    A(lambda: nc.scalar.activation(out=stt_[:, 4:6], in_=stt_[:, 2:4], func=AF.Exp, scale=-0.5), [b_st], [b_st])
                    V(lambda: nc.vector.scalar_tensor_tensor(out=cqn[:, 0:384], in0=bf32[:, 0:384], scalar=stt_[:, 4:5],
                                                             in1=gq_bc[:, :], op0=ALU.mult, op1=ALU.mult),
                      [b_bf32, b_st, b_tab], [b_cqn])
                    V(lambda: nc.vector.scalar_tensor_tensor(out=cqn[:, 384:640], in0=bf32[:, 384:640], scalar=stt_[:, 5:6],
                                                             in1=gkv_bc[:, :], op0=ALU.mult, op1=ALU.mult),
                      [b_bf32, b_st, b_tab], [b_cqn])
                    yield
                    p5b = pb[5][:, :].bitcast(BF16)
                    for j in range(5):
                        PE(lambda: nc.tensor.transpose(p5b[:, j * 128:(j + 1) * 128], cqn[:, j * 128:(j + 1) * 128], ident_b[:]),
                           [b_cqn, b_const], [bp[5]], signal=(j == 4))
                    V(lambda: nc.vector.tensor_copy(cT[:, :, :], p5b[:, 0:640].rearrange("p (c t) -> p c t", t=128)),
                      [bp[5]], [b_cT])
                    for j in range(3):
                        PE(lambda: nc.tensor.matmul(pb[6][:, 0:384], lhsT=cT[:, j, :], rhs=w_uq[:, j, :], start=(j == 0),
                                                    stop=(j == 2)), [b_cT, b_w], [bp[6]], signal=(j == 2))
                    for j in range(2):
                        PE(lambda: nc.tensor.matmul(pb[7][:, 0:512], lhsT=cT[:, 3 + j, :], rhs=w_ukv[:, j, :], start=(j == 0),
                                                    stop=(j == 1)), [b_cT, b_w], [bp[7]], signal=(j == 1))
                    yield
                    q3 = pb[6][:, 0:384].rearrange("p (h d) -> p h d", d=96)
                    kv3 = pb[7][:, 0:512].rearrange("p (h d) -> p h d", d=128)
                    A(lambda: nc.scalar.activation(out=qB[:, :, 0:64], in_=q3[:, :, 0:64], func=AF.Identity, scale=SC_B),
                      [bp[6]], [b_qB])
                    Tq = ropeB[:, sg, 0, :]
                    Tk = ropeB[:, sg, 1, :]
                    t1q = t1[:, 0:128].rearrange("p (h d) -> p h d", d=32)
                    t2q = t2[:, 0:128].rearrange("p (h d) -> p h d", d=32)
                    V(lambda: nc.vector.tensor_tensor(out=t1q, in0=q3[:, :, 64:96],
                                                      in1=Tq[:, 0:32].unsqueeze(1).to_broadcast([128, 4, 32]), op=ALU.mult),
                      [bp[6], b_tab], [b_t1])
                    V(lambda: nc.vector.tensor_tensor(out=t2q[:, :, 0:16], in0=q3[:, :, 80:96],
                                                      in1=Tq[:, 32:48].unsqueeze(1).to_broadcast([128, 4, 16]), op=ALU.mult),
                      [bp[6], b_tab], [b_t2])
                    V(lambda: nc.vector.tensor_tensor(out=t2q[:, :, 16:32], in0=q3[:, :, 64:80],
                                                      in1=Tq[:, 48:64].unsqueeze(1).to_broadcast([128, 4, 16]), op=ALU.mult),
                      [bp[6], b_tab], [b_t2])
                    G(lambda: nc.gpsimd.tensor_tensor(out=qB[:, :, 64:96], in0=t1q, in1=t2q, op=ALU.add), [b_t1, b_t2], [b_qB])
                    V(lambda: nc.vector.tensor_copy(kB[:, :, 0:64], kv3[:, :, 0:64]), [bp[7]], [b_kB])
                    A(lambda: nc.scalar.activation(func=AF.Identity, out=VB[:, :, 0:64], in_=kv3[:, :, 64:128]), [bp[7]], [b_VB])
                    yield
                    kr = bf32[:, 640:672]
                    V(lambda: nc.vector.tensor_tensor(out=t1[:, 128:160], in0=kr, in1=Tk[:, 0:32], op=ALU.mult),
                      [b_bf32, b_tab], [b_t1])
                    V(lambda: nc.vector.tensor_tensor(out=t2[:, 128:144], in0=bf32[:, 656:672], in1=Tk[:, 32:48], op=ALU.mult),
                      [b_bf32, b_tab], [b_t2])
                    V(lambda: nc.vector.tensor_tensor(out=t2[:, 144:160], in0=bf32[:, 640:656], in1=Tk[:, 48:64], op=ALU.mult),
                      [b_bf32, b_tab], [b_t2])
                    G(lambda: nc.gpsimd.tensor_tensor(out=kB[:, :, 64:96],
                                                      in0=t1[:, 128:160].unsqueeze(1).to_broadcast([128, 4, 32]),
                                                      in1=t2[:, 128:160].unsqueeze(1).to_broadcast([128, 4, 32]), op=ALU.add),
                      [b_t1, b_t2], [b_kB])
                    yield
                    c3 = cqk[:, :].rearrange("p (h d) -> p h d", d=64)
                    V(lambda: nc.vector.tensor_tensor(out=junk[:, :], in0=cqk[:, :], in1=cqk[:, :], op=ALU.mult),
                      [b_cqk], [b_junk])
                    V(lambda: nc.vector.tensor_reduce(out=stt_[:, 6:12], in_=junk[:, :].rearrange("p (h d) -> p h d", d=64),
                                                      axis=AX.X, op=ALU.add), [b_junk], [b_st])
                    A(lambda: nc.scalar.activation(out=stt_[:, 6:12], in_=stt_[:, 6:12], func=AF.Ln, scale=1.0 / 64,
                                                   bias=eps6[:, 0:1]), [b_st, b_const], [b_st])
                    A(lambda: nc.scalar.activation(out=stt_[:, 6:12], in_=stt_[:, 6:12], func=AF.Exp, scale=-0.5), [b_st], [b_st])
                    yield
                    n3 = nq[:, :].rearrange("p (h d) -> p h d", d=64)
                    V(lambda: nc.vector.tensor_tensor(out=n3, in0=c3, in1=stt_[:, 6:12].unsqueeze(2).to_broadcast([128, 6, 64]),
                                                      op=ALU.mult), [b_cqk, b_st], [b_nq])
                    G(lambda: nc.gpsimd.tensor_tensor(out=n3, in0=n3, in1=gC_bc[:, :, :], op=ALU.mult), [b_nq, b_tab], [b_nq])
                    yield
                    cosA = ropeC[:, sg, 0:64]
                    sinS = ropeC[:, sg, 64:128].rearrange("p (r f i) -> p r f i", r=2, f=2)
                    V(lambda: nc.vector.tensor_tensor(out=t1[:, :].rearrange("p (h d) -> p h d", d=64), in0=n3,
                                                      in1=cosA.unsqueeze(1).to_broadcast([128, 6, 64]), op=ALU.mult),
                      [b_nq, b_tab], [b_t1])
                    n5 = nq[:, :].rearrange("p (h r f i) -> p h r f i", h=6, r=2, f=2)
                    t5 = t2[:, :].rearrange("p (h r f i) -> p h r f i", h=6, r=2, f=2)
                    for f in range(2):
                        V(lambda: nc.vector.tensor_tensor(out=t5[:, :, :, f, :], in0=n5[:, :, :, 1 - f, :],
                                                          in1=sinS[:, :, f, :].unsqueeze(1).to_broadcast([128, 6, 2, 16]),
                                                          op=ALU.mult), [b_nq, b_tab], [b_t2])
                    G(lambda: nc.gpsimd.tensor_tensor(out=cC[:, :], in0=t1[:, :], in1=t2[:, :], op=ALU.add), [b_t1, b_t2], [b_cC])
                    yield
                    tb0 = pb[6][:, :].bitcast(BF16)
                    tb1 = pb[7][:, :].bitcast(BF16)
                    for j in range(4):
                        PE(lambda: nc.tensor.transpose(tb0[:, j * 128:(j + 1) * 128], qkA[:, j * 128:(j + 1) * 128], ident_b[:]),
                           [b_qkA, b_const], [bp[6]], signal=False)
                    for h in range(4):
                        PE(lambda: nc.tensor.transpose(tb0[0:96, (4 + h) * 128:(5 + h) * 128], qB[:, h, :], ident_b[:]),
                           [b_qB, b_const], [bp[6]], signal=(h == 3))
                    for h in range(4):
                        PE(lambda: nc.tensor.transpose(tb1[0:96, h * 128:(h + 1) * 128], kB[:, h, :], ident_b[:]),
                           [b_kB, b_const], [bp[7]], signal=False)
                    for j in range(3):
                        PE(lambda: nc.tensor.transpose(tb1[:, (4 + j) * 128:(5 + j) * 128], cC[:, j * 128:(j + 1) * 128],
                                                       ident_b[:]), [b_cC, b_const], [bp[7]], signal=(j == 2))
                    V(lambda: nc.vector.tensor_copy(stg[:, 0:8, tsl], tb0[:, 0:1024].rearrange("p (c t) -> p c t", t=128)),
                      [bp[6]], [bstg])
                    A(lambda: nc.scalar.activation(func=AF.Identity, out=stg[:, 8:15, tsl], in_=tb1[:, 0:896].rearrange("p (c t) -> p c t", t=128)),
                      [bp[7]], [bstg])
                    yield
                    rows = slice(sg * 128, (sg + 1) * 128)
                    S.dma("sp", VG[0, rows, :], VA[:, :, :].rearrange("p h d -> p (h d)"), reads=[b_VA])
                    S.dma("sp", VG[1, rows, :], VB[:, :, :].rearrange("p h d -> p (h d)"), reads=[b_VB])
                    S.dma("sp", VG[2, rows, :], VC[:, :, :].rearrange("p h d -> p (h d)"), reads=[b_VC])
                    S.dma("sp", DTOK[rows, 0:512], qkD[:, :], reads=[b_qkD])
                    S.dma("sp", DTOK[rows, 512:512 + VW], VD[:, :, :].rearrange("p h d -> p (h d)"), reads=[b_VD])
                    if s == 3:
                        for (ca, cb) in [(0, 4), (4, 8), (8, 12), (12, 15)]:
                            S.dma("sp", QKT[ca:cb, :, T * 512:(T + 1) * 512].rearrange("c r t -> r c t"), stg[:, ca:cb, :],
                                  reads=[bstg])

                nsub_ = NSUB if 'nsub' not in DBG else DBG['nsub']
                for _ in stage1(0):
                    pass
                for sg in range(nsub_):
                    gens = [stage2(sg)] + ([stage1(sg + 1)] if sg + 1 < nsub_ else [])
                    while gens:
                        for g_ in list(gens):
                            try:
                                next(g_)
                            except StopIteration:
                                gens.remove(g_)
                S.barrier()

        def phase_att(l, y_res, b_y):
            with ExitStack() as st:
                qT = [sb(st, f"qT{i}", [128, SEQ], BF16) for i in range(4)]
                kT = [sb(st, f"kT{i}", [128, SEQ], BF16) for i in range(4)]
                Vg = sb(st, "Vg", [128, NSUB, VW + 64], BF16)
                alibi = sb(st, "alibi", [128, 5, 512], F32)
                Sb = [sb(st, f"Sb{i}", [128, 512], F32) for i in range(3)]
                E = [sb(st, f"E{i}", [128, 512], BF16) for i in range(4)]
                lam_t = sb(st, "lam_t", [128, 128], F32)
                lam = sb(st, "lam", [128, 8], F32)
                gA_bc = sb(st, "gA_bc", [128, 64], F32)
                o1 = sb(st, "o1", [128, 4, 64], F32)
                o2 = sb(st, "o2", [128, 4, 64], F32)
                osq = sb(st, "osq", [128, 4, 64], F32)
                rc = sb(st, "rc", [128, 16], F32)
                pS = [pbank(st, f"pS{i}") for i in range(4)]
                pOT = [pbank(st, f"pOT{i}") for i in range(2)]
                pO = [pbank(st, f"pO{i}") for i in range(2)]
                otS = [sb(st, f"otS{i}", [65, 512], F32) for i in range(2)]
                b_pOT = [PB(), PB()]
                b_otS = [Buf(), Buf()]
                b_qT = [Buf() for _ in range(4)]
                b_kT = [Buf() for _ in range(4)]
                b_Vg, b_al, b_lam, b_gA = Buf(), Buf(), Buf(), Buf()
                b_Sb = [Buf() for _ in range(3)]
                b_E = [Buf() for _ in range(4)]
                b_pS = [PB() for _ in range(4)]
                b_pO = [PB() for _ in range(4)]
                b_o1, b_o2, b_osq, b_rc = Buf(), Buf(), Buf(), Buf()
                V(lambda: nc.vector.memset(Vg[:, :, VW:VW + 64], 0.0), [], [b_Vg])
                S.dma("sp", alibi[:], I["alibi"], writes=[b_al])
                lam_init = 0.8 - 0.6 * math.exp(-0.3 * l)
                S.dma("sp", lam_t[:], I["diff_lambda"][l, :].partition_broadcast(128), writes=[b_lam])
                V(lambda: nc.vector.scalar_tensor_tensor(out=lam_t[:, 0:32], in0=lam_t[:, 0:32], scalar=1.0, in1=lam_t[:, 32:64],
                                                         op0=ALU.mult, op1=ALU.mult, accum_out=lam[:, 0:1]), [b_lam], [b_lam])
                V(lambda: nc.vector.scalar_tensor_tensor(out=lam_t[:, 64:96], in0=lam_t[:, 64:96], scalar=1.0,
                                                         in1=lam_t[:, 96:128], op0=ALU.mult, op1=ALU.mult,
                                                         accum_out=lam[:, 1:2]), [b_lam], [b_lam])
                A(lambda: nc.scalar.activation(out=lam[:, 2:4], in_=lam[:, 0:2], func=AF.Exp), [b_lam], [b_lam])
                V(lambda: nc.vector.tensor_tensor(out=lam[:, 4:5], in0=lam[:, 2:3], in1=lam[:, 3:4], op=ALU.subtract),
                  [b_lam], [b_lam])
                V(lambda: nc.vector.tensor_scalar(out=lam[:, 5:6], in0=lam[:, 4:5], scalar1=lam_init, scalar2=-1.0,
                                                  op0=ALU.add, op1=ALU.mult), [b_lam], [b_lam])
                S.dma("sp", gA_bc[:], I["diff_subln_g"][l, :].partition_broadcast(128), writes=[b_gA])
                V(lambda: nc.vector.tensor_scalar(out=gA_bc[:], in0=gA_bc[:], scalar1=1.0 - lam_init, scalar2=None,
                                                  op0=ALU.mult), [b_gA], [b_gA])

                state = {"qi": 0, "ei": 0, "si": 0, "sbi": 0}

                def load_map(chunk, r0, nrows, isq):
                    i = state["qi"] % 4
                    t, b = (qT[i], b_qT[i]) if isq else (kT[i], b_kT[i])
                    S.dma("sp", t[0:nrows, :], QKT[chunk, r0:r0 + nrows, :], writes=[b])
                    return t, b

                def load_V(g):
                    for a in range(4):
                        S.dma("sp", Vg[:, a * 8:(a + 1) * 8, 0:VW],
                              VG[g, a * 1024:(a + 1) * 1024, :].rearrange("(t p) c -> p t c", p=128), writes=[b_Vg])

                def job(maps, vcol, ycol, alibi_slope=None, ycols=None):
                    nm = len(maps)
                    pend = [None]
                    for qt in range(8):
                        if nm == 2:
                            groups = [[(kt, 0), (kt, 1)] for kt in range(NSUB)]
                        else:
                            groups = [[(2 * j, 0), (2 * j + 1, 0)] for j in range(NSUB // 2)]
                        ng = len(groups)
                        for gi in range(ng + 2):
                            if gi < ng:
                                banks = [2 * (gi % 2), 2 * (gi % 2) + 1]
                                S.prewait("pe", [mp[1] for mp in maps] + [mp[3] for mp in maps], [b_pS[bk] for bk in banks])
                                for idx, (kt, m) in enumerate(groups[gi]):
                                    q_t, bq, k_t, bk_, nr = maps[m][:5]
                                    p0 = maps[m][5] if len(maps[m]) > 5 else 0
                                    si = banks[idx]
                                    tp_ = (p0, 0) if p0 == 96 else None
                                    PE(lambda: nc.tensor.matmul(pS[si][:, :], lhsT=k_t[p0:p0 + nr, kt * 128:(kt + 1) * 128],
                                                                rhs=q_t[p0:p0 + nr, qt * 512:(qt + 1) * 512], start=True, stop=True,
                                                                tile_position=tp_),
                                       [bq, bk_], [b_pS[si]], signal=(idx == 1))
                            j = gi - 1
                            if 0 <= j < ng:
                                for idx, (kt, m) in enumerate(groups[j]):
                                    si = 2 * (j % 2) + idx
                                    ei = 2 * (j % 2) + idx
                                    if alibi_slope is None:
                                        A(lambda: nc.scalar.activation(out=E[ei][:, :], in_=pS[si][:, :], func=AF.Exp),
                                          [b_pS[si]], [b_E[ei]])
                                    else:
                                        k0, q0 = kt * 128, qt * 512
                                        if k0 + 128 <= q0:
                                            tab, sc_, bias = alibi[:, 0, :], -alibi_slope, -alibi_slope * (q0 - k0)
                                        elif k0 >= q0 + 512:
                                            tab, sc_, bias = alibi[:, 0, :], alibi_slope, -alibi_slope * (k0 - q0)
                                        else:
                                            tab, sc_, bias = alibi[:, 1 + (k0 - q0) // 128, :], -alibi_slope, 0.0
                                        sbi = state["sbi"] % 3
                                        state["sbi"] += 1
                                        V(lambda: nc.vector.scalar_tensor_tensor(out=Sb[sbi][:, :], in0=tab, scalar=sc_,
                                                                                 in1=pS[si][:, :], op0=ALU.mult, op1=ALU.add),
                                          [b_al, b_pS[si]], [b_Sb[sbi]])
                                        A(lambda: nc.scalar.activation(out=E[ei][:, :], in_=Sb[sbi][:, :], func=AF.Exp, bias=bias),
                                          [b_Sb[sbi]], [b_E[ei]])
                            j = gi - 2
                            if 0 <= j < ng:
                                eis_ = [2 * (j % 2), 2 * (j % 2) + 1]
                                S.prewait("pe", [b_E[e] for e in eis_] + [b_Vg], [b_pOT[m] for (_, m) in groups[j]])
                                for idx, (kt, m) in enumerate(groups[j]):
                                    ei = eis_[idx]
                                    PE(lambda: nc.tensor.matmul(pOT[m][:, :], lhsT=Vg[:, kt, vcol:vcol + 128], rhs=E[ei][:, :],
                                                                start=(kt == 0), stop=(kt == NSUB - 1)),
                                       [b_E[ei], b_Vg], [b_pOT[m]], signal=(idx == 1))
                            if gi == 4 and pend[0] is not None:
                                pend[0]()
                                pend[0] = None
                        for m in range(nm):
                            if alibi_slope is not None:
                                A(lambda: nc.scalar.activation(out=otS[m][:, :], in_=pOT[m][0:65, :], func=AF.Identity),
                                  [b_pOT[m]], [b_otS[m]])
                            else:
                                V(lambda: nc.vector.tensor_copy(otS[m][:, :], pOT[m][0:65, :]), [b_pOT[m]], [b_otS[m]])
                        pend[0] = (lambda qt=qt: finalize(qt, nm, ycol, ycols))
                    pend[0]()
                    pend[0] = None

                def finalize(qt, nm, ycol, ycols=None):
                    if True:
                        for m in range(nm):
                            for jj in range(4):
                                PE(lambda: nc.tensor.transpose(pO[m][:, jj * 65:(jj + 1) * 65], otS[m][0:65, jj * 128:(jj + 1) * 128],
                                                               ident_f[0:65, 0:65]), [b_otS[m], b_const], [b_pO[m]], signal=(jj == 3))
                        if ycols is not None:
                            for m in range(nm):
                                Om = pO[m][:, 0:260].rearrange("p (j d) -> p j d", d=65)
                                V(lambda: nc.vector.reciprocal(out=rc[:, 4 * m:4 * m + 4], in_=Om[:, :, 64]), [b_pO[m]], [b_rc])
                                V(lambda: nc.vector.tensor_tensor(
                                    out=y_res[:, qt * 4:(qt + 1) * 4, ycols[m]:ycols[m] + 64], in0=Om[:, :, 0:64],
                                    in1=rc[:, 4 * m:4 * m + 4].unsqueeze(2).to_broadcast([128, 4, 64]), op=ALU.mult),
                                  [b_pO[m], b_rc], b_y[qt * 4:(qt + 1) * 4])
                            return
                        O1 = pO[0][:, 0:260].rearrange("p (j d) -> p j d", d=65)
                        ydst = y_res[:, qt * 4:(qt + 1) * 4, ycol:ycol + 64]
                        by = b_y[qt * 4:(qt + 1) * 4]
                        V(lambda: nc.vector.reciprocal(out=rc[:, 0:4], in_=O1[:, :, 64]), [b_pO[0]], [b_rc])
                        if nm == 1:
                            V(lambda: nc.vector.tensor_tensor(out=ydst, in0=O1[:, :, 0:64],
                                                              in1=rc[:, 0:4].unsqueeze(2).to_broadcast([128, 4, 64]),
                                                              op=ALU.mult), [b_pO[0], b_rc], by)
                        else:
                            O2 = pO[1][:, 0:260].rearrange("p (j d) -> p j d", d=65)
                            V(lambda: nc.vector.reciprocal(out=rc[:, 4:8], in_=O2[:, :, 64]), [b_pO[1]], [b_rc])
                            V(lambda: nc.vector.tensor_scalar(out=rc[:, 4:8], in0=rc[:, 4:8], scalar1=lam[:, 5:6], scalar2=None,
                                                              op0=ALU.mult), [b_rc, b_lam], [b_rc])
                            V(lambda: nc.vector.tensor_tensor(out=o1[:, :, :], in0=O1[:, :, 0:64],
                                                              in1=rc[:, 0:4].unsqueeze(2).to_broadcast([128, 4, 64]),
                                                              op=ALU.mult), [b_pO[0], b_rc], [b_o1])
                            V(lambda: nc.vector.tensor_tensor(out=o2[:, :, :], in0=O2[:, :, 0:64],
                                                              in1=rc[:, 4:8].unsqueeze(2).to_broadcast([128, 4, 64]),
                                                              op=ALU.mult), [b_pO[1], b_rc], [b_o2])
                            G(lambda: nc.gpsimd.tensor_tensor(out=o1[:, :, :], in0=o1[:, :, :], in1=o2[:, :, :], op=ALU.add),
                              [b_o1, b_o2], [b_o1])
                            G(lambda: nc.gpsimd.tensor_tensor(out=osq[:, :, :], in0=o1[:, :, :], in1=o1[:, :, :], op=ALU.mult),
                              [b_o1], [b_osq])
                            V(lambda: nc.vector.tensor_reduce(out=rc[:, 8:12], in_=osq[:, :, :], axis=AX.X, op=ALU.add),
                              [b_osq], [b_rc])
                            A(lambda: nc.scalar.activation(out=rc[:, 8:12], in_=rc[:, 8:12], func=AF.Ln, scale=1.0 / 64,
                                                           bias=eps6[:, 0:1]), [b_rc, b_const], [b_rc])
                            A(lambda: nc.scalar.activation(out=rc[:, 12:16], in_=rc[:, 8:12], func=AF.Exp, scale=-0.5),
                              [b_rc], [b_rc])
                            V(lambda: nc.vector.tensor_tensor(out=o2[:, :, :], in0=o1[:, :, :],
                                                              in1=rc[:, 12:16].unsqueeze(2).to_broadcast([128, 4, 64]),
                                                              op=ALU.mult), [b_o1, b_rc], [b_o2])
                            V(lambda: nc.vector.tensor_tensor(out=ydst, in0=o2[:, :, :],
                                                              in1=gA_bc[:, :].unsqueeze(1).to_broadcast([128, 4, 64]),
                                                              op=ALU.mult), [b_o2, b_gA], by)

                load_V(0)
                for h in range(4):
                    state["qi"] += 1
                    i_ = state["qi"] % 4
                    g0 = 2 * (h % 2)
                    S.dma("sp", qT[i_][32 * g0:32 * g0 + 64, :], QKT[h // 2, 32 * g0:32 * g0 + 64, :], writes=[b_qT[i_]])
                    S.dma("sp", kT[i_][32 * g0:32 * g0 + 64, :], QKT[2 + h // 2, 32 * g0:32 * g0 + 64, :], writes=[b_kT[i_]])
                    maps = [(qT[i_], b_qT[i_], kT[i_], b_kT[i_], 32, 32 * (g0 + c_)) for c_ in range(2)]
                    job(maps, h * VP, h * 64, alibi_slope=SLOPES_A[h])
                load_V(1)
                for h in range(4):
                    state["qi"] += 1
                    q_t, bq = load_map(4 + h, 0, 96, True)
                    k_t, bk_ = load_map(8 + h, 0, 96, False)
                    job([(q_t, bq, k_t, bk_, 96)], h * VP, 256 + h * 64)
                load_V(2)
                for j in range(2):
                    state["qi"] += 1
                    i_ = state["qi"] % 4
                    S.dma("sp", qT[i_][:, :], QKT[12 + j, :, :], writes=[b_qT[i_]])
                    S.dma("sp", kT[i_][0:64, :], QKT[14, 64 * j:64 * j + 64, :], writes=[b_kT[i_]])
                    S.dma("sp", kT[i_][64:128, :], QKT[14, 64 * j:64 * j + 64, :], writes=[b_kT[i_]])
                    maps = [(qT[i_], b_qT[i_], kT[i_], b_kT[i_], 64, 64 * c_) for c_ in range(2)]
                    job(maps, j * VP, None, ycols=[512 + (2 * j + c_) * 64 for c_ in range(2)])
                S.barrier()

        def phase_D(l, y_res, b_y):
            with ExitStack() as st:
                tokD = sb(st, "tokD", [128, 8, 512], BF16)
                Vsh = sb(st, "Vsh", [128, 9, VW], BF16)
                QTd = sb(st, "QTd", [128, 2, 1024], BF16)
                KTd = sb(st, "KTd", [128, 2, 1024 + 128], BF16)
                dtab = sb(st, "dtab", [128, 2, 512], F32)
                Sb = [sb(st, f"dSb{i}", [128, 512], F32) for i in range(4)]
                E = [sb(st, f"dE{i}", [128, 512], BF16) for i in range(4)]
                ost = [sb(st, f"ost{i}", [128, 4, 260], F32) for i in range(2)]
                acc_l = [sb(st, f"acc{i}", [128, 260], F32) for i in range(2)]
                od_l = [[sb(st, f"od{j}_{i}", [128, 260], F32) for i in range(3)] for j in range(2)]
                rcd_l = [sb(st, f"rcd{i}", [128, 4], F32) for i in range(2)]
                pT = [pbank(st, f"dT{i}") for i in range(2)]
                pS = [pbank(st, f"dS{i}") for i in range(4)]
                pO = [pbank(st, f"dO{i}") for i in range(2)]
                b_tok, b_Vsh, b_QT, b_KT, b_dt = Buf(), Buf(), Buf(), Buf(), Buf()
                b_Sb = [Buf() for _ in range(4)]
                b_E = [Buf() for _ in range(4)]
                b_ost = [Buf(), Buf()]
                b_pT = [PB(), PB()]
                b_pS = [PB() for _ in range(4)]
                b_pO = [PB(), PB()]
                b_acc_l, b_od_l, b_rcd_l = [Buf(), Buf()], [[Buf() for _ in range(3)] for _ in range(2)], [Buf(), Buf()]
                S.dma("sp", dtab[:], I["dtab"], writes=[b_dt])
                cnt = {"s": 0, "e": 0, "o": 0}
                for bi, d in enumerate(DILS):
                    Ltot = SEQ // d
                    nseg = max(1, Ltot // 1024)
                    for r in range(d):
                        for seg in range(nseg):
                            Lc = min(Ltot, 1024)
                            i0 = seg * 1024
                            nt = Lc // 128
                            Lc_ = min(Ltot, 1024)
                            G(lambda: nc.gpsimd.memset(KTd[:, :, 0:64], 0.0), [], [b_KT])
                            G(lambda: nc.gpsimd.memset(KTd[:, :, 64 + Lc_:128 + Lc_], 0.0), [], [b_KT])
                            if seg * 1024 - 64 < 0:
                                G(lambda: nc.gpsimd.memset(Vsh[0:64, 0, :], 0.0), [], [b_Vsh])
                            if seg * 1024 + Lc_ + 64 > Ltot:
                                G(lambda: nc.gpsimd.memset(Vsh[64:128, Lc_ // 128, :], 0.0), [], [b_Vsh])
                            base = r + d * i0
                            src = DTOK[base: base + d * (Lc - 1) + 1: d, :]
                            S.dma("sp", tokD[:, 0:nt, :], src[:, 0:512].rearrange("(t p) c -> p t c", p=128), writes=[b_tok])
                            lo = i0 - 64
                            hi = i0 + Lc + 64
                            lo_c, hi_c = max(lo, 0), min(hi, Ltot)
                            u0 = lo_c - lo
                            n_rows = hi_c - lo_c
                            pos = 0
                            while pos < n_rows:
                                u = u0 + pos
                                tj, pj = u // 128, u % 128
                                take = min(128 - pj, n_rows - pos)
                                if pj == 0 and take == 128:
                                    nfull = (n_rows - pos) // 128
                                    t_first = r + d * (lo_c + pos)
                                    srcv = DTOK[t_first: t_first + d * (128 * nfull - 1) + 1: d, 512:512 + VW]
                                    S.dma("sp", Vsh[:, tj:tj + nfull, :], srcv.rearrange("(t p) c -> p t c", p=128),
                                          writes=[b_Vsh])
                                    pos += 128 * nfull
                                else:
                                    t_first = r + d * (lo_c + pos)
                                    srcv = DTOK[t_first: t_first + d * (take - 1) + 1: d, 512:512 + VW]
                                    S.dma("sp", Vsh[pj:pj + take, tj, :], srcv, writes=[b_Vsh])
                                    pos += take
                            for t in range(nt):
                                pTb = pT[t % 2][:, :].bitcast(BF16)
                                for j in range(4):
                                    PE(lambda: nc.tensor.transpose(pTb[:, j * 128:(j + 1) * 128], tokD[:, t, j * 128:(j + 1) * 128],
                                                                   ident_b[:]), [b_tok, b_const], [b_pT[t % 2]], signal=(j == 3))
                                V(lambda: nc.vector.tensor_copy(QTd[:, :, t * 128:(t + 1) * 128],
                                                                pTb[:, 0:256].rearrange("p (c t) -> p c t", t=128)),
                                  [b_pT[t % 2]], [b_QT])
                                A(lambda: nc.scalar.activation(func=AF.Identity, out=KTd[:, :, 64 + t * 128: 64 + (t + 1) * 128],
                                                         in_=pTb[:, 256:512].rearrange("p (c t) -> p c t", t=128)),
                                  [b_pT[t % 2]], [b_KT])
                            if nseg > 1:
                                for side in range(2):
                                    hs = i0 - 64 if side == 0 else i0 + Lc
                                    if hs < 0 or hs >= Ltot:
                                        continue
                                    t_first = r + d * hs
                                    srck = DTOK[t_first: t_first + d * 63 + 1: d, 256:512]
                                    S.dma("sp", tokD[0:64, 0, 0:256], srck, writes=[b_tok])
                                    pTb = pT[0][:, :].bitcast(BF16)
                                    for j in range(2):
                                        PE(lambda: nc.tensor.transpose(pTb[:, j * 128: j * 128 + 64],
                                                                       tokD[0:64, 0, j * 128:(j + 1) * 128], ident_b[0:64, 0:64]),
                                           [b_tok, b_const], [b_pT[0]], signal=(j == 1))
                                    col = 0 if side == 0 else 64 + Lc
                                    V(lambda: nc.vector.tensor_copy(
                                        KTd[:, :, col:col + 64],
                                        pTb[:, 0:256].rearrange("p (c t) -> p c t", t=128)[:, :, 0:64]), [b_pT[0]], [b_KT])
                            steps = [(g0, h, ab) for g0 in range(0, nt, 4) for h in range(4) for ab in range(2)]
                            n = len(steps)
                            sis, eis = [0] * n, [0] * n
                            LS, LP = 1, 2
                            for i in range(n + LP):
                                if i < n:
                                    g0, h, ab = steps[i]
                                    ng = min(4, nt - g0)
                                    ch, pr = h // 2, (h % 2) * 64
                                    si = cnt["s"] % 4
                                    cnt["s"] += 1
                                    sis[i] = si
                                    for jj in range(ng):
                                        qc = (g0 + jj) * 128
                                        kc0 = qc + ab * 128
                                        PE(lambda: nc.tensor.matmul(pS[si][:, jj * 128:(jj + 1) * 128],
                                                                    lhsT=KTd[pr:pr + 64, ch, kc0:kc0 + 128],
                                                                    rhs=QTd[pr:pr + 64, ch, qc:qc + 128], start=True, stop=True),
                                           [b_KT, b_QT], [b_pS[si]], signal=(jj == ng - 1))
                                j = i - LS
                                if 0 <= j < n:
                                    g0, h, ab = steps[j]
                                    ng = min(4, nt - g0)
                                    w = ng * 128
                                    si = sis[j]
                                    ei = cnt["e"] % 4
                                    cnt["e"] += 1
                                    eis[j] = ei
                                    V(lambda: nc.vector.scalar_tensor_tensor(out=Sb[ei][:, 0:w], in0=dtab[:, ab, 0:w],
                                                                             scalar=-SLOPES_D[h] * d, in1=pS[si][:, 0:w],
                                                                             op0=ALU.mult, op1=ALU.add),
                                      [b_dt, b_pS[si]], [b_Sb[ei]])
                                    A(lambda: nc.scalar.activation(out=E[ei][:, 0:w], in_=Sb[ei][:, 0:w], func=AF.Exp),
                                      [b_Sb[ei]], [b_E[ei]])
                                j = i - LP
                                if 0 <= j < n:
                                    g0, h, ab = steps[j]
                                    ng = min(4, nt - g0)
                                    ei = eis[j]
                                    oi = (g0 // 4) % 2
                                    for jj in range(ng):
                                        PE(lambda: nc.tensor.matmul(pO[h % 2][:, jj * 65:(jj + 1) * 65],
                                                                    lhsT=E[ei][:, jj * 128:(jj + 1) * 128],
                                                                    rhs=Vsh[:, g0 + jj + ab, h * VP:h * VP + 65],
                                                                    start=(ab == 0 and jj == 0), stop=(ab == 1),
                                                                    skip_group_check=True),
                                           [b_E[ei], b_Vsh], [b_pO[h % 2]], signal=(jj == ng - 1))
                                    if ab == 1:
                                        V(lambda: nc.vector.tensor_copy(ost[oi][:, 0:ng, h * 65:(h + 1) * 65],
                                                                        pO[h % 2][:, 0:ng * 65].rearrange("p (j d) -> p j d", d=65)),
                                          [b_pO[h % 2]], [b_ost[oi]])
                                        if h == 3:
                                            for jj in range(ng):
                                                t_first = r + d * (i0 + (g0 + jj) * 128)
                                                S.dma("sp", OD[bi, t_first: t_first + d * 127 + 1: d, :], ost[oi][:, jj, :],
                                                      reads=[b_ost[oi]])
                S.barrier()
                for sg in range(NSUB):
                    rows = slice(sg * 128, (sg + 1) * 128)
                    acc, od, rcd = acc_l[sg % 2], od_l[sg % 2], rcd_l[sg % 2]
                    b_acc, b_od, b_rcd = b_acc_l[sg % 2], b_od_l[sg % 2], b_rcd_l[sg % 2]
                    for bi in range(3):
                        S.dma("sp", od[bi][:], OD[bi, rows, :], writes=[b_od[bi]])
                    V(lambda: nc.vector.tensor_tensor(out=acc[:], in0=od[0][:], in1=od[1][:], op=ALU.add),
                      [b_od[0], b_od[1]], [b_acc])
                    V(lambda: nc.vector.tensor_tensor(out=acc[:], in0=acc[:], in1=od[2][:], op=ALU.add), [b_acc, b_od[2]], [b_acc])
                    a3 = acc[:, :].rearrange("p (h d) -> p h d", d=65)
                    V(lambda: nc.vector.reciprocal(out=rcd[:, 0:4], in_=a3[:, :, 64]), [b_acc], [b_rcd])
                    V(lambda: nc.vector.tensor_tensor(out=y_res[:, sg, 768:1024].rearrange("p (h d) -> p h d", d=64),
                                                      in0=a3[:, :, 0:64], in1=rcd[:, 0:4].unsqueeze(2).to_broadcast([128, 4, 64]),
                                                      op=ALU.mult), [b_acc, b_rcd], [b_y[sg]])
                S.barrier()

        def layer_norm(v, bv, dst, bdst, g_bc, b_bc, btab, stats, mv, bst, eng2):
            v4 = v.rearrange("p (c f) -> p c f", f=256)
            for c_ in range(4):
                V(lambda: nc.vector.bn_stats(out=stats[:, c_, :], in_=v4[:, c_, :]), [bv], [bst])
            V(lambda: nc.vector.bn_aggr(out=mv[:, 0:2], in_=stats[:, :, :].rearrange("p c f -> p (c f)")), [bst], [bst])
            A(lambda: nc.scalar.activation(out=mv[:, 2:3], in_=mv[:, 1:2], func=AF.Ln, bias=eps5[:, 0:1]), [bst, b_const], [bst])
            A(lambda: nc.scalar.activation(out=mv[:, 3:4], in_=mv[:, 2:3], func=AF.Exp, scale=-0.5), [bst], [bst])
            V(lambda: nc.vector.scalar_tensor_tensor(out=mv[:, 4:5], in0=mv[:, 0:1], scalar=-1.0, in1=mv[:, 3:4], op0=ALU.mult,
                                                     op1=ALU.mult), [bst], [bst])
            A(lambda: nc.scalar.activation(out=v, in_=v, func=AF.Identity, scale=mv[:, 3:4], bias=mv[:, 4:5]), [bv, bst], [bv])
            S.op(eng2, lambda: (nc.gpsimd if eng2 == "pool" else nc.vector).tensor_tensor(out=v, in0=v, in1=g_bc, op=ALU.mult),
                 [bv, btab], [bv])
            S.op(eng2, lambda: (nc.gpsimd if eng2 == "pool" else nc.vector).tensor_tensor(out=dst, in0=v, in1=b_bc, op=ALU.add),
                 [bv, btab], [bdst])

        def phase_O1(l, xsrc, y_res, b_y):
            with ExitStack() as st:
                w_o = sb(st, "w_o", [128, 8, D], BF16)
                gA = sb(st, "gA", [128, D], F32)
                lng = sb(st, "lng", [128, D], F32)
                lnb = sb(st, "lnb", [128, D], F32)
                xs = [sb(st, f"oxs{i}", [128, D], F32) for i in range(2)]
                yT = [sb(st, f"yT{i}", [128, 8, 128], BF16) for i in range(2)]
                v = [sb(st, f"ov{i}", [128, D], F32) for i in range(2)]
                x1 = [sb(st, f"ox1{i}", [128, D], F32) for i in range(2)]
                h2 = [sb(st, f"oh2{i}", [128, 8, 512], BF16) for i in range(2)]
                stats_l = [sb(st, f"ostats{i}", [128, 4, 6], F32) for i in range(2)]
                mv_l = [sb(st, f"omv{i}", [128, 8], F32) for i in range(2)]
                pb = [pbank(st, f"po{i}") for i in range(8)]
                bp = [PB() for _ in range(8)]
                b_w, b_tab, b_stt_l = Buf(), Buf(), [Buf(), Buf()]
                b_xs, b_yT, b_v, b_x1, b_h2 = ([Buf(), Buf()] for _ in range(5))
                wsrc = I["w_o"][l].rearrange("(kc p) n -> p kc n", p=128)
                for kc in range(8):
                    S.dma("pool", w_o[:, kc, :], wsrc[:, kc, :], writes=[b_w])
                S.dma("sp", gA[:], GB[l, 0], writes=[b_tab])
                S.dma("sp", lng[:], I["ln_attn_g"][l, :].partition_broadcast(128), writes=[b_tab])
                S.dma("sp", lnb[:], I["ln_attn_b"][l, :].partition_broadcast(128), writes=[b_tab])
                def o1_sub(sg):
                    T, s = sg // 4, sg % 4
                    i2 = sg % 2
                    tsl = slice(s * 128, (s + 1) * 128)
                    S.dma("sp", xs[i2][:], xsrc[sg * 128:(sg + 1) * 128, :], writes=[b_xs[i2]])
                    tb = pb[0 + i2][:, :].bitcast(BF16)
                    for kc in range(8):
                        PE(lambda: nc.tensor.transpose(tb[:, kc * 128:(kc + 1) * 128], y_res[:, sg, kc * 128:(kc + 1) * 128],
                                                       ident_b[:]), [b_y[sg], b_const], [bp[i2]], signal=(kc == 7))
                    yield
                    V(lambda: nc.vector.tensor_copy(yT[i2][:, :, :], tb[:, 0:1024].rearrange("p (c t) -> p c t", t=128)),
                      [bp[i2]], [b_yT[i2]])
                    for hf in range(2):
                        yield
                        bk = 2 + 2 * i2 + hf
                        for kc in range(8):
                            PE(lambda: nc.tensor.matmul(pb[bk][:, :], lhsT=yT[i2][:, kc, :], rhs=w_o[:, kc, hf * 512:(hf + 1) * 512],
                                                        start=(kc == 0), stop=(kc == 7)), [b_yT[i2], b_w], [bp[bk]], signal=(kc == 7))
                        hs = slice(hf * 512, (hf + 1) * 512)
                        V(lambda: nc.vector.tensor_tensor(out=v[i2][:, hs], in0=pb[bk][:, :], in1=gA[:, hs], op=ALU.mult),
                          [bp[bk], b_tab], [b_v[i2]])
                    yield
                    V(lambda: nc.vector.scalar_tensor_tensor(out=v[i2][:, :], in0=xs[i2][:, :], scalar=ALPHA,
                                                             in1=v[i2][:, :], op0=ALU.mult, op1=ALU.add),
                      [b_xs[i2], b_v[i2]], [b_v[i2]])
                    yield
                    layer_norm(v[i2][:, :], b_v[i2], x1[i2][:, :], b_x1[i2], lng[:, :], lnb[:, :], b_tab, stats_l[i2], mv_l[i2], b_stt_l[i2], "pool")
                    yield
                    S.dma("sp", X1[sg * 128:(sg + 1) * 128, :], x1[i2][:], reads=[b_x1[i2]])
                    for hf in range(2):
                        yield
                        bk = 6 + hf
                        for cc in range(4):
                            kc = hf * 4 + cc
                            PE(lambda: nc.tensor.transpose(pb[bk][:, cc * 128:(cc + 1) * 128], x1[i2][:, kc * 128:(kc + 1) * 128],
                                                           ident_f[:]), [b_x1[i2], b_const], [bp[bk]], signal=(cc == 3))
                        for cc in range(4):
                            kc = hf * 4 + cc
                            if cc % 2 == 0:
                                A(lambda: nc.scalar.activation(out=h2[T % 2][:, kc, tsl], in_=pb[bk][:, cc * 128:(cc + 1) * 128],
                                                               func=AF.Identity, scale=modT[:, l, 32 + kc:33 + kc],
                                                               bias=modT[:, l, 24 + kc:25 + kc]), [bp[bk], b_modT], [b_h2[T % 2]])
                            else:
                                V(lambda: nc.vector.tensor_scalar(out=h2[T % 2][:, kc, tsl], in0=pb[bk][:, cc * 128:(cc + 1) * 128],
                                                                  scalar1=modT[:, l, 32 + kc:33 + kc],
                                                                  scalar2=modT[:, l, 24 + kc:25 + kc], op0=ALU.mult, op1=ALU.add),
                                  [bp[bk], b_modT], [b_h2[T % 2]])
                    if s == 3:
                        S.dma("sp", H2T[:, :, T * 512:(T + 1) * 512].rearrange("c r t -> r c t"), h2[T % 2][:, :, :],
                              reads=[b_h2[T % 2]])
                for sg0 in range(0, NSUB, 2):
                    gens = [o1_sub(sg0), o1_sub(sg0 + 1)]
                    while gens:
                        for g_ in list(gens):
                            try:
                                next(g_)
                            except StopIteration:
                                gens.remove(g_)
                S.barrier()

        def phase_O2(l, dst):
            with ExitStack() as st:
                w_up = sb(st, "w_up", [128, 8, HID], BF16)
                w_dn = sb(st, "w_dn", [128, 32, D], BF16)
                gM = sb(st, "gM", [128, D], F32)
                lng = sb(st, "lng2", [128, D], F32)
                lnb = sb(st, "lnb2", [128, D], F32)
                h2 = [sb(st, f"mh2{i}", [128, 8, 512], BF16) for i in range(1)]
                uT = sb(st, "uT", [128, 32, 512], BF16)
                rr = [sb(st, f"rr{i}", [128, 512], F32) for i in range(3)]
                x1 = [sb(st, f"mx1{i}", [128, D], F32) for i in range(2)]
                v = [sb(st, f"mv{i}", [128, D], F32) for i in range(2)]
                stats = sb(st, "mstats", [128, 4, 6], F32)
                mv = sb(st, "mmv", [128, 8], F32)
                pb = [pbank(st, f"pm{i}") for i in range(8)]
                bp = [PB() for _ in range(8)]
                b_wu, b_wd, b_tab, b_stt, b_uT = Buf(), Buf(), Buf(), Buf(), Buf()
                b_h2, b_x1, b_v = ([Buf(), Buf()] for _ in range(3))
                b_rr = [Buf() for _ in range(3)]
                usrc = I["w_up"][l].rearrange("(kc p) n -> p kc n", p=128)
                b_wu_l = [Buf() for _ in range(8)]
                b_wd_l = [Buf() for _ in range(8)]
                for cb in range(8):
                    S.dma("pool", w_up[:, :, cb * 512:(cb + 1) * 512], usrc[:, :, cb * 512:(cb + 1) * 512], writes=[b_wu_l[cb]])
                dsrc = I["w_down"][l].rearrange("(kc p) n -> p kc n", p=128)
                for k4 in range(8):
                    S.dma("pool", w_dn[:, k4 * 4:(k4 + 1) * 4, :], dsrc[:, k4 * 4:(k4 + 1) * 4, :], writes=[b_wd_l[k4]])
                S.dma("sp", gM[:], GB[l, 1], writes=[b_tab])
                S.dma("sp", lng[:], I["ln_mlp_g"][l, :].partition_broadcast(128), writes=[b_tab])
                S.dma("sp", lnb[:], I["ln_mlp_b"][l, :].partition_broadcast(128), writes=[b_tab])
                ri = 0
                for T in range(8):
                    h_t, bh = h2[0], b_h2[0]
                    S.dma("sp", h_t[:, :, :], H2T[:, :, T * 512:(T + 1) * 512].rearrange("c r t -> r c t"), writes=[bh])
                    for hc in range(32):
                        bk = hc % 4
                        for kc in range(8):
                            PE(lambda: nc.tensor.matmul(pb[bk][:, :], lhsT=w_up[:, kc, hc * 128:(hc + 1) * 128], rhs=h_t[:, kc, :],
                                                        start=(kc == 0), stop=(kc == 7)), [b_wu_l[hc // 4], bh], [bp[bk]], signal=(kc == 7))
                        r_, br = rr[ri % 3], b_rr[ri % 3]
                        ri += 1
                        A(lambda: nc.scalar.activation(out=r_[:, :], in_=pb[bk][:, :], func=AF.Relu), [bp[bk]], [br])
                        if hc % 2 == 0:
                            V(lambda: nc.vector.tensor_tensor(out=uT[:, hc, :], in0=r_[:, :], in1=r_[:, :], op=ALU.mult), [br], [b_uT])
                        else:
                            G(lambda: nc.gpsimd.tensor_tensor(out=uT[:, hc, :], in0=r_[:, :], in1=r_[:, :], op=ALU.mult), [br], [b_uT])
                    for s in range(4):
                        sg = T * 4 + s
                        i2 = sg % 2
                        rows = slice(sg * 128, (sg + 1) * 128)
                        S.dma("sp", x1[i2][:], X1[rows, :], writes=[b_x1[i2]])
                        for hf in range(2):
                            bk = 4 + 2 * i2 + hf
                            for hc in range(32):
                                PE(lambda: nc.tensor.matmul(pb[bk][:, :], lhsT=uT[:, hc, s * 128:(s + 1) * 128],
                                                            rhs=w_dn[:, hc, hf * 512:(hf + 1) * 512], start=(hc == 0), stop=(hc == 31)),
                                   [b_uT, b_wd_l[hc // 4]], [bp[bk]], signal=(hc == 31))
                            hs = slice(hf * 512, (hf + 1) * 512)
                            V(lambda: nc.vector.tensor_tensor(out=v[i2][:, hs], in0=pb[bk][:, :], in1=gM[:, hs], op=ALU.mult),
                              [bp[bk], b_tab], [b_v[i2]])
                        V(lambda: nc.vector.scalar_tensor_tensor(out=v[i2][:, :], in0=x1[i2][:, :], scalar=ALPHA, in1=v[i2][:, :],
                                                                 op0=ALU.mult, op1=ALU.add), [b_x1[i2], b_v[i2]], [b_v[i2]])
                        layer_norm(v[i2][:, :], b_v[i2], v[i2][:, :], b_v[i2], lng[:, :], lnb[:, :], b_tab, stats, mv, b_stt, "pool")
                        S.dma("sp", dst[rows, :], v[i2][:], reads=[b_v[i2]])
                S.barrier()

        for l in range(nlayers):
            if "stop0" in dbg:
                break
            xsrc = I["x"] if l == 0 else XN
            phase_P(l, xsrc)
            if "stopP" in dbg:
                break
            with ExitStack() as lst:
                y_res = sb(lst, f"y_res{l}", [128, NSUB, D], BF16)
                b_y = [Buf(f"y{i}") for i in range(NSUB)]
                phase_att(l, y_res, b_y)
                phase_D(l, y_res, b_y)
                if YDBG is not None and l == 0:
                    for sg in range(NSUB):
                        S.dma("sp", YDBG[sg * 128:(sg + 1) * 128, :], y_res[:, sg, :], reads=[b_y[sg]])
                    S.barrier()
                if "stopA" in dbg:
                    break
                phase_O1(l, xsrc, y_res, b_y)
            phase_O2(l, out if l == nlayers - 1 else XN)
        S.barrier()
        print("ops", S.n_ops, "dmas", S.n_dma, "sems", S.nsem)
    return nc


DBG = {}
_CONSTS = None


def make_in_maps(inputs):
    global _CONSTS
    if _CONSTS is None:
        _CONSTS = _host_consts()
    shared = {}
    for n in W_NAMES:
        a = np.ascontiguousarray(np.asarray(inputs[n], dtype=np.float32))
        shared[n] = a.reshape(W_SHAPES[n])
    for n, a in _CONSTS.items():
        shared["k_" + n] = a
    x = np.asarray(inputs["x"], dtype=np.float32)
    c = np.asarray(inputs["c"], dtype=np.float32)
    maps = []
    for b in range(8):
        m = dict(shared)
        m["x"] = np.ascontiguousarray(x[b])
        m["c"] = np.ascontiguousarray(c[b].reshape(8, 128).T)
        maps.append(m)
    return maps


def kernel(**inputs):
    nc = build()
    in_maps = make_in_maps(inputs)
    res = run_bass_kernel_spmd(nc, in_maps, core_ids=list(range(8)))
    return np.stack([np.asarray(r["out"]) for r in res.results], axis=0).astype(np.float32)
```
